# Optimizing a Trainium2 kernel written in Bass

```python
import jax
import jax.numpy as jnp
from jax import lax
import numpy as np


D_MODEL = 1024
BATCH = 16
SEQ = 4096
DEPTH = 2
DEC_BATCH = 32
DEC_SEQ = 32
PAST_LEN = 2048

CHUNK = 64
PLE_DIM = 256
D_BRANCH = D_MODEL // 2
CONV_A = 3
CONV_B = 31
C_HEADS = 4
C_CHUNK = 128
D_HEADS = 8
D_HEAD_DIM = D_BRANCH // D_HEADS
Q_BLOCK = 128
N_AB_LAYERS = (DEPTH + 1) // 2
N_CD_LAYERS = DEPTH // 2
AB_IN = 7 * D_BRANCH
CD_IN = 7 * D_BRANCH
EPS = 1e-6

kernel_name = 'hybrid_conv_sgmlp_stickbreak_stream_step'


def _rmsnorm(x, g):
    x32 = x.astype(jnp.float32)
    y = x32 * lax.rsqrt(jnp.mean(x32 * x32, axis=-1, keepdims=True) + EPS)
    return (y * g.astype(jnp.float32)).astype(x.dtype)


def _layernorm(x, g, b):
    x32 = x.astype(jnp.float32)
    mu = jnp.mean(x32, axis=-1, keepdims=True)
    xc = x32 - mu
    y = xc * lax.rsqrt(jnp.mean(xc * xc, axis=-1, keepdims=True) + EPS)
    return (y * g.astype(jnp.float32) + b.astype(jnp.float32)).astype(x.dtype)


def _causal_dwconv(u, hist, w):
    full = jnp.concatenate([hist.astype(u.dtype), u], axis=1)
    out = lax.conv_general_dilated(
        full, w[:, None, :].astype(u.dtype), window_strides=(1,), padding='VALID',
        dimension_numbers=('NWC', 'WIO', 'NWC'), feature_group_count=u.shape[-1])
    return out, full[:, -(w.shape[0] - 1):]


def _chunk_mix(v, ws, bias):
    bn, t, _ = v.shape
    length = min(t, C_CHUNK)
    n = t // length
    vh = v.reshape(bn, n, length, C_HEADS, D_BRANCH // C_HEADS)
    w = jnp.tril(ws[:, :length, :length]).astype(v.dtype)
    b = jnp.transpose(bias[:, :length])[:, :, None].astype(v.dtype)
    mixed = jnp.einsum('hts,bnshc->bnthc', w, vh) + b
    return mixed.reshape(bn, t, D_BRANCH)


def _sb_block(q, q_pos, k, v, k_pos):
    z = jnp.einsum('bqhd,bkhd->bhqk', q, k).astype(jnp.float32) * (D_HEAD_DIM ** -0.5)
    mask = k_pos[None, :] < q_pos[:, None]
    log_keep = jnp.where(mask, jax.nn.log_sigmoid(-z), 0.0)
    log_w = jax.nn.log_sigmoid(z) + lax.cumsum(log_keep, axis=3, reverse=True) - log_keep
    a = jnp.where(mask, jnp.exp(log_w), 0.0)
    return jnp.einsum('bhqk,bkhd->bqhd', a.astype(v.dtype), v)


def _sb_attention(q, k, v, q_start):
    tq = q.shape[1]
    k_pos = jnp.arange(k.shape[1], dtype=jnp.int32)
    q_pos = q_start + jnp.arange(tq, dtype=jnp.int32)
    if tq > Q_BLOCK and tq % Q_BLOCK == 0:
        nb = tq // Q_BLOCK
        qb = jnp.moveaxis(q.reshape(q.shape[0], nb, Q_BLOCK, D_HEADS, D_HEAD_DIM), 1, 0)
        pb = q_pos.reshape(nb, Q_BLOCK)
        ob = lax.map(lambda qp: _sb_block(qp[0], qp[1], k, v, k_pos), (qb, pb))
        return jnp.moveaxis(ob, 0, 1).reshape(q.shape)
    return _sb_block(q, q_pos, k, v, k_pos)


def _ab_mixer(xn, hist_a, hist_b, w_in, a_conv_w, b_conv_w, b_ln_g, b_ln_b, w_out):
    proj = xn @ w_in
    a_x, a_c, a_b, a_z, b_val, b_glu, b_z = jnp.split(proj, 7, axis=-1)
    a_conv, new_a = _causal_dwconv(a_c * a_x, hist_a, a_conv_w)
    a_out = a_b * a_conv * jax.nn.silu(a_z)
    b_conv, new_b = _causal_dwconv(b_val * jax.nn.sigmoid(b_glu), hist_b, b_conv_w)
    b_out = jax.nn.silu(_layernorm(b_conv, b_ln_g, b_ln_b)) * jax.nn.silu(b_z)
    y = jnp.concatenate([a_out, b_out], axis=-1) @ w_out
    return y, new_a, new_b


def _cd_mixer(xn, k_past, v_past, q_start, w_in, c_ln_g, c_ln_b, c_ws, c_b, w_out):
    bn, t, _ = xn.shape
    proj = xn @ w_in
    c_u, c_v, c_z, d_q, d_k, d_v, d_z = jnp.split(proj, 7, axis=-1)
    c_vn = _layernorm(c_v, c_ln_g, c_ln_b)
    c_out = c_u * _chunk_mix(c_vn, c_ws, c_b) * jax.nn.silu(c_z)
    q = d_q.reshape(bn, t, D_HEADS, D_HEAD_DIM)
    k = d_k.reshape(bn, t, D_HEADS, D_HEAD_DIM)
    v = d_v.reshape(bn, t, D_HEADS, D_HEAD_DIM)
    if k_past is None:
        k_all, v_all = k, v
    else:
        k_all = jnp.concatenate([k_past.astype(k.dtype), k], axis=1)
        v_all = jnp.concatenate([v_past.astype(v.dtype), v], axis=1)
    o = _sb_attention(q, k_all, v_all, q_start).reshape(bn, t, D_BRANCH)
    d_out = o * jax.nn.silu(d_z)
    y = jnp.concatenate([c_out, d_out], axis=-1) @ w_out
    return y, c_vn, k, v


def _trunk(h, p, a_hist, b_hist, k_past, v_past, q_start, norm_g, ple_gate, ple_proj,
           ab_w_in, a_conv_w, b_conv_w, b_ln_g, b_ln_b, ab_w_out,
           cd_w_in, c_ln_g, c_ln_b, c_ws, c_b, cd_w_out, final_g):
    new_a, new_b, new_cv, new_k, new_v = [], [], [], [], []
    for i in range(DEPTH):
        j = i // 2
        xn = _rmsnorm(h, norm_g[i])
        if i % 2 == 0:
            y, sa, sb = _ab_mixer(xn, a_hist[j], b_hist[j], ab_w_in[j], a_conv_w[j],
                                  b_conv_w[j], b_ln_g[j], b_ln_b[j], ab_w_out[j])
            new_a.append(sa)
            new_b.append(sb)
        else:
            kp = None if k_past is None else k_past[j]
            vp = None if v_past is None else v_past[j]
            y, cv, kn, vn = _cd_mixer(xn, kp, vp, q_start, cd_w_in[j], c_ln_g[j], c_ln_b[j],
                                      c_ws[j], c_b[j], cd_w_out[j])
            new_cv.append(cv)
            new_k.append(kn)
            new_v.append(vn)
        h = h + y
        h = h + jax.nn.sigmoid(h @ ple_gate[i]) * (p[i].astype(h.dtype) @ ple_proj[i])
    return _rmsnorm(h, final_g), new_a, new_b, new_cv, new_k, new_v


def setup_inputs(seed: int = 0) -> dict:
    key = jax.random.key(seed)
    ks = jax.random.split(key, 32)
    f32 = jnp.float32

    def nrm(k, shape, scale=1.0):
        return jax.random.normal(k, shape, f32) * scale

    return {
        'x_prompt': nrm(ks[0], (BATCH, SEQ, D_MODEL)),
        'x_sample': nrm(ks[1], (DEC_BATCH, DEC_SEQ, D_MODEL)),
        'state_a_conv': nrm(ks[2], (N_AB_LAYERS, DEC_BATCH, CONV_A - 1, D_BRANCH)),
        'state_b_conv': nrm(ks[3], (N_AB_LAYERS, DEC_BATCH, CONV_B - 1, D_BRANCH)),
        'cache_d_k': nrm(ks[4], (N_CD_LAYERS, DEC_BATCH, PAST_LEN, D_HEADS, D_HEAD_DIM)),
        'cache_d_v': nrm(ks[5], (N_CD_LAYERS, DEC_BATCH, PAST_LEN, D_HEADS, D_HEAD_DIM)),
        'p_prompt': nrm(ks[6], (DEPTH, BATCH, SEQ, PLE_DIM)),
        'p_sample': nrm(ks[7], (DEPTH, DEC_BATCH, DEC_SEQ, PLE_DIM)),
        'norm_g': 1.0 + nrm(ks[8], (DEPTH, D_MODEL), 0.02),
        'ple_gate': nrm(ks[9], (DEPTH, D_MODEL, D_MODEL), D_MODEL ** -0.5),
        'ple_proj': nrm(ks[10], (DEPTH, PLE_DIM, D_MODEL), PLE_DIM ** -0.5),
        'ab_w_in': nrm(ks[11], (N_AB_LAYERS, D_MODEL, AB_IN), D_MODEL ** -0.5),
        'a_conv_w': nrm(ks[12], (N_AB_LAYERS, CONV_A, D_BRANCH), CONV_A ** -0.5),
        'b_conv_w': nrm(ks[13], (N_AB_LAYERS, CONV_B, D_BRANCH), CONV_B ** -0.5),
        'b_ln_g': 1.0 + nrm(ks[14], (N_AB_LAYERS, D_BRANCH), 0.02),
        'b_ln_b': nrm(ks[15], (N_AB_LAYERS, D_BRANCH), 0.02),
        'ab_w_out': nrm(ks[16], (N_AB_LAYERS, 2 * D_BRANCH, D_MODEL), (2 * D_BRANCH) ** -0.5),
        'cd_w_in': nrm(ks[17], (N_CD_LAYERS, D_MODEL, CD_IN), D_MODEL ** -0.5),
        'c_ln_g': 1.0 + nrm(ks[18], (N_CD_LAYERS, D_BRANCH), 0.02),
        'c_ln_b': nrm(ks[19], (N_CD_LAYERS, D_BRANCH), 0.02),
        'c_ws': nrm(ks[20], (N_CD_LAYERS, C_HEADS, C_CHUNK, C_CHUNK), C_CHUNK ** -0.5),
        'c_b': 1.0 + nrm(ks[21], (N_CD_LAYERS, C_HEADS, C_CHUNK), 0.1),
        'cd_w_out': nrm(ks[22], (N_CD_LAYERS, 2 * D_BRANCH, D_MODEL), (2 * D_BRANCH) ** -0.5),
        'final_g': 1.0 + nrm(ks[23], (D_MODEL,), 0.02),
    }


def reference(x_prompt, x_sample, state_a_conv, state_b_conv, cache_d_k, cache_d_v,
              p_prompt, p_sample, norm_g, ple_gate, ple_proj, ab_w_in, a_conv_w, b_conv_w,
              b_ln_g, b_ln_b, ab_w_out, cd_w_in, c_ln_g, c_ln_b, c_ws, c_b, cd_w_out, final_g):
    bp = x_prompt.shape[0]
    a_hist0 = jnp.zeros((N_AB_LAYERS, bp, CONV_A - 1, D_BRANCH), x_prompt.dtype)
    b_hist0 = jnp.zeros((N_AB_LAYERS, bp, CONV_B - 1, D_BRANCH), x_prompt.dtype)
    y_prompt, pa, pb, _, pk, pv = _trunk(
        x_prompt, p_prompt, a_hist0, b_hist0, None, None, 0, norm_g, ple_gate, ple_proj,
        ab_w_in, a_conv_w, b_conv_w, b_ln_g, b_ln_b, ab_w_out,
        cd_w_in, c_ln_g, c_ln_b, c_ws, c_b, cd_w_out, final_g)
    y_sample, sa, sb, scv, sk, sv = _trunk(
        x_sample, p_sample, state_a_conv, state_b_conv, cache_d_k, cache_d_v, PAST_LEN,
        norm_g, ple_gate, ple_proj, ab_w_in, a_conv_w, b_conv_w, b_ln_g, b_ln_b, ab_w_out,
        cd_w_in, c_ln_g, c_ln_b, c_ws, c_b, cd_w_out, final_g)
    return (y_prompt, y_sample, jnp.stack(pa), jnp.stack(sa), jnp.stack(pb), jnp.stack(sb),
            jnp.stack(scv), jnp.stack(pk), jnp.stack(pv), jnp.stack(sk), jnp.stack(sv))
```

```python
import contextlib
import numpy as np
import concourse.bass as bass
import concourse.mybir as mybir
from concourse.bass_utils import run_bass_kernel_spmd

F32 = mybir.dt.float32
BF16 = mybir.dt.bfloat16
AF = mybir.ActivationFunctionType
ALU = mybir.AluOpType

ENGINES = ["pe", "act", "dve", "pool", "sp"]
D = 1024
DB = 512
PLE = 256
EPS = 1e-6
HB = 30
HA = 2


class Prog:
    EPOCH = 12000

    def __init__(self, nc, n_dma_slots=48):
        self.nc = nc
        self.ops = {e: [] for e in ENGINES}
        self.cnt = {e: 0 for e in ENGINES}
        self.last_w = {}
        self.readers = {}
        self.seen = {e: {} for e in ENGINES}
        self.n_dma_slots = n_dma_slots
        self.dma_next = 0
        self.dma_val = [0] * n_dma_slots
        self.out_tokens = []
        self.bank_i = 0

    def _need(self, eng, tok, waits):
        if tok is None:
            return
        if tok[0] == "E":
            _, e2, idx = tok
            if e2 == eng and eng == "pe":
                return
            k = ("E", e2)
        else:
            _, slot, idx = tok
            k = ("D", slot)
        if self.seen[eng].get(k, -1) >= idx:
            return
        waits[k] = max(waits.get(k, -1), idx)

    def op(self, eng, name, R=(), W=(), dma=False, is_output=False, **kw):
        waits = {}
        for k in R:
            self._need(eng, self.last_w.get(k), waits)
        for k in W:
            self._need(eng, self.last_w.get(k), waits)
            for t in self.readers.get(k, ()):
                self._need(eng, t, waits)
        if dma:
            slot = self.dma_next
            self.dma_next = (self.dma_next + 1) % self.n_dma_slots
            prev = self.dma_val[slot]
            if prev > 0:
                self._need(eng, ("D", slot, prev), waits)
            self.dma_val[slot] = prev + 16
            tok = ("D", slot, prev + 16)
        else:
            idx = self.cnt[eng]
            self.cnt[eng] += 1
            tok = ("E", eng, idx)
        for k, v in waits.items():
            self.seen[eng][k] = v
        self.ops[eng].append((name, kw, waits, tok))
        for k in W:
            self.last_w[k] = tok
            self.readers[k] = []
        for k in R:
            if k in W:
                continue
            self.readers.setdefault(k, []).append(tok)
        if is_output:
            self.out_tokens.append(tok)
        return tok

    def pe(self, name, **kw):
        return self.op("pe", name, **kw)

    def act(self, name, **kw):
        return self.op("act", name, **kw)

    def dve(self, name, **kw):
        return self.op("dve", name, **kw)

    def pool(self, name, **kw):
        return self.op("pool", name, **kw)

    def dma(self, **kw):
        return self.op("sp", "dma_start", dma=True, **kw)

    def barrier(self):
        for eng in ENGINES:
            waits = {}
            for e2 in ENGINES:
                if e2 != eng and self.cnt[e2] > 0:
                    self._need(eng, ("E", e2, self.cnt[e2] - 1), waits)
            for slot in range(self.n_dma_slots):
                if self.dma_val[slot] > 0:
                    self._need(eng, ("D", slot, self.dma_val[slot]), waits)
            for k, v in waits.items():
                self.seen[eng][k] = v
            self.ops[eng].append((None, None, waits, None))

    def emit(self):
        nc = self.nc
        with contextlib.ExitStack() as st:
            esem = {}
            for e in ENGINES:
                n_ep = (self.cnt[e] + self.EPOCH - 1) // self.EPOCH
                esem[e] = [st.enter_context(nc.semaphore(f"s_{e}{i}")) for i in range(max(n_ep, 1))]
            dsem = [st.enter_context(nc.semaphore(f"s_d{i}")) for i in range(self.n_dma_slots)]
            block = st.enter_context(nc.Block())

            def do_wait(h, k, v):
                if k[0] == "E":
                    h.wait_ge(esem[k[1]][v // self.EPOCH], v % self.EPOCH + 1)
                else:
                    h.wait_ge(dsem[k[1]], v)

            def run(ename):
                def body(h):
                    for name, kw, waits, tok in self.ops[ename]:
                        for k, v in waits.items():
                            do_wait(h, k, v)
                        if name is None:
                            continue
                        ins = getattr(h, name)(**kw)
                        if tok[0] == "E":
                            ins.then_inc(esem[tok[1]][tok[2] // self.EPOCH], 1)
                        else:
                            ins.then_inc(dsem[tok[1]], 16)
                    if ename == "sp":
                        for tok in self.out_tokens:
                            do_wait(h, ("D", tok[1]), tok[2])
                return body

            block.tensor(run("pe"))
            block.scalar(run("act"))
            block.vector(run("dve"))
            block.gpsimd(run("pool"))
            block.sync(run("sp"))


class Arena:
    def __init__(self, nc, nbytes):
        self.t = nc.alloc_sbuf_tensor("arena", [128, nbytes // 4], F32)
        self.off = 0
        self.cap = nbytes // 4

    def alloc(self, shape, dt=F32):
        n = int(np.prod(shape[1:]))
        nw = (n if dt == F32 else (n + 1) // 2)
        nw = (nw + 7) // 8 * 8
        assert self.off + nw <= self.cap, f"SBUF arena overflow: need {(self.off + nw) * 4} B"
        ap = self.t[0:shape[0], self.off:self.off + nw]
        self.off += nw
        if dt != F32:
            ap = ap.bitcast(dt)
        ap = ap[:, 0:n]
        if len(shape) == 3:
            ap = ap.rearrange("p (a b) -> p a b", a=shape[1])
        elif len(shape) == 4:
            ap = ap.rearrange("p (a b c) -> p a b c", a=shape[1], b=shape[2])
        return ap


class Cfg:
    def __init__(self, NP=2, S=4096, NS=4, DS=32, PL=2048, passes=(1, 2, 3), debug=False, stop=0):
        self.NP, self.S, self.NS, self.DS, self.PL = NP, S, NS, DS, PL
        self.passes = passes
        self.debug = debug
        self.stop = stop
        assert NS * DS == 128 and S % 512 == 0 and PL % 128 == 0
        self.NTOK = NP * S + 128


class Tile:
    def __init__(self, kind, seq, t0, T, nseg, L, row0, last):
        self.kind, self.seq, self.t0, self.T, self.nseg, self.L = kind, seq, t0, T, nseg, L
        self.row0 = row0
        self.last = last
        self.NT = T // 128


def make_tiles(cfg):
    tiles = []
    for q in range(cfg.NP):
        n = cfg.S // 512
        for i in range(n):
            tiles.append(Tile("p", q, i * 512, 512, 1, 512, q * cfg.S + i * 512, i == n - 1))
    tiles.append(Tile("s", 0, 0, 128, cfg.NS, cfg.DS, cfg.NP * cfg.S, True))
    return tiles


def build_program(cfg):
    nc = bass.Bass("TRN2", target_bir_lowering=False)
    NP, S, NS, DS, PL = cfg.NP, cfg.S, cfg.NS, cfg.DS, cfg.PL
    NTOK = cfg.NTOK

    def din(name, shape):
        return nc.dram_tensor(name, list(shape), F32, kind="ExternalInput").ap()

    def dout(name, shape):
        return nc.dram_tensor(name, list(shape), F32, kind="ExternalOutput").ap()

    x_p = din("x_prompt", [NP * S, D])
    x_s = din("x_sample", [128, D])
    st_a = din("state_a_conv", [NS, HA, DB])
    st_b = din("state_b_conv", [NS, HB, DB])
    ck = din("cache_d_k", [NS, PL, DB])
    cv = din("cache_d_v", [NS, PL, DB])
    p_p = din("p_prompt", [2, NP * S, PLE])
    p_s = din("p_sample", [2, 128, PLE])
    norm_g = din("norm_g", [2, D])
    ple_gate = din("ple_gate", [2, D, D])
    ple_proj = din("ple_proj", [2, PLE, D])
    ab_w_in = din("ab_w_in", [D, 7 * DB])
    a_conv_w = din("a_conv_w", [3, DB])
    b_conv_w = din("b_conv_w", [31, DB])
    b_ln_g = din("b_ln_g", [DB])
    b_ln_b = din("b_ln_b", [DB])
    ab_w_out = din("ab_w_out", [D, D])
    cd_w_in = din("cd_w_in", [D, 7 * DB])
    c_ln_g = din("c_ln_g", [DB])
    c_ln_b = din("c_ln_b", [DB])
    c_ws = din("c_ws", [4, 128, 128])
    c_b = din("c_b", [4, 128])
    cd_w_out = din("cd_w_out", [D, D])
    final_g = din("final_g", [D])

    y_p = dout("y_prompt", [NP * S, D])
    y_s = dout("y_sample", [128, D])
    na_p = dout("new_a_prompt", [NP, HA, DB])
    na_s = dout("new_a_sample", [NS, HA, DB])
    nb_p = dout("new_b_prompt", [NP, HB, DB])
    nb_s = dout("new_b_sample", [NS, HB, DB])
    ncv_s = dout("new_cv_sample", [128, DB])
    nk_p = dout("new_k_prompt", [NP * S, DB])
    nv_p = dout("new_v_prompt", [NP * S, DB])
    nk_s = dout("new_k_sample", [128, DB])
    nv_s = dout("new_v_sample", [128, DB])

    kind_scr = "ExternalOutput" if cfg.debug else "Internal"
    hA = nc.dram_tensor("hA", [NTOK, D], F32, kind=kind_scr).ap()
    catD = nc.dram_tensor("catD", [D, NTOK], BF16, kind=kind_scr).ap()

    P = Prog(nc)
    tiles = make_tiles(cfg)
    AR = Arena(nc, 207 * 1024)

    psf = [nc.alloc_psum_tensor(f"ps{i}", [128, 512], F32)[:] for i in range(8)]
    psb = [p.bitcast(BF16) for p in psf]

    def bank():
        b = P.bank_i
        P.bank_i = (P.bank_i + 1) % 6
        return b

    def PS(b):
        return ("ps", b)

    identf = AR.alloc([128, 128])
    identb = AR.alloc([128, 128], BF16)
    onesf = AR.alloc([128, 128])
    ngcol = AR.alloc([128, 2, 8])
    P.pool("memset", W=["identf"], ap=identf, constant=0.0)
    P.pool("affine_select", R=["identf"], W=["identf"], out=identf, in_=identf, pattern=[[-1, 128]],
           compare_op=ALU.not_equal, fill=1.0, base=0, channel_multiplier=1)
    P.dve("tensor_copy", R=["identf"], W=["identb"], out=identb, in_=identf)
    P.pool("memset", W=["onesf"], ap=onesf, constant=1.0)
    import os as _os
    for _i in range(int(_os.environ.get('KDUMMY', '0'))):
        P.dve("tensor_copy", R=["identf"], W=["identb"], out=identb, in_=identf)
    for i in range(2):
        P.dma(W=["ngcol"], out=ngcol[:, i, :], in_=norm_g[i].rearrange("(kc p) -> p kc", p=128),
              allow_slow_non_contiguous=True)
    base_off = AR.off

    def x_rows(t):
        return x_p[t.row0:t.row0 + t.T, :] if t.kind == "p" else x_s

    def p_rows(t, layer):
        return p_p[layer, t.row0:t.row0 + t.T, :] if t.kind == "p" else p_s[layer]

    class Env:
        pass

    def common_bufs(n_h=2, n_tmp=6):
        E = Env()
        E.hbuf = [AR.alloc([128, 4, D]) for _ in range(n_h)]
        E.xnb = [AR.alloc([128, D], BF16) for _ in range(2)]
        E.actT = AR.alloc([128, 8, 512], BF16)
        E.catT = AR.alloc([128, 8, 512], BF16)
        E.ss = AR.alloc([128, 4])
        E.rstd = AR.alloc([128, 4])
        E.ftmp = [AR.alloc([128, 512]) for _ in range(n_tmp)]
        E.ftmp_i = 0
        E.cast_i = 0
        return E

    def tmp(E):
        i = E.ftmp_i
        E.ftmp_i = (i + 1) % len(E.ftmp)
        return E.ftmp[i], ("ftmp", i)

    def load_weight(E, dst, dkey, src, nk, ncols, gain_i=None):
        for kc in range(nk):
            for c0 in range(0, ncols, 512):
                st_, sk = tmp(E)
                P.dma(W=[sk], out=st_, in_=src[kc * 128:(kc + 1) * 128, c0:c0 + 512])
                if gain_i is not None:
                    P.act("activation", R=[sk, "ngcol"], W=[dkey], out=dst[:, kc, c0:c0 + 512], in_=st_, func=AF.Copy,
                          scale=ngcol[:, gain_i, kc:kc + 1])
                elif E.cast_i % 2 == 0:
                    P.act("activation", R=[sk], W=[dkey], out=dst[:, kc, c0:c0 + 512], in_=st_, func=AF.Copy)
                else:
                    P.dve("tensor_copy", R=[sk], W=[dkey], out=dst[:, kc, c0:c0 + 512], in_=st_)
                E.cast_i += 1

    def transpose_into(src, skey, nchunk, dstT, dkey, s):
        b = bank()
        for c in range(nchunk):
            P.pe("transpose", R=[skey, "identb"], W=[PS(b)], out=psb[b][:, c * 128:(c + 1) * 128],
                 in_=src[:, c * 128:(c + 1) * 128], identity=identb)
        P.dve("tensor_copy", R=[PS(b)], W=[dkey], out=dstT[:, 0:nchunk, s * 128:(s + 1) * 128],
              in_=psb[b][:, 0:nchunk * 128].rearrange("p (c t) -> p c t", c=nchunk))

    def rmsnorm_to_T(E, t, hb, hkey):
        NT = t.NT
        P.pool("memset", W=["ss"], ap=E.ss, constant=0.0)
        for s in range(NT):
            P.act("activation", R=[(hkey, s)], W=[("xnb", s % 2), "ss"], out=E.xnb[s % 2], in_=hb[:, s, :],
                  func=AF.Square, accum_out=E.ss[:, s:s + 1])
        P.act("activation", R=["ss"], W=["rstd"], out=E.rstd[:, 0:NT], in_=E.ss[:, 0:NT], func=AF.Ln, scale=1.0 / D,
              bias=EPS)
        P.act("activation", R=["rstd"], W=["rstd"], out=E.rstd[:, 0:NT], in_=E.rstd[:, 0:NT], func=AF.Exp, scale=-0.5)
        for s in range(NT):
            P.act("activation", R=[(hkey, s), "rstd"], W=[("xnb", s % 2)], out=E.xnb[s % 2], in_=hb[:, s, :],
                  func=AF.Copy, scale=E.rstd[:, s:s + 1])
            transpose_into(E.xnb[s % 2], ("xnb", s % 2), 8, E.actT, ("actT", s), s)

    def proj_fm(E, b, w, oc, T):
        for kc in range(8):
            P.pe("matmul", R=[("actT", s_) for s_ in range(T // 128)] + ["w_big"], W=[PS(b)], out=psf[b][:, 0:T],
                 lhsT=w[:, kc, oc * 128:(oc + 1) * 128], rhs=E.actT[:, kc, 0:T], start=(kc == 0), stop=(kc == 7))

    def sigmoid_from(E, src_ap, skey, T, scale=-1.0, bias=None, extra=()):
        tt, tk = tmp(E)
        kw = {} if bias is None else {"bias": bias}
        P.act("activation", R=[skey] + list(extra), W=[tk], out=tt[:, 0:T], in_=src_ap, func=AF.Exp, scale=scale, **kw)
        P.act("activation", R=[tk], W=[tk], out=tt[:, 0:T], in_=tt[:, 0:T], func=AF.Ln, bias=1.0)
        P.act("activation", R=[tk], W=[tk], out=tt[:, 0:T], in_=tt[:, 0:T], func=AF.Exp, scale=-1.0)
        return tt, tk

    def sigmoid_multi(E, srcs, T):
        outs = []
        for (ap, key, scale, bias, extra) in srcs:
            tt, tk = tmp(E)
            kw = {} if bias is None else {"bias": bias}
            P.act("activation", R=[key] + list(extra), W=[tk], out=tt[:, 0:T], in_=ap, func=AF.Exp, scale=scale, **kw)
            outs.append((tt, tk))
        for (tt, tk) in outs:
            P.act("activation", R=[tk], W=[tk], out=tt[:, 0:T], in_=tt[:, 0:T], func=AF.Ln, bias=1.0)
        for (tt, tk) in outs:
            P.act("activation", R=[tk], W=[tk], out=tt[:, 0:T], in_=tt[:, 0:T], func=AF.Exp, scale=-1.0)
        return outs

    def tail(E, t, hb, hkey, pb, pkey, catkey="catT"):
        NT = t.NT
        halves = [slice(0, 512), slice(512, 1024)]
        for s in range(NT):
            sl = slice(s * 128, (s + 1) * 128)
            for hs in halves:
                b = bank()
                for kc in range(8):
                    P.pe("matmul", R=[catkey, "w_out"], W=[PS(b)], out=psf[b], lhsT=E.catT[:, kc, sl],
                         rhs=E.w_out[:, kc, hs], start=(kc == 0), stop=(kc == 7))
                P.dve("tensor_tensor", R=[PS(b), (hkey, s)], W=[(hkey, s)], out=hb[:, s, hs], in0=hb[:, s, hs],
                      in1=psf[b], op=ALU.add)
        for s in range(NT):
            xi = s % 2
            P.act("activation", R=[(hkey, s)], W=[("xnb", xi)], out=E.xnb[xi], in_=hb[:, s, :], func=AF.Copy)
            transpose_into(E.xnb[xi], ("xnb", xi), 8, E.actT, ("actT", s), s)
            P.act("activation", R=[pkey], W=[("xnp", xi)], out=E.xnp[xi], in_=pb[:, s, :], func=AF.Copy)
            transpose_into(E.xnp[xi], ("xnp", xi), 2, E.pT, ("pT", s), s)
        for s in range(NT):
            sl = slice(s * 128, (s + 1) * 128)
            bgs, bps = [], []
            for hs in halves:
                bg = bank()
                for kc in range(8):
                    P.pe("matmul", R=[("actT", s), "w_gate"], W=[PS(bg)], out=psf[bg], lhsT=E.actT[:, kc, sl],
                         rhs=E.w_gate[:, kc, hs], start=(kc == 0), stop=(kc == 7))
                bp = bank()
                for kc in range(2):
                    P.pe("matmul", R=[("pT", s), "w_proj"], W=[PS(bp)], out=psf[bp], lhsT=E.pT[:, kc, sl],
                         rhs=E.w_proj[:, kc, hs], start=(kc == 0), stop=(kc == 1))
                bgs.append(bg)
                bps.append(bp)
            sgs = sigmoid_multi(E, [(psf[bg], PS(bg), -1.0, None, ()) for bg in bgs], 512)
            for (sg, sgk), bp in zip(sgs, bps):
                P.dve("tensor_tensor", R=[PS(bp), sgk], W=[sgk], out=sg, in0=psf[bp], in1=sg, op=ALU.mult)
            for (sg, sgk), hs in zip(sgs, halves):
                P.dve("tensor_tensor", R=[sgk, (hkey, s)], W=[(hkey, s)], out=hb[:, s, hs], in0=hb[:, s, hs], in1=sg,
                      op=ALU.add)

    def state_out(E, t, bufs, key, H, L, dst):
        for q in range(t.nseg):
            b = bank()
            for j in range(4):
                P.pe("transpose", R=[(key, j), "identf"], W=[PS(b)], out=psf[b][0:H, j * 128:(j + 1) * 128],
                     in_=bufs[j][:, q, L:L + H], identity=identf)
            P.act("activation", R=[PS(b)], W=["hst"], out=E.hst[0:H, :], in_=psf[b][0:H, :], func=AF.Copy)
            P.dma(R=["hst"], out=dst[t.seq if t.kind == "p" else q], in_=E.hst[0:H, :], is_output=True)

    if 1 in cfg.passes:
        E = common_bufs()
        E.pbuf = [AR.alloc([128, 4, PLE]) for _ in range(2)]
        E.pT = AR.alloc([128, 2, 512], BF16)
        E.xnp = [AR.alloc([128, PLE], BF16) for _ in range(2)]
        E.w_big = AR.alloc([128, 8, 7 * DB], BF16)
        E.w_out = AR.alloc([128, 8, D], BF16)
        E.w_gate = AR.alloc([128, 8, D], BF16)
        E.w_proj = AR.alloc([128, 2, D], BF16)
        awc = AR.alloc([128, 4, 3])
        bwc = AR.alloc([128, 4, 31])
        lncol = AR.alloc([128, 4, 4])
        ubuf_p = [AR.alloc([128, 1, HA + 512]) for j in range(4)]
        gbuf_p = [AR.alloc([128, 1, HB + 512]) for j in range(4)]
        ubuf_s = [AR.alloc([128, NS, HA + DS]) for j in range(4)]
        gbuf_s = [AR.alloc([128, NS, HB + DS]) for j in range(4)]
        bconv = [AR.alloc([128, 512]) for j in range(4)]
        negmean = AR.alloc([128, 512])
        rstdB = AR.alloc([128, 512])
        E.hst = AR.alloc([32, 512])
        print("P1 SBUF bytes/partition:", AR.off * 4)

        load_weight(E, E.w_big, "w_big", ab_w_in, 8, 7 * DB, gain_i=0)
        load_weight(E, E.w_out, "w_out", ab_w_out, 8, D)
        load_weight(E, E.w_gate, "w_gate", ple_gate[0], 8, D)
        load_weight(E, E.w_proj, "w_proj", ple_proj[0], 2, D)
        for j in range(4):
            P.dma(W=["awc"], out=awc[:, j, :], in_=a_conv_w[:, j * 128:(j + 1) * 128].rearrange("w p -> p w"),
                  allow_slow_non_contiguous=True)
            P.dma(W=["bwc"], out=bwc[:, j, :], in_=b_conv_w[:, j * 128:(j + 1) * 128].rearrange("w p -> p w"),
                  allow_slow_non_contiguous=True)
        P.dma(W=["lncol"], out=lncol[:, :, 0], in_=b_ln_g.rearrange("(j p) -> p j", p=128),
              allow_slow_non_contiguous=True)
        P.dma(W=["lncol"], out=lncol[:, :, 1], in_=b_ln_b.rearrange("(j p) -> p j", p=128),
              allow_slow_non_contiguous=True)
        P.pool("tensor_scalar", R=["lncol"], W=["lncol"], out=lncol[:, :, 2:4], in0=lncol[:, :, 0:2], scalar1=-1.0,
               scalar2=None, op0=ALU.mult)

        def p1_load(ti):
            t = tiles[ti]
            par = ti % 2
            P.dma(W=[(("hbuf", par), s_) for s_ in range(4)], out=E.hbuf[par][:, 0:t.NT, :],
                  in_=x_rows(t).rearrange("(n p) d -> p n d", p=128))
            P.dma(W=[("pbuf", par)], out=E.pbuf[par][:, 0:t.NT, :],
                  in_=p_rows(t, 0).rearrange("(n p) d -> p n d", p=128))

        for ti, t in enumerate(tiles):
            T, NT, nseg, L = t.T, t.NT, t.nseg, t.L
            par = ti % 2
            hb, hkey = E.hbuf[par], ("hbuf", par)
            pb, pkey = E.pbuf[par], ("pbuf", par)
            isp = t.kind == "p"
            ub, gb = (ubuf_p, gbuf_p) if isp else (ubuf_s, gbuf_s)
            ukey, gkey = ("ubuf_p", "gbuf_p") if isp else ("ubuf_s", "gbuf_s")

            def v3(ap):
                return ap.rearrange("p (n l) -> p n l", n=nseg)

            if ti == 0:
                p1_load(0)
            if isp and t.t0 == 0:
                for j in range(4):
                    P.pool("memset", W=[(ukey, j)], ap=ub[j][:, :, 0:HA], constant=0.0)
                    P.pool("memset", W=[(gkey, j)], ap=gb[j][:, :, 0:HB], constant=0.0)
            if not isp:
                for q in range(NS):
                    for (stt, H, bufs, key) in ((st_a, HA, ub, ukey), (st_b, HB, gb, gkey)):
                        P.dma(W=["hst"], out=E.hst[0:H, :], in_=stt[q])
                        b = bank()
                        for j in range(4):
                            P.pe("transpose", R=["hst", "identf"], W=[PS(b)], out=psf[b][:, j * 32:j * 32 + H],
                                 in_=E.hst[0:H, j * 128:(j + 1) * 128], identity=identf[0:H, 0:H])
                        for j in range(4):
                            P.act("activation", R=[PS(b)], W=[(key, j)], out=bufs[j][:, q, 0:H],
                                  in_=psf[b][:, j * 32:j * 32 + H], func=AF.Copy)

            rmsnorm_to_T(E, t, hb, hkey)
            if ti + 1 < len(tiles):
                p1_load(ti + 1)

            for pair in ((0, 1), (2, 3)):
                bxs, bcs = {}, {}
                for j in pair:
                    bxs[j], bcs[j] = bank(), bank()
                    proj_fm(E, bxs[j], E.w_big, 0 + j, T)
                    proj_fm(E, bcs[j], E.w_big, 4 + j, T)
                tcs = {}
                for j in pair:
                    tcs[j] = tmp(E)
                    P.act("activation", R=[PS(bcs[j])], W=[tcs[j][1]], out=tcs[j][0][:, 0:T], in_=psf[bcs[j]][:, 0:T],
                          func=AF.Copy)
                for j in pair:
                    P.dve("tensor_tensor", R=[PS(bxs[j]), tcs[j][1]], W=[(ukey, j)], out=ub[j][:, :, HA:HA + L],
                          in0=v3(psf[bxs[j]][:, 0:T]), in1=v3(tcs[j][0][:, 0:T]), op=ALU.mult)
                bzs = {}
                for j in pair:
                    bzs[j] = bank()
                    proj_fm(E, bzs[j], E.w_big, 12 + j, T)
                sgs = sigmoid_multi(E, [(psf[bzs[j]][:, 0:T], PS(bzs[j]), -1.0, None, ()) for j in pair], T)
                cvs = {}
                for j in pair:
                    cvs[j] = tmp(E)
                    P.dve("tensor_scalar", R=[(ukey, j), "awc"], W=[cvs[j][1]], out=v3(cvs[j][0][:, 0:T]),
                          in0=ub[j][:, :, 2:2 + L], scalar1=awc[:, j, 2:3], scalar2=None, op0=ALU.mult)
                for w in (1, 0):
                    for j in pair:
                        P.dve("scalar_tensor_tensor", R=[(ukey, j), "awc", cvs[j][1]], W=[cvs[j][1]],
                              out=v3(cvs[j][0][:, 0:T]), in0=ub[j][:, :, w:w + L], scalar=awc[:, j, w:w + 1],
                              in1=v3(cvs[j][0][:, 0:T]), op0=ALU.mult, op1=ALU.add)
                bbs = {}
                for j in pair:
                    bbs[j] = bank()
                    proj_fm(E, bbs[j], E.w_big, 8 + j, T)
                for (sg, sgk), j in zip(sgs, pair):
                    P.dve("tensor_tensor", R=[PS(bzs[j]), sgk], W=[sgk], out=sg[:, 0:T], in0=psf[bzs[j]][:, 0:T],
                          in1=sg[:, 0:T], op=ALU.mult)
                for j in pair:
                    P.dve("tensor_tensor", R=[PS(bbs[j]), cvs[j][1]], W=[cvs[j][1]], out=cvs[j][0][:, 0:T],
                          in0=psf[bbs[j]][:, 0:T], in1=cvs[j][0][:, 0:T], op=ALU.mult)
                for (sg, sgk), j in zip(sgs, pair):
                    P.dve("tensor_tensor", R=[sgk, cvs[j][1]], W=["catT"], out=E.catT[:, j, 0:T], in0=sg[:, 0:T],
                          in1=cvs[j][0][:, 0:T], op=ALU.mult)
            if t.last:
                state_out(E, t, ub, ukey, HA, L, na_p if isp else na_s)
            else:
                for j in range(4):
                    P.pool("tensor_copy", R=[(ukey, j)], W=[(ukey, j)], out=ub[j][:, :, 0:HA], in_=ub[j][:, :, L:L + HA])

            for pair in ((0, 1), (2, 3)):
                bvs, bgs = {}, {}
                for j in pair:
                    bvs[j], bgs[j] = bank(), bank()
                    proj_fm(E, bvs[j], E.w_big, 16 + j, T)
                    proj_fm(E, bgs[j], E.w_big, 20 + j, T)
                sgs = sigmoid_multi(E, [(psf[bgs[j]][:, 0:T], PS(bgs[j]), -1.0, None, ()) for j in pair], T)
                for (sg, sgk), j in zip(sgs, pair):
                    P.dve("tensor_tensor", R=[PS(bvs[j]), sgk], W=[(gkey, j)], out=gb[j][:, :, HB:HB + L],
                          in0=v3(psf[bvs[j]][:, 0:T]), in1=v3(sg[:, 0:T]), op=ALU.mult)
            for w in range(31):
                for j in range(4):
                    if w == 0:
                        P.dve("tensor_scalar", R=[(gkey, j), "bwc"], W=[("bconv", j)], out=v3(bconv[j][:, 0:T]),
                              in0=gb[j][:, :, 0:L], scalar1=bwc[:, j, 0:1], scalar2=None, op0=ALU.mult)
                    else:
                        P.dve("scalar_tensor_tensor", R=[(gkey, j), "bwc", ("bconv", j)], W=[("bconv", j)],
                              out=v3(bconv[j][:, 0:T]), in0=gb[j][:, :, w:w + L], scalar=bwc[:, j, w:w + 1],
                              in1=v3(bconv[j][:, 0:T]), op0=ALU.mult, op1=ALU.add)
            if t.last:
                state_out(E, t, gb, gkey, HB, L, nb_p if isp else nb_s)
            else:
                for j in range(4):
                    P.pool("tensor_copy", R=[(gkey, j)], W=[(gkey, j)], out=gb[j][:, :, 0:HB], in_=gb[j][:, :, L:L + HB])
            sqs = []
            for j in range(4):
                sq, sqk = tmp(E)
                P.act("activation", R=[("bconv", j)], W=[sqk], out=sq[:, 0:T], in_=bconv[j][:, 0:T], func=AF.Square)
                sqs.append((sq, sqk))
            for j in range(4):
                P.pe("matmul", R=[("bconv", j), "onesf"], W=[PS(6)], out=psf[6][:, 0:T], lhsT=onesf,
                     rhs=bconv[j][:, 0:T], start=(j == 0), stop=(j == 3))
            for j in range(4):
                sq, sqk = sqs[j]
                P.pe("matmul", R=[sqk, "onesf"], W=[PS(7)], out=psf[7][:, 0:T], lhsT=onesf, rhs=sq[:, 0:T],
                     start=(j == 0), stop=(j == 3))
            P.act("activation", R=[PS(6)], W=["negmean"], out=negmean[:, 0:T], in_=psf[6][:, 0:T], func=AF.Copy,
                  scale=-1.0 / DB)
            P.dve("tensor_tensor", R=["negmean"], W=["rstdB"], out=rstdB[:, 0:T], in0=negmean[:, 0:T],
                  in1=negmean[:, 0:T], op=ALU.mult)
            P.dve("scalar_tensor_tensor", R=[PS(7), "rstdB"], W=["rstdB"], out=rstdB[:, 0:T], in0=psf[7][:, 0:T],
                  scalar=1.0 / DB, in1=rstdB[:, 0:T], op0=ALU.mult, op1=ALU.subtract)
            P.act("activation", R=["rstdB"], W=["rstdB"], out=rstdB[:, 0:T], in_=rstdB[:, 0:T], func=AF.Ln, bias=EPS)
            P.act("activation", R=["rstdB"], W=["rstdB"], out=rstdB[:, 0:T], in_=rstdB[:, 0:T], func=AF.Exp, scale=-0.5)
            for pair in ((0, 1), (2, 3)):
                bzs = {}
                for j in pair:
                    bzs[j] = bank()
                    proj_fm(E, bzs[j], E.w_big, 24 + j, T)
                yvs = {}
                for j in pair:
                    yvs[j] = tmp(E)
                    P.dve("tensor_tensor", R=[("bconv", j), "negmean"], W=[yvs[j][1]], out=yvs[j][0][:, 0:T],
                          in0=bconv[j][:, 0:T], in1=negmean[:, 0:T], op=ALU.add)
                for j in pair:
                    P.dve("tensor_tensor", R=[yvs[j][1], "rstdB"], W=[yvs[j][1]], out=yvs[j][0][:, 0:T],
                          in0=yvs[j][0][:, 0:T], in1=rstdB[:, 0:T], op=ALU.mult)
                sgs = sigmoid_multi(E, [(psf[bzs[j]][:, 0:T], PS(bzs[j]), -1.0, None, ()) for j in pair] +
                                    [(yvs[j][0][:, 0:T], yvs[j][1], lncol[:, j, 2:3], lncol[:, j, 3:4], ["lncol"])
                                     for j in pair], T)
                for k_, j in enumerate(pair):
                    sgz, sgzk = sgs[k_]
                    P.dve("tensor_tensor", R=[PS(bzs[j]), sgzk], W=[sgzk], out=sgz[:, 0:T], in0=psf[bzs[j]][:, 0:T],
                          in1=sgz[:, 0:T], op=ALU.mult)
                for k_, j in enumerate(pair):
                    sgy, sgyk = sgs[2 + k_]
                    P.dve("tensor_scalar", R=[yvs[j][1], "lncol", sgyk], W=[yvs[j][1]], out=yvs[j][0][:, 0:T],
                          in0=yvs[j][0][:, 0:T], scalar1=lncol[:, j, 0:1], scalar2=lncol[:, j, 1:2], op0=ALU.mult,
                          op1=ALU.add)
                for k_, j in enumerate(pair):
                    sgy, sgyk = sgs[2 + k_]
                    P.dve("tensor_tensor", R=[yvs[j][1], sgyk], W=[yvs[j][1]], out=yvs[j][0][:, 0:T],
                          in0=yvs[j][0][:, 0:T], in1=sgy[:, 0:T], op=ALU.mult)
                for k_, j in enumerate(pair):
                    sgz, sgzk = sgs[k_]
                    P.dve("tensor_tensor", R=[yvs[j][1], sgzk], W=["catT"], out=E.catT[:, 4 + j, 0:T],
                          in0=yvs[j][0][:, 0:T], in1=sgz[:, 0:T], op=ALU.mult)

            tail(E, t, hb, hkey, pb, pkey)
            P.dma(R=[(hkey, s_) for s_ in range(4)], W=[("hA", ti)],
                  out=hA[t.row0:t.row0 + T, :].rearrange("(n p) d -> p n d", p=128),
                  in_=hb[:, 0:NT, :], is_output=cfg.debug)
        P.barrier()
        AR.off = base_off


    if 2 in cfg.passes:
        CAP = max(S, PL)
        E = common_bufs(n_h=1, n_tmp=4)
        E.w_big = AR.alloc([128, 8, 7 * DB], BF16)
        kT = AR.alloc([128, 4, CAP], BF16)
        Vc = AR.alloc([128, CAP // 128, DB], BF16)
        qT = AR.alloc([128, 4, 512], BF16)
        cvn = AR.alloc([128, 4, DB], BF16)
        kst = [AR.alloc([128, DB]) for _ in range(3)]
        kst_i = [0]
        kbf = [AR.alloc([128, DB], BF16) for _ in range(2)]
        spb = [AR.alloc([128, 512], BF16) for _ in range(4)]
        abf = [AR.alloc([128, 512], BF16) for _ in range(4)]
        rr = [0, 0, 0]
        sacc = [AR.alloc([128, 512], BF16) for _ in range(2)]
        g_bc = AR.alloc([128, DB])
        b_bc = AR.alloc([128, DB])
        WT = AR.alloc([128, 4, 128], BF16)
        WTs = AR.alloc([128, 4, 128], BF16)
        negU = AR.alloc([128, 128], BF16)
        negOnes = AR.alloc([128, 128], BF16)
        ones_row = AR.alloc([1, 128])
        cb_row = AR.alloc([1, 4, 128])
        cb_row_s = AR.alloc([1, 4, 128])
        lnst = AR.alloc([128, 8])
        kT_new = AR.alloc([128, 4, 128], BF16)
        if CAP - PL >= 2048:
            Vn = [kT[0:32, q, PL:PL + DB] for q in range(4)]
            dzs = kT[:, 0, PL + DB:PL + DB + 1024].bitcast(F32).rearrange("p (a b) -> p a b", a=4)
        else:
            Vn = [AR.alloc([32, DB], BF16) for _ in range(4)]
            dzs = AR.alloc([128, 4, 128])
        print("P2 SBUF bytes/partition:", AR.off * 4)

        def stage():
            i = kst_i[0]
            kst_i[0] = (i + 1) % len(kst)
            return kst[i], ("kst", i)

        load_weight(E, E.w_big, "w_big", cd_w_in, 8, 7 * DB, gain_i=1)
        P.dma(W=["g_bc"], out=g_bc, in_=c_ln_g.partition_broadcast(128))
        P.dma(W=["b_bc"], out=b_bc, in_=c_ln_b.partition_broadcast(128))
        P.dma(W=["cb_row"], out=cb_row, in_=c_b.rearrange("(o h) t -> o h t", o=1))
        for q in range(NS):
            P.dma(W=["cb_row_s"], out=cb_row_s[:, :, q * DS:(q + 1) * DS],
                  in_=c_b[:, 0:DS].rearrange("(o h) t -> o h t", o=1))
        P.pool("memset", W=["ones_row"], ap=ones_row, constant=1.0)
        P.pool("memset", W=["negOnes"], ap=negOnes, constant=-1.0)
        tU, tUk = tmp(E)
        P.pool("memset", W=[tUk], ap=tU[:, 0:128], constant=-1.0)
        P.pool("affine_select", R=[tUk], W=[tUk], out=tU[:, 0:128], in_=tU[:, 0:128], pattern=[[-1, 128]],
               compare_op=ALU.is_ge, fill=0.0, base=0, channel_multiplier=1)
        P.dve("tensor_copy", R=[tUk], W=["negU"], out=negU, in_=tU[:, 0:128])
        for variant, dstW, dk in ((0, WT, "WT"), (1, WTs, "WTs")):
            for hh in range(4):
                wt_, wk = tmp(E)
                if variant == 0:
                    P.dma(W=[wk], out=wt_[:, 0:128], in_=c_ws[hh])
                else:
                    P.pool("memset", W=[wk], ap=wt_[:, 0:128], constant=0.0)
                    for q in range(NS):
                        P.dma(W=[wk], out=wt_[q * DS:(q + 1) * DS, q * DS:(q + 1) * DS], in_=c_ws[hh, 0:DS, 0:DS])
                P.pool("affine_select", R=[wk], W=[wk], out=wt_[:, 0:128], in_=wt_[:, 0:128], pattern=[[-1, 128]],
                       compare_op=ALU.is_ge, fill=0.0, base=0, channel_multiplier=1)
                P.act("activation", R=[wk], W=[("xnb", 0)], out=E.xnb[0][:, 0:128], in_=wt_[:, 0:128], func=AF.Copy)
                b = bank()
                P.pe("transpose", R=[("xnb", 0), "identb"], W=[PS(b)], out=psb[b][:, 0:128], in_=E.xnb[0][:, 0:128],
                     identity=identb)
                P.dve("tensor_copy", R=[PS(b)], W=[dk], out=dstW[:, hh, :], in_=psb[b][:, 0:128])

        def tri_mask(buf, key, nk, c0):
            P.pool("affine_select", R=[key], W=[key], out=buf[0:nk, c0:c0 + nk], in_=buf[0:nk, c0:c0 + nk],
                   pattern=[[1, nk]], compare_op=ALU.is_gt, fill=0.0, base=0, channel_multiplier=-1)

        def attention(hp, qa, qb, blocks):
            for h in range(2):
                P.pool("memset", W=[("sacc", h)], ap=sacc[h][:, qa:qb], constant=0.0)
            nb = len(blocks)
            items = [(bi, h) for bi in range(nb) for h in range(2)]
            n = len(items)
            st = {}

            def Sz(i):
                bi, h = items[i]
                kTa, Va, nk, c0, tri = blocks[bi]
                r0 = 64 * h
                bz = rr[2] % 6
                rr[2] += 1
                P.pe("matmul", R=["kT", "qT"], W=[PS(bz)], out=psf[bz][0:nk, c0:qb], lhsT=kTa[r0:r0 + 64, 0:nk],
                     rhs=qT[r0:r0 + 64, hp, c0:qb], start=True, stop=True)
                st[i] = {"bz": bz}

            def Se(i):
                bi, h = items[i]
                kTa, Va, nk, c0, tri = blocks[bi]
                bz = st[i]["bz"]
                e_, ek = tmp(E)
                P.act("activation", R=[PS(bz)], W=[ek], out=e_[0:nk, c0:qb], in_=psf[bz][0:nk, c0:qb], func=AF.Exp)
                st[i]["e"] = (e_, ek)

            def Sl(i):
                bi, h = items[i]
                kTa, Va, nk, c0, tri = blocks[bi]
                e_, ek = st[i]["e"]
                si = rr[0] % 4
                rr[0] += 1
                sp_, spk = spb[si], ("spb", si)
                P.act("activation", R=[ek], W=[spk], out=sp_[0:nk, c0:qb], in_=e_[0:nk, c0:qb], func=AF.Ln, bias=1.0)
                if tri:
                    tri_mask(sp_, spk, nk, c0)
                st[i]["sp"] = (sp_, spk)

            def Sw(i):
                bi, h = items[i]
                kTa, Va, nk, c0, tri = blocks[bi]
                bz = st[i]["bz"]
                sp_, spk = st[i]["sp"]
                P.pe("matmul", R=[spk, "negU"], W=[PS(bz)], out=psf[bz][0:nk, c0:qb], lhsT=negU[0:nk, 0:nk],
                     rhs=sp_[0:nk, c0:qb], start=False, stop=(bi == 0), skip_group_check=True)
                if bi > 0:
                    P.pe("matmul", R=[("sacc", h), "negOnes"], W=[PS(bz)], out=psf[bz][0:nk, c0:qb],
                         lhsT=negOnes[:, 0:nk], rhs=sacc[h][:, c0:qb], start=False, stop=True, skip_group_check=True)
                if bi < nb - 1:
                    P.dve("tensor_tensor", R=[spk, ("sacc", h)], W=[("sacc", h)], out=sacc[h][0:nk, c0:qb],
                          in0=sacc[h][0:nk, c0:qb], in1=sp_[0:nk, c0:qb], op=ALU.add)

            def Sa(i):
                bi, h = items[i]
                kTa, Va, nk, c0, tri = blocks[bi]
                bz = st[i]["bz"]
                ai = rr[1] % 4
                rr[1] += 1
                a_, ak = abf[ai], ("abf", ai)
                P.act("activation", R=[PS(bz)], W=[ak], out=a_[0:nk, c0:qb], in_=psf[bz][0:nk, c0:qb], func=AF.Exp)
                if tri:
                    tri_mask(a_, ak, nk, c0)
                if c0 > qa:
                    P.pool("memset", W=[ak], ap=a_[0:nk, qa:c0], constant=0.0)
                st[i]["a"] = (a_, ak)

            def Sv(i):
                bi, h = items[i]
                kTa, Va, nk, c0, tri = blocks[bi]
                a_, ak = st.pop(i)["a"]
                P.pe("matmul", R=[ak, "Vc"], W=[PS(6 + h)], out=psf[6 + h][:, qa:qb], lhsT=Va[0:nk, :],
                     rhs=a_[0:nk, qa:qb], start=(bi == 0), stop=(bi == nb - 1))

            for k in range(-3, n):
                if 0 <= k + 3 < n:
                    Sz(k + 3)
                if 0 <= k + 2 < n:
                    Se(k + 2)
                if 0 <= k + 1 < n:
                    Sl(k + 1)
                    Sw(k + 1)
                if 0 <= k < n:
                    Sa(k)
                    Sv(k)

        def p2_load(ti):
            t = tiles[ti]
            P.dma(W=[(("hbuf", 0), s_) for s_ in range(4)], out=E.hbuf[0][:, 0:t.NT, :],
                  in_=hA[t.row0:t.row0 + t.T, :].rearrange("(n p) d -> p n d", p=128), R=[("hA", ti)])

        p2_load(0)
        for ti, t in enumerate(tiles):
            if cfg.stop == 1:
                break
            T, NT, nseg, L = t.T, t.NT, t.nseg, t.L
            isp = t.kind == "p"
            hb, hkey = E.hbuf[0], ("hbuf", 0)
            rmsnorm_to_T(E, t, hb, hkey)
            if ti + 1 < len(tiles):
                p2_load(ti + 1)
            kb0 = t.t0 // 128

            for s in range(NT):
                sl = slice(s * 128, (s + 1) * 128)
                rows = slice(t.row0 + s * 128, t.row0 + (s + 1) * 128)

                def proj_tm(c0):
                    b = bank()
                    for kc in range(8):
                        P.pe("matmul", R=[("actT", s), "w_big"], W=[PS(b)], out=psf[b], lhsT=E.actT[:, kc, sl],
                             rhs=E.w_big[:, kc, c0:c0 + DB], start=(kc == 0), stop=(kc == 7))
                    return b

                if cfg.stop == 21:
                    continue
                b = proj_tm(DB)
                cf, cfk = tmp(E)
                P.pool("memset", W=["lnst"], ap=lnst, constant=0.0)
                P.act("activation", R=[PS(b)], W=[cfk, "lnst"], out=cf, in_=psf[b], func=AF.Copy, accum_out=lnst[:, 0:1])
                jk, jkk = tmp(E)
                P.act("activation", R=[PS(b)], W=[jkk, "lnst"], out=jk, in_=psf[b], func=AF.Square,
                      accum_out=lnst[:, 1:2])
                P.dve("tensor_scalar", R=["lnst"], W=["lnst"], out=lnst[:, 2:3], in0=lnst[:, 0:1], scalar1=1.0 / DB,
                      scalar2=None, op0=ALU.mult)
                P.dve("tensor_tensor", R=["lnst"], W=["lnst"], out=lnst[:, 3:4], in0=lnst[:, 2:3], in1=lnst[:, 2:3],
                      op=ALU.mult)
                P.dve("scalar_tensor_tensor", R=["lnst"], W=["lnst"], out=lnst[:, 4:5], in0=lnst[:, 1:2],
                      scalar=1.0 / DB, in1=lnst[:, 3:4], op0=ALU.mult, op1=ALU.subtract)
                P.act("activation", R=["lnst"], W=["lnst"], out=lnst[:, 5:6], in_=lnst[:, 4:5], func=AF.Ln, bias=EPS)
                P.act("activation", R=["lnst"], W=["lnst"], out=lnst[:, 5:6], in_=lnst[:, 5:6], func=AF.Exp, scale=-0.5)
                P.dve("tensor_scalar", R=[cfk, "lnst"], W=[cfk], out=cf, in0=cf, scalar1=lnst[:, 2:3],
                      scalar2=lnst[:, 5:6], op0=ALU.subtract, op1=ALU.mult)
                P.dve("tensor_tensor", R=[cfk, "g_bc"], W=[cfk], out=cf, in0=cf, in1=g_bc, op=ALU.mult)
                if isp:
                    P.dve("tensor_tensor", R=[cfk, "b_bc"], W=["cvn"], out=cvn[:, s, :], in0=cf, in1=b_bc, op=ALU.add)
                else:
                    P.dve("tensor_tensor", R=[cfk, "b_bc"], W=[cfk], out=cf, in0=cf, in1=b_bc, op=ALU.add)
                    P.act("activation", R=[cfk], W=["cvn"], out=cvn[:, s, :], in_=cf, func=AF.Copy)
                    P.dma(R=[cfk], out=ncv_s, in_=cf, is_output=True)
                if cfg.stop == 22:
                    continue
                b = proj_tm(4 * DB if cfg.stop != 25 else 5 * DB)
                st_, stk = stage()
                P.act("activation", R=[PS(b)], W=[stk], out=st_, in_=psf[b], func=AF.Copy)
                P.dma(R=[stk], out=((nk_p if cfg.stop != 25 else nv_p)[rows, :] if isp else (nk_s if cfg.stop != 25 else nv_s)), in_=st_, is_output=True)
                kb_ = kbf[s % 2]
                P.dve("tensor_copy", R=[PS(b)], W=[("kbf", s % 2)], out=kb_, in_=psf[b])
                if isp:
                    transpose_into(kb_, ("kbf", s % 2), 4, kT, "kT", kb0 + s)
                else:
                    transpose_into(kb_, ("kbf", s % 2), 4, kT_new, "kT_new", 0)
                if cfg.stop in (23, 25):
                    continue
                b = proj_tm(5 * DB)
                st_, stk = stage()
                P.act("activation", R=[PS(b)], W=[stk], out=st_, in_=psf[b], func=AF.Copy)
                P.dma(R=[stk], out=(nv_p[rows, :] if isp else nv_s), in_=st_, is_output=True)
                if cfg.stop == 26 or (cfg.stop == 27 and isp) or (cfg.stop == 28 and not isp):
                    continue
                if isp:
                    P.act("activation", R=[PS(b)], W=["Vc"], out=Vc[:, kb0 + s, :], in_=psf[b], func=AF.Copy)
                else:
                    P.act("activation", R=[PS(b)], W=[("kbf", 1)], out=kbf[1], in_=psf[b], func=AF.Copy)
                    for q in range(NS if cfg.stop not in (24, 27, 28) else 0):
                        P.dma(R=[("kbf", 1)], W=[("Vn", q)], out=Vn[q][0:DS, :], in_=kbf[1][q * DS:(q + 1) * DS, :])

            if cfg.stop in (2, 21, 22, 23, 24, 25, 26, 27, 28):
                continue
            Wm, Wk = (WT, "WT") if isp else (WTs, "WTs")
            cbr, cbk = (cb_row, "cb_row") if isp else (cb_row_s, "cb_row_s")
            for hh in range(4):
                bm = bank()
                for s in range(NT):
                    sl = slice(s * 128, (s + 1) * 128)
                    P.pe("matmul", R=["cvn", Wk], W=[PS(bm)], out=psf[bm][:, sl], lhsT=cvn[:, s, hh * 128:(hh + 1) * 128],
                         rhs=Wm[:, hh, :], start=True, stop=False)
                    P.pe("matmul", R=["ones_row", cbk], W=[PS(bm)], out=psf[bm][:, sl], lhsT=ones_row[0:1, :],
                         rhs=cbr[0:1, hh, :], start=False, stop=True)
                bu, bzc = bank(), bank()
                proj_fm(E, bu, E.w_big, 0 + hh, T)
                proj_fm(E, bzc, E.w_big, 8 + hh, T)
                sg, sgk = sigmoid_from(E, psf[bzc][:, 0:T], PS(bzc), T)
                P.dve("tensor_tensor", R=[PS(bzc), sgk], W=[sgk], out=sg[:, 0:T], in0=psf[bzc][:, 0:T], in1=sg[:, 0:T],
                      op=ALU.mult)
                P.dve("tensor_tensor", R=[PS(bm), sgk], W=[sgk], out=sg[:, 0:T], in0=psf[bm][:, 0:T], in1=sg[:, 0:T],
                      op=ALU.mult)
                P.dve("tensor_tensor", R=[PS(bu), sgk], W=["catT"], out=E.catT[:, hh, 0:T], in0=psf[bu][:, 0:T],
                      in1=sg[:, 0:T], op=ALU.mult)

            if cfg.stop == 3:
                continue
            for hp in range(4):
                bq = bank()
                proj_fm(E, bq, E.w_big, 12 + hp, T)
                P.act("activation", R=[PS(bq)], W=["qT"], out=qT[:, hp, 0:T], in_=psf[bq][:, 0:T], func=AF.Copy,
                      scale=0.125)

            def silu_dz(hp, dst, dkey):
                bd = bank()
                proj_fm(E, bd, E.w_big, 24 + hp, T)
                sg, sgk = sigmoid_from(E, psf[bd][:, 0:T], PS(bd), T)
                P.dve("tensor_tensor", R=[PS(bd), sgk], W=[dkey], out=dst, in0=psf[bd][:, 0:T], in1=sg[:, 0:T],
                      op=ALU.mult)

            def finalize(hp, qa, qb, m1, m1k):
                for h in range(2):
                    r0 = 64 * h
                    P.dve("tensor_tensor", R=[PS(6 + h), m1k], W=["catT"], out=E.catT[r0:r0 + 64, 4 + hp, qa:qb],
                          in0=psf[6 + h][r0:r0 + 64, qa:qb], in1=m1[r0:r0 + 64, qa:qb], op=ALU.mult)

            if cfg.stop == 4 or (cfg.stop == 5 and not isp):
                continue
            if isp:
                nkb = kb0 + NT
                for hp in range(4):
                    blocks = []
                    for kb in range(nkb - 1, -1, -1):
                        i = kb - kb0
                        blocks.append((kT[:, hp, kb * 128:(kb + 1) * 128], Vc[:, kb, hp * 128:(hp + 1) * 128], 128,
                                       max(i, 0) * 128, i >= 0))
                    attention(hp, 0, T, blocks)
                    m1, m1k = tmp(E)
                    silu_dz(hp, m1[:, 0:T], m1k)
                    finalize(hp, 0, T, m1, m1k)
            else:
                for hp in range(4):
                    silu_dz(hp, dzs[:, hp, :], "dzs")
                npast = PL // 128
                for q in range(NS):
                    for kb in range(npast):
                        st_, stk = stage()
                        P.dma(W=[stk], out=st_, in_=ck[q, kb * 128:(kb + 1) * 128, :])
                        P.act("activation", R=[stk], W=[("kbf", kb % 2)], out=kbf[kb % 2], in_=st_, func=AF.Copy)
                        transpose_into(kbf[kb % 2], ("kbf", kb % 2), 4, kT, "kT", kb)
                        st_, stk = stage()
                        P.dma(W=[stk], out=st_, in_=cv[q, kb * 128:(kb + 1) * 128, :])
                        P.dve("tensor_copy", R=[stk], W=["Vc"], out=Vc[:, kb, :], in_=st_)
                    qa, qb = q * DS, (q + 1) * DS
                    for hp in range(4):
                        blocks = [(kT_new[:, hp, qa:qb], Vn[q][0:DS, hp * 128:(hp + 1) * 128], DS, qa, True)]
                        for kb in range(npast - 1, -1, -1):
                            blocks.append((kT[:, hp, kb * 128:(kb + 1) * 128], Vc[:, kb, hp * 128:(hp + 1) * 128], 128,
                                           qa, False))
                        attention(hp, qa, qb, blocks)
                        finalize(hp, qa, qb, dzs[:, hp, :], "dzs")
            P.dma(R=["catT"], W=[("catD", ti)], out=catD[:, t.row0:t.row0 + T].rearrange("(c p) t -> p c t", p=128),
                  in_=E.catT[:, :, 0:T], is_output=cfg.debug)
        P.barrier()
        AR.off = base_off

    if 3 in cfg.passes:
        E = common_bufs(n_h=2, n_tmp=6)
        E.catT2 = [E.catT, AR.alloc([128, 8, 512], BF16)]
        E.pbuf = [AR.alloc([128, 4, PLE]) for _ in range(2)]
        E.pT = AR.alloc([128, 2, 512], BF16)
        E.xnp = [AR.alloc([128, PLE], BF16) for _ in range(2)]
        E.w_out = AR.alloc([128, 8, D], BF16)
        E.w_gate = AR.alloc([128, 8, D], BF16)
        E.w_proj = AR.alloc([128, 2, D], BF16)
        fg_bc = AR.alloc([128, D])
        print("P3 SBUF bytes/partition:", AR.off * 4)
        load_weight(E, E.w_out, "w_out", cd_w_out, 8, D)
        load_weight(E, E.w_gate, "w_gate", ple_gate[1], 8, D)
        load_weight(E, E.w_proj, "w_proj", ple_proj[1], 2, D)
        P.dma(W=["fg_bc"], out=fg_bc, in_=final_g.partition_broadcast(128))

        def p3_load(ti):
            t = tiles[ti]
            par = ti % 2
            P.dma(W=[(("hbuf", par), s_) for s_ in range(4)], R=[("hA", ti)], out=E.hbuf[par][:, 0:t.NT, :],
                  in_=hA[t.row0:t.row0 + t.T, :].rearrange("(n p) d -> p n d", p=128))
            P.dma(W=[("pbuf", par)], out=E.pbuf[par][:, 0:t.NT, :],
                  in_=p_rows(t, 1).rearrange("(n p) d -> p n d", p=128))
            P.dma(W=[("catT", par)], R=[("catD", ti)], out=E.catT2[par][:, :, 0:t.T],
                  in_=catD[:, t.row0:t.row0 + t.T].rearrange("(c p) t -> p c t", p=128))

        p3_load(0)
        for ti, t in enumerate(tiles):
            T, NT = t.T, t.NT
            par = ti % 2
            hb, hkey = E.hbuf[par], ("hbuf", par)
            pb, pkey = E.pbuf[par], ("pbuf", par)
            if ti + 1 < len(tiles):
                p3_load(ti + 1)
            E.catT = E.catT2[par]
            tail(E, t, hb, hkey, pb, pkey, catkey=("catT", par))
            P.pool("memset", W=["ss"], ap=E.ss, constant=0.0)
            for s in range(NT):
                P.act("activation", R=[(hkey, s)], W=[("xnb", s % 2), "ss"], out=E.xnb[s % 2], in_=hb[:, s, :],
                      func=AF.Square, accum_out=E.ss[:, s:s + 1])
            P.act("activation", R=["ss"], W=["rstd"], out=E.rstd[:, 0:NT], in_=E.ss[:, 0:NT], func=AF.Ln,
                  scale=1.0 / D, bias=EPS)
            P.act("activation", R=["rstd"], W=["rstd"], out=E.rstd[:, 0:NT], in_=E.rstd[:, 0:NT], func=AF.Exp,
                  scale=-0.5)
            for s in range(NT):
                P.dve("scalar_tensor_tensor", R=[(hkey, s), "rstd", "fg_bc"], W=[(hkey, s)], out=hb[:, s, :], in0=hb[:, s, :],
                      scalar=E.rstd[:, s:s + 1], in1=fg_bc, op0=ALU.mult, op1=ALU.mult)
            dst = y_p[t.row0:t.row0 + T, :] if t.kind == "p" else y_s
            P.dma(R=[(hkey, s_) for s_ in range(4)], out=dst.rearrange("(n p) d -> p n d", p=128), in_=hb[:, 0:NT, :],
                  is_output=True)
        P.barrier()
        AR.off = base_off

    P.emit()
    global LAST_PROG
    LAST_PROG = P
    return nc


LAST_PROG = None


def shard_inputs(inp, cfg, n_cores):
    NP, S, NS, DS, PL = cfg.NP, cfg.S, cfg.NS, cfg.DS, cfg.PL
    f = lambda a: np.ascontiguousarray(np.asarray(a, dtype=np.float32))
    maps = []
    for c in range(n_cores):
        ps = slice(c * NP, (c + 1) * NP)
        ss = slice(c * NS, (c + 1) * NS)
        m = {
            "x_prompt": f(inp["x_prompt"][ps]).reshape(NP * S, D),
            "x_sample": f(inp["x_sample"][ss]).reshape(NS * DS, D),
            "state_a_conv": f(inp["state_a_conv"][0, ss]),
            "state_b_conv": f(inp["state_b_conv"][0, ss]),
            "cache_d_k": f(inp["cache_d_k"][0, ss]).reshape(NS, PL, DB),
            "cache_d_v": f(inp["cache_d_v"][0, ss]).reshape(NS, PL, DB),
            "p_prompt": f(inp["p_prompt"][:, ps]).reshape(2, NP * S, PLE),
            "p_sample": f(inp["p_sample"][:, ss]).reshape(2, NS * DS, PLE),
            "norm_g": f(inp["norm_g"]),
            "ple_gate": f(inp["ple_gate"]),
            "ple_proj": f(inp["ple_proj"]),
            "ab_w_in": f(inp["ab_w_in"][0]),
            "a_conv_w": f(inp["a_conv_w"][0]),
            "b_conv_w": f(inp["b_conv_w"][0]),
            "b_ln_g": f(inp["b_ln_g"][0]),
            "b_ln_b": f(inp["b_ln_b"][0]),
            "ab_w_out": f(inp["ab_w_out"][0]),
            "cd_w_in": f(inp["cd_w_in"][0]),
            "c_ln_g": f(inp["c_ln_g"][0]),
            "c_ln_b": f(inp["c_ln_b"][0]),
            "c_ws": f(inp["c_ws"][0]),
            "c_b": f(inp["c_b"][0]),
            "cd_w_out": f(inp["cd_w_out"][0]),
            "final_g": f(inp["final_g"]),
        }
        maps.append(m)
    return maps


def kernel(**inputs):
    n_cores = 8
    cfg = Cfg(NP=2, S=4096, NS=4, DS=32, PL=2048)
    nc = build_program(cfg)
    in_maps = shard_inputs(inputs, cfg, n_cores)
    res = run_bass_kernel_spmd(nc, in_maps, core_ids=list(range(n_cores)))
    r = res.results
    NP, S, NS, DS = cfg.NP, cfg.S, cfg.NS, cfg.DS
    cat = lambda name, shp: np.concatenate([np.asarray(r[c][name], dtype=np.float32).reshape(shp) for c in range(n_cores)], axis=0)
    y_prompt = cat("y_prompt", (NP, S, D))
    y_sample = cat("y_sample", (NS, DS, D))
    na_p = cat("new_a_prompt", (NP, HA, DB))[None]
    na_s = cat("new_a_sample", (NS, HA, DB))[None]
    nb_p = cat("new_b_prompt", (NP, HB, DB))[None]
    nb_s = cat("new_b_sample", (NS, HB, DB))[None]
    ncv = cat("new_cv_sample", (NS, DS, DB))[None]
    nk_p = cat("new_k_prompt", (NP, S, 8, 64))[None]
    nv_p = cat("new_v_prompt", (NP, S, 8, 64))[None]
    nk_s = cat("new_k_sample", (NS, DS, 8, 64))[None]
    nv_s = cat("new_v_sample", (NS, DS, 8, 64))[None]
    return (y_prompt, y_sample, na_p, na_s, nb_p, nb_s, ncv, nk_p, nv_p, nk_s, nv_s)
```

```python
import contextlib
import numpy as np
import concourse.bass as bass
import concourse.mybir as mybir
from concourse.bass_utils import run_bass_kernel_spmd

F32 = mybir.dt.float32
BF16 = mybir.dt.bfloat16
AF = mybir.ActivationFunctionType
ALU = mybir.AluOpType

ENGINES = ["pe", "act", "dve", "pool", "sp"]
D = 1024
DB = 512
PLE = 256
EPS = 1e-6
HB = 30
HA = 2


class Prog:
    EPOCH = 12000

    def __init__(self, nc, n_dma_slots=48):
        self.nc = nc
        self.ops = {e: [] for e in ENGINES}
        self.cnt = {e: 0 for e in ENGINES}
        self.last_w = {}
        self.readers = {}
        self.seen = {e: {} for e in ENGINES}
        self.n_dma_slots = n_dma_slots
        self.dma_next = 0
        self.dma_val = [0] * n_dma_slots
        self.out_tokens = []
        self.bank_i = 0

    def _need(self, eng, tok, waits):
        if tok is None:
            return
        if tok[0] == "E":
            _, e2, idx = tok
            if e2 == eng and eng == "pe":
                return
            k = ("E", e2)
        else:
            _, slot, idx = tok
            k = ("D", slot)
        if self.seen[eng].get(k, -1) >= idx:
            return
        waits[k] = max(waits.get(k, -1), idx)

    def op(self, eng, name, R=(), W=(), dma=False, is_output=False, **kw):
        waits = {}
        for k in R:
            self._need(eng, self.last_w.get(k), waits)
        for k in W:
            self._need(eng, self.last_w.get(k), waits)
            for t in self.readers.get(k, ()):
                self._need(eng, t, waits)
        if dma:
            slot = self.dma_next
            self.dma_next = (self.dma_next + 1) % self.n_dma_slots
            prev = self.dma_val[slot]
            if prev > 0:
                self._need(eng, ("D", slot, prev), waits)
            self.dma_val[slot] = prev + 16
            tok = ("D", slot, prev + 16)
        else:
            idx = self.cnt[eng]
            self.cnt[eng] += 1
            tok = ("E", eng, idx)
        for k, v in waits.items():
            self.seen[eng][k] = v
        self.ops[eng].append((name, kw, waits, tok))
        for k in W:
            self.last_w[k] = tok
            self.readers[k] = []
        for k in R:
            if k in W:
                continue
            self.readers.setdefault(k, []).append(tok)
        if is_output:
            self.out_tokens.append(tok)
        return tok

    def pe(self, name, **kw):
        return self.op("pe", name, **kw)

    def act(self, name, **kw):
        return self.op("act", name, **kw)

    def dve(self, name, **kw):
        return self.op("dve", name, **kw)

    def pool(self, name, **kw):
        return self.op("pool", name, **kw)

    def dma(self, **kw):
        return self.op("sp", "dma_start", dma=True, **kw)

    def barrier(self):
        for eng in ENGINES:
            waits = {}
            for e2 in ENGINES:
                if e2 != eng and self.cnt[e2] > 0:
                    self._need(eng, ("E", e2, self.cnt[e2] - 1), waits)
            for slot in range(self.n_dma_slots):
                if self.dma_val[slot] > 0:
                    self._need(eng, ("D", slot, self.dma_val[slot]), waits)
            for k, v in waits.items():
                self.seen[eng][k] = v
            self.ops[eng].append((None, None, waits, None))

    def emit(self):
        nc = self.nc
        with contextlib.ExitStack() as st:
            esem = {}
            for e in ENGINES:
                n_ep = (self.cnt[e] + self.EPOCH - 1) // self.EPOCH
                esem[e] = [st.enter_context(nc.semaphore(f"s_{e}{i}")) for i in range(max(n_ep, 1))]
            dsem = [st.enter_context(nc.semaphore(f"s_d{i}")) for i in range(self.n_dma_slots)]
            block = st.enter_context(nc.Block())

            def do_wait(h, k, v):
                if k[0] == "E":
                    h.wait_ge(esem[k[1]][v // self.EPOCH], v % self.EPOCH + 1)
                else:
                    h.wait_ge(dsem[k[1]], v)

            def run(ename):
                def body(h):
                    for name, kw, waits, tok in self.ops[ename]:
                        ws = list(waits.items())
                        if name is None:
                            for k, v in ws:
                                do_wait(h, k, v)
                            continue
                        for k, v in ws[1:]:
                            do_wait(h, k, v)
                        ins = getattr(h, name)(**kw)
                        if ws:
                            k, v = ws[0]
                            if k[0] == "E":
                                ins._wait_ge(esem[k[1]][v // self.EPOCH], v % self.EPOCH + 1)
                            else:
                                ins._wait_ge(dsem[k[1]], v)
                        if tok[0] == "E":
                            ins.then_inc(esem[tok[1]][tok[2] // self.EPOCH], 1)
                        else:
                            ins.then_inc(dsem[tok[1]], 16)
                    if ename == "sp":
                        for tok in self.out_tokens:
                            do_wait(h, ("D", tok[1]), tok[2])
                return body

            block.tensor(run("pe"))
            block.scalar(run("act"))
            block.vector(run("dve"))
            block.gpsimd(run("pool"))
            block.sync(run("sp"))


class Arena:
    def __init__(self, nc, nbytes):
        self.t = nc.alloc_sbuf_tensor("arena", [128, nbytes // 4], F32)
        self.off = 0
        self.cap = nbytes // 4

    def alloc(self, shape, dt=F32):
        n = int(np.prod(shape[1:]))
        nw = (n if dt == F32 else (n + 1) // 2)
        nw = (nw + 7) // 8 * 8
        assert self.off + nw <= self.cap, f"SBUF arena overflow: need {(self.off + nw) * 4} B"
        ap = self.t[0:shape[0], self.off:self.off + nw]
        self.off += nw
        if dt != F32:
            ap = ap.bitcast(dt)
        ap = ap[:, 0:n]
        if len(shape) == 3:
            ap = ap.rearrange("p (a b) -> p a b", a=shape[1])
        elif len(shape) == 4:
            ap = ap.rearrange("p (a b c) -> p a b c", a=shape[1], b=shape[2])
        return ap


class Cfg:
    def __init__(self, NP=2, S=4096, NS=4, DS=32, PL=2048, passes=(1, 2, 3), debug=False, stop=0):
        self.NP, self.S, self.NS, self.DS, self.PL = NP, S, NS, DS, PL
        self.passes = passes
        self.debug = debug
        self.stop = stop
        assert NS * DS == 128 and S % 512 == 0 and PL % 128 == 0
        self.NTOK = NP * S + 128


class Tile:
    def __init__(self, kind, seq, t0, T, nseg, L, row0, last):
        self.kind, self.seq, self.t0, self.T, self.nseg, self.L = kind, seq, t0, T, nseg, L
        self.row0 = row0
        self.last = last
        self.NT = T // 128


def make_tiles(cfg):
    tiles = []
    for q in range(cfg.NP):
        n = cfg.S // 512
        for i in range(n):
            tiles.append(Tile("p", q, i * 512, 512, 1, 512, q * cfg.S + i * 512, i == n - 1))
    tiles.append(Tile("s", 0, 0, 128, cfg.NS, cfg.DS, cfg.NP * cfg.S, True))
    return tiles


def build_program(cfg):
    nc = bass.Bass("TRN2", target_bir_lowering=False)
    NP, S, NS, DS, PL = cfg.NP, cfg.S, cfg.NS, cfg.DS, cfg.PL
    NTOK = cfg.NTOK

    def din(name, shape):
        return nc.dram_tensor(name, list(shape), F32, kind="ExternalInput").ap()

    def dout(name, shape):
        return nc.dram_tensor(name, list(shape), F32, kind="ExternalOutput").ap()

    x_p = din("x_prompt", [NP * S, D])
    x_s = din("x_sample", [128, D])
    st_a = din("state_a_conv", [NS, HA, DB])
    st_b = din("state_b_conv", [NS, HB, DB])
    ck = din("cache_d_k", [NS, PL, DB])
    cv = din("cache_d_v", [NS, PL, DB])
    p_p = din("p_prompt", [2, NP * S, PLE])
    p_s = din("p_sample", [2, 128, PLE])
    norm_g = din("norm_g", [2, D])
    ple_gate = din("ple_gate", [2, D, D])
    ple_proj = din("ple_proj", [2, PLE, D])
    ab_w_in = din("ab_w_in", [D, 7 * DB])
    a_conv_w = din("a_conv_w", [3, DB])
    b_conv_w = din("b_conv_w", [31, DB])
    b_ln_g = din("b_ln_g", [DB])
    b_ln_b = din("b_ln_b", [DB])
    ab_w_out = din("ab_w_out", [D, D])
    cd_w_in = din("cd_w_in", [D, 7 * DB])
    c_ln_g = din("c_ln_g", [DB])
    c_ln_b = din("c_ln_b", [DB])
    c_ws = din("c_ws", [4, 128, 128])
    c_b = din("c_b", [4, 128])
    cd_w_out = din("cd_w_out", [D, D])
    final_g = din("final_g", [D])

    y_p = dout("y_prompt", [NP * S, D])
    y_s = dout("y_sample", [128, D])
    na_p = dout("new_a_prompt", [NP, HA, DB])
    na_s = dout("new_a_sample", [NS, HA, DB])
    nb_p = dout("new_b_prompt", [NP, HB, DB])
    nb_s = dout("new_b_sample", [NS, HB, DB])
    ncv_s = dout("new_cv_sample", [128, DB])
    nk_p = dout("new_k_prompt", [NP * S, DB])
    nv_p = dout("new_v_prompt", [NP * S, DB])
    nk_s = dout("new_k_sample", [128, DB])
    nv_s = dout("new_v_sample", [128, DB])

    kind_scr = "ExternalOutput" if cfg.debug else "Internal"
    hA = nc.dram_tensor("hA", [NTOK, D], F32, kind=kind_scr).ap()
    catD = nc.dram_tensor("catD", [D, NTOK], BF16, kind=kind_scr).ap()

    P = Prog(nc)
    tiles = make_tiles(cfg)
    AR = Arena(nc, 207 * 1024)

    psf = [nc.alloc_psum_tensor(f"ps{i}", [128, 512], F32)[:] for i in range(8)]
    psb = [p.bitcast(BF16) for p in psf]

    def bank():
        b = P.bank_i
        P.bank_i = (P.bank_i + 1) % 6
        return b

    def PS(b):
        return ("ps", b)

    identf = AR.alloc([128, 128])
    identb = AR.alloc([128, 128], BF16)
    onesf = AR.alloc([128, 128])
    ngcol = AR.alloc([128, 2, 8])
    P.pool("memset", W=["identf"], ap=identf, constant=0.0)
    P.pool("affine_select", R=["identf"], W=["identf"], out=identf, in_=identf, pattern=[[-1, 128]],
           compare_op=ALU.not_equal, fill=1.0, base=0, channel_multiplier=1)
    P.dve("tensor_copy", R=["identf"], W=["identb"], out=identb, in_=identf)
    P.pool("memset", W=["onesf"], ap=onesf, constant=1.0)
    import os as _os
    for _i in range(int(_os.environ.get('KDUMMY', '0'))):
        P.dve("tensor_copy", R=["identf"], W=["identb"], out=identb, in_=identf)
    for i in range(2):
        P.dma(W=["ngcol"], out=ngcol[:, i, :], in_=norm_g[i].rearrange("(kc p) -> p kc", p=128),
              allow_slow_non_contiguous=True)
    base_off = AR.off

    def x_rows(t):
        return x_p[t.row0:t.row0 + t.T, :] if t.kind == "p" else x_s

    def p_rows(t, layer):
        return p_p[layer, t.row0:t.row0 + t.T, :] if t.kind == "p" else p_s[layer]

    class Env:
        pass

    def common_bufs(n_h=2, n_tmp=6):
        E = Env()
        E.hbuf = [AR.alloc([128, 4, D]) for _ in range(n_h)]
        E.xnb = [AR.alloc([128, D], BF16) for _ in range(2)]
        E.actT = AR.alloc([128, 8, 512], BF16)
        E.catT = AR.alloc([128, 8, 512], BF16)
        E.ss = AR.alloc([128, 4])
        E.rstd = AR.alloc([128, 4])
        E.ftmp = [AR.alloc([128, 512]) for _ in range(n_tmp)]
        E.ftmp_i = 0
        E.cast_i = 0
        return E

    def tmp(E):
        i = E.ftmp_i
        E.ftmp_i = (i + 1) % len(E.ftmp)
        return E.ftmp[i], ("ftmp", i)

    def load_weight(E, dst, dkey, src, nk, ncols, gain_i=None):
        for kc in range(nk):
            for c0 in range(0, ncols, 512):
                st_, sk = tmp(E)
                P.dma(W=[sk], out=st_, in_=src[kc * 128:(kc + 1) * 128, c0:c0 + 512])
                if gain_i is not None:
                    P.act("activation", R=[sk, "ngcol"], W=[dkey], out=dst[:, kc, c0:c0 + 512], in_=st_, func=AF.Copy,
                          scale=ngcol[:, gain_i, kc:kc + 1])
                elif E.cast_i % 2 == 0:
                    P.act("activation", R=[sk], W=[dkey], out=dst[:, kc, c0:c0 + 512], in_=st_, func=AF.Copy)
                else:
                    P.dve("tensor_copy", R=[sk], W=[dkey], out=dst[:, kc, c0:c0 + 512], in_=st_)
                E.cast_i += 1

    def transpose_into(src, skey, nchunk, dstT, dkey, s):
        b = bank()
        for c in range(nchunk):
            P.pe("transpose", R=[skey, "identb"], W=[PS(b)], out=psb[b][:, c * 128:(c + 1) * 128],
                 in_=src[:, c * 128:(c + 1) * 128], identity=identb)
        P.dve("tensor_copy", R=[PS(b)], W=[dkey], out=dstT[:, 0:nchunk, s * 128:(s + 1) * 128],
              in_=psb[b][:, 0:nchunk * 128].rearrange("p (c t) -> p c t", c=nchunk))

    def rmsnorm_to_T(E, t, hb, hkey):
        NT = t.NT
        P.pool("memset", W=["ss"], ap=E.ss, constant=0.0)
        for s in range(NT):
            P.act("activation", R=[(hkey, s)], W=[("xnb", s % 2), "ss"], out=E.xnb[s % 2], in_=hb[:, s, :],
                  func=AF.Square, accum_out=E.ss[:, s:s + 1])
        P.act("activation", R=["ss"], W=["rstd"], out=E.rstd[:, 0:NT], in_=E.ss[:, 0:NT], func=AF.Ln, scale=1.0 / D,
              bias=EPS)
        P.act("activation", R=["rstd"], W=["rstd"], out=E.rstd[:, 0:NT], in_=E.rstd[:, 0:NT], func=AF.Exp, scale=-0.5)
        for s in range(NT):
            P.act("activation", R=[(hkey, s), "rstd"], W=[("xnb", s % 2)], out=E.xnb[s % 2], in_=hb[:, s, :],
                  func=AF.Copy, scale=E.rstd[:, s:s + 1])
            transpose_into(E.xnb[s % 2], ("xnb", s % 2), 8, E.actT, ("actT", s), s)

    def proj_fm(E, b, w, oc, T):
        for kc in range(8):
            P.pe("matmul", R=[("actT", s_) for s_ in range(T // 128)] + ["w_big"], W=[PS(b)], out=psf[b][:, 0:T],
                 lhsT=w[:, kc, oc * 128:(oc + 1) * 128], rhs=E.actT[:, kc, 0:T], start=(kc == 0), stop=(kc == 7))

    def sigmoid_from(E, src_ap, skey, T, scale=-1.0, bias=None, extra=()):
        tt, tk = tmp(E)
        kw = {} if bias is None else {"bias": bias}
        P.act("activation", R=[skey] + list(extra), W=[tk], out=tt[:, 0:T], in_=src_ap, func=AF.Exp, scale=scale, **kw)
        P.act("activation", R=[tk], W=[tk], out=tt[:, 0:T], in_=tt[:, 0:T], func=AF.Ln, bias=1.0)
        P.act("activation", R=[tk], W=[tk], out=tt[:, 0:T], in_=tt[:, 0:T], func=AF.Exp, scale=-1.0)
        return tt, tk

    def sigmoid_multi(E, srcs, T):
        outs = []
        for (ap, key, scale, bias, extra) in srcs:
            tt, tk = tmp(E)
            kw = {} if bias is None else {"bias": bias}
            P.act("activation", R=[key] + list(extra), W=[tk], out=tt[:, 0:T], in_=ap, func=AF.Exp, scale=scale, **kw)
            outs.append((tt, tk))
        for (tt, tk) in outs:
            P.act("activation", R=[tk], W=[tk], out=tt[:, 0:T], in_=tt[:, 0:T], func=AF.Ln, bias=1.0)
        for (tt, tk) in outs:
            P.act("activation", R=[tk], W=[tk], out=tt[:, 0:T], in_=tt[:, 0:T], func=AF.Exp, scale=-1.0)
        return outs

    def tail(E, t, hb, hkey, pb, pkey, catkey="catT"):
        NT = t.NT
        halves = [slice(0, 512), slice(512, 1024)]
        for s in range(NT):
            sl = slice(s * 128, (s + 1) * 128)
            for hs in halves:
                b = bank()
                for kc in range(8):
                    P.pe("matmul", R=[catkey, "w_out"], W=[PS(b)], out=psf[b], lhsT=E.catT[:, kc, sl],
                         rhs=E.w_out[:, kc, hs], start=(kc == 0), stop=(kc == 7))
                P.dve("tensor_tensor", R=[PS(b), (hkey, s)], W=[(hkey, s)], out=hb[:, s, hs], in0=hb[:, s, hs],
                      in1=psf[b], op=ALU.add)
        for s in range(NT):
            xi = s % 2
            P.act("activation", R=[(hkey, s)], W=[("xnb", xi)], out=E.xnb[xi], in_=hb[:, s, :], func=AF.Copy)
            transpose_into(E.xnb[xi], ("xnb", xi), 8, E.actT, ("actT", s), s)
            P.act("activation", R=[pkey], W=[("xnp", xi)], out=E.xnp[xi], in_=pb[:, s, :], func=AF.Copy)
            transpose_into(E.xnp[xi], ("xnp", xi), 2, E.pT, ("pT", s), s)
        for s in range(NT):
            sl = slice(s * 128, (s + 1) * 128)
            bgs, bps = [], []
            for hs in halves:
                bg = bank()
                for kc in range(8):
                    P.pe("matmul", R=[("actT", s), "w_gate"], W=[PS(bg)], out=psf[bg], lhsT=E.actT[:, kc, sl],
                         rhs=E.w_gate[:, kc, hs], start=(kc == 0), stop=(kc == 7))
                bp = bank()
                for kc in range(2):
                    P.pe("matmul", R=[("pT", s), "w_proj"], W=[PS(bp)], out=psf[bp], lhsT=E.pT[:, kc, sl],
                         rhs=E.w_proj[:, kc, hs], start=(kc == 0), stop=(kc == 1))
                bgs.append(bg)
                bps.append(bp)
            sgs = sigmoid_multi(E, [(psf[bg], PS(bg), -1.0, None, ()) for bg in bgs], 512)
            for (sg, sgk), bp in zip(sgs, bps):
                P.dve("tensor_tensor", R=[PS(bp), sgk], W=[sgk], out=sg, in0=psf[bp], in1=sg, op=ALU.mult)
            for (sg, sgk), hs in zip(sgs, halves):
                P.dve("tensor_tensor", R=[sgk, (hkey, s)], W=[(hkey, s)], out=hb[:, s, hs], in0=hb[:, s, hs], in1=sg,
                      op=ALU.add)

    def state_out(E, t, bufs, key, H, L, dst):
        for q in range(t.nseg):
            b = bank()
            for j in range(4):
                P.pe("transpose", R=[(key, j), "identf"], W=[PS(b)], out=psf[b][0:H, j * 128:(j + 1) * 128],
                     in_=bufs[j][:, q, L:L + H], identity=identf)
            P.act("activation", R=[PS(b)], W=["hst"], out=E.hst[0:H, :], in_=psf[b][0:H, :], func=AF.Copy)
            P.dma(R=["hst"], out=dst[t.seq if t.kind == "p" else q], in_=E.hst[0:H, :], is_output=True)

    if 1 in cfg.passes:
        E = common_bufs()
        E.pbuf = [AR.alloc([128, 4, PLE]) for _ in range(2)]
        E.pT = AR.alloc([128, 2, 512], BF16)
        E.xnp = [AR.alloc([128, PLE], BF16) for _ in range(2)]
        E.w_big = AR.alloc([128, 8, 7 * DB], BF16)
        E.w_out = AR.alloc([128, 8, D], BF16)
        E.w_gate = AR.alloc([128, 8, D], BF16)
        E.w_proj = AR.alloc([128, 2, D], BF16)
        awc = AR.alloc([128, 4, 3])
        bwc = AR.alloc([128, 4, 31])
        lncol = AR.alloc([128, 4, 4])
        ubuf_p = [AR.alloc([128, 1, HA + 512]) for j in range(4)]
        gbuf_p = [AR.alloc([128, 1, HB + 512]) for j in range(4)]
        ubuf_s = [AR.alloc([128, NS, HA + DS]) for j in range(4)]
        gbuf_s = [AR.alloc([128, NS, HB + DS]) for j in range(4)]
        bconv = [AR.alloc([128, 512]) for j in range(4)]
        negmean = AR.alloc([128, 512])
        rstdB = AR.alloc([128, 512])
        E.hst = AR.alloc([32, 512])
        print("P1 SBUF bytes/partition:", AR.off * 4)

        load_weight(E, E.w_big, "w_big", ab_w_in, 8, 7 * DB, gain_i=0)
        load_weight(E, E.w_out, "w_out", ab_w_out, 8, D)
        load_weight(E, E.w_gate, "w_gate", ple_gate[0], 8, D)
        load_weight(E, E.w_proj, "w_proj", ple_proj[0], 2, D)
        for j in range(4):
            P.dma(W=["awc"], out=awc[:, j, :], in_=a_conv_w[:, j * 128:(j + 1) * 128].rearrange("w p -> p w"),
                  allow_slow_non_contiguous=True)
            P.dma(W=["bwc"], out=bwc[:, j, :], in_=b_conv_w[:, j * 128:(j + 1) * 128].rearrange("w p -> p w"),
                  allow_slow_non_contiguous=True)
        P.dma(W=["lncol"], out=lncol[:, :, 0], in_=b_ln_g.rearrange("(j p) -> p j", p=128),
              allow_slow_non_contiguous=True)
        P.dma(W=["lncol"], out=lncol[:, :, 1], in_=b_ln_b.rearrange("(j p) -> p j", p=128),
              allow_slow_non_contiguous=True)
        P.pool("tensor_scalar", R=["lncol"], W=["lncol"], out=lncol[:, :, 2:4], in0=lncol[:, :, 0:2], scalar1=-1.0,
               scalar2=None, op0=ALU.mult)

        def p1_load(ti):
            t = tiles[ti]
            par = ti % 2
            P.dma(W=[(("hbuf", par), s_) for s_ in range(4)], out=E.hbuf[par][:, 0:t.NT, :],
                  in_=x_rows(t).rearrange("(n p) d -> p n d", p=128))
            P.dma(W=[("pbuf", par)], out=E.pbuf[par][:, 0:t.NT, :],
                  in_=p_rows(t, 0).rearrange("(n p) d -> p n d", p=128))

        for ti, t in enumerate(tiles):
            T, NT, nseg, L = t.T, t.NT, t.nseg, t.L
            par = ti % 2
            hb, hkey = E.hbuf[par], ("hbuf", par)
            pb, pkey = E.pbuf[par], ("pbuf", par)
            isp = t.kind == "p"
            ub, gb = (ubuf_p, gbuf_p) if isp else (ubuf_s, gbuf_s)
            ukey, gkey = ("ubuf_p", "gbuf_p") if isp else ("ubuf_s", "gbuf_s")

            def v3(ap):
                return ap.rearrange("p (n l) -> p n l", n=nseg)

            if ti == 0:
                p1_load(0)
            if isp and t.t0 == 0:
                for j in range(4):
                    P.pool("memset", W=[(ukey, j)], ap=ub[j][:, :, 0:HA], constant=0.0)
                    P.pool("memset", W=[(gkey, j)], ap=gb[j][:, :, 0:HB], constant=0.0)
            if not isp:
                for q in range(NS):
                    for (stt, H, bufs, key) in ((st_a, HA, ub, ukey), (st_b, HB, gb, gkey)):
                        P.dma(W=["hst"], out=E.hst[0:H, :], in_=stt[q])
                        b = bank()
                        for j in range(4):
                            P.pe("transpose", R=["hst", "identf"], W=[PS(b)], out=psf[b][:, j * 32:j * 32 + H],
                                 in_=E.hst[0:H, j * 128:(j + 1) * 128], identity=identf[0:H, 0:H])
                        for j in range(4):
                            P.act("activation", R=[PS(b)], W=[(key, j)], out=bufs[j][:, q, 0:H],
                                  in_=psf[b][:, j * 32:j * 32 + H], func=AF.Copy)

            rmsnorm_to_T(E, t, hb, hkey)
            if ti + 1 < len(tiles):
                p1_load(ti + 1)

            for pair in ((0, 1), (2, 3)):
                bvs, bgs = {}, {}
                for j in pair:
                    bvs[j], bgs[j] = bank(), bank()
                    proj_fm(E, bvs[j], E.w_big, 16 + j, T)
                    proj_fm(E, bgs[j], E.w_big, 20 + j, T)
                sgs = sigmoid_multi(E, [(psf[bgs[j]][:, 0:T], PS(bgs[j]), -1.0, None, ()) for j in pair], T)
                for (sg, sgk), j in zip(sgs, pair):
                    P.dve("tensor_tensor", R=[PS(bvs[j]), sgk], W=[(gkey, j)], out=gb[j][:, :, HB:HB + L],
                          in0=v3(psf[bvs[j]][:, 0:T]), in1=v3(sg[:, 0:T]), op=ALU.mult)
            def gen_A():
                for pair in ((0, 1), (2, 3)):
                    bxs, bcs = {}, {}
                    for j in pair:
                        bxs[j], bcs[j] = bank(), bank()
                        proj_fm(E, bxs[j], E.w_big, 0 + j, T)
                        proj_fm(E, bcs[j], E.w_big, 4 + j, T)
                    yield
                    tcs = {}
                    for j in pair:
                        tcs[j] = tmp(E)
                        P.act("activation", R=[PS(bcs[j])], W=[tcs[j][1]], out=tcs[j][0][:, 0:T], in_=psf[bcs[j]][:, 0:T],
                              func=AF.Copy)
                    for j in pair:
                        P.dve("tensor_tensor", R=[PS(bxs[j]), tcs[j][1]], W=[(ukey, j)], out=ub[j][:, :, HA:HA + L],
                              in0=v3(psf[bxs[j]][:, 0:T]), in1=v3(tcs[j][0][:, 0:T]), op=ALU.mult)
                    yield
                    bzs = {}
                    for j in pair:
                        bzs[j] = bank()
                        proj_fm(E, bzs[j], E.w_big, 12 + j, T)
                    yield
                    sgs = sigmoid_multi(E, [(psf[bzs[j]][:, 0:T], PS(bzs[j]), -1.0, None, ()) for j in pair], T)
                    yield
                    cvs = {}
                    for j in pair:
                        cvs[j] = tmp(E)
                        P.dve("tensor_scalar", R=[(ukey, j), "awc"], W=[cvs[j][1]], out=v3(cvs[j][0][:, 0:T]),
                              in0=ub[j][:, :, 2:2 + L], scalar1=awc[:, j, 2:3], scalar2=None, op0=ALU.mult)
                    yield
                    for w in (1, 0):
                        for j in pair:
                            P.dve("scalar_tensor_tensor", R=[(ukey, j), "awc", cvs[j][1]], W=[cvs[j][1]],
                                  out=v3(cvs[j][0][:, 0:T]), in0=ub[j][:, :, w:w + L], scalar=awc[:, j, w:w + 1],
                                  in1=v3(cvs[j][0][:, 0:T]), op0=ALU.mult, op1=ALU.add)
                    yield
                    bbs = {}
                    for j in pair:
                        bbs[j] = bank()
                        proj_fm(E, bbs[j], E.w_big, 8 + j, T)
                    yield
                    for (sg, sgk), j in zip(sgs, pair):
                        P.dve("tensor_tensor", R=[PS(bzs[j]), sgk], W=[sgk], out=sg[:, 0:T], in0=psf[bzs[j]][:, 0:T],
                              in1=sg[:, 0:T], op=ALU.mult)
                    for j in pair:
                        P.dve("tensor_tensor", R=[PS(bbs[j]), cvs[j][1]], W=[cvs[j][1]], out=cvs[j][0][:, 0:T],
                              in0=psf[bbs[j]][:, 0:T], in1=cvs[j][0][:, 0:T], op=ALU.mult)
                    yield
                    for (sg, sgk), j in zip(sgs, pair):
                        P.dve("tensor_tensor", R=[sgk, cvs[j][1]], W=["catT"], out=E.catT[:, j, 0:T], in0=sg[:, 0:T],
                              in1=cvs[j][0][:, 0:T], op=ALU.mult)
                    yield
            def gen_bz():
                for pair in ((0, 1), (2, 3)):
                    bzs = {}
                    for j in pair:
                        bzs[j] = bank()
                        proj_fm(E, bzs[j], E.w_big, 24 + j, T)
                    yield
                    sgs = sigmoid_multi(E, [(psf[bzs[j]][:, 0:T], PS(bzs[j]), -1.0, None, ()) for j in pair], T)
                    yield
                    for (sg, sgk), j in zip(sgs, pair):
                        P.dve("tensor_tensor", R=[PS(bzs[j]), sgk], W=["catT"], out=E.catT[:, 4 + j, 0:T],
                              in0=psf[bzs[j]][:, 0:T], in1=sg[:, 0:T], op=ALU.mult)
                    yield
            def gen_conv():
                for w in range(31):
                    for j in range(4):
                        if w == 0:
                            P.dve("tensor_scalar", R=[(gkey, j), "bwc"], W=[("bconv", j)], out=v3(bconv[j][:, 0:T]),
                                  in0=gb[j][:, :, 0:L], scalar1=bwc[:, j, 0:1], scalar2=None, op0=ALU.mult)
                        else:
                            P.dve("scalar_tensor_tensor", R=[(gkey, j), "bwc", ("bconv", j)], W=[("bconv", j)],
                                  out=v3(bconv[j][:, 0:T]), in0=gb[j][:, :, w:w + L], scalar=bwc[:, j, w:w + 1],
                                  in1=v3(bconv[j][:, 0:T]), op0=ALU.mult, op1=ALU.add)
                    yield
            def chain(*gs):
                for g in gs:
                    yield from g

            g1, g2 = gen_conv(), chain(gen_A(), gen_bz())
            alive = [g1, g2]
            while alive:
                for g in list(alive):
                    try:
                        next(g)
                    except StopIteration:
                        alive.remove(g)
            if t.last:
                state_out(E, t, ub, ukey, HA, L, na_p if isp else na_s)
            else:
                for j in range(4):
                    P.pool("tensor_copy", R=[(ukey, j)], W=[(ukey, j)], out=ub[j][:, :, 0:HA], in_=ub[j][:, :, L:L + HA])

            if t.last:
                state_out(E, t, gb, gkey, HB, L, nb_p if isp else nb_s)
            else:
                for j in range(4):
                    P.pool("tensor_copy", R=[(gkey, j)], W=[(gkey, j)], out=gb[j][:, :, 0:HB], in_=gb[j][:, :, L:L + HB])
            sqs = []
            for j in range(4):
                sq, sqk = tmp(E)
                P.act("activation", R=[("bconv", j)], W=[sqk], out=sq[:, 0:T], in_=bconv[j][:, 0:T], func=AF.Square)
                sqs.append((sq, sqk))
            for j in range(4):
                P.pe("matmul", R=[("bconv", j), "onesf"], W=[PS(6)], out=psf[6][:, 0:T], lhsT=onesf,
                     rhs=bconv[j][:, 0:T], start=(j == 0), stop=(j == 3))
            for j in range(4):
                sq, sqk = sqs[j]
                P.pe("matmul", R=[sqk, "onesf"], W=[PS(7)], out=psf[7][:, 0:T], lhsT=onesf, rhs=sq[:, 0:T],
                     start=(j == 0), stop=(j == 3))
            P.act("activation", R=[PS(6)], W=["negmean"], out=negmean[:, 0:T], in_=psf[6][:, 0:T], func=AF.Copy,
                  scale=-1.0 / DB)
            P.dve("tensor_tensor", R=["negmean"], W=["rstdB"], out=rstdB[:, 0:T], in0=negmean[:, 0:T],
                  in1=negmean[:, 0:T], op=ALU.mult)
            P.dve("scalar_tensor_tensor", R=[PS(7), "rstdB"], W=["rstdB"], out=rstdB[:, 0:T], in0=psf[7][:, 0:T],
                  scalar=1.0 / DB, in1=rstdB[:, 0:T], op0=ALU.mult, op1=ALU.subtract)
            P.act("activation", R=["rstdB"], W=["rstdB"], out=rstdB[:, 0:T], in_=rstdB[:, 0:T], func=AF.Ln, bias=EPS)
            P.act("activation", R=["rstdB"], W=["rstdB"], out=rstdB[:, 0:T], in_=rstdB[:, 0:T], func=AF.Exp, scale=-0.5)
            for pair in ((0, 1), (2, 3)):
                yvs = {}
                for j in pair:
                    yvs[j] = tmp(E)
                    P.dve("tensor_tensor", R=[("bconv", j), "negmean"], W=[yvs[j][1]], out=yvs[j][0][:, 0:T],
                          in0=bconv[j][:, 0:T], in1=negmean[:, 0:T], op=ALU.add)
                for j in pair:
                    P.dve("tensor_tensor", R=[yvs[j][1], "rstdB"], W=[yvs[j][1]], out=yvs[j][0][:, 0:T],
                          in0=yvs[j][0][:, 0:T], in1=rstdB[:, 0:T], op=ALU.mult)
                sgs = sigmoid_multi(E, [(yvs[j][0][:, 0:T], yvs[j][1], lncol[:, j, 2:3], lncol[:, j, 3:4], ["lncol"])
                                        for j in pair], T)
                for k_, j in enumerate(pair):
                    sgy, sgyk = sgs[k_]
                    P.dve("tensor_scalar", R=[yvs[j][1], "lncol", sgyk], W=[yvs[j][1]], out=yvs[j][0][:, 0:T],
                          in0=yvs[j][0][:, 0:T], scalar1=lncol[:, j, 0:1], scalar2=lncol[:, j, 1:2], op0=ALU.mult,
                          op1=ALU.add)
                for k_, j in enumerate(pair):
                    sgy, sgyk = sgs[k_]
                    P.dve("tensor_tensor", R=[yvs[j][1], sgyk], W=[yvs[j][1]], out=yvs[j][0][:, 0:T],
                          in0=yvs[j][0][:, 0:T], in1=sgy[:, 0:T], op=ALU.mult)
                for k_, j in enumerate(pair):
                    P.dve("tensor_tensor", R=[yvs[j][1], "catT"], W=["catT"], out=E.catT[:, 4 + j, 0:T],
                          in0=yvs[j][0][:, 0:T], in1=E.catT[:, 4 + j, 0:T], op=ALU.mult)

            tail(E, t, hb, hkey, pb, pkey)
            P.dma(R=[(hkey, s_) for s_ in range(4)], W=[("hA", ti)],
                  out=hA[t.row0:t.row0 + T, :].rearrange("(n p) d -> p n d", p=128),
                  in_=hb[:, 0:NT, :], is_output=cfg.debug)
        P.barrier()
        AR.off = base_off


    if 2 in cfg.passes:
        CAP = max(S, PL)
        E = common_bufs(n_h=1, n_tmp=4)
        E.w_big = AR.alloc([128, 8, 7 * DB], BF16)
        kT = AR.alloc([128, 4, CAP], BF16)
        Vc = AR.alloc([128, CAP // 128, DB], BF16)
        qT = AR.alloc([128, 4, 512], BF16)
        cvn = AR.alloc([128, 4, DB], BF16)
        kst = [AR.alloc([128, DB]) for _ in range(3)]
        kst_i = [0]
        kbf = [AR.alloc([128, DB], BF16) for _ in range(2)]
        spb = [AR.alloc([128, 512], BF16) for _ in range(4)]
        abf = [AR.alloc([128, 512], BF16) for _ in range(4)]
        rr = [0, 0, 0]
        sacc = [AR.alloc([128, 512], BF16) for _ in range(2)]
        g_bc = AR.alloc([128, DB])
        b_bc = AR.alloc([128, DB])
        WT = AR.alloc([128, 4, 128], BF16)
        WTs = AR.alloc([128, 4, 128], BF16)
        negU = AR.alloc([128, 128], BF16)
        negOnes = AR.alloc([128, 128], BF16)
        ones_row = AR.alloc([1, 128])
        cb_row = AR.alloc([1, 4, 128])
        cb_row_s = AR.alloc([1, 4, 128])
        lnst = AR.alloc([128, 8])
        kT_new = AR.alloc([128, 4, 128], BF16)
        if CAP - PL >= 2048:
            Vn = [kT[0:32, q, PL:PL + DB] for q in range(4)]
            dzs = kT[:, 0, PL + DB:PL + DB + 1024].bitcast(F32).rearrange("p (a b) -> p a b", a=4)
        else:
            Vn = [AR.alloc([32, DB], BF16) for _ in range(4)]
            dzs = AR.alloc([128, 4, 128])
        print("P2 SBUF bytes/partition:", AR.off * 4)

        def stage():
            i = kst_i[0]
            kst_i[0] = (i + 1) % len(kst)
            return kst[i], ("kst", i)

        load_weight(E, E.w_big, "w_big", cd_w_in, 8, 7 * DB, gain_i=1)
        P.dma(W=["g_bc"], out=g_bc, in_=c_ln_g.partition_broadcast(128))
        P.dma(W=["b_bc"], out=b_bc, in_=c_ln_b.partition_broadcast(128))
        P.dma(W=["cb_row"], out=cb_row, in_=c_b.rearrange("(o h) t -> o h t", o=1))
        for q in range(NS):
            P.dma(W=["cb_row_s"], out=cb_row_s[:, :, q * DS:(q + 1) * DS],
                  in_=c_b[:, 0:DS].rearrange("(o h) t -> o h t", o=1))
        P.pool("memset", W=["ones_row"], ap=ones_row, constant=1.0)
        P.pool("memset", W=["negOnes"], ap=negOnes, constant=-1.0)
        tU, tUk = tmp(E)
        P.pool("memset", W=[tUk], ap=tU[:, 0:128], constant=-1.0)
        P.pool("affine_select", R=[tUk], W=[tUk], out=tU[:, 0:128], in_=tU[:, 0:128], pattern=[[-1, 128]],
               compare_op=ALU.is_ge, fill=0.0, base=0, channel_multiplier=1)
        P.dve("tensor_copy", R=[tUk], W=["negU"], out=negU, in_=tU[:, 0:128])
        for variant, dstW, dk in ((0, WT, "WT"), (1, WTs, "WTs")):
            for hh in range(4):
                wt_, wk = tmp(E)
                if variant == 0:
                    P.dma(W=[wk], out=wt_[:, 0:128], in_=c_ws[hh])
                else:
                    P.pool("memset", W=[wk], ap=wt_[:, 0:128], constant=0.0)
                    for q in range(NS):
                        P.dma(W=[wk], out=wt_[q * DS:(q + 1) * DS, q * DS:(q + 1) * DS], in_=c_ws[hh, 0:DS, 0:DS])
                P.pool("affine_select", R=[wk], W=[wk], out=wt_[:, 0:128], in_=wt_[:, 0:128], pattern=[[-1, 128]],
                       compare_op=ALU.is_ge, fill=0.0, base=0, channel_multiplier=1)
                P.act("activation", R=[wk], W=[("xnb", 0)], out=E.xnb[0][:, 0:128], in_=wt_[:, 0:128], func=AF.Copy)
                b = bank()
                P.pe("transpose", R=[("xnb", 0), "identb"], W=[PS(b)], out=psb[b][:, 0:128], in_=E.xnb[0][:, 0:128],
                     identity=identb)
                P.dve("tensor_copy", R=[PS(b)], W=[dk], out=dstW[:, hh, :], in_=psb[b][:, 0:128])

        def tri_mask(buf, key, nk, c0):
            P.pool("affine_select", R=[key], W=[key], out=buf[0:nk, c0:c0 + nk], in_=buf[0:nk, c0:c0 + nk],
                   pattern=[[1, nk]], compare_op=ALU.is_gt, fill=0.0, base=0, channel_multiplier=-1)

        def attention(hp, qa, qb, blocks):
            for h in range(2):
                P.pool("memset", W=[("sacc", h)], ap=sacc[h][:, qa:qb], constant=0.0)
            nb = len(blocks)
            items = [(bi, h) for bi in range(nb) for h in range(2)]
            n = len(items)
            st = {}

            def Sz(i):
                bi, h = items[i]
                kTa, Va, nk, c0, tri = blocks[bi]
                r0 = 64 * h
                bz = rr[2] % 6
                rr[2] += 1
                P.pe("matmul", R=["kT", "qT"], W=[PS(bz)], out=psf[bz][0:nk, c0:qb], lhsT=kTa[r0:r0 + 64, 0:nk],
                     rhs=qT[r0:r0 + 64, hp, c0:qb], start=True, stop=True)
                st[i] = {"bz": bz}

            def Se(i):
                bi, h = items[i]
                kTa, Va, nk, c0, tri = blocks[bi]
                bz = st[i]["bz"]
                e_, ek = tmp(E)
                P.act("activation", R=[PS(bz)], W=[ek], out=e_[0:nk, c0:qb], in_=psf[bz][0:nk, c0:qb], func=AF.Exp)
                st[i]["e"] = (e_, ek)

            def Sl(i):
                bi, h = items[i]
                kTa, Va, nk, c0, tri = blocks[bi]
                e_, ek = st[i]["e"]
                si = rr[0] % 4
                rr[0] += 1
                sp_, spk = spb[si], ("spb", si)
                P.act("activation", R=[ek], W=[spk], out=sp_[0:nk, c0:qb], in_=e_[0:nk, c0:qb], func=AF.Ln, bias=1.0)
                if tri:
                    tri_mask(sp_, spk, nk, c0)
                st[i]["sp"] = (sp_, spk)

            def Sw(i):
                bi, h = items[i]
                kTa, Va, nk, c0, tri = blocks[bi]
                bz = st[i]["bz"]
                sp_, spk = st[i]["sp"]
                P.pe("matmul", R=[spk, "negU"], W=[PS(bz)], out=psf[bz][0:nk, c0:qb], lhsT=negU[0:nk, 0:nk],
                     rhs=sp_[0:nk, c0:qb], start=False, stop=(bi == 0), skip_group_check=True)
                if bi > 0:
                    P.pe("matmul", R=[("sacc", h), "negOnes"], W=[PS(bz)], out=psf[bz][0:nk, c0:qb],
                         lhsT=negOnes[:, 0:nk], rhs=sacc[h][:, c0:qb], start=False, stop=True, skip_group_check=True)
                if bi < nb - 1:
                    P.dve("tensor_tensor", R=[spk, ("sacc", h)], W=[("sacc", h)], out=sacc[h][0:nk, c0:qb],
                          in0=sacc[h][0:nk, c0:qb], in1=sp_[0:nk, c0:qb], op=ALU.add)

            def Sa(i):
                bi, h = items[i]
                kTa, Va, nk, c0, tri = blocks[bi]
                bz = st[i]["bz"]
                ai = rr[1] % 4
                rr[1] += 1
                a_, ak = abf[ai], ("abf", ai)
                P.act("activation", R=[PS(bz)], W=[ak], out=a_[0:nk, c0:qb], in_=psf[bz][0:nk, c0:qb], func=AF.Exp)
                if tri:
                    tri_mask(a_, ak, nk, c0)
                if c0 > qa:
                    P.pool("memset", W=[ak], ap=a_[0:nk, qa:c0], constant=0.0)
                st[i]["a"] = (a_, ak)

            def Sv(i):
                bi, h = items[i]
                kTa, Va, nk, c0, tri = blocks[bi]
                a_, ak = st.pop(i)["a"]
                P.pe("matmul", R=[ak, "Vc"], W=[PS(6 + h)], out=psf[6 + h][:, qa:qb], lhsT=Va[0:nk, :],
                     rhs=a_[0:nk, qa:qb], start=(bi == 0), stop=(bi == nb - 1))

            for k in range(-3, n):
                if 0 <= k + 3 < n:
                    Sz(k + 3)
                if 0 <= k + 2 < n:
                    Se(k + 2)
                if 0 <= k + 1 < n:
                    Sl(k + 1)
                    Sw(k + 1)
                if 0 <= k < n:
                    Sa(k)
                    Sv(k)

        def attention_prompt(T, kb0, NT):
            nkb = kb0 + NT
            blk = []
            for kb in range(nkb - 1, -1, -1):
                i = kb - kb0
                blk.append((kb, 128, max(i, 0) * 128, i >= 0))
            nb = len(blk)
            items = [(hp, bi, h) for hp in range(4) for bi in range(nb) for h in range(2)]
            n = len(items)
            st = {}
            qa, qb = 0, T

            def obank(hp, h):
                return 4 + 2 * (hp % 2) + h

            def Sz(i):
                hp, bi, h = items[i]
                kb, nk, c0, tri = blk[bi]
                r0 = 64 * h
                bz = rr[2] % 4
                rr[2] += 1
                P.pe("matmul", R=["kT", "qT"], W=[PS(bz)], out=psf[bz][0:nk, c0:qb],
                     lhsT=kT[r0:r0 + 64, hp, kb * 128:kb * 128 + nk], rhs=qT[r0:r0 + 64, hp, c0:qb], start=True, stop=True)
                st[i] = {"bz": bz}

            def Se(i):
                hp, bi, h = items[i]
                kb, nk, c0, tri = blk[bi]
                bz = st[i]["bz"]
                e_, ek = tmp(E)
                P.act("activation", R=[PS(bz)], W=[ek], out=e_[0:nk, c0:qb], in_=psf[bz][0:nk, c0:qb], func=AF.Exp)
                st[i]["e"] = (e_, ek)

            def Sl(i):
                hp, bi, h = items[i]
                kb, nk, c0, tri = blk[bi]
                e_, ek = st[i]["e"]
                si = rr[0] % 4
                rr[0] += 1
                sp_, spk = spb[si], ("spb", si)
                P.act("activation", R=[ek], W=[spk], out=sp_[0:nk, c0:qb], in_=e_[0:nk, c0:qb], func=AF.Ln, bias=1.0)
                if tri:
                    tri_mask(sp_, spk, nk, c0)
                st[i]["sp"] = (sp_, spk)

            def Sw(i):
                hp, bi, h = items[i]
                kb, nk, c0, tri = blk[bi]
                bz = st[i]["bz"]
                sp_, spk = st[i]["sp"]
                if bi == 0 and h == 0:
                    for h2 in range(2):
                        P.pool("memset", W=[("sacc", h2)], ap=sacc[h2][:, qa:qb], constant=0.0)
                P.pe("matmul", R=[spk, "negU"], W=[PS(bz)], out=psf[bz][0:nk, c0:qb], lhsT=negU[0:nk, 0:nk],
                     rhs=sp_[0:nk, c0:qb], start=False, stop=(bi == 0), skip_group_check=True)
                if bi > 0:
                    P.pe("matmul", R=[("sacc", h), "negOnes"], W=[PS(bz)], out=psf[bz][0:nk, c0:qb],
                         lhsT=negOnes[:, 0:nk], rhs=sacc[h][:, c0:qb], start=False, stop=True, skip_group_check=True)
                if bi < nb - 1:
                    P.dve("tensor_tensor", R=[spk, ("sacc", h)], W=[("sacc", h)], out=sacc[h][0:nk, c0:qb],
                          in0=sacc[h][0:nk, c0:qb], in1=sp_[0:nk, c0:qb], op=ALU.add)

            def Sa(i):
                hp, bi, h = items[i]
                kb, nk, c0, tri = blk[bi]
                bz = st[i]["bz"]
                ai = rr[1] % 4
                rr[1] += 1
                a_, ak = abf[ai], ("abf", ai)
                P.act("activation", R=[PS(bz)], W=[ak], out=a_[0:nk, c0:qb], in_=psf[bz][0:nk, c0:qb], func=AF.Exp)
                if tri:
                    tri_mask(a_, ak, nk, c0)
                if c0 > qa:
                    P.pool("memset", W=[ak], ap=a_[0:nk, qa:c0], constant=0.0)
                st[i]["a"] = (a_, ak)

            def Sv(i):
                hp, bi, h = items[i]
                kb, nk, c0, tri = blk[bi]
                a_, ak = st.pop(i)["a"]
                ob = obank(hp, h)
                P.pe("matmul", R=[ak, "Vc"], W=[PS(ob)], out=psf[ob][:, qa:qb], lhsT=Vc[0:nk, kb, hp * 128:(hp + 1) * 128],
                     rhs=a_[0:nk, qa:qb], start=(bi == 0), stop=(bi == nb - 1))
                if bi == nb - 1:
                    r0 = 64 * h
                    P.dve("tensor_tensor", R=[PS(ob), "catT"], W=["catT"], out=E.catT[r0:r0 + 64, 4 + hp, qa:qb],
                          in0=psf[ob][r0:r0 + 64, qa:qb], in1=E.catT[r0:r0 + 64, 4 + hp, qa:qb], op=ALU.mult)

            for k in range(-3, n):
                if 0 <= k + 3 < n:
                    Sz(k + 3)
                if 0 <= k + 2 < n:
                    Se(k + 2)
                if 0 <= k + 1 < n:
                    Sl(k + 1)
                    Sw(k + 1)
                if 0 <= k < n:
                    Sa(k)
                    Sv(k)

        def p2_load(ti):
            t = tiles[ti]
            P.dma(W=[(("hbuf", 0), s_) for s_ in range(4)], out=E.hbuf[0][:, 0:t.NT, :],
                  in_=hA[t.row0:t.row0 + t.T, :].rearrange("(n p) d -> p n d", p=128), R=[("hA", ti)])

        p2_load(0)
        for ti, t in enumerate(tiles):
            if cfg.stop == 1:
                break
            T, NT, nseg, L = t.T, t.NT, t.nseg, t.L
            isp = t.kind == "p"
            hb, hkey = E.hbuf[0], ("hbuf", 0)
            rmsnorm_to_T(E, t, hb, hkey)
            if ti + 1 < len(tiles):
                p2_load(ti + 1)
            kb0 = t.t0 // 128

            for s in range(NT):
                sl = slice(s * 128, (s + 1) * 128)
                rows = slice(t.row0 + s * 128, t.row0 + (s + 1) * 128)

                def proj_tm(c0):
                    b = bank()
                    for kc in range(8):
                        P.pe("matmul", R=[("actT", s), "w_big"], W=[PS(b)], out=psf[b], lhsT=E.actT[:, kc, sl],
                             rhs=E.w_big[:, kc, c0:c0 + DB], start=(kc == 0), stop=(kc == 7))
                    return b

                if cfg.stop == 21:
                    continue
                b = proj_tm(DB)
                cf, cfk = tmp(E)
                P.pool("memset", W=["lnst"], ap=lnst, constant=0.0)
                P.act("activation", R=[PS(b)], W=[cfk, "lnst"], out=cf, in_=psf[b], func=AF.Copy, accum_out=lnst[:, 0:1])
                jk, jkk = tmp(E)
                P.act("activation", R=[PS(b)], W=[jkk, "lnst"], out=jk, in_=psf[b], func=AF.Square,
                      accum_out=lnst[:, 1:2])
                P.dve("tensor_scalar", R=["lnst"], W=["lnst"], out=lnst[:, 2:3], in0=lnst[:, 0:1], scalar1=1.0 / DB,
                      scalar2=None, op0=ALU.mult)
                P.dve("tensor_tensor", R=["lnst"], W=["lnst"], out=lnst[:, 3:4], in0=lnst[:, 2:3], in1=lnst[:, 2:3],
                      op=ALU.mult)
                P.dve("scalar_tensor_tensor", R=["lnst"], W=["lnst"], out=lnst[:, 4:5], in0=lnst[:, 1:2],
                      scalar=1.0 / DB, in1=lnst[:, 3:4], op0=ALU.mult, op1=ALU.subtract)
                P.act("activation", R=["lnst"], W=["lnst"], out=lnst[:, 5:6], in_=lnst[:, 4:5], func=AF.Ln, bias=EPS)
                P.act("activation", R=["lnst"], W=["lnst"], out=lnst[:, 5:6], in_=lnst[:, 5:6], func=AF.Exp, scale=-0.5)
                P.dve("tensor_scalar", R=[cfk, "lnst"], W=[cfk], out=cf, in0=cf, scalar1=lnst[:, 2:3],
                      scalar2=lnst[:, 5:6], op0=ALU.subtract, op1=ALU.mult)
                P.dve("tensor_tensor", R=[cfk, "g_bc"], W=[cfk], out=cf, in0=cf, in1=g_bc, op=ALU.mult)
                if isp:
                    P.dve("tensor_tensor", R=[cfk, "b_bc"], W=["cvn"], out=cvn[:, s, :], in0=cf, in1=b_bc, op=ALU.add)
                else:
                    P.dve("tensor_tensor", R=[cfk, "b_bc"], W=[cfk], out=cf, in0=cf, in1=b_bc, op=ALU.add)
                    P.act("activation", R=[cfk], W=["cvn"], out=cvn[:, s, :], in_=cf, func=AF.Copy)
                    P.dma(R=[cfk], out=ncv_s, in_=cf, is_output=True)
                if cfg.stop == 22:
                    continue
                b = proj_tm(4 * DB if cfg.stop != 25 else 5 * DB)
                st_, stk = stage()
                P.act("activation", R=[PS(b)], W=[stk], out=st_, in_=psf[b], func=AF.Copy)
                P.dma(R=[stk], out=((nk_p if cfg.stop != 25 else nv_p)[rows, :] if isp else (nk_s if cfg.stop != 25 else nv_s)), in_=st_, is_output=True)
                kb_ = kbf[s % 2]
                P.dve("tensor_copy", R=[PS(b)], W=[("kbf", s % 2)], out=kb_, in_=psf[b])
                if isp:
                    transpose_into(kb_, ("kbf", s % 2), 4, kT, "kT", kb0 + s)
                else:
                    transpose_into(kb_, ("kbf", s % 2), 4, kT_new, "kT_new", 0)
                if cfg.stop in (23, 25):
                    continue
                b = proj_tm(5 * DB)
                st_, stk = stage()
                P.act("activation", R=[PS(b)], W=[stk], out=st_, in_=psf[b], func=AF.Copy)
                P.dma(R=[stk], out=(nv_p[rows, :] if isp else nv_s), in_=st_, is_output=True)
                if cfg.stop == 26 or (cfg.stop == 27 and isp) or (cfg.stop == 28 and not isp):
                    continue
                if isp:
                    P.act("activation", R=[PS(b)], W=["Vc"], out=Vc[:, kb0 + s, :], in_=psf[b], func=AF.Copy)
                else:
                    P.act("activation", R=[PS(b)], W=[("kbf", 1)], out=kbf[1], in_=psf[b], func=AF.Copy)
                    for q in range(NS if cfg.stop not in (24, 27, 28) else 0):
                        P.dma(R=[("kbf", 1)], W=[("Vn", q)], out=Vn[q][0:DS, :], in_=kbf[1][q * DS:(q + 1) * DS, :])

            if cfg.stop in (2, 21, 22, 23, 24, 25, 26, 27, 28):
                continue
            Wm, Wk = (WT, "WT") if isp else (WTs, "WTs")
            cbr, cbk = (cb_row, "cb_row") if isp else (cb_row_s, "cb_row_s")
            for hh in range(4):
                bm = bank()
                for s in range(NT):
                    sl = slice(s * 128, (s + 1) * 128)
                    P.pe("matmul", R=["cvn", Wk], W=[PS(bm)], out=psf[bm][:, sl], lhsT=cvn[:, s, hh * 128:(hh + 1) * 128],
                         rhs=Wm[:, hh, :], start=True, stop=False)
                    P.pe("matmul", R=["ones_row", cbk], W=[PS(bm)], out=psf[bm][:, sl], lhsT=ones_row[0:1, :],
                         rhs=cbr[0:1, hh, :], start=False, stop=True)
                bu, bzc = bank(), bank()
                proj_fm(E, bu, E.w_big, 0 + hh, T)
                proj_fm(E, bzc, E.w_big, 8 + hh, T)
                sg, sgk = sigmoid_from(E, psf[bzc][:, 0:T], PS(bzc), T)
                P.dve("tensor_tensor", R=[PS(bzc), sgk], W=[sgk], out=sg[:, 0:T], in0=psf[bzc][:, 0:T], in1=sg[:, 0:T],
                      op=ALU.mult)
                P.dve("tensor_tensor", R=[PS(bm), sgk], W=[sgk], out=sg[:, 0:T], in0=psf[bm][:, 0:T], in1=sg[:, 0:T],
                      op=ALU.mult)
                P.dve("tensor_tensor", R=[PS(bu), sgk], W=["catT"], out=E.catT[:, hh, 0:T], in0=psf[bu][:, 0:T],
                      in1=sg[:, 0:T], op=ALU.mult)

            if cfg.stop == 3:
                continue
            for hp in range(4):
                bq = bank()
                proj_fm(E, bq, E.w_big, 12 + hp, T)
                P.act("activation", R=[PS(bq)], W=["qT"], out=qT[:, hp, 0:T], in_=psf[bq][:, 0:T], func=AF.Copy,
                      scale=0.125)

            def silu_dz(hp, dst, dkey):
                bd = bank()
                proj_fm(E, bd, E.w_big, 24 + hp, T)
                sg, sgk = sigmoid_from(E, psf[bd][:, 0:T], PS(bd), T)
                P.dve("tensor_tensor", R=[PS(bd), sgk], W=[dkey], out=dst, in0=psf[bd][:, 0:T], in1=sg[:, 0:T],
                      op=ALU.mult)

            def finalize(hp, qa, qb, m1, m1k):
                for h in range(2):
                    r0 = 64 * h
                    P.dve("tensor_tensor", R=[PS(6 + h), m1k], W=["catT"], out=E.catT[r0:r0 + 64, 4 + hp, qa:qb],
                          in0=psf[6 + h][r0:r0 + 64, qa:qb], in1=m1[r0:r0 + 64, qa:qb], op=ALU.mult)

            if cfg.stop == 4 or (cfg.stop == 5 and not isp):
                continue
            if isp:
                for hp in range(4):
                    silu_dz(hp, E.catT[:, 4 + hp, 0:T], "catT")
                attention_prompt(T, kb0, NT)
            else:
                for hp in range(4):
                    silu_dz(hp, dzs[:, hp, :], "dzs")
                npast = PL // 128
                for q in range(NS):
                    for kb in range(npast):
                        st_, stk = stage()
                        P.dma(W=[stk], out=st_, in_=ck[q, kb * 128:(kb + 1) * 128, :])
                        P.act("activation", R=[stk], W=[("kbf", kb % 2)], out=kbf[kb % 2], in_=st_, func=AF.Copy)
                        transpose_into(kbf[kb % 2], ("kbf", kb % 2), 4, kT, "kT", kb)
                        st_, stk = stage()
                        P.dma(W=[stk], out=st_, in_=cv[q, kb * 128:(kb + 1) * 128, :])
                        P.dve("tensor_copy", R=[stk], W=["Vc"], out=Vc[:, kb, :], in_=st_)
                    qa, qb = q * DS, (q + 1) * DS
                    for hp in range(4):
                        blocks = [(kT_new[:, hp, qa:qb], Vn[q][0:DS, hp * 128:(hp + 1) * 128], DS, qa, True)]
                        for kb in range(npast - 1, -1, -1):
                            blocks.append((kT[:, hp, kb * 128:(kb + 1) * 128], Vc[:, kb, hp * 128:(hp + 1) * 128], 128,
                                           qa, False))
                        attention(hp, qa, qb, blocks)
                        finalize(hp, qa, qb, dzs[:, hp, :], "dzs")
            P.dma(R=["catT"], W=[("catD", ti)], out=catD[:, t.row0:t.row0 + T].rearrange("(c p) t -> p c t", p=128),
                  in_=E.catT[:, :, 0:T], is_output=cfg.debug)
        P.barrier()
        AR.off = base_off

    if 3 in cfg.passes:
        E = common_bufs(n_h=2, n_tmp=6)
        E.catT2 = [E.catT, AR.alloc([128, 8, 512], BF16)]
        E.pbuf = [AR.alloc([128, 4, PLE]) for _ in range(2)]
        E.pT = AR.alloc([128, 2, 512], BF16)
        E.xnp = [AR.alloc([128, PLE], BF16) for _ in range(2)]
        E.w_out = AR.alloc([128, 8, D], BF16)
        E.w_gate = AR.alloc([128, 8, D], BF16)
        E.w_proj = AR.alloc([128, 2, D], BF16)
        fg_bc = AR.alloc([128, D])
        print("P3 SBUF bytes/partition:", AR.off * 4)
        load_weight(E, E.w_out, "w_out", cd_w_out, 8, D)
        load_weight(E, E.w_gate, "w_gate", ple_gate[1], 8, D)
        load_weight(E, E.w_proj, "w_proj", ple_proj[1], 2, D)
        P.dma(W=["fg_bc"], out=fg_bc, in_=final_g.partition_broadcast(128))

        def p3_load(ti):
            t = tiles[ti]
            par = ti % 2
            P.dma(W=[(("hbuf", par), s_) for s_ in range(4)], R=[("hA", ti)], out=E.hbuf[par][:, 0:t.NT, :],
                  in_=hA[t.row0:t.row0 + t.T, :].rearrange("(n p) d -> p n d", p=128))
            P.dma(W=[("pbuf", par)], out=E.pbuf[par][:, 0:t.NT, :],
                  in_=p_rows(t, 1).rearrange("(n p) d -> p n d", p=128))
            P.dma(W=[("catT", par)], R=[("catD", ti)], out=E.catT2[par][:, :, 0:t.T],
                  in_=catD[:, t.row0:t.row0 + t.T].rearrange("(c p) t -> p c t", p=128))

        p3_load(0)
        for ti, t in enumerate(tiles):
            T, NT = t.T, t.NT
            par = ti % 2
            hb, hkey = E.hbuf[par], ("hbuf", par)
            pb, pkey = E.pbuf[par], ("pbuf", par)
            if ti + 1 < len(tiles):
                p3_load(ti + 1)
            E.catT = E.catT2[par]
            tail(E, t, hb, hkey, pb, pkey, catkey=("catT", par))
            P.pool("memset", W=["ss"], ap=E.ss, constant=0.0)
            for s in range(NT):
                P.act("activation", R=[(hkey, s)], W=[("xnb", s % 2), "ss"], out=E.xnb[s % 2], in_=hb[:, s, :],
                      func=AF.Square, accum_out=E.ss[:, s:s + 1])
            P.act("activation", R=["ss"], W=["rstd"], out=E.rstd[:, 0:NT], in_=E.ss[:, 0:NT], func=AF.Ln,
                  scale=1.0 / D, bias=EPS)
            P.act("activation", R=["rstd"], W=["rstd"], out=E.rstd[:, 0:NT], in_=E.rstd[:, 0:NT], func=AF.Exp,
                  scale=-0.5)
            for s in range(NT):
                P.dve("scalar_tensor_tensor", R=[(hkey, s), "rstd", "fg_bc"], W=[(hkey, s)], out=hb[:, s, :], in0=hb[:, s, :],
                      scalar=E.rstd[:, s:s + 1], in1=fg_bc, op0=ALU.mult, op1=ALU.mult)
            dst = y_p[t.row0:t.row0 + T, :] if t.kind == "p" else y_s
            P.dma(R=[(hkey, s_) for s_ in range(4)], out=dst.rearrange("(n p) d -> p n d", p=128), in_=hb[:, 0:NT, :],
                  is_output=True)
        P.barrier()
        AR.off = base_off

    P.emit()
    global LAST_PROG
    LAST_PROG = P
    return nc


LAST_PROG = None


def shard_inputs(inp, cfg, n_cores):
    NP, S, NS, DS, PL = cfg.NP, cfg.S, cfg.NS, cfg.DS, cfg.PL
    f = lambda a: np.ascontiguousarray(np.asarray(a, dtype=np.float32))
    maps = []
    for c in range(n_cores):
        ps = slice(c * NP, (c + 1) * NP)
        ss = slice(c * NS, (c + 1) * NS)
        m = {
            "x_prompt": f(inp["x_prompt"][ps]).reshape(NP * S, D),
            "x_sample": f(inp["x_sample"][ss]).reshape(NS * DS, D),
            "state_a_conv": f(inp["state_a_conv"][0, ss]),
            "state_b_conv": f(inp["state_b_conv"][0, ss]),
            "cache_d_k": f(inp["cache_d_k"][0, ss]).reshape(NS, PL, DB),
            "cache_d_v": f(inp["cache_d_v"][0, ss]).reshape(NS, PL, DB),
            "p_prompt": f(inp["p_prompt"][:, ps]).reshape(2, NP * S, PLE),
            "p_sample": f(inp["p_sample"][:, ss]).reshape(2, NS * DS, PLE),
            "norm_g": f(inp["norm_g"]),
            "ple_gate": f(inp["ple_gate"]),
            "ple_proj": f(inp["ple_proj"]),
            "ab_w_in": f(inp["ab_w_in"][0]),
            "a_conv_w": f(inp["a_conv_w"][0]),
            "b_conv_w": f(inp["b_conv_w"][0]),
            "b_ln_g": f(inp["b_ln_g"][0]),
            "b_ln_b": f(inp["b_ln_b"][0]),
            "ab_w_out": f(inp["ab_w_out"][0]),
            "cd_w_in": f(inp["cd_w_in"][0]),
            "c_ln_g": f(inp["c_ln_g"][0]),
            "c_ln_b": f(inp["c_ln_b"][0]),
            "c_ws": f(inp["c_ws"][0]),
            "c_b": f(inp["c_b"][0]),
            "cd_w_out": f(inp["cd_w_out"][0]),
            "final_g": f(inp["final_g"]),
        }
        maps.append(m)
    return maps


def kernel(**inputs):
    n_cores = 8
    cfg = Cfg(NP=2, S=4096, NS=4, DS=32, PL=2048)
    nc = build_program(cfg)
    in_maps = shard_inputs(inputs, cfg, n_cores)
    res = run_bass_kernel_spmd(nc, in_maps, core_ids=list(range(n_cores)))
    r = res.results
    NP, S, NS, DS = cfg.NP, cfg.S, cfg.NS, cfg.DS
    cat = lambda name, shp: np.concatenate([np.asarray(r[c][name], dtype=np.float32).reshape(shp) for c in range(n_cores)], axis=0)
    y_prompt = cat("y_prompt", (NP, S, D))
    y_sample = cat("y_sample", (NS, DS, D))
    na_p = cat("new_a_prompt", (NP, HA, DB))[None]
    na_s = cat("new_a_sample", (NS, HA, DB))[None]
    nb_p = cat("new_b_prompt", (NP, HB, DB))[None]
    nb_s = cat("new_b_sample", (NS, HB, DB))[None]
    ncv = cat("new_cv_sample", (NS, DS, DB))[None]
    nk_p = cat("new_k_prompt", (NP, S, 8, 64))[None]
    nv_p = cat("new_v_prompt", (NP, S, 8, 64))[None]
    nk_s = cat("new_k_sample", (NS, DS, 8, 64))[None]
    nv_s = cat("new_v_sample", (NS, DS, 8, 64))[None]
    return (y_prompt, y_sample, na_p, na_s, nb_p, nb_s, ncv, nk_p, nv_p, nk_s, nv_s)
```

```python
import contextlib
import numpy as np
import concourse.bass as bass
import concourse.mybir as mybir
from concourse.bass_utils import run_bass_kernel_spmd

F32 = mybir.dt.float32
BF16 = mybir.dt.bfloat16
AF = mybir.ActivationFunctionType
ALU = mybir.AluOpType

ENGINES = ["pe", "act", "dve", "pool", "sp"]
D = 1024
DB = 512
PLE = 256
EPS = 1e-6
HB = 30
HA = 2


class Prog:
    EPOCH = 12000

    def __init__(self, nc, n_dma_slots=48):
        self.nc = nc
        self.ops = {e: [] for e in ENGINES}
        self.cnt = {e: 0 for e in ENGINES}
        self.last_w = {}
        self.readers = {}
        self.seen = {e: {} for e in ENGINES}
        self.n_dma_slots = n_dma_slots
        self.dma_next = 0
        self.dma_val = [0] * n_dma_slots
        self.out_tokens = []
        self.bank_i = 0

    def _need(self, eng, tok, waits):
        if tok is None:
            return
        if tok[0] == "E":
            _, e2, idx = tok
            if e2 == eng and eng == "pe":
                return
            k = ("E", e2)
        else:
            _, slot, idx = tok
            k = ("D", slot)
        if self.seen[eng].get(k, -1) >= idx:
            return
        waits[k] = max(waits.get(k, -1), idx)

    def op(self, eng, name, R=(), W=(), dma=False, is_output=False, **kw):
        waits = {}
        for k in R:
            self._need(eng, self.last_w.get(k), waits)
        for k in W:
            self._need(eng, self.last_w.get(k), waits)
            for t in self.readers.get(k, ()):
                self._need(eng, t, waits)
        if dma:
            slot = self.dma_next
            self.dma_next = (self.dma_next + 1) % self.n_dma_slots
            prev = self.dma_val[slot]
            if prev > 0:
                self._need(eng, ("D", slot, prev), waits)
            self.dma_val[slot] = prev + 16
            tok = ("D", slot, prev + 16)
        else:
            idx = self.cnt[eng]
            self.cnt[eng] += 1
            tok = ("E", eng, idx)
        for k, v in waits.items():
            self.seen[eng][k] = v
        self.ops[eng].append((name, kw, waits, tok))
        for k in W:
            self.last_w[k] = tok
            self.readers[k] = []
        for k in R:
            if k in W:
                continue
            self.readers.setdefault(k, []).append(tok)
        if is_output:
            self.out_tokens.append(tok)
        return tok

    def pe(self, name, **kw):
        return self.op("pe", name, **kw)

    def act(self, name, **kw):
        return self.op("act", name, **kw)

    def dve(self, name, **kw):
        return self.op("dve", name, **kw)

    def pool(self, name, **kw):
        return self.op("pool", name, **kw)

    def dma(self, **kw):
        return self.op("sp", "dma_start", dma=True, **kw)

    def barrier(self):
        for eng in ENGINES:
            waits = {}
            for e2 in ENGINES:
                if e2 != eng and self.cnt[e2] > 0:
                    self._need(eng, ("E", e2, self.cnt[e2] - 1), waits)
            for slot in range(self.n_dma_slots):
                if self.dma_val[slot] > 0:
                    self._need(eng, ("D", slot, self.dma_val[slot]), waits)
            for k, v in waits.items():
                self.seen[eng][k] = v
            self.ops[eng].append((None, None, waits, None))

    def emit(self):
        nc = self.nc
        with contextlib.ExitStack() as st:
            esem = {}
            for e in ENGINES:
                n_ep = (self.cnt[e] + self.EPOCH - 1) // self.EPOCH
                esem[e] = [st.enter_context(nc.semaphore(f"s_{e}{i}")) for i in range(max(n_ep, 1))]
            dsem = [st.enter_context(nc.semaphore(f"s_d{i}")) for i in range(self.n_dma_slots)]
            block = st.enter_context(nc.Block())

            def do_wait(h, k, v):
                if k[0] == "E":
                    h.wait_ge(esem[k[1]][v // self.EPOCH], v % self.EPOCH + 1)
                else:
                    h.wait_ge(dsem[k[1]], v)

            def run(ename):
                def body(h):
                    for name, kw, waits, tok in self.ops[ename]:
                        ws = list(waits.items())
                        if name is None:
                            for k, v in ws:
                                do_wait(h, k, v)
                            continue
                        for k, v in ws[1:]:
                            do_wait(h, k, v)
                        ins = getattr(h, name)(**kw)
                        if ws:
                            k, v = ws[0]
                            if k[0] == "E":
                                ins._wait_ge(esem[k[1]][v // self.EPOCH], v % self.EPOCH + 1)
                            else:
                                ins._wait_ge(dsem[k[1]], v)
                        if tok[0] == "E":
                            ins.then_inc(esem[tok[1]][tok[2] // self.EPOCH], 1)
                        else:
                            ins.then_inc(dsem[tok[1]], 16)
                    if ename == "sp":
                        for tok in self.out_tokens:
                            do_wait(h, ("D", tok[1]), tok[2])
                return body

            block.tensor(run("pe"))
            block.scalar(run("act"))
            block.vector(run("dve"))
            block.gpsimd(run("pool"))
            block.sync(run("sp"))


class Arena:
    def __init__(self, nc, nbytes):
        self.t = nc.alloc_sbuf_tensor("arena", [128, nbytes // 4], F32)
        self.off = 0
        self.cap = nbytes // 4

    def alloc(self, shape, dt=F32):
        n = int(np.prod(shape[1:]))
        nw = (n if dt == F32 else (n + 1) // 2)
        nw = (nw + 7) // 8 * 8
        assert self.off + nw <= self.cap, f"SBUF arena overflow: need {(self.off + nw) * 4} B"
        ap = self.t[0:shape[0], self.off:self.off + nw]
        self.off += nw
        if dt != F32:
            ap = ap.bitcast(dt)
        ap = ap[:, 0:n]
        if len(shape) == 3:
            ap = ap.rearrange("p (a b) -> p a b", a=shape[1])
        elif len(shape) == 4:
            ap = ap.rearrange("p (a b c) -> p a b c", a=shape[1], b=shape[2])
        return ap


class Cfg:
    def __init__(self, NP=2, S=4096, NS=4, DS=32, PL=2048, passes=(1, 2, 3), debug=False, stop=0):
        self.NP, self.S, self.NS, self.DS, self.PL = NP, S, NS, DS, PL
        self.passes = passes
        self.debug = debug
        self.stop = stop
        assert NS * DS == 128 and S % 512 == 0 and PL % 128 == 0
        self.NTOK = NP * S + 128


class Tile:
    def __init__(self, kind, seq, t0, T, nseg, L, row0, last):
        self.kind, self.seq, self.t0, self.T, self.nseg, self.L = kind, seq, t0, T, nseg, L
        self.row0 = row0
        self.last = last
        self.NT = T // 128


def make_tiles(cfg):
    tiles = []
    for q in range(cfg.NP):
        n = cfg.S // 512
        for i in range(n):
            tiles.append(Tile("p", q, i * 512, 512, 1, 512, q * cfg.S + i * 512, i == n - 1))
    tiles.append(Tile("s", 0, 0, 128, cfg.NS, cfg.DS, cfg.NP * cfg.S, True))
    return tiles


def build_program(cfg):
    nc = bass.Bass("TRN2", target_bir_lowering=False)
    NP, S, NS, DS, PL = cfg.NP, cfg.S, cfg.NS, cfg.DS, cfg.PL
    NTOK = cfg.NTOK

    def din(name, shape):
        return nc.dram_tensor(name, list(shape), F32, kind="ExternalInput").ap()

    def dout(name, shape):
        return nc.dram_tensor(name, list(shape), F32, kind="ExternalOutput").ap()

    x_p = din("x_prompt", [NP * S, D])
    x_s = din("x_sample", [128, D])
    st_a = din("state_a_conv", [NS, HA, DB])
    st_b = din("state_b_conv", [NS, HB, DB])
    ck = din("cache_d_k", [NS, PL, DB])
    cv = din("cache_d_v", [NS, PL, DB])
    p_p = din("p_prompt", [2, NP * S, PLE])
    p_s = din("p_sample", [2, 128, PLE])
    norm_g = din("norm_g", [2, D])
    ple_gate = din("ple_gate", [2, D, D])
    ple_proj = din("ple_proj", [2, PLE, D])
    ab_w_in = din("ab_w_in", [D, 7 * DB])
    a_conv_w = din("a_conv_w", [3, DB])
    b_conv_w = din("b_conv_w", [31, DB])
    b_ln_g = din("b_ln_g", [DB])
    b_ln_b = din("b_ln_b", [DB])
    ab_w_out = din("ab_w_out", [D, D])
    cd_w_in = din("cd_w_in", [D, 7 * DB])
    c_ln_g = din("c_ln_g", [DB])
    c_ln_b = din("c_ln_b", [DB])
    c_ws = din("c_ws", [4, 128, 128])
    c_b = din("c_b", [4, 128])
    cd_w_out = din("cd_w_out", [D, D])
    final_g = din("final_g", [D])

    y_p = dout("y_prompt", [NP * S, D])
    y_s = dout("y_sample", [128, D])
    na_p = dout("new_a_prompt", [NP, HA, DB])
    na_s = dout("new_a_sample", [NS, HA, DB])
    nb_p = dout("new_b_prompt", [NP, HB, DB])
    nb_s = dout("new_b_sample", [NS, HB, DB])
    ncv_s = dout("new_cv_sample", [128, DB])
    nk_p = dout("new_k_prompt", [NP * S, DB])
    nv_p = dout("new_v_prompt", [NP * S, DB])
    nk_s = dout("new_k_sample", [128, DB])
    nv_s = dout("new_v_sample", [128, DB])

    kind_scr = "ExternalOutput" if cfg.debug else "Internal"
    hA = nc.dram_tensor("hA", [NTOK, D], F32, kind=kind_scr).ap()
    catD = nc.dram_tensor("catD", [D, NTOK], BF16, kind=kind_scr).ap()

    P = Prog(nc)
    tiles = make_tiles(cfg)
    AR = Arena(nc, 207 * 1024)

    psf = [nc.alloc_psum_tensor(f"ps{i}", [128, 512], F32)[:] for i in range(8)]
    psb = [p.bitcast(BF16) for p in psf]

    def bank():
        b = P.bank_i
        P.bank_i = (P.bank_i + 1) % 6
        return b

    def PS(b):
        return ("ps", b)

    identf = AR.alloc([128, 128])
    identb = AR.alloc([128, 128], BF16)
    onesf = AR.alloc([128, 128])
    ngcol = AR.alloc([128, 2, 8])
    P.pool("memset", W=["identf"], ap=identf, constant=0.0)
    P.pool("affine_select", R=["identf"], W=["identf"], out=identf, in_=identf, pattern=[[-1, 128]],
           compare_op=ALU.not_equal, fill=1.0, base=0, channel_multiplier=1)
    P.dve("tensor_copy", R=["identf"], W=["identb"], out=identb, in_=identf)
    P.pool("memset", W=["onesf"], ap=onesf, constant=1.0)
    import os as _os
    for _i in range(int(_os.environ.get('KDUMMY', '0'))):
        P.dve("tensor_copy", R=["identf"], W=["identb"], out=identb, in_=identf)
    for i in range(2):
        P.dma(W=["ngcol"], out=ngcol[:, i, :], in_=norm_g[i].rearrange("(kc p) -> p kc", p=128),
              allow_slow_non_contiguous=True)
    base_off = AR.off

    def x_rows(t):
        return x_p[t.row0:t.row0 + t.T, :] if t.kind == "p" else x_s

    def p_rows(t, layer):
        return p_p[layer, t.row0:t.row0 + t.T, :] if t.kind == "p" else p_s[layer]

    class Env:
        pass

    def common_bufs(n_h=2, n_tmp=6):
        E = Env()
        E.hbuf = [AR.alloc([128, 4, D]) for _ in range(n_h)]
        E.xnb = [AR.alloc([128, D], BF16) for _ in range(2)]
        E.actT = AR.alloc([128, 8, 512], BF16)
        E.catT = AR.alloc([128, 8, 512], BF16)
        E.ss = AR.alloc([128, 4])
        E.rstd = AR.alloc([128, 4])
        E.ftmp = [AR.alloc([128, 512]) for _ in range(n_tmp)]
        E.ftmp_i = 0
        E.cast_i = 0
        return E

    def tmp(E):
        i = E.ftmp_i
        E.ftmp_i = (i + 1) % len(E.ftmp)
        return E.ftmp[i], ("ftmp", i)

    def load_weight(E, dst, dkey, src, nk, ncols, gain_i=None):
        for kc in range(nk):
            for c0 in range(0, ncols, 512):
                st_, sk = tmp(E)
                P.dma(W=[sk], out=st_, in_=src[kc * 128:(kc + 1) * 128, c0:c0 + 512])
                if gain_i is not None:
                    P.act("activation", R=[sk, "ngcol"], W=[dkey], out=dst[:, kc, c0:c0 + 512], in_=st_, func=AF.Copy,
                          scale=ngcol[:, gain_i, kc:kc + 1])
                elif E.cast_i % 2 == 0:
                    P.act("activation", R=[sk], W=[dkey], out=dst[:, kc, c0:c0 + 512], in_=st_, func=AF.Copy)
                else:
                    P.dve("tensor_copy", R=[sk], W=[dkey], out=dst[:, kc, c0:c0 + 512], in_=st_)
                E.cast_i += 1

    def transpose_into(src, skey, nchunk, dstT, dkey, s):
        b = bank()
        for c in range(nchunk):
            P.pe("transpose", R=[skey, "identb"], W=[PS(b)], out=psb[b][:, c * 128:(c + 1) * 128],
                 in_=src[:, c * 128:(c + 1) * 128], identity=identb)
        P.dve("tensor_copy", R=[PS(b)], W=[dkey], out=dstT[:, 0:nchunk, s * 128:(s + 1) * 128],
              in_=psb[b][:, 0:nchunk * 128].rearrange("p (c t) -> p c t", c=nchunk))

    def rmsnorm_to_T(E, t, hb, hkey):
        NT = t.NT
        P.pool("memset", W=["ss"], ap=E.ss, constant=0.0)
        for s in range(NT):
            P.act("activation", R=[(hkey, s)], W=[("xnb", s % 2), "ss"], out=E.xnb[s % 2], in_=hb[:, s, :],
                  func=AF.Square, accum_out=E.ss[:, s:s + 1])
        P.act("activation", R=["ss"], W=["rstd"], out=E.rstd[:, 0:NT], in_=E.ss[:, 0:NT], func=AF.Ln, scale=1.0 / D,
              bias=EPS)
        P.act("activation", R=["rstd"], W=["rstd"], out=E.rstd[:, 0:NT], in_=E.rstd[:, 0:NT], func=AF.Exp, scale=-0.5)
        for s in range(NT):
            P.act("activation", R=[(hkey, s), "rstd"], W=[("xnb", s % 2)], out=E.xnb[s % 2], in_=hb[:, s, :],
                  func=AF.Copy, scale=E.rstd[:, s:s + 1])
            transpose_into(E.xnb[s % 2], ("xnb", s % 2), 8, E.actT, ("actT", s), s)

    def proj_fm(E, b, w, oc, T):
        for kc in range(8):
            P.pe("matmul", R=[("actT", s_) for s_ in range(T // 128)] + ["w_big"], W=[PS(b)], out=psf[b][:, 0:T],
                 lhsT=w[:, kc, oc * 128:(oc + 1) * 128], rhs=E.actT[:, kc, 0:T], start=(kc == 0), stop=(kc == 7))

    def sigmoid_from(E, src_ap, skey, T, scale=-1.0, bias=None, extra=()):
        tt, tk = tmp(E)
        kw = {} if bias is None else {"bias": bias}
        P.act("activation", R=[skey] + list(extra), W=[tk], out=tt[:, 0:T], in_=src_ap, func=AF.Exp, scale=scale, **kw)
        P.act("activation", R=[tk], W=[tk], out=tt[:, 0:T], in_=tt[:, 0:T], func=AF.Ln, bias=1.0)
        P.act("activation", R=[tk], W=[tk], out=tt[:, 0:T], in_=tt[:, 0:T], func=AF.Exp, scale=-1.0)
        return tt, tk

    def sigmoid_multi(E, srcs, T):
        outs = []
        for (ap, key, scale, bias, extra) in srcs:
            tt, tk = tmp(E)
            kw = {} if bias is None else {"bias": bias}
            P.act("activation", R=[key] + list(extra), W=[tk], out=tt[:, 0:T], in_=ap, func=AF.Exp, scale=scale, **kw)
            outs.append((tt, tk))
        for (tt, tk) in outs:
            P.act("activation", R=[tk], W=[tk], out=tt[:, 0:T], in_=tt[:, 0:T], func=AF.Ln, bias=1.0)
        for (tt, tk) in outs:
            P.act("activation", R=[tk], W=[tk], out=tt[:, 0:T], in_=tt[:, 0:T], func=AF.Exp, scale=-1.0)
        return outs

    def tail(E, t, hb, hkey, pb, pkey, catkey="catT"):
        NT = t.NT
        halves = [slice(0, 512), slice(512, 1024)]
        for s in range(NT):
            sl = slice(s * 128, (s + 1) * 128)
            for hs in halves:
                b = bank()
                for kc in range(8):
                    P.pe("matmul", R=[(catkey, s), "w_out"], W=[PS(b)], out=psf[b], lhsT=E.catT[:, kc, sl],
                         rhs=E.w_out[:, kc, hs], start=(kc == 0), stop=(kc == 7))
                P.dve("tensor_tensor", R=[PS(b), (hkey, s)], W=[(hkey, s)], out=hb[:, s, hs], in0=hb[:, s, hs],
                      in1=psf[b], op=ALU.add)
        for s in range(NT):
            xi = s % 2
            P.act("activation", R=[(hkey, s)], W=[("xnb", xi)], out=E.xnb[xi], in_=hb[:, s, :], func=AF.Copy)
            transpose_into(E.xnb[xi], ("xnb", xi), 8, E.catT, (catkey, s), s)
            P.act("activation", R=[pkey], W=[("xnp", xi)], out=E.xnp[xi], in_=pb[:, s, :], func=AF.Copy)
            transpose_into(E.xnp[xi], ("xnp", xi), 2, E.pT, ("pT", s), s)
        for s in range(NT):
            sl = slice(s * 128, (s + 1) * 128)
            bgs, bps = [], []
            for hs in halves:
                bg = bank()
                for kc in range(8):
                    P.pe("matmul", R=[(catkey, s), "w_gate"], W=[PS(bg)], out=psf[bg], lhsT=E.catT[:, kc, sl],
                         rhs=E.w_gate[:, kc, hs], start=(kc == 0), stop=(kc == 7))
                bp = bank()
                for kc in range(2):
                    P.pe("matmul", R=[("pT", s), "w_proj"], W=[PS(bp)], out=psf[bp], lhsT=E.pT[:, kc, sl],
                         rhs=E.w_proj[:, kc, hs], start=(kc == 0), stop=(kc == 1))
                bgs.append(bg)
                bps.append(bp)
            sgs = sigmoid_multi(E, [(psf[bg], PS(bg), -1.0, None, ()) for bg in bgs], 512)
            for (sg, sgk), bp in zip(sgs, bps):
                P.dve("tensor_tensor", R=[PS(bp), sgk], W=[sgk], out=sg, in0=psf[bp], in1=sg, op=ALU.mult)
            for (sg, sgk), hs in zip(sgs, halves):
                P.dve("tensor_tensor", R=[sgk, (hkey, s)], W=[(hkey, s)], out=hb[:, s, hs], in0=hb[:, s, hs], in1=sg,
                      op=ALU.add)

    def state_out(E, t, bufs, key, H, L, dst):
        for q in range(t.nseg):
            b = bank()
            for j in range(4):
                P.pe("transpose", R=[(key, j), "identf"], W=[PS(b)], out=psf[b][0:H, j * 128:(j + 1) * 128],
                     in_=bufs[j][:, q, L:L + H], identity=identf)
            P.act("activation", R=[PS(b)], W=["hst"], out=E.hst[0:H, :], in_=psf[b][0:H, :], func=AF.Copy)
            P.dma(R=["hst"], out=dst[t.seq if t.kind == "p" else q], in_=E.hst[0:H, :], is_output=True)

    if 1 in cfg.passes:
        E = common_bufs()
        E.pbuf = [AR.alloc([128, 4, PLE]) for _ in range(2)]
        E.pT = AR.alloc([128, 2, 512], BF16)
        E.xnp = [AR.alloc([128, PLE], BF16) for _ in range(2)]
        E.w_big = AR.alloc([128, 8, 7 * DB], BF16)
        E.w_out = AR.alloc([128, 8, D], BF16)
        E.w_gate = AR.alloc([128, 8, D], BF16)
        E.w_proj = AR.alloc([128, 2, D], BF16)
        awc = AR.alloc([128, 4, 3])
        bwc = AR.alloc([128, 4, 31])
        lncol = AR.alloc([128, 4, 4])
        ubuf_p = [AR.alloc([128, 1, HA + 512]) for j in range(4)]
        gbuf_p = [AR.alloc([128, 1, HB + 512]) for j in range(4)]
        ubuf_s = [AR.alloc([128, NS, HA + DS]) for j in range(4)]
        gbuf_s = [AR.alloc([128, NS, HB + DS]) for j in range(4)]
        bconv = [AR.alloc([128, 512]) for j in range(4)]
        negmean = AR.alloc([128, 512])
        rstdB = AR.alloc([128, 512])
        E.hst = AR.alloc([32, 512])
        print("P1 SBUF bytes/partition:", AR.off * 4)

        load_weight(E, E.w_big, "w_big", ab_w_in, 8, 7 * DB, gain_i=0)
        load_weight(E, E.w_out, "w_out", ab_w_out, 8, D)
        load_weight(E, E.w_gate, "w_gate", ple_gate[0], 8, D)
        load_weight(E, E.w_proj, "w_proj", ple_proj[0], 2, D)
        for j in range(4):
            P.dma(W=["awc"], out=awc[:, j, :], in_=a_conv_w[:, j * 128:(j + 1) * 128].rearrange("w p -> p w"),
                  allow_slow_non_contiguous=True)
            P.dma(W=["bwc"], out=bwc[:, j, :], in_=b_conv_w[:, j * 128:(j + 1) * 128].rearrange("w p -> p w"),
                  allow_slow_non_contiguous=True)
        P.dma(W=["lncol"], out=lncol[:, :, 0], in_=b_ln_g.rearrange("(j p) -> p j", p=128),
              allow_slow_non_contiguous=True)
        P.dma(W=["lncol"], out=lncol[:, :, 1], in_=b_ln_b.rearrange("(j p) -> p j", p=128),
              allow_slow_non_contiguous=True)
        P.pool("tensor_scalar", R=["lncol"], W=["lncol"], out=lncol[:, :, 2:4], in0=lncol[:, :, 0:2], scalar1=-1.0,
               scalar2=None, op0=ALU.mult)

        def p1_load(ti):
            t = tiles[ti]
            par = ti % 2
            P.dma(W=[(("hbuf", par), s_) for s_ in range(4)], out=E.hbuf[par][:, 0:t.NT, :],
                  in_=x_rows(t).rearrange("(n p) d -> p n d", p=128))
            P.dma(W=[("pbuf", par)], out=E.pbuf[par][:, 0:t.NT, :],
                  in_=p_rows(t, 0).rearrange("(n p) d -> p n d", p=128))

        CATK = [("catT", s_) for s_ in range(4)]
        for ti, t in enumerate(tiles):
            T, NT, nseg, L = t.T, t.NT, t.nseg, t.L
            par = ti % 2
            hb, hkey = E.hbuf[par], ("hbuf", par)
            pb, pkey = E.pbuf[par], ("pbuf", par)
            isp = t.kind == "p"
            ub, gb = (ubuf_p, gbuf_p) if isp else (ubuf_s, gbuf_s)
            ukey, gkey = ("ubuf_p", "gbuf_p") if isp else ("ubuf_s", "gbuf_s")

            def v3(ap):
                return ap.rearrange("p (n l) -> p n l", n=nseg)

            if ti == 0:
                p1_load(0)
            if isp and t.t0 == 0:
                for j in range(4):
                    P.pool("memset", W=[(ukey, j)], ap=ub[j][:, :, 0:HA], constant=0.0)
                    P.pool("memset", W=[(gkey, j)], ap=gb[j][:, :, 0:HB], constant=0.0)
            if not isp:
                for q in range(NS):
                    for (stt, H, bufs, key) in ((st_a, HA, ub, ukey), (st_b, HB, gb, gkey)):
                        P.dma(W=["hst"], out=E.hst[0:H, :], in_=stt[q])
                        b = bank()
                        for j in range(4):
                            P.pe("transpose", R=["hst", "identf"], W=[PS(b)], out=psf[b][:, j * 32:j * 32 + H],
                                 in_=E.hst[0:H, j * 128:(j + 1) * 128], identity=identf[0:H, 0:H])
                        for j in range(4):
                            P.act("activation", R=[PS(b)], W=[(key, j)], out=bufs[j][:, q, 0:H],
                                  in_=psf[b][:, j * 32:j * 32 + H], func=AF.Copy)

            if ti == 0:
                rmsnorm_to_T(E, t, hb, hkey)
            if ti + 1 < len(tiles):
                p1_load(ti + 1)

            for pair in ((0, 1), (2, 3)):
                bvs, bgs = {}, {}
                for j in pair:
                    bvs[j], bgs[j] = bank(), bank()
                    proj_fm(E, bvs[j], E.w_big, 16 + j, T)
                    proj_fm(E, bgs[j], E.w_big, 20 + j, T)
                sgs = sigmoid_multi(E, [(psf[bgs[j]][:, 0:T], PS(bgs[j]), -1.0, None, ()) for j in pair], T)
                for (sg, sgk), j in zip(sgs, pair):
                    P.dve("tensor_tensor", R=[PS(bvs[j]), sgk], W=[(gkey, j)], out=gb[j][:, :, HB:HB + L],
                          in0=v3(psf[bvs[j]][:, 0:T]), in1=v3(sg[:, 0:T]), op=ALU.mult)
            def gen_A():
                for pair in ((0, 1), (2, 3)):
                    bxs, bcs = {}, {}
                    for j in pair:
                        bxs[j], bcs[j] = bank(), bank()
                        proj_fm(E, bxs[j], E.w_big, 0 + j, T)
                        proj_fm(E, bcs[j], E.w_big, 4 + j, T)
                    yield
                    tcs = {}
                    for j in pair:
                        tcs[j] = tmp(E)
                        P.act("activation", R=[PS(bcs[j])], W=[tcs[j][1]], out=tcs[j][0][:, 0:T], in_=psf[bcs[j]][:, 0:T],
                              func=AF.Copy)
                    for j in pair:
                        P.dve("tensor_tensor", R=[PS(bxs[j]), tcs[j][1]], W=[(ukey, j)], out=ub[j][:, :, HA:HA + L],
                              in0=v3(psf[bxs[j]][:, 0:T]), in1=v3(tcs[j][0][:, 0:T]), op=ALU.mult)
                    yield
                    bzs = {}
                    for j in pair:
                        bzs[j] = bank()
                        proj_fm(E, bzs[j], E.w_big, 12 + j, T)
                    yield
                    sgs = sigmoid_multi(E, [(psf[bzs[j]][:, 0:T], PS(bzs[j]), -1.0, None, ()) for j in pair], T)
                    yield
                    cvs = {}
                    for j in pair:
                        cvs[j] = tmp(E)
                        P.dve("tensor_scalar", R=[(ukey, j), "awc"], W=[cvs[j][1]], out=v3(cvs[j][0][:, 0:T]),
                              in0=ub[j][:, :, 2:2 + L], scalar1=awc[:, j, 2:3], scalar2=None, op0=ALU.mult)
                    yield
                    for w in (1, 0):
                        for j in pair:
                            P.dve("scalar_tensor_tensor", R=[(ukey, j), "awc", cvs[j][1]], W=[cvs[j][1]],
                                  out=v3(cvs[j][0][:, 0:T]), in0=ub[j][:, :, w:w + L], scalar=awc[:, j, w:w + 1],
                                  in1=v3(cvs[j][0][:, 0:T]), op0=ALU.mult, op1=ALU.add)
                    yield
                    bbs = {}
                    for j in pair:
                        bbs[j] = bank()
                        proj_fm(E, bbs[j], E.w_big, 8 + j, T)
                    yield
                    for (sg, sgk), j in zip(sgs, pair):
                        P.dve("tensor_tensor", R=[PS(bzs[j]), sgk], W=[sgk], out=sg[:, 0:T], in0=psf[bzs[j]][:, 0:T],
                              in1=sg[:, 0:T], op=ALU.mult)
                    for j in pair:
                        P.dve("tensor_tensor", R=[PS(bbs[j]), cvs[j][1]], W=[cvs[j][1]], out=cvs[j][0][:, 0:T],
                              in0=psf[bbs[j]][:, 0:T], in1=cvs[j][0][:, 0:T], op=ALU.mult)
                    yield
                    for (sg, sgk), j in zip(sgs, pair):
                        P.dve("tensor_tensor", R=[sgk, cvs[j][1]], W=CATK, out=E.catT[:, j, 0:T], in0=sg[:, 0:T],
                              in1=cvs[j][0][:, 0:T], op=ALU.mult)
                    yield
            def gen_bz():
                for pair in ((0, 1), (2, 3)):
                    bzs = {}
                    for j in pair:
                        bzs[j] = bank()
                        proj_fm(E, bzs[j], E.w_big, 24 + j, T)
                    yield
                    sgs = sigmoid_multi(E, [(psf[bzs[j]][:, 0:T], PS(bzs[j]), -1.0, None, ()) for j in pair], T)
                    yield
                    for (sg, sgk), j in zip(sgs, pair):
                        P.dve("tensor_tensor", R=[PS(bzs[j]), sgk], W=CATK, out=E.catT[:, 4 + j, 0:T],
                              in0=psf[bzs[j]][:, 0:T], in1=sg[:, 0:T], op=ALU.mult)
                    yield
            def gen_conv():
                for w in range(31):
                    for j in range(4):
                        if w == 0:
                            P.dve("tensor_scalar", R=[(gkey, j), "bwc"], W=[("bconv", j)], out=v3(bconv[j][:, 0:T]),
                                  in0=gb[j][:, :, 0:L], scalar1=bwc[:, j, 0:1], scalar2=None, op0=ALU.mult)
                        else:
                            P.dve("scalar_tensor_tensor", R=[(gkey, j), "bwc", ("bconv", j)], W=[("bconv", j)],
                                  out=v3(bconv[j][:, 0:T]), in0=gb[j][:, :, w:w + L], scalar=bwc[:, j, w:w + 1],
                                  in1=v3(bconv[j][:, 0:T]), op0=ALU.mult, op1=ALU.add)
                    yield
            def chain(*gs):
                for g in gs:
                    yield from g

            g1, g2 = gen_conv(), chain(gen_A(), gen_bz())
            alive = [g1, g2]
            while alive:
                for g in list(alive):
                    try:
                        next(g)
                    except StopIteration:
                        alive.remove(g)
            if t.last:
                state_out(E, t, ub, ukey, HA, L, na_p if isp else na_s)
            else:
                for j in range(4):
                    P.pool("tensor_copy", R=[(ukey, j)], W=[(ukey, j)], out=ub[j][:, :, 0:HA], in_=ub[j][:, :, L:L + HA])

            if t.last:
                state_out(E, t, gb, gkey, HB, L, nb_p if isp else nb_s)
            else:
                for j in range(4):
                    P.pool("tensor_copy", R=[(gkey, j)], W=[(gkey, j)], out=gb[j][:, :, 0:HB], in_=gb[j][:, :, L:L + HB])
            sqs = []
            for j in range(4):
                sq, sqk = tmp(E)
                P.act("activation", R=[("bconv", j)], W=[sqk], out=sq[:, 0:T], in_=bconv[j][:, 0:T], func=AF.Square)
                sqs.append((sq, sqk))
            for j in range(4):
                P.pe("matmul", R=[("bconv", j), "onesf"], W=[PS(6)], out=psf[6][:, 0:T], lhsT=onesf,
                     rhs=bconv[j][:, 0:T], start=(j == 0), stop=(j == 3))
            for j in range(4):
                sq, sqk = sqs[j]
                P.pe("matmul", R=[sqk, "onesf"], W=[PS(7)], out=psf[7][:, 0:T], lhsT=onesf, rhs=sq[:, 0:T],
                     start=(j == 0), stop=(j == 3))
            P.act("activation", R=[PS(6)], W=["negmean"], out=negmean[:, 0:T], in_=psf[6][:, 0:T], func=AF.Copy,
                  scale=-1.0 / DB)
            P.dve("tensor_tensor", R=["negmean"], W=["rstdB"], out=rstdB[:, 0:T], in0=negmean[:, 0:T],
                  in1=negmean[:, 0:T], op=ALU.mult)
            P.dve("scalar_tensor_tensor", R=[PS(7), "rstdB"], W=["rstdB"], out=rstdB[:, 0:T], in0=psf[7][:, 0:T],
                  scalar=1.0 / DB, in1=rstdB[:, 0:T], op0=ALU.mult, op1=ALU.subtract)
            P.act("activation", R=["rstdB"], W=["rstdB"], out=rstdB[:, 0:T], in_=rstdB[:, 0:T], func=AF.Ln, bias=EPS)
            P.act("activation", R=["rstdB"], W=["rstdB"], out=rstdB[:, 0:T], in_=rstdB[:, 0:T], func=AF.Exp, scale=-0.5)
            for pair in ((0, 1), (2, 3)):
                yvs = {}
                for j in pair:
                    yvs[j] = tmp(E)
                    P.dve("tensor_tensor", R=[("bconv", j), "negmean"], W=[yvs[j][1]], out=yvs[j][0][:, 0:T],
                          in0=bconv[j][:, 0:T], in1=negmean[:, 0:T], op=ALU.add)
                for j in pair:
                    P.dve("tensor_tensor", R=[yvs[j][1], "rstdB"], W=[yvs[j][1]], out=yvs[j][0][:, 0:T],
                          in0=yvs[j][0][:, 0:T], in1=rstdB[:, 0:T], op=ALU.mult)
                sgs = sigmoid_multi(E, [(yvs[j][0][:, 0:T], yvs[j][1], lncol[:, j, 2:3], lncol[:, j, 3:4], ["lncol"])
                                        for j in pair], T)
                for k_, j in enumerate(pair):
                    sgy, sgyk = sgs[k_]
                    P.dve("tensor_scalar", R=[yvs[j][1], "lncol", sgyk], W=[yvs[j][1]], out=yvs[j][0][:, 0:T],
                          in0=yvs[j][0][:, 0:T], scalar1=lncol[:, j, 0:1], scalar2=lncol[:, j, 1:2], op0=ALU.mult,
                          op1=ALU.add)
                for k_, j in enumerate(pair):
                    sgy, sgyk = sgs[k_]
                    P.dve("tensor_tensor", R=[yvs[j][1], sgyk], W=[yvs[j][1]], out=yvs[j][0][:, 0:T],
                          in0=yvs[j][0][:, 0:T], in1=sgy[:, 0:T], op=ALU.mult)
                for k_, j in enumerate(pair):
                    P.dve("tensor_tensor", R=[yvs[j][1]] + CATK, W=CATK, out=E.catT[:, 4 + j, 0:T],
                          in0=yvs[j][0][:, 0:T], in1=E.catT[:, 4 + j, 0:T], op=ALU.mult)

            if ti + 1 < len(tiles):
                rmsnorm_to_T(E, tiles[ti + 1], E.hbuf[(ti + 1) % 2], ("hbuf", (ti + 1) % 2))
            tail(E, t, hb, hkey, pb, pkey)
            P.dma(R=[(hkey, s_) for s_ in range(4)], W=[("hA", ti)],
                  out=hA[t.row0:t.row0 + T, :].rearrange("(n p) d -> p n d", p=128),
                  in_=hb[:, 0:NT, :], is_output=cfg.debug)
        P.barrier()
        AR.off = base_off


    if 2 in cfg.passes:
        CAP = max(S, PL)
        E = common_bufs(n_h=1, n_tmp=4)
        E.w_big = AR.alloc([128, 8, 7 * DB], BF16)
        kT = AR.alloc([128, 4, CAP], BF16)
        Vc = AR.alloc([128, CAP // 128, DB], BF16)
        qT = AR.alloc([128, 4, 512], BF16)
        cvn = AR.alloc([128, 4, DB], BF16)
        kst = [AR.alloc([128, DB]) for _ in range(3)]
        kst_i = [0]
        kbf = [AR.alloc([128, DB], BF16) for _ in range(2)]
        spb = [AR.alloc([128, 512], BF16) for _ in range(4)]
        abf = [AR.alloc([128, 512], BF16) for _ in range(4)]
        rr = [0, 0, 0]
        sacc = [AR.alloc([128, 512], BF16) for _ in range(2)]
        g_bc = AR.alloc([128, DB])
        b_bc = AR.alloc([128, DB])
        WT = AR.alloc([128, 4, 128], BF16)
        WTs = AR.alloc([128, 4, 128], BF16)
        negU = AR.alloc([128, 128], BF16)
        negOnes = AR.alloc([128, 128], BF16)
        ones_row = AR.alloc([1, 128])
        cb_row = AR.alloc([1, 4, 128])
        cb_row_s = AR.alloc([1, 4, 128])
        lnst = AR.alloc([128, 8])
        kT_new = AR.alloc([128, 4, 128], BF16)
        if CAP - PL >= 2048:
            Vn = [kT[0:32, q, PL:PL + DB] for q in range(4)]
            dzs = kT[:, 0, PL + DB:PL + DB + 1024].bitcast(F32).rearrange("p (a b) -> p a b", a=4)
        else:
            Vn = [AR.alloc([32, DB], BF16) for _ in range(4)]
            dzs = AR.alloc([128, 4, 128])
        print("P2 SBUF bytes/partition:", AR.off * 4)

        def stage():
            i = kst_i[0]
            kst_i[0] = (i + 1) % len(kst)
            return kst[i], ("kst", i)

        load_weight(E, E.w_big, "w_big", cd_w_in, 8, 7 * DB, gain_i=1)
        P.dma(W=["g_bc"], out=g_bc, in_=c_ln_g.partition_broadcast(128))
        P.dma(W=["b_bc"], out=b_bc, in_=c_ln_b.partition_broadcast(128))
        P.dma(W=["cb_row"], out=cb_row, in_=c_b.rearrange("(o h) t -> o h t", o=1))
        for q in range(NS):
            P.dma(W=["cb_row_s"], out=cb_row_s[:, :, q * DS:(q + 1) * DS],
                  in_=c_b[:, 0:DS].rearrange("(o h) t -> o h t", o=1))
        P.pool("memset", W=["ones_row"], ap=ones_row, constant=1.0)
        P.pool("memset", W=["negOnes"], ap=negOnes, constant=-1.0)
        tU, tUk = tmp(E)
        P.pool("memset", W=[tUk], ap=tU[:, 0:128], constant=-1.0)
        P.pool("affine_select", R=[tUk], W=[tUk], out=tU[:, 0:128], in_=tU[:, 0:128], pattern=[[-1, 128]],
               compare_op=ALU.is_ge, fill=0.0, base=0, channel_multiplier=1)
        P.dve("tensor_copy", R=[tUk], W=["negU"], out=negU, in_=tU[:, 0:128])
        for variant, dstW, dk in ((0, WT, "WT"), (1, WTs, "WTs")):
            for hh in range(4):
                wt_, wk = tmp(E)
                if variant == 0:
                    P.dma(W=[wk], out=wt_[:, 0:128], in_=c_ws[hh])
                else:
                    P.pool("memset", W=[wk], ap=wt_[:, 0:128], constant=0.0)
                    for q in range(NS):
                        P.dma(W=[wk], out=wt_[q * DS:(q + 1) * DS, q * DS:(q + 1) * DS], in_=c_ws[hh, 0:DS, 0:DS])
                P.pool("affine_select", R=[wk], W=[wk], out=wt_[:, 0:128], in_=wt_[:, 0:128], pattern=[[-1, 128]],
                       compare_op=ALU.is_ge, fill=0.0, base=0, channel_multiplier=1)
                P.act("activation", R=[wk], W=[("xnb", 0)], out=E.xnb[0][:, 0:128], in_=wt_[:, 0:128], func=AF.Copy)
                b = bank()
                P.pe("transpose", R=[("xnb", 0), "identb"], W=[PS(b)], out=psb[b][:, 0:128], in_=E.xnb[0][:, 0:128],
                     identity=identb)
                P.dve("tensor_copy", R=[PS(b)], W=[dk], out=dstW[:, hh, :], in_=psb[b][:, 0:128])

        def tri_mask(buf, key, nk, c0):
            P.pool("affine_select", R=[key], W=[key], out=buf[0:nk, c0:c0 + nk], in_=buf[0:nk, c0:c0 + nk],
                   pattern=[[1, nk]], compare_op=ALU.is_gt, fill=0.0, base=0, channel_multiplier=-1)

        def attention(hp, qa, qb, blocks):
            for h in range(2):
                P.pool("memset", W=[("sacc", h)], ap=sacc[h][:, qa:qb], constant=0.0)
            nb = len(blocks)
            items = [(bi, h) for bi in range(nb) for h in range(2)]
            n = len(items)
            st = {}

            def Sz(i):
                bi, h = items[i]
                kTa, Va, nk, c0, tri = blocks[bi]
                r0 = 64 * h
                bz = rr[2] % 6
                rr[2] += 1
                P.pe("matmul", R=["kT", "qT"], W=[PS(bz)], out=psf[bz][0:nk, c0:qb], lhsT=kTa[r0:r0 + 64, 0:nk],
                     rhs=qT[r0:r0 + 64, hp, c0:qb], start=True, stop=True)
                st[i] = {"bz": bz}

            def Se(i):
                bi, h = items[i]
                kTa, Va, nk, c0, tri = blocks[bi]
                bz = st[i]["bz"]
                e_, ek = tmp(E)
                P.act("activation", R=[PS(bz)], W=[ek], out=e_[0:nk, c0:qb], in_=psf[bz][0:nk, c0:qb], func=AF.Exp)
                st[i]["e"] = (e_, ek)

            def Sl(i):
                bi, h = items[i]
                kTa, Va, nk, c0, tri = blocks[bi]
                e_, ek = st[i]["e"]
                si = rr[0] % 4
                rr[0] += 1
                sp_, spk = spb[si], ("spb", si)
                P.act("activation", R=[ek], W=[spk], out=sp_[0:nk, c0:qb], in_=e_[0:nk, c0:qb], func=AF.Ln, bias=1.0)
                if tri:
                    tri_mask(sp_, spk, nk, c0)
                st[i]["sp"] = (sp_, spk)

            def Sw(i):
                bi, h = items[i]
                kTa, Va, nk, c0, tri = blocks[bi]
                bz = st[i]["bz"]
                sp_, spk = st[i]["sp"]
                P.pe("matmul", R=[spk, "negU"], W=[PS(bz)], out=psf[bz][0:nk, c0:qb], lhsT=negU[0:nk, 0:nk],
                     rhs=sp_[0:nk, c0:qb], start=False, stop=(bi == 0), skip_group_check=True)
                if bi > 0:
                    P.pe("matmul", R=[("sacc", h), "negOnes"], W=[PS(bz)], out=psf[bz][0:nk, c0:qb],
                         lhsT=negOnes[:, 0:nk], rhs=sacc[h][:, c0:qb], start=False, stop=True, skip_group_check=True)
                if bi < nb - 1:
                    P.dve("tensor_tensor", R=[spk, ("sacc", h)], W=[("sacc", h)], out=sacc[h][0:nk, c0:qb],
                          in0=sacc[h][0:nk, c0:qb], in1=sp_[0:nk, c0:qb], op=ALU.add)

            def Sa(i):
                bi, h = items[i]
                kTa, Va, nk, c0, tri = blocks[bi]
                bz = st[i]["bz"]
                ai = rr[1] % 4
                rr[1] += 1
                a_, ak = abf[ai], ("abf", ai)
                P.act("activation", R=[PS(bz)], W=[ak], out=a_[0:nk, c0:qb], in_=psf[bz][0:nk, c0:qb], func=AF.Exp)
                if tri:
                    tri_mask(a_, ak, nk, c0)
                if c0 > qa:
                    P.pool("memset", W=[ak], ap=a_[0:nk, qa:c0], constant=0.0)
                st[i]["a"] = (a_, ak)

            def Sv(i):
                bi, h = items[i]
                kTa, Va, nk, c0, tri = blocks[bi]
                a_, ak = st.pop(i)["a"]
                P.pe("matmul", R=[ak, "Vc"], W=[PS(6 + h)], out=psf[6 + h][:, qa:qb], lhsT=Va[0:nk, :],
                     rhs=a_[0:nk, qa:qb], start=(bi == 0), stop=(bi == nb - 1))

            for k in range(-3, n):
                if 0 <= k + 3 < n:
                    Sz(k + 3)
                if 0 <= k + 2 < n:
                    Se(k + 2)
                if 0 <= k + 1 < n:
                    Sl(k + 1)
                    Sw(k + 1)
                if 0 <= k < n:
                    Sa(k)
                    Sv(k)

        def attention_prompt(T, kb0, NT):
            nkb = kb0 + NT
            blk = []
            for kb in range(nkb - 1, -1, -1):
                i = kb - kb0
                blk.append((kb, 128, max(i, 0) * 128, i >= 0))
            nb = len(blk)
            items = [(hp, bi, h) for hp in range(4) for bi in range(nb) for h in range(2)]
            n = len(items)
            st = {}
            qa, qb = 0, T

            def obank(hp, h):
                return 4 + 2 * (hp % 2) + h

            def Sz(i):
                hp, bi, h = items[i]
                kb, nk, c0, tri = blk[bi]
                r0 = 64 * h
                bz = rr[2] % 4
                rr[2] += 1
                P.pe("matmul", R=["kT", "qT"], W=[PS(bz)], out=psf[bz][0:nk, c0:qb],
                     lhsT=kT[r0:r0 + 64, hp, kb * 128:kb * 128 + nk], rhs=qT[r0:r0 + 64, hp, c0:qb], start=True, stop=True)
                st[i] = {"bz": bz}

            def Se(i):
                hp, bi, h = items[i]
                kb, nk, c0, tri = blk[bi]
                bz = st[i]["bz"]
                e_, ek = tmp(E)
                P.act("activation", R=[PS(bz)], W=[ek], out=e_[0:nk, c0:qb], in_=psf[bz][0:nk, c0:qb], func=AF.Exp)
                st[i]["e"] = (e_, ek)

            def Sl(i):
                hp, bi, h = items[i]
                kb, nk, c0, tri = blk[bi]
                e_, ek = st[i]["e"]
                si = rr[0] % 4
                rr[0] += 1
                sp_, spk = spb[si], ("spb", si)
                P.act("activation", R=[ek], W=[spk], out=sp_[0:nk, c0:qb], in_=e_[0:nk, c0:qb], func=AF.Ln, bias=1.0)
                if tri:
                    tri_mask(sp_, spk, nk, c0)
                st[i]["sp"] = (sp_, spk)

            def Sw(i):
                hp, bi, h = items[i]
                kb, nk, c0, tri = blk[bi]
                bz = st[i]["bz"]
                sp_, spk = st[i]["sp"]
                if bi == 0 and h == 0:
                    for h2 in range(2):
                        P.pool("memset", W=[("sacc", h2)], ap=sacc[h2][:, qa:qb], constant=0.0)
                P.pe("matmul", R=[spk, "negU"], W=[PS(bz)], out=psf[bz][0:nk, c0:qb], lhsT=negU[0:nk, 0:nk],
                     rhs=sp_[0:nk, c0:qb], start=False, stop=(bi == 0), skip_group_check=True)
                if bi > 0:
                    P.pe("matmul", R=[("sacc", h), "negOnes"], W=[PS(bz)], out=psf[bz][0:nk, c0:qb],
                         lhsT=negOnes[:, 0:nk], rhs=sacc[h][:, c0:qb], start=False, stop=True, skip_group_check=True)
                if bi < nb - 1:
                    P.dve("tensor_tensor", R=[spk, ("sacc", h)], W=[("sacc", h)], out=sacc[h][0:nk, c0:qb],
                          in0=sacc[h][0:nk, c0:qb], in1=sp_[0:nk, c0:qb], op=ALU.add)

            def Sa(i):
                hp, bi, h = items[i]
                kb, nk, c0, tri = blk[bi]
                bz = st[i]["bz"]
                ai = rr[1] % 4
                rr[1] += 1
                a_, ak = abf[ai], ("abf", ai)
                P.act("activation", R=[PS(bz)], W=[ak], out=a_[0:nk, c0:qb], in_=psf[bz][0:nk, c0:qb], func=AF.Exp)
                if tri:
                    tri_mask(a_, ak, nk, c0)
                if c0 > qa:
                    P.pool("memset", W=[ak], ap=a_[0:nk, qa:c0], constant=0.0)
                st[i]["a"] = (a_, ak)

            def Sv(i):
                hp, bi, h = items[i]
                kb, nk, c0, tri = blk[bi]
                a_, ak = st.pop(i)["a"]
                ob = obank(hp, h)
                P.pe("matmul", R=[ak, "Vc"], W=[PS(ob)], out=psf[ob][:, qa:qb], lhsT=Vc[0:nk, kb, hp * 128:(hp + 1) * 128],
                     rhs=a_[0:nk, qa:qb], start=(bi == 0), stop=(bi == nb - 1))
                if bi == nb - 1:
                    r0 = 64 * h
                    P.dve("tensor_tensor", R=[PS(ob), "catT"], W=["catT"], out=E.catT[r0:r0 + 64, 4 + hp, qa:qb],
                          in0=psf[ob][r0:r0 + 64, qa:qb], in1=E.catT[r0:r0 + 64, 4 + hp, qa:qb], op=ALU.mult)

            for k in range(-3, n):
                if 0 <= k + 3 < n:
                    Sz(k + 3)
                if 0 <= k + 2 < n:
                    Se(k + 2)
                if 0 <= k + 1 < n:
                    Sl(k + 1)
                    Sw(k + 1)
                if 0 <= k < n:
                    Sa(k)
                    Sv(k)

        def p2_load(ti):
            t = tiles[ti]
            P.dma(W=[(("hbuf", 0), s_) for s_ in range(4)], out=E.hbuf[0][:, 0:t.NT, :],
                  in_=hA[t.row0:t.row0 + t.T, :].rearrange("(n p) d -> p n d", p=128), R=[("hA", ti)])

        p2_load(0)
        for ti, t in enumerate(tiles):
            if cfg.stop == 1:
                break
            T, NT, nseg, L = t.T, t.NT, t.nseg, t.L
            isp = t.kind == "p"
            hb, hkey = E.hbuf[0], ("hbuf", 0)
            if ti == 0:
                rmsnorm_to_T(E, t, hb, hkey)
                p2_load(1)
            kb0 = t.t0 // 128

            for s in range(NT):
                sl = slice(s * 128, (s + 1) * 128)
                rows = slice(t.row0 + s * 128, t.row0 + (s + 1) * 128)

                def proj_tm(c0):
                    b = bank()
                    for kc in range(8):
                        P.pe("matmul", R=[("actT", s), "w_big"], W=[PS(b)], out=psf[b], lhsT=E.actT[:, kc, sl],
                             rhs=E.w_big[:, kc, c0:c0 + DB], start=(kc == 0), stop=(kc == 7))
                    return b

                if cfg.stop == 21:
                    continue
                b = proj_tm(DB)
                cf, cfk = tmp(E)
                P.pool("memset", W=["lnst"], ap=lnst, constant=0.0)
                P.act("activation", R=[PS(b)], W=[cfk, "lnst"], out=cf, in_=psf[b], func=AF.Copy, accum_out=lnst[:, 0:1])
                jk, jkk = tmp(E)
                P.act("activation", R=[PS(b)], W=[jkk, "lnst"], out=jk, in_=psf[b], func=AF.Square,
                      accum_out=lnst[:, 1:2])
                P.dve("tensor_scalar", R=["lnst"], W=["lnst"], out=lnst[:, 2:3], in0=lnst[:, 0:1], scalar1=1.0 / DB,
                      scalar2=None, op0=ALU.mult)
                P.dve("tensor_tensor", R=["lnst"], W=["lnst"], out=lnst[:, 3:4], in0=lnst[:, 2:3], in1=lnst[:, 2:3],
                      op=ALU.mult)
                P.dve("scalar_tensor_tensor", R=["lnst"], W=["lnst"], out=lnst[:, 4:5], in0=lnst[:, 1:2],
                      scalar=1.0 / DB, in1=lnst[:, 3:4], op0=ALU.mult, op1=ALU.subtract)
                P.act("activation", R=["lnst"], W=["lnst"], out=lnst[:, 5:6], in_=lnst[:, 4:5], func=AF.Ln, bias=EPS)
                P.act("activation", R=["lnst"], W=["lnst"], out=lnst[:, 5:6], in_=lnst[:, 5:6], func=AF.Exp, scale=-0.5)
                P.dve("tensor_scalar", R=[cfk, "lnst"], W=[cfk], out=cf, in0=cf, scalar1=lnst[:, 2:3],
                      scalar2=lnst[:, 5:6], op0=ALU.subtract, op1=ALU.mult)
                P.dve("tensor_tensor", R=[cfk, "g_bc"], W=[cfk], out=cf, in0=cf, in1=g_bc, op=ALU.mult)
                if isp:
                    P.dve("tensor_tensor", R=[cfk, "b_bc"], W=["cvn"], out=cvn[:, s, :], in0=cf, in1=b_bc, op=ALU.add)
                else:
                    P.dve("tensor_tensor", R=[cfk, "b_bc"], W=[cfk], out=cf, in0=cf, in1=b_bc, op=ALU.add)
                    P.act("activation", R=[cfk], W=["cvn"], out=cvn[:, s, :], in_=cf, func=AF.Copy)
                    P.dma(R=[cfk], out=ncv_s, in_=cf, is_output=True)
                if cfg.stop == 22:
                    continue
                b = proj_tm(4 * DB if cfg.stop != 25 else 5 * DB)
                st_, stk = stage()
                P.act("activation", R=[PS(b)], W=[stk], out=st_, in_=psf[b], func=AF.Copy)
                P.dma(R=[stk], out=((nk_p if cfg.stop != 25 else nv_p)[rows, :] if isp else (nk_s if cfg.stop != 25 else nv_s)), in_=st_, is_output=True)
                kb_ = kbf[s % 2]
                P.dve("tensor_copy", R=[PS(b)], W=[("kbf", s % 2)], out=kb_, in_=psf[b])
                if isp:
                    transpose_into(kb_, ("kbf", s % 2), 4, kT, "kT", kb0 + s)
                else:
                    transpose_into(kb_, ("kbf", s % 2), 4, kT_new, "kT_new", 0)
                if cfg.stop in (23, 25):
                    continue
                b = proj_tm(5 * DB)
                st_, stk = stage()
                P.act("activation", R=[PS(b)], W=[stk], out=st_, in_=psf[b], func=AF.Copy)
                P.dma(R=[stk], out=(nv_p[rows, :] if isp else nv_s), in_=st_, is_output=True)
                if cfg.stop == 26 or (cfg.stop == 27 and isp) or (cfg.stop == 28 and not isp):
                    continue
                if isp:
                    P.act("activation", R=[PS(b)], W=["Vc"], out=Vc[:, kb0 + s, :], in_=psf[b], func=AF.Copy)
                else:
                    P.act("activation", R=[PS(b)], W=[("kbf", 1)], out=kbf[1], in_=psf[b], func=AF.Copy)
                    for q in range(NS if cfg.stop not in (24, 27, 28) else 0):
                        P.dma(R=[("kbf", 1)], W=[("Vn", q)], out=Vn[q][0:DS, :], in_=kbf[1][q * DS:(q + 1) * DS, :])

            if cfg.stop in (2, 21, 22, 23, 24, 25, 26, 27, 28):
                continue
            Wm, Wk = (WT, "WT") if isp else (WTs, "WTs")
            cbr, cbk = (cb_row, "cb_row") if isp else (cb_row_s, "cb_row_s")
            for hh in range(4):
                bm = bank()
                for s in range(NT):
                    sl = slice(s * 128, (s + 1) * 128)
                    P.pe("matmul", R=["cvn", Wk], W=[PS(bm)], out=psf[bm][:, sl], lhsT=cvn[:, s, hh * 128:(hh + 1) * 128],
                         rhs=Wm[:, hh, :], start=True, stop=False)
                    P.pe("matmul", R=["ones_row", cbk], W=[PS(bm)], out=psf[bm][:, sl], lhsT=ones_row[0:1, :],
                         rhs=cbr[0:1, hh, :], start=False, stop=True)
                bu, bzc = bank(), bank()
                proj_fm(E, bu, E.w_big, 0 + hh, T)
                proj_fm(E, bzc, E.w_big, 8 + hh, T)
                sg, sgk = sigmoid_from(E, psf[bzc][:, 0:T], PS(bzc), T)
                P.dve("tensor_tensor", R=[PS(bzc), sgk], W=[sgk], out=sg[:, 0:T], in0=psf[bzc][:, 0:T], in1=sg[:, 0:T],
                      op=ALU.mult)
                P.dve("tensor_tensor", R=[PS(bm), sgk], W=[sgk], out=sg[:, 0:T], in0=psf[bm][:, 0:T], in1=sg[:, 0:T],
                      op=ALU.mult)
                P.dve("tensor_tensor", R=[PS(bu), sgk], W=["catT"], out=E.catT[:, hh, 0:T], in0=psf[bu][:, 0:T],
                      in1=sg[:, 0:T], op=ALU.mult)

            if cfg.stop == 3:
                continue
            for hp in range(4):
                bq = bank()
                proj_fm(E, bq, E.w_big, 12 + hp, T)
                P.act("activation", R=[PS(bq)], W=["qT"], out=qT[:, hp, 0:T], in_=psf[bq][:, 0:T], func=AF.Copy,
                      scale=0.125)

            def silu_dz(hp, dst, dkey):
                bd = bank()
                proj_fm(E, bd, E.w_big, 24 + hp, T)
                sg, sgk = sigmoid_from(E, psf[bd][:, 0:T], PS(bd), T)
                P.dve("tensor_tensor", R=[PS(bd), sgk], W=[dkey], out=dst, in0=psf[bd][:, 0:T], in1=sg[:, 0:T],
                      op=ALU.mult)

            def finalize(hp, qa, qb, m1, m1k):
                for h in range(2):
                    r0 = 64 * h
                    P.dve("tensor_tensor", R=[PS(6 + h), m1k], W=["catT"], out=E.catT[r0:r0 + 64, 4 + hp, qa:qb],
                          in0=psf[6 + h][r0:r0 + 64, qa:qb], in1=m1[r0:r0 + 64, qa:qb], op=ALU.mult)

            if cfg.stop == 4 or (cfg.stop == 5 and not isp):
                continue
            if isp:
                for hp in range(4):
                    silu_dz(hp, E.catT[:, 4 + hp, 0:T], "catT")
                rmsnorm_to_T(E, tiles[ti + 1], hb, hkey)
                if ti + 2 < len(tiles):
                    p2_load(ti + 2)
                attention_prompt(T, kb0, NT)
            else:
                for hp in range(4):
                    silu_dz(hp, dzs[:, hp, :], "dzs")
                npast = PL // 128
                for q in range(NS):
                    for kb in range(npast):
                        st_, stk = stage()
                        P.dma(W=[stk], out=st_, in_=ck[q, kb * 128:(kb + 1) * 128, :])
                        P.act("activation", R=[stk], W=[("kbf", kb % 2)], out=kbf[kb % 2], in_=st_, func=AF.Copy)
                        transpose_into(kbf[kb % 2], ("kbf", kb % 2), 4, kT, "kT", kb)
                        st_, stk = stage()
                        P.dma(W=[stk], out=st_, in_=cv[q, kb * 128:(kb + 1) * 128, :])
                        P.dve("tensor_copy", R=[stk], W=["Vc"], out=Vc[:, kb, :], in_=st_)
                    qa, qb = q * DS, (q + 1) * DS
                    for hp in range(4):
                        blocks = [(kT_new[:, hp, qa:qb], Vn[q][0:DS, hp * 128:(hp + 1) * 128], DS, qa, True)]
                        for kb in range(npast - 1, -1, -1):
                            blocks.append((kT[:, hp, kb * 128:(kb + 1) * 128], Vc[:, kb, hp * 128:(hp + 1) * 128], 128,
                                           qa, False))
                        attention(hp, qa, qb, blocks)
                        finalize(hp, qa, qb, dzs[:, hp, :], "dzs")
            P.dma(R=["catT"], W=[("catD", ti)], out=catD[:, t.row0:t.row0 + T].rearrange("(c p) t -> p c t", p=128),
                  in_=E.catT[:, :, 0:T], is_output=cfg.debug)
        P.barrier()
        AR.off = base_off

    if 3 in cfg.passes:
        E = common_bufs(n_h=2, n_tmp=6)
        E.catT2 = [E.catT, AR.alloc([128, 8, 512], BF16)]
        E.pbuf = [AR.alloc([128, 4, PLE]) for _ in range(2)]
        E.pT = AR.alloc([128, 2, 512], BF16)
        E.xnp = [AR.alloc([128, PLE], BF16) for _ in range(2)]
        E.w_out = AR.alloc([128, 8, D], BF16)
        E.w_gate = AR.alloc([128, 8, D], BF16)
        E.w_proj = AR.alloc([128, 2, D], BF16)
        fg_bc = AR.alloc([128, D])
        print("P3 SBUF bytes/partition:", AR.off * 4)
        load_weight(E, E.w_out, "w_out", cd_w_out, 8, D)
        load_weight(E, E.w_gate, "w_gate", ple_gate[1], 8, D)
        load_weight(E, E.w_proj, "w_proj", ple_proj[1], 2, D)
        P.dma(W=["fg_bc"], out=fg_bc, in_=final_g.partition_broadcast(128))

        def p3_load(ti):
            t = tiles[ti]
            par = ti % 2
            P.dma(W=[(("hbuf", par), s_) for s_ in range(4)], R=[("hA", ti)], out=E.hbuf[par][:, 0:t.NT, :],
                  in_=hA[t.row0:t.row0 + t.T, :].rearrange("(n p) d -> p n d", p=128))
            P.dma(W=[("pbuf", par)], out=E.pbuf[par][:, 0:t.NT, :],
                  in_=p_rows(t, 1).rearrange("(n p) d -> p n d", p=128))
            P.dma(W=[(("catT", par), s_) for s_ in range(4)], R=[("catD", ti)], out=E.catT2[par][:, :, 0:t.T],
                  in_=catD[:, t.row0:t.row0 + t.T].rearrange("(c p) t -> p c t", p=128))

        p3_load(0)
        for ti, t in enumerate(tiles):
            T, NT = t.T, t.NT
            par = ti % 2
            hb, hkey = E.hbuf[par], ("hbuf", par)
            pb, pkey = E.pbuf[par], ("pbuf", par)
            if ti + 1 < len(tiles):
                p3_load(ti + 1)
            E.catT = E.catT2[par]
            tail(E, t, hb, hkey, pb, pkey, catkey=("catT", par))
            P.pool("memset", W=["ss"], ap=E.ss, constant=0.0)
            for s in range(NT):
                P.act("activation", R=[(hkey, s)], W=[("xnb", s % 2), "ss"], out=E.xnb[s % 2], in_=hb[:, s, :],
                      func=AF.Square, accum_out=E.ss[:, s:s + 1])
            P.act("activation", R=["ss"], W=["rstd"], out=E.rstd[:, 0:NT], in_=E.ss[:, 0:NT], func=AF.Ln,
                  scale=1.0 / D, bias=EPS)
            P.act("activation", R=["rstd"], W=["rstd"], out=E.rstd[:, 0:NT], in_=E.rstd[:, 0:NT], func=AF.Exp,
                  scale=-0.5)
            for s in range(NT):
                P.dve("scalar_tensor_tensor", R=[(hkey, s), "rstd", "fg_bc"], W=[(hkey, s)], out=hb[:, s, :], in0=hb[:, s, :],
                      scalar=E.rstd[:, s:s + 1], in1=fg_bc, op0=ALU.mult, op1=ALU.mult)
            dst = y_p[t.row0:t.row0 + T, :] if t.kind == "p" else y_s
            P.dma(R=[(hkey, s_) for s_ in range(4)], out=dst.rearrange("(n p) d -> p n d", p=128), in_=hb[:, 0:NT, :],
                  is_output=True)
        P.barrier()
        AR.off = base_off

    P.emit()
    global LAST_PROG
    LAST_PROG = P
    return nc


LAST_PROG = None


def shard_inputs(inp, cfg, n_cores):
    NP, S, NS, DS, PL = cfg.NP, cfg.S, cfg.NS, cfg.DS, cfg.PL
    f = lambda a: np.ascontiguousarray(np.asarray(a, dtype=np.float32))
    maps = []
    for c in range(n_cores):
        ps = slice(c * NP, (c + 1) * NP)
        ss = slice(c * NS, (c + 1) * NS)
        m = {
            "x_prompt": f(inp["x_prompt"][ps]).reshape(NP * S, D),
            "x_sample": f(inp["x_sample"][ss]).reshape(NS * DS, D),
            "state_a_conv": f(inp["state_a_conv"][0, ss]),
            "state_b_conv": f(inp["state_b_conv"][0, ss]),
            "cache_d_k": f(inp["cache_d_k"][0, ss]).reshape(NS, PL, DB),
            "cache_d_v": f(inp["cache_d_v"][0, ss]).reshape(NS, PL, DB),
            "p_prompt": f(inp["p_prompt"][:, ps]).reshape(2, NP * S, PLE),
            "p_sample": f(inp["p_sample"][:, ss]).reshape(2, NS * DS, PLE),
            "norm_g": f(inp["norm_g"]),
            "ple_gate": f(inp["ple_gate"]),
            "ple_proj": f(inp["ple_proj"]),
            "ab_w_in": f(inp["ab_w_in"][0]),
            "a_conv_w": f(inp["a_conv_w"][0]),
            "b_conv_w": f(inp["b_conv_w"][0]),
            "b_ln_g": f(inp["b_ln_g"][0]),
            "b_ln_b": f(inp["b_ln_b"][0]),
            "ab_w_out": f(inp["ab_w_out"][0]),
            "cd_w_in": f(inp["cd_w_in"][0]),
            "c_ln_g": f(inp["c_ln_g"][0]),
            "c_ln_b": f(inp["c_ln_b"][0]),
            "c_ws": f(inp["c_ws"][0]),
            "c_b": f(inp["c_b"][0]),
            "cd_w_out": f(inp["cd_w_out"][0]),
            "final_g": f(inp["final_g"]),
        }
        maps.append(m)
    return maps


def kernel(**inputs):
    n_cores = 8
    cfg = Cfg(NP=2, S=4096, NS=4, DS=32, PL=2048)
    nc = build_program(cfg)
    in_maps = shard_inputs(inputs, cfg, n_cores)
    res = run_bass_kernel_spmd(nc, in_maps, core_ids=list(range(n_cores)))
    r = res.results
    NP, S, NS, DS = cfg.NP, cfg.S, cfg.NS, cfg.DS
    cat = lambda name, shp: np.concatenate([np.asarray(r[c][name], dtype=np.float32).reshape(shp) for c in range(n_cores)], axis=0)
    y_prompt = cat("y_prompt", (NP, S, D))
    y_sample = cat("y_sample", (NS, DS, D))
    na_p = cat("new_a_prompt", (NP, HA, DB))[None]
    na_s = cat("new_a_sample", (NS, HA, DB))[None]
    nb_p = cat("new_b_prompt", (NP, HB, DB))[None]
    nb_s = cat("new_b_sample", (NS, HB, DB))[None]
    ncv = cat("new_cv_sample", (NS, DS, DB))[None]
    nk_p = cat("new_k_prompt", (NP, S, 8, 64))[None]
    nv_p = cat("new_v_prompt", (NP, S, 8, 64))[None]
    nk_s = cat("new_k_sample", (NS, DS, 8, 64))[None]
    nv_s = cat("new_v_sample", (NS, DS, 8, 64))[None]
    return (y_prompt, y_sample, na_p, na_s, nb_p, nb_s, ncv, nk_p, nv_p, nk_s, nv_s)
```

```python
import contextlib
import numpy as np
import concourse.bass as bass
import concourse.mybir as mybir
from concourse.bass_utils import run_bass_kernel_spmd

F32 = mybir.dt.float32
BF16 = mybir.dt.bfloat16
AF = mybir.ActivationFunctionType
ALU = mybir.AluOpType

ENGINES = ["pe", "act", "dve", "pool", "sp"]
D = 1024
DB = 512
PLE = 256
EPS = 1e-6
HB = 30
HA = 2


class Prog:
    EPOCH = 12000

    def __init__(self, nc, n_dma_slots=48):
        self.nc = nc
        self.ops = {e: [] for e in ENGINES}
        self.cnt = {e: 0 for e in ENGINES}
        self.last_w = {}
        self.readers = {}
        self.seen = {e: {} for e in ENGINES}
        self.n_dma_slots = n_dma_slots
        self.dma_next = 0
        self.dma_val = [0] * n_dma_slots
        self.out_tokens = []
        self.bank_i = 0

    def _need(self, eng, tok, waits):
        if tok is None:
            return
        if tok[0] == "E":
            _, e2, idx = tok
            if e2 == eng and eng == "pe":
                return
            k = ("E", e2)
        else:
            _, slot, idx = tok
            k = ("D", slot)
        if self.seen[eng].get(k, -1) >= idx:
            return
        waits[k] = max(waits.get(k, -1), idx)

    def op(self, eng, name, R=(), W=(), dma=False, is_output=False, **kw):
        waits = {}
        for k in R:
            self._need(eng, self.last_w.get(k), waits)
        for k in W:
            self._need(eng, self.last_w.get(k), waits)
            for t in self.readers.get(k, ()):
                self._need(eng, t, waits)
        if dma:
            slot = self.dma_next
            self.dma_next = (self.dma_next + 1) % self.n_dma_slots
            prev = self.dma_val[slot]
            if prev > 0:
                self._need(eng, ("D", slot, prev), waits)
            self.dma_val[slot] = prev + 16
            tok = ("D", slot, prev + 16)
        else:
            idx = self.cnt[eng]
            self.cnt[eng] += 1
            tok = ("E", eng, idx)
        for k, v in waits.items():
            self.seen[eng][k] = v
        self.ops[eng].append((name, kw, waits, tok))
        for k in W:
            self.last_w[k] = tok
            self.readers[k] = []
        for k in R:
            if k in W:
                continue
            self.readers.setdefault(k, []).append(tok)
        if is_output:
            self.out_tokens.append(tok)
        return tok

    def pe(self, name, **kw):
        return self.op("pe", name, **kw)

    def act(self, name, **kw):
        return self.op("act", name, **kw)

    def dve(self, name, **kw):
        return self.op("dve", name, **kw)

    def pool(self, name, **kw):
        return self.op("pool", name, **kw)

    def dma(self, **kw):
        return self.op("sp", "dma_start", dma=True, **kw)

    def barrier(self):
        for eng in ENGINES:
            waits = {}
            for e2 in ENGINES:
                if e2 != eng and self.cnt[e2] > 0:
                    self._need(eng, ("E", e2, self.cnt[e2] - 1), waits)
            for slot in range(self.n_dma_slots):
                if self.dma_val[slot] > 0:
                    self._need(eng, ("D", slot, self.dma_val[slot]), waits)
            for k, v in waits.items():
                self.seen[eng][k] = v
            self.ops[eng].append((None, None, waits, None))

    def emit(self):
        nc = self.nc
        with contextlib.ExitStack() as st:
            esem = {}
            for e in ENGINES:
                n_ep = (self.cnt[e] + self.EPOCH - 1) // self.EPOCH
                esem[e] = [st.enter_context(nc.semaphore(f"s_{e}{i}")) for i in range(max(n_ep, 1))]
            dsem = [st.enter_context(nc.semaphore(f"s_d{i}")) for i in range(self.n_dma_slots)]
            block = st.enter_context(nc.Block())

            def do_wait(h, k, v):
                if k[0] == "E":
                    h.wait_ge(esem[k[1]][v // self.EPOCH], v % self.EPOCH + 1)
                else:
                    h.wait_ge(dsem[k[1]], v)

            def run(ename):
                def body(h):
                    for name, kw, waits, tok in self.ops[ename]:
                        ws = list(waits.items())
                        if name is None:
                            for k, v in ws:
                                do_wait(h, k, v)
                            continue
                        for k, v in ws[1:]:
                            do_wait(h, k, v)
                        ins = getattr(h, name)(**kw)
                        if ws:
                            k, v = ws[0]
                            if k[0] == "E":
                                ins._wait_ge(esem[k[1]][v // self.EPOCH], v % self.EPOCH + 1)
                            else:
                                ins._wait_ge(dsem[k[1]], v)
                        if tok[0] == "E":
                            ins.then_inc(esem[tok[1]][tok[2] // self.EPOCH], 1)
                        else:
                            ins.then_inc(dsem[tok[1]], 16)
                    if ename == "sp":
                        for tok in self.out_tokens:
                            do_wait(h, ("D", tok[1]), tok[2])
                return body

            block.tensor(run("pe"))
            block.scalar(run("act"))
            block.vector(run("dve"))
            block.gpsimd(run("pool"))
            block.sync(run("sp"))


class Arena:
    def __init__(self, nc, nbytes):
        self.t = nc.alloc_sbuf_tensor("arena", [128, nbytes // 4], F32)
        self.off = 0
        self.cap = nbytes // 4

    def alloc(self, shape, dt=F32):
        n = int(np.prod(shape[1:]))
        nw = (n if dt == F32 else (n + 1) // 2)
        nw = (nw + 7) // 8 * 8
        assert self.off + nw <= self.cap, f"SBUF arena overflow: need {(self.off + nw) * 4} B"
        ap = self.t[0:shape[0], self.off:self.off + nw]
        self.off += nw
        if dt != F32:
            ap = ap.bitcast(dt)
        ap = ap[:, 0:n]
        if len(shape) == 3:
            ap = ap.rearrange("p (a b) -> p a b", a=shape[1])
        elif len(shape) == 4:
            ap = ap.rearrange("p (a b c) -> p a b c", a=shape[1], b=shape[2])
        return ap


class Cfg:
    def __init__(self, NP=2, S=4096, NS=4, DS=32, PL=2048, passes=(1, 2, 3), debug=False, stop=0):
        self.NP, self.S, self.NS, self.DS, self.PL = NP, S, NS, DS, PL
        self.passes = passes
        self.debug = debug
        self.stop = stop
        assert NS * DS == 128 and S % 512 == 0 and PL % 128 == 0
        self.NTOK = NP * S + 128


class Tile:
    def __init__(self, kind, seq, t0, T, nseg, L, row0, last):
        self.kind, self.seq, self.t0, self.T, self.nseg, self.L = kind, seq, t0, T, nseg, L
        self.row0 = row0
        self.last = last
        self.NT = T // 128


def make_tiles(cfg):
    tiles = []
    for q in range(cfg.NP):
        n = cfg.S // 512
        for i in range(n):
            tiles.append(Tile("p", q, i * 512, 512, 1, 512, q * cfg.S + i * 512, i == n - 1))
    tiles.append(Tile("s", 0, 0, 128, cfg.NS, cfg.DS, cfg.NP * cfg.S, True))
    return tiles


def build_program(cfg):
    nc = bass.Bass("TRN2", target_bir_lowering=False)
    NP, S, NS, DS, PL = cfg.NP, cfg.S, cfg.NS, cfg.DS, cfg.PL
    NTOK = cfg.NTOK

    def din(name, shape):
        return nc.dram_tensor(name, list(shape), F32, kind="ExternalInput").ap()

    def dout(name, shape):
        return nc.dram_tensor(name, list(shape), F32, kind="ExternalOutput").ap()

    x_p = din("x_prompt", [NP * S, D])
    x_s = din("x_sample", [128, D])
    st_a = din("state_a_conv", [NS, HA, DB])
    st_b = din("state_b_conv", [NS, HB, DB])
    ck = din("cache_d_k", [NS, PL, DB])
    cv = din("cache_d_v", [NS, PL, DB])
    p_p = din("p_prompt", [2, NP * S, PLE])
    p_s = din("p_sample", [2, 128, PLE])
    norm_g = din("norm_g", [2, D])
    ple_gate = din("ple_gate", [2, D, D])
    ple_proj = din("ple_proj", [2, PLE, D])
    ab_w_in = din("ab_w_in", [D, 7 * DB])
    a_conv_w = din("a_conv_w", [3, DB])
    b_conv_w = din("b_conv_w", [31, DB])
    b_ln_g = din("b_ln_g", [DB])
    b_ln_b = din("b_ln_b", [DB])
    ab_w_out = din("ab_w_out", [D, D])
    cd_w_in = din("cd_w_in", [D, 7 * DB])
    c_ln_g = din("c_ln_g", [DB])
    c_ln_b = din("c_ln_b", [DB])
    c_ws = din("c_ws", [4, 128, 128])
    c_b = din("c_b", [4, 128])
    cd_w_out = din("cd_w_out", [D, D])
    final_g = din("final_g", [D])

    y_p = dout("y_prompt", [NP * S, D])
    y_s = dout("y_sample", [128, D])
    na_p = dout("new_a_prompt", [NP, HA, DB])
    na_s = dout("new_a_sample", [NS, HA, DB])
    nb_p = dout("new_b_prompt", [NP, HB, DB])
    nb_s = dout("new_b_sample", [NS, HB, DB])
    ncv_s = dout("new_cv_sample", [128, DB])
    nk_p = dout("new_k_prompt", [NP * S, DB])
    nv_p = dout("new_v_prompt", [NP * S, DB])
    nk_s = dout("new_k_sample", [128, DB])
    nv_s = dout("new_v_sample", [128, DB])

    kind_scr = "ExternalOutput" if cfg.debug else "Internal"
    hA = nc.dram_tensor("hA", [NTOK, D], F32, kind=kind_scr).ap()
    catD = nc.dram_tensor("catD", [D, NTOK], BF16, kind=kind_scr).ap()

    P = Prog(nc)
    tiles = make_tiles(cfg)
    AR = Arena(nc, 207 * 1024)

    psf = [nc.alloc_psum_tensor(f"ps{i}", [128, 512], F32)[:] for i in range(8)]
    psb = [p.bitcast(BF16) for p in psf]

    def bank():
        b = P.bank_i
        P.bank_i = (P.bank_i + 1) % 6
        return b

    def PS(b):
        return ("ps", b)

    identf = AR.alloc([128, 128])
    identb = AR.alloc([128, 128], BF16)
    onesf = AR.alloc([128, 128])
    ngcol = AR.alloc([128, 2, 8])
    P.pool("memset", W=["identf"], ap=identf, constant=0.0)
    P.pool("affine_select", R=["identf"], W=["identf"], out=identf, in_=identf, pattern=[[-1, 128]],
           compare_op=ALU.not_equal, fill=1.0, base=0, channel_multiplier=1)
    P.dve("tensor_copy", R=["identf"], W=["identb"], out=identb, in_=identf)
    P.pool("memset", W=["onesf"], ap=onesf, constant=1.0)
    import os as _os
    for _i in range(int(_os.environ.get('KDUMMY', '0'))):
        P.dve("tensor_copy", R=["identf"], W=["identb"], out=identb, in_=identf)
    for i in range(2):
        P.dma(W=["ngcol"], out=ngcol[:, i, :], in_=norm_g[i].rearrange("(kc p) -> p kc", p=128),
              allow_slow_non_contiguous=True)
    base_off = AR.off

    def x_rows(t):
        return x_p[t.row0:t.row0 + t.T, :] if t.kind == "p" else x_s

    def p_rows(t, layer):
        return p_p[layer, t.row0:t.row0 + t.T, :] if t.kind == "p" else p_s[layer]

    class Env:
        pass

    def common_bufs(n_h=2, n_tmp=6):
        E = Env()
        E.hbuf = [AR.alloc([128, 4, D]) for _ in range(n_h)]
        E.xnb = [AR.alloc([128, D], BF16) for _ in range(2)]
        E.actT = AR.alloc([128, 8, 512], BF16)
        E.catT = AR.alloc([128, 8, 512], BF16)
        E.ss = AR.alloc([128, 4])
        E.rstd = AR.alloc([128, 4])
        E.ftmp = [AR.alloc([128, 512]) for _ in range(n_tmp)]
        E.ftmp_i = 0
        E.cast_i = 0
        return E

    def tmp(E):
        i = E.ftmp_i
        E.ftmp_i = (i + 1) % len(E.ftmp)
        return E.ftmp[i], ("ftmp", i)

    def load_weight(E, dst, dkey, src, nk, ncols, gain_i=None):
        for kc in range(nk):
            for c0 in range(0, ncols, 512):
                st_, sk = tmp(E)
                P.dma(W=[sk], out=st_, in_=src[kc * 128:(kc + 1) * 128, c0:c0 + 512])
                if gain_i is not None:
                    P.act("activation", R=[sk, "ngcol"], W=[dkey], out=dst[:, kc, c0:c0 + 512], in_=st_, func=AF.Copy,
                          scale=ngcol[:, gain_i, kc:kc + 1])
                elif E.cast_i % 2 == 0:
                    P.act("activation", R=[sk], W=[dkey], out=dst[:, kc, c0:c0 + 512], in_=st_, func=AF.Copy)
                else:
                    P.dve("tensor_copy", R=[sk], W=[dkey], out=dst[:, kc, c0:c0 + 512], in_=st_)
                E.cast_i += 1

    def transpose_into(src, skey, nchunk, dstT, dkey, s):
        b = bank()
        for c in range(nchunk):
            P.pe("transpose", R=[skey, "identb"], W=[PS(b)], out=psb[b][:, c * 128:(c + 1) * 128],
                 in_=src[:, c * 128:(c + 1) * 128], identity=identb)
        P.dve("tensor_copy", R=[PS(b)], W=[dkey], out=dstT[:, 0:nchunk, s * 128:(s + 1) * 128],
              in_=psb[b][:, 0:nchunk * 128].rearrange("p (c t) -> p c t", c=nchunk))

    def rmsnorm_to_T(E, t, hb, hkey):
        NT = t.NT
        P.pool("memset", W=["ss"], ap=E.ss, constant=0.0)
        for s in range(NT):
            P.act("activation", R=[(hkey, s)], W=[("xnb", s % 2), "ss"], out=E.xnb[s % 2], in_=hb[:, s, :],
                  func=AF.Square, accum_out=E.ss[:, s:s + 1])
        P.act("activation", R=["ss"], W=["rstd"], out=E.rstd[:, 0:NT], in_=E.ss[:, 0:NT], func=AF.Ln, scale=1.0 / D,
              bias=EPS)
        P.act("activation", R=["rstd"], W=["rstd"], out=E.rstd[:, 0:NT], in_=E.rstd[:, 0:NT], func=AF.Exp, scale=-0.5)
        for s in range(NT):
            P.act("activation", R=[(hkey, s), "rstd"], W=[("xnb", s % 2)], out=E.xnb[s % 2], in_=hb[:, s, :],
                  func=AF.Copy, scale=E.rstd[:, s:s + 1])
            transpose_into(E.xnb[s % 2], ("xnb", s % 2), 8, E.actT, ("actT", s), s)

    def proj_fm(E, b, w, oc, T):
        for kc in range(8):
            P.pe("matmul", R=[("actT", s_) for s_ in range(T // 128)] + ["w_big"], W=[PS(b)], out=psf[b][:, 0:T],
                 lhsT=w[:, kc, oc * 128:(oc + 1) * 128], rhs=E.actT[:, kc, 0:T], start=(kc == 0), stop=(kc == 7))

    def sigmoid_from(E, src_ap, skey, T, scale=-1.0, bias=None, extra=()):
        tt, tk = tmp(E)
        kw = {} if bias is None else {"bias": bias}
        P.act("activation", R=[skey] + list(extra), W=[tk], out=tt[:, 0:T], in_=src_ap, func=AF.Exp, scale=scale, **kw)
        P.act("activation", R=[tk], W=[tk], out=tt[:, 0:T], in_=tt[:, 0:T], func=AF.Ln, bias=1.0)
        P.act("activation", R=[tk], W=[tk], out=tt[:, 0:T], in_=tt[:, 0:T], func=AF.Exp, scale=-1.0)
        return tt, tk

    def sigmoid_multi(E, srcs, T):
        outs = []
        for (ap, key, scale, bias, extra) in srcs:
            tt, tk = tmp(E)
            kw = {} if bias is None else {"bias": bias}
            P.act("activation", R=[key] + list(extra), W=[tk], out=tt[:, 0:T], in_=ap, func=AF.Exp, scale=scale, **kw)
            outs.append((tt, tk))
        for (tt, tk) in outs:
            P.act("activation", R=[tk], W=[tk], out=tt[:, 0:T], in_=tt[:, 0:T], func=AF.Ln, bias=1.0)
        for (tt, tk) in outs:
            P.act("activation", R=[tk], W=[tk], out=tt[:, 0:T], in_=tt[:, 0:T], func=AF.Exp, scale=-1.0)
        return outs

    def tail(E, t, hb, hkey, pb, pkey, catkey="catT"):
        for _ in tail_gen(E, t, hb, hkey, pb, pkey, catkey):
            pass

    def tail_gen(E, t, hb, hkey, pb, pkey, catkey="catT"):
        NT = t.NT
        halves = [slice(0, 512), slice(512, 1024)]
        for s in range(NT):
            sl = slice(s * 128, (s + 1) * 128)
            for hs in halves:
                b = bank()
                for kc in range(8):
                    P.pe("matmul", R=[(catkey, s), "w_out"], W=[PS(b)], out=psf[b], lhsT=E.catT[:, kc, sl],
                         rhs=E.w_out[:, kc, hs], start=(kc == 0), stop=(kc == 7))
                P.dve("tensor_tensor", R=[PS(b), (hkey, s)], W=[(hkey, s)], out=hb[:, s, hs], in0=hb[:, s, hs],
                      in1=psf[b], op=ALU.add)
                yield
        for s in range(NT):
            xi = s % 2
            P.act("activation", R=[(hkey, s)], W=[("xnb", xi)], out=E.xnb[xi], in_=hb[:, s, :], func=AF.Copy)
            transpose_into(E.xnb[xi], ("xnb", xi), 8, E.catT, (catkey, s), s)
            P.act("activation", R=[pkey], W=[("xnp", xi)], out=E.xnp[xi], in_=pb[:, s, :], func=AF.Copy)
            transpose_into(E.xnp[xi], ("xnp", xi), 2, E.pT, ("pT", s), s)
            yield
        for s in range(NT):
            sl = slice(s * 128, (s + 1) * 128)
            bgs, bps = [], []
            for hs in halves:
                bg = bank()
                for kc in range(8):
                    P.pe("matmul", R=[(catkey, s), "w_gate"], W=[PS(bg)], out=psf[bg], lhsT=E.catT[:, kc, sl],
                         rhs=E.w_gate[:, kc, hs], start=(kc == 0), stop=(kc == 7))
                bp = bank()
                for kc in range(2):
                    P.pe("matmul", R=[("pT", s), "w_proj"], W=[PS(bp)], out=psf[bp], lhsT=E.pT[:, kc, sl],
                         rhs=E.w_proj[:, kc, hs], start=(kc == 0), stop=(kc == 1))
                bgs.append(bg)
                bps.append(bp)
            yield
            sgs = sigmoid_multi(E, [(psf[bg], PS(bg), -1.0, None, ()) for bg in bgs], 512)
            yield
            for (sg, sgk), bp in zip(sgs, bps):
                P.dve("tensor_tensor", R=[PS(bp), sgk], W=[sgk], out=sg, in0=psf[bp], in1=sg, op=ALU.mult)
            for (sg, sgk), hs in zip(sgs, halves):
                P.dve("tensor_tensor", R=[sgk, (hkey, s)], W=[(hkey, s)], out=hb[:, s, hs], in0=hb[:, s, hs], in1=sg,
                      op=ALU.add)
            yield

    def state_out(E, t, bufs, key, H, L, dst):
        for q in range(t.nseg):
            b = bank()
            for j in range(4):
                P.pe("transpose", R=[(key, j), "identf"], W=[PS(b)], out=psf[b][0:H, j * 128:(j + 1) * 128],
                     in_=bufs[j][:, q, L:L + H], identity=identf)
            P.act("activation", R=[PS(b)], W=["hst"], out=E.hst[0:H, :], in_=psf[b][0:H, :], func=AF.Copy)
            P.dma(R=["hst"], out=dst[t.seq if t.kind == "p" else q], in_=E.hst[0:H, :], is_output=True)

    if 1 in cfg.passes:
        E = common_bufs()
        E.pbuf = [AR.alloc([128, 4, PLE]) for _ in range(2)]
        E.pT = AR.alloc([128, 2, 512], BF16)
        E.xnp = [AR.alloc([128, PLE], BF16) for _ in range(2)]
        E.w_big = AR.alloc([128, 8, 7 * DB], BF16)
        E.w_out = AR.alloc([128, 8, D], BF16)
        E.w_gate = AR.alloc([128, 8, D], BF16)
        E.w_proj = AR.alloc([128, 2, D], BF16)
        awc = AR.alloc([128, 4, 3])
        bwc = AR.alloc([128, 4, 31])
        lncol = AR.alloc([128, 4, 4])
        ubuf_p = [AR.alloc([128, 1, HA + 512]) for j in range(4)]
        gbuf_p = [AR.alloc([128, 1, HB + 512]) for j in range(4)]
        ubuf_s = [AR.alloc([128, NS, HA + DS]) for j in range(4)]
        gbuf_s = [AR.alloc([128, NS, HB + DS]) for j in range(4)]
        bconv = [AR.alloc([128, 512]) for j in range(4)]
        negmean = AR.alloc([128, 512])
        rstdB = AR.alloc([128, 512])
        E.hst = AR.alloc([32, 512])
        print("P1 SBUF bytes/partition:", AR.off * 4)

        load_weight(E, E.w_big, "w_big", ab_w_in, 8, 7 * DB, gain_i=0)
        load_weight(E, E.w_out, "w_out", ab_w_out, 8, D)
        load_weight(E, E.w_gate, "w_gate", ple_gate[0], 8, D)
        load_weight(E, E.w_proj, "w_proj", ple_proj[0], 2, D)
        for j in range(4):
            P.dma(W=["awc"], out=awc[:, j, :], in_=a_conv_w[:, j * 128:(j + 1) * 128].rearrange("w p -> p w"),
                  allow_slow_non_contiguous=True)
            P.dma(W=["bwc"], out=bwc[:, j, :], in_=b_conv_w[:, j * 128:(j + 1) * 128].rearrange("w p -> p w"),
                  allow_slow_non_contiguous=True)
        P.dma(W=["lncol"], out=lncol[:, :, 0], in_=b_ln_g.rearrange("(j p) -> p j", p=128),
              allow_slow_non_contiguous=True)
        P.dma(W=["lncol"], out=lncol[:, :, 1], in_=b_ln_b.rearrange("(j p) -> p j", p=128),
              allow_slow_non_contiguous=True)
        P.pool("tensor_scalar", R=["lncol"], W=["lncol"], out=lncol[:, :, 2:4], in0=lncol[:, :, 0:2], scalar1=-1.0,
               scalar2=None, op0=ALU.mult)

        def p1_load(ti):
            t = tiles[ti]
            par = ti % 2
            P.dma(W=[(("hbuf", par), s_) for s_ in range(4)], out=E.hbuf[par][:, 0:t.NT, :],
                  in_=x_rows(t).rearrange("(n p) d -> p n d", p=128))
            P.dma(W=[("pbuf", par)], out=E.pbuf[par][:, 0:t.NT, :],
                  in_=p_rows(t, 0).rearrange("(n p) d -> p n d", p=128))

        CATK = [("catT", s_) for s_ in range(4)]
        for ti, t in enumerate(tiles):
            T, NT, nseg, L = t.T, t.NT, t.nseg, t.L
            par = ti % 2
            hb, hkey = E.hbuf[par], ("hbuf", par)
            pb, pkey = E.pbuf[par], ("pbuf", par)
            isp = t.kind == "p"
            ub, gb = (ubuf_p, gbuf_p) if isp else (ubuf_s, gbuf_s)
            ukey, gkey = ("ubuf_p", "gbuf_p") if isp else ("ubuf_s", "gbuf_s")

            def v3(ap):
                return ap.rearrange("p (n l) -> p n l", n=nseg)

            if ti == 0:
                p1_load(0)
            if isp and t.t0 == 0:
                for j in range(4):
                    P.pool("memset", W=[(ukey, j)], ap=ub[j][:, :, 0:HA], constant=0.0)
                    P.pool("memset", W=[(gkey, j)], ap=gb[j][:, :, 0:HB], constant=0.0)
            if not isp:
                for q in range(NS):
                    for (stt, H, bufs, key) in ((st_a, HA, ub, ukey), (st_b, HB, gb, gkey)):
                        P.dma(W=["hst"], out=E.hst[0:H, :], in_=stt[q])
                        b = bank()
                        for j in range(4):
                            P.pe("transpose", R=["hst", "identf"], W=[PS(b)], out=psf[b][:, j * 32:j * 32 + H],
                                 in_=E.hst[0:H, j * 128:(j + 1) * 128], identity=identf[0:H, 0:H])
                        for j in range(4):
                            P.act("activation", R=[PS(b)], W=[(key, j)], out=bufs[j][:, q, 0:H],
                                  in_=psf[b][:, j * 32:j * 32 + H], func=AF.Copy)

            if ti == 0:
                rmsnorm_to_T(E, t, hb, hkey)

            for pair in ((0, 1), (2, 3)):
                bvs, bgs = {}, {}
                for j in pair:
                    bvs[j], bgs[j] = bank(), bank()
                    proj_fm(E, bvs[j], E.w_big, 16 + j, T)
                    proj_fm(E, bgs[j], E.w_big, 20 + j, T)
                sgs = sigmoid_multi(E, [(psf[bgs[j]][:, 0:T], PS(bgs[j]), -1.0, None, ()) for j in pair], T)
                for (sg, sgk), j in zip(sgs, pair):
                    P.dve("tensor_tensor", R=[PS(bvs[j]), sgk], W=[(gkey, j)], out=gb[j][:, :, HB:HB + L],
                          in0=v3(psf[bvs[j]][:, 0:T]), in1=v3(sg[:, 0:T]), op=ALU.mult)
            def gen_A():
                for pair in ((0, 1), (2, 3)):
                    bxs, bcs = {}, {}
                    for j in pair:
                        bxs[j], bcs[j] = bank(), bank()
                        proj_fm(E, bxs[j], E.w_big, 0 + j, T)
                        proj_fm(E, bcs[j], E.w_big, 4 + j, T)
                    yield
                    tcs = {}
                    for j in pair:
                        tcs[j] = tmp(E)
                        P.act("activation", R=[PS(bcs[j])], W=[tcs[j][1]], out=tcs[j][0][:, 0:T], in_=psf[bcs[j]][:, 0:T],
                              func=AF.Copy)
                    for j in pair:
                        P.dve("tensor_tensor", R=[PS(bxs[j]), tcs[j][1]], W=[(ukey, j)], out=ub[j][:, :, HA:HA + L],
                              in0=v3(psf[bxs[j]][:, 0:T]), in1=v3(tcs[j][0][:, 0:T]), op=ALU.mult)
                    yield
                    bzs = {}
                    for j in pair:
                        bzs[j] = bank()
                        proj_fm(E, bzs[j], E.w_big, 12 + j, T)
                    yield
                    sgs = sigmoid_multi(E, [(psf[bzs[j]][:, 0:T], PS(bzs[j]), -1.0, None, ()) for j in pair], T)
                    yield
                    cvs = {}
                    for j in pair:
                        cvs[j] = tmp(E)
                        P.dve("tensor_scalar", R=[(ukey, j), "awc"], W=[cvs[j][1]], out=v3(cvs[j][0][:, 0:T]),
                              in0=ub[j][:, :, 2:2 + L], scalar1=awc[:, j, 2:3], scalar2=None, op0=ALU.mult)
                    yield
                    for w in (1, 0):
                        for j in pair:
                            P.dve("scalar_tensor_tensor", R=[(ukey, j), "awc", cvs[j][1]], W=[cvs[j][1]],
                                  out=v3(cvs[j][0][:, 0:T]), in0=ub[j][:, :, w:w + L], scalar=awc[:, j, w:w + 1],
                                  in1=v3(cvs[j][0][:, 0:T]), op0=ALU.mult, op1=ALU.add)
                    yield
                    bbs = {}
                    for j in pair:
                        bbs[j] = bank()
                        proj_fm(E, bbs[j], E.w_big, 8 + j, T)
                    yield
                    for (sg, sgk), j in zip(sgs, pair):
                        P.dve("tensor_tensor", R=[PS(bzs[j]), sgk], W=[sgk], out=sg[:, 0:T], in0=psf[bzs[j]][:, 0:T],
                              in1=sg[:, 0:T], op=ALU.mult)
                    for j in pair:
                        P.dve("tensor_tensor", R=[PS(bbs[j]), cvs[j][1]], W=[cvs[j][1]], out=cvs[j][0][:, 0:T],
                              in0=psf[bbs[j]][:, 0:T], in1=cvs[j][0][:, 0:T], op=ALU.mult)
                    yield
                    for (sg, sgk), j in zip(sgs, pair):
                        P.dve("tensor_tensor", R=[sgk, cvs[j][1]], W=CATK, out=E.catT[:, j, 0:T], in0=sg[:, 0:T],
                              in1=cvs[j][0][:, 0:T], op=ALU.mult)
                    yield
            def gen_bz():
                for pair in ((0, 1), (2, 3)):
                    bzs = {}
                    for j in pair:
                        bzs[j] = bank()
                        proj_fm(E, bzs[j], E.w_big, 24 + j, T)
                    yield
                    sgs = sigmoid_multi(E, [(psf[bzs[j]][:, 0:T], PS(bzs[j]), -1.0, None, ()) for j in pair], T)
                    yield
                    for (sg, sgk), j in zip(sgs, pair):
                        P.dve("tensor_tensor", R=[PS(bzs[j]), sgk], W=CATK, out=E.catT[:, 4 + j, 0:T],
                              in0=psf[bzs[j]][:, 0:T], in1=sg[:, 0:T], op=ALU.mult)
                    yield
            def gen_conv():
                for w in range(31):
                    for j in range(4):
                        if w == 0:
                            P.dve("tensor_scalar", R=[(gkey, j), "bwc"], W=[("bconv", j)], out=v3(bconv[j][:, 0:T]),
                                  in0=gb[j][:, :, 0:L], scalar1=bwc[:, j, 0:1], scalar2=None, op0=ALU.mult)
                        else:
                            P.dve("scalar_tensor_tensor", R=[(gkey, j), "bwc", ("bconv", j)], W=[("bconv", j)],
                                  out=v3(bconv[j][:, 0:T]), in0=gb[j][:, :, w:w + L], scalar=bwc[:, j, w:w + 1],
                                  in1=v3(bconv[j][:, 0:T]), op0=ALU.mult, op1=ALU.add)
                    yield
            def chain(*gs):
                for g in gs:
                    yield from g

            def gen_prev_tail():
                if ti == 0:
                    return
                tp, pp = tiles[ti - 1], (ti - 1) % 2
                yield from tail_gen(E, tp, E.hbuf[pp], ("hbuf", pp), E.pbuf[pp], ("pbuf", pp))
                P.dma(R=[(("hbuf", pp), s_) for s_ in range(4)], W=[("hA", ti - 1)],
                      out=hA[tp.row0:tp.row0 + tp.T, :].rearrange("(n p) d -> p n d", p=128),
                      in_=E.hbuf[pp][:, 0:tp.NT, :], is_output=cfg.debug)
                if ti + 1 < len(tiles):
                    p1_load(ti + 1)
                yield

            g1, g2 = gen_conv(), chain(gen_prev_tail(), gen_A(), gen_bz())
            alive = [g1, g2]
            while alive:
                for g in list(alive):
                    try:
                        next(g)
                    except StopIteration:
                        alive.remove(g)
            if t.last:
                state_out(E, t, ub, ukey, HA, L, na_p if isp else na_s)
            else:
                for j in range(4):
                    P.pool("tensor_copy", R=[(ukey, j)], W=[(ukey, j)], out=ub[j][:, :, 0:HA], in_=ub[j][:, :, L:L + HA])

            if t.last:
                state_out(E, t, gb, gkey, HB, L, nb_p if isp else nb_s)
            else:
                for j in range(4):
                    P.pool("tensor_copy", R=[(gkey, j)], W=[(gkey, j)], out=gb[j][:, :, 0:HB], in_=gb[j][:, :, L:L + HB])
            sqs = []
            for j in range(4):
                sq, sqk = tmp(E)
                P.act("activation", R=[("bconv", j)], W=[sqk], out=sq[:, 0:T], in_=bconv[j][:, 0:T], func=AF.Square)
                sqs.append((sq, sqk))
            for j in range(4):
                P.pe("matmul", R=[("bconv", j), "onesf"], W=[PS(6)], out=psf[6][:, 0:T], lhsT=onesf,
                     rhs=bconv[j][:, 0:T], start=(j == 0), stop=(j == 3))
            for j in range(4):
                sq, sqk = sqs[j]
                P.pe("matmul", R=[sqk, "onesf"], W=[PS(7)], out=psf[7][:, 0:T], lhsT=onesf, rhs=sq[:, 0:T],
                     start=(j == 0), stop=(j == 3))
            P.act("activation", R=[PS(6)], W=["negmean"], out=negmean[:, 0:T], in_=psf[6][:, 0:T], func=AF.Copy,
                  scale=-1.0 / DB)
            P.dve("tensor_tensor", R=["negmean"], W=["rstdB"], out=rstdB[:, 0:T], in0=negmean[:, 0:T],
                  in1=negmean[:, 0:T], op=ALU.mult)
            P.dve("scalar_tensor_tensor", R=[PS(7), "rstdB"], W=["rstdB"], out=rstdB[:, 0:T], in0=psf[7][:, 0:T],
                  scalar=1.0 / DB, in1=rstdB[:, 0:T], op0=ALU.mult, op1=ALU.subtract)
            P.act("activation", R=["rstdB"], W=["rstdB"], out=rstdB[:, 0:T], in_=rstdB[:, 0:T], func=AF.Ln, bias=EPS)
            P.act("activation", R=["rstdB"], W=["rstdB"], out=rstdB[:, 0:T], in_=rstdB[:, 0:T], func=AF.Exp, scale=-0.5)
            for pair in ((0, 1), (2, 3)):
                yvs = {}
                for j in pair:
                    yvs[j] = tmp(E)
                    P.dve("tensor_tensor", R=[("bconv", j), "negmean"], W=[yvs[j][1]], out=yvs[j][0][:, 0:T],
                          in0=bconv[j][:, 0:T], in1=negmean[:, 0:T], op=ALU.add)
                for j in pair:
                    P.dve("tensor_tensor", R=[yvs[j][1], "rstdB"], W=[yvs[j][1]], out=yvs[j][0][:, 0:T],
                          in0=yvs[j][0][:, 0:T], in1=rstdB[:, 0:T], op=ALU.mult)
                sgs = sigmoid_multi(E, [(yvs[j][0][:, 0:T], yvs[j][1], lncol[:, j, 2:3], lncol[:, j, 3:4], ["lncol"])
                                        for j in pair], T)
                for k_, j in enumerate(pair):
                    sgy, sgyk = sgs[k_]
                    P.dve("tensor_scalar", R=[yvs[j][1], "lncol", sgyk], W=[yvs[j][1]], out=yvs[j][0][:, 0:T],
                          in0=yvs[j][0][:, 0:T], scalar1=lncol[:, j, 0:1], scalar2=lncol[:, j, 1:2], op0=ALU.mult,
                          op1=ALU.add)
                for k_, j in enumerate(pair):
                    sgy, sgyk = sgs[k_]
                    P.dve("tensor_tensor", R=[yvs[j][1], sgyk], W=[yvs[j][1]], out=yvs[j][0][:, 0:T],
                          in0=yvs[j][0][:, 0:T], in1=sgy[:, 0:T], op=ALU.mult)
                for k_, j in enumerate(pair):
                    P.dve("tensor_tensor", R=[yvs[j][1]] + CATK, W=CATK, out=E.catT[:, 4 + j, 0:T],
                          in0=yvs[j][0][:, 0:T], in1=E.catT[:, 4 + j, 0:T], op=ALU.mult)

            if ti == 0 and len(tiles) > 1:
                p1_load(1)
            if ti + 1 < len(tiles):
                rmsnorm_to_T(E, tiles[ti + 1], E.hbuf[(ti + 1) % 2], ("hbuf", (ti + 1) % 2))
            else:
                tail(E, t, hb, hkey, pb, pkey)
                P.dma(R=[(hkey, s_) for s_ in range(4)], W=[("hA", ti)],
                      out=hA[t.row0:t.row0 + T, :].rearrange("(n p) d -> p n d", p=128),
                      in_=hb[:, 0:NT, :], is_output=cfg.debug)
        P.barrier()
        AR.off = base_off


    if 2 in cfg.passes:
        CAP = max(S, PL)
        E = common_bufs(n_h=1, n_tmp=4)
        E.w_big = AR.alloc([128, 8, 7 * DB], BF16)
        kT = AR.alloc([128, 4, CAP], BF16)
        Vc = AR.alloc([128, CAP // 128, DB], BF16)
        qT = AR.alloc([128, 4, 512], BF16)
        cvn = AR.alloc([128, 4, DB], BF16)
        kst = [AR.alloc([128, DB]) for _ in range(3)]
        kst_i = [0]
        kbf = [AR.alloc([128, DB], BF16) for _ in range(2)]
        spb = [AR.alloc([128, 512], BF16) for _ in range(4)]
        abf = [AR.alloc([128, 512], BF16) for _ in range(4)]
        rr = [0, 0, 0]
        sacc = [AR.alloc([128, 512], BF16) for _ in range(2)]
        g_bc = AR.alloc([128, DB])
        b_bc = AR.alloc([128, DB])
        WT = AR.alloc([128, 4, 128], BF16)
        WTs = AR.alloc([128, 4, 128], BF16)
        negU = AR.alloc([128, 128], BF16)
        negOnes = AR.alloc([128, 128], BF16)
        ones_row = AR.alloc([1, 128])
        cb_row = AR.alloc([1, 4, 128])
        cb_row_s = AR.alloc([1, 4, 128])
        lnst = AR.alloc([128, 8])
        kT_new = AR.alloc([128, 4, 128], BF16)
        if CAP - PL >= 2048:
            Vn = [kT[0:32, q, PL:PL + DB] for q in range(4)]
            dzs = kT[:, 0, PL + DB:PL + DB + 1024].bitcast(F32).rearrange("p (a b) -> p a b", a=4)
        else:
            Vn = [AR.alloc([32, DB], BF16) for _ in range(4)]
            dzs = AR.alloc([128, 4, 128])
        print("P2 SBUF bytes/partition:", AR.off * 4)

        def stage():
            i = kst_i[0]
            kst_i[0] = (i + 1) % len(kst)
            return kst[i], ("kst", i)

        load_weight(E, E.w_big, "w_big", cd_w_in, 8, 7 * DB, gain_i=1)
        P.dma(W=["g_bc"], out=g_bc, in_=c_ln_g.partition_broadcast(128))
        P.dma(W=["b_bc"], out=b_bc, in_=c_ln_b.partition_broadcast(128))
        P.dma(W=["cb_row"], out=cb_row, in_=c_b.rearrange("(o h) t -> o h t", o=1))
        for q in range(NS):
            P.dma(W=["cb_row_s"], out=cb_row_s[:, :, q * DS:(q + 1) * DS],
                  in_=c_b[:, 0:DS].rearrange("(o h) t -> o h t", o=1))
        P.pool("memset", W=["ones_row"], ap=ones_row, constant=1.0)
        P.pool("memset", W=["negOnes"], ap=negOnes, constant=-1.0)
        tU, tUk = tmp(E)
        P.pool("memset", W=[tUk], ap=tU[:, 0:128], constant=-1.0)
        P.pool("affine_select", R=[tUk], W=[tUk], out=tU[:, 0:128], in_=tU[:, 0:128], pattern=[[-1, 128]],
               compare_op=ALU.is_ge, fill=0.0, base=0, channel_multiplier=1)
        P.dve("tensor_copy", R=[tUk], W=["negU"], out=negU, in_=tU[:, 0:128])
        for variant, dstW, dk in ((0, WT, "WT"), (1, WTs, "WTs")):
            for hh in range(4):
                wt_, wk = tmp(E)
                if variant == 0:
                    P.dma(W=[wk], out=wt_[:, 0:128], in_=c_ws[hh])
                else:
                    P.pool("memset", W=[wk], ap=wt_[:, 0:128], constant=0.0)
                    for q in range(NS):
                        P.dma(W=[wk], out=wt_[q * DS:(q + 1) * DS, q * DS:(q + 1) * DS], in_=c_ws[hh, 0:DS, 0:DS])
                P.pool("affine_select", R=[wk], W=[wk], out=wt_[:, 0:128], in_=wt_[:, 0:128], pattern=[[-1, 128]],
                       compare_op=ALU.is_ge, fill=0.0, base=0, channel_multiplier=1)
                P.act("activation", R=[wk], W=[("xnb", 0)], out=E.xnb[0][:, 0:128], in_=wt_[:, 0:128], func=AF.Copy)
                b = bank()
                P.pe("transpose", R=[("xnb", 0), "identb"], W=[PS(b)], out=psb[b][:, 0:128], in_=E.xnb[0][:, 0:128],
                     identity=identb)
                P.dve("tensor_copy", R=[PS(b)], W=[dk], out=dstW[:, hh, :], in_=psb[b][:, 0:128])

        def tri_mask(buf, key, nk, c0):
            P.pool("affine_select", R=[key], W=[key], out=buf[0:nk, c0:c0 + nk], in_=buf[0:nk, c0:c0 + nk],
                   pattern=[[1, nk]], compare_op=ALU.is_gt, fill=0.0, base=0, channel_multiplier=-1)

        def attention(hp, qa, qb, blocks):
            for h in range(2):
                P.pool("memset", W=[("sacc", h)], ap=sacc[h][:, qa:qb], constant=0.0)
            nb = len(blocks)
            items = [(bi, h) for bi in range(nb) for h in range(2)]
            n = len(items)
            st = {}

            def Sz(i):
                bi, h = items[i]
                kTa, Va, nk, c0, tri = blocks[bi]
                r0 = 64 * h
                bz = rr[2] % 6
                rr[2] += 1
                P.pe("matmul", R=["kT", "qT"], W=[PS(bz)], out=psf[bz][0:nk, c0:qb], lhsT=kTa[r0:r0 + 64, 0:nk],
                     rhs=qT[r0:r0 + 64, hp, c0:qb], start=True, stop=True)
                st[i] = {"bz": bz}

            def Se(i):
                bi, h = items[i]
                kTa, Va, nk, c0, tri = blocks[bi]
                bz = st[i]["bz"]
                e_, ek = tmp(E)
                P.act("activation", R=[PS(bz)], W=[ek], out=e_[0:nk, c0:qb], in_=psf[bz][0:nk, c0:qb], func=AF.Exp)
                st[i]["e"] = (e_, ek)

            def Sl(i):
                bi, h = items[i]
                kTa, Va, nk, c0, tri = blocks[bi]
                e_, ek = st[i]["e"]
                si = rr[0] % 4
                rr[0] += 1
                sp_, spk = spb[si], ("spb", si)
                P.act("activation", R=[ek], W=[spk], out=sp_[0:nk, c0:qb], in_=e_[0:nk, c0:qb], func=AF.Ln, bias=1.0)
                if tri:
                    tri_mask(sp_, spk, nk, c0)
                st[i]["sp"] = (sp_, spk)

            def Sw(i):
                bi, h = items[i]
                kTa, Va, nk, c0, tri = blocks[bi]
                bz = st[i]["bz"]
                sp_, spk = st[i]["sp"]
                P.pe("matmul", R=[spk, "negU"], W=[PS(bz)], out=psf[bz][0:nk, c0:qb], lhsT=negU[0:nk, 0:nk],
                     rhs=sp_[0:nk, c0:qb], start=False, stop=(bi == 0), skip_group_check=True)
                if bi > 0:
                    P.pe("matmul", R=[("sacc", h), "negOnes"], W=[PS(bz)], out=psf[bz][0:nk, c0:qb],
                         lhsT=negOnes[:, 0:nk], rhs=sacc[h][:, c0:qb], start=False, stop=True, skip_group_check=True)
                if bi < nb - 1:
                    P.dve("tensor_tensor", R=[spk, ("sacc", h)], W=[("sacc", h)], out=sacc[h][0:nk, c0:qb],
                          in0=sacc[h][0:nk, c0:qb], in1=sp_[0:nk, c0:qb], op=ALU.add)

            def Sa(i):
                bi, h = items[i]
                kTa, Va, nk, c0, tri = blocks[bi]
                bz = st[i]["bz"]
                ai = rr[1] % 4
                rr[1] += 1
                a_, ak = abf[ai], ("abf", ai)
                P.act("activation", R=[PS(bz)], W=[ak], out=a_[0:nk, c0:qb], in_=psf[bz][0:nk, c0:qb], func=AF.Exp)
                if tri:
                    tri_mask(a_, ak, nk, c0)
                if c0 > qa:
                    P.pool("memset", W=[ak], ap=a_[0:nk, qa:c0], constant=0.0)
                st[i]["a"] = (a_, ak)

            def Sv(i):
                bi, h = items[i]
                kTa, Va, nk, c0, tri = blocks[bi]
                a_, ak = st.pop(i)["a"]
                P.pe("matmul", R=[ak, "Vc"], W=[PS(6 + h)], out=psf[6 + h][:, qa:qb], lhsT=Va[0:nk, :],
                     rhs=a_[0:nk, qa:qb], start=(bi == 0), stop=(bi == nb - 1))

            for k in range(-3, n):
                if 0 <= k + 3 < n:
                    Sz(k + 3)
                if 0 <= k + 2 < n:
                    Se(k + 2)
                if 0 <= k + 1 < n:
                    Sl(k + 1)
                    Sw(k + 1)
                if 0 <= k < n:
                    Sa(k)
                    Sv(k)

        def attention_prompt(T, kb0, NT):
            nkb = kb0 + NT
            blk = []
            for kb in range(nkb - 1, -1, -1):
                i = kb - kb0
                blk.append((kb, 128, max(i, 0) * 128, i >= 0))
            nb = len(blk)
            items = [(hp, bi, h) for hp in range(4) for bi in range(nb) for h in range(2)]
            n = len(items)
            st = {}
            qa, qb = 0, T

            def obank(hp, h):
                return 4 + 2 * (hp % 2) + h

            def Sz(i):
                hp, bi, h = items[i]
                kb, nk, c0, tri = blk[bi]
                r0 = 64 * h
                bz = rr[2] % 4
                rr[2] += 1
                P.pe("matmul", R=["kT", "qT"], W=[PS(bz)], out=psf[bz][0:nk, c0:qb],
                     lhsT=kT[r0:r0 + 64, hp, kb * 128:kb * 128 + nk], rhs=qT[r0:r0 + 64, hp, c0:qb], start=True, stop=True)
                st[i] = {"bz": bz}

            def Se(i):
                hp, bi, h = items[i]
                kb, nk, c0, tri = blk[bi]
                bz = st[i]["bz"]
                e_, ek = tmp(E)
                P.act("activation", R=[PS(bz)], W=[ek], out=e_[0:nk, c0:qb], in_=psf[bz][0:nk, c0:qb], func=AF.Exp)
                st[i]["e"] = (e_, ek)

            def Sl(i):
                hp, bi, h = items[i]
                kb, nk, c0, tri = blk[bi]
                e_, ek = st[i]["e"]
                si = rr[0] % 4
                rr[0] += 1
                sp_, spk = spb[si], ("spb", si)
                P.act("activation", R=[ek], W=[spk], out=sp_[0:nk, c0:qb], in_=e_[0:nk, c0:qb], func=AF.Ln, bias=1.0)
                if tri:
                    tri_mask(sp_, spk, nk, c0)
                st[i]["sp"] = (sp_, spk)

            def Sw(i):
                hp, bi, h = items[i]
                kb, nk, c0, tri = blk[bi]
                bz = st[i]["bz"]
                sp_, spk = st[i]["sp"]
                if bi == 0 and h == 0:
                    for h2 in range(2):
                        P.pool("memset", W=[("sacc", h2)], ap=sacc[h2][:, qa:qb], constant=0.0)
                P.pe("matmul", R=[spk, "negU"], W=[PS(bz)], out=psf[bz][0:nk, c0:qb], lhsT=negU[0:nk, 0:nk],
                     rhs=sp_[0:nk, c0:qb], start=False, stop=(bi == 0), skip_group_check=True)
                if bi > 0:
                    P.pe("matmul", R=[("sacc", h), "negOnes"], W=[PS(bz)], out=psf[bz][0:nk, c0:qb],
                         lhsT=negOnes[:, 0:nk], rhs=sacc[h][:, c0:qb], start=False, stop=True, skip_group_check=True)
                if bi < nb - 1:
                    P.dve("tensor_tensor", R=[spk, ("sacc", h)], W=[("sacc", h)], out=sacc[h][0:nk, c0:qb],
                          in0=sacc[h][0:nk, c0:qb], in1=sp_[0:nk, c0:qb], op=ALU.add)

            def Sa(i):
                hp, bi, h = items[i]
                kb, nk, c0, tri = blk[bi]
                bz = st[i]["bz"]
                ai = rr[1] % 4
                rr[1] += 1
                a_, ak = abf[ai], ("abf", ai)
                P.act("activation", R=[PS(bz)], W=[ak], out=a_[0:nk, c0:qb], in_=psf[bz][0:nk, c0:qb], func=AF.Exp)
                if tri:
                    tri_mask(a_, ak, nk, c0)
                if c0 > qa:
                    P.pool("memset", W=[ak], ap=a_[0:nk, qa:c0], constant=0.0)
                st[i]["a"] = (a_, ak)

            def Sv(i):
                hp, bi, h = items[i]
                kb, nk, c0, tri = blk[bi]
                a_, ak = st.pop(i)["a"]
                ob = obank(hp, h)
                P.pe("matmul", R=[ak, "Vc"], W=[PS(ob)], out=psf[ob][:, qa:qb], lhsT=Vc[0:nk, kb, hp * 128:(hp + 1) * 128],
                     rhs=a_[0:nk, qa:qb], start=(bi == 0), stop=(bi == nb - 1))
                if bi == nb - 1:
                    r0 = 64 * h
                    P.dve("tensor_tensor", R=[PS(ob), "catT"], W=["catT"], out=E.catT[r0:r0 + 64, 4 + hp, qa:qb],
                          in0=psf[ob][r0:r0 + 64, qa:qb], in1=E.catT[r0:r0 + 64, 4 + hp, qa:qb], op=ALU.mult)

            for k in range(-3, n):
                if 0 <= k + 3 < n:
                    Sz(k + 3)
                if 0 <= k + 2 < n:
                    Se(k + 2)
                if 0 <= k + 1 < n:
                    Sl(k + 1)
                    Sw(k + 1)
                if 0 <= k < n:
                    Sa(k)
                    Sv(k)

        def p2_load(ti):
            t = tiles[ti]
            P.dma(W=[(("hbuf", 0), s_) for s_ in range(4)], out=E.hbuf[0][:, 0:t.NT, :],
                  in_=hA[t.row0:t.row0 + t.T, :].rearrange("(n p) d -> p n d", p=128), R=[("hA", ti)])

        p2_load(0)
        for ti, t in enumerate(tiles):
            if cfg.stop == 1:
                break
            T, NT, nseg, L = t.T, t.NT, t.nseg, t.L
            isp = t.kind == "p"
            hb, hkey = E.hbuf[0], ("hbuf", 0)
            if ti == 0:
                rmsnorm_to_T(E, t, hb, hkey)
                p2_load(1)
            kb0 = t.t0 // 128

            for s in range(NT):
                sl = slice(s * 128, (s + 1) * 128)
                rows = slice(t.row0 + s * 128, t.row0 + (s + 1) * 128)

                def proj_tm(c0):
                    b = bank()
                    for kc in range(8):
                        P.pe("matmul", R=[("actT", s), "w_big"], W=[PS(b)], out=psf[b], lhsT=E.actT[:, kc, sl],
                             rhs=E.w_big[:, kc, c0:c0 + DB], start=(kc == 0), stop=(kc == 7))
                    return b

                if cfg.stop == 21:
                    continue
                b = proj_tm(DB)
                cf, cfk = tmp(E)
                P.pool("memset", W=["lnst"], ap=lnst, constant=0.0)
                P.act("activation", R=[PS(b)], W=[cfk, "lnst"], out=cf, in_=psf[b], func=AF.Copy, accum_out=lnst[:, 0:1])
                jk, jkk = tmp(E)
                P.act("activation", R=[PS(b)], W=[jkk, "lnst"], out=jk, in_=psf[b], func=AF.Square,
                      accum_out=lnst[:, 1:2])
                P.dve("tensor_scalar", R=["lnst"], W=["lnst"], out=lnst[:, 2:3], in0=lnst[:, 0:1], scalar1=1.0 / DB,
                      scalar2=None, op0=ALU.mult)
                P.dve("tensor_tensor", R=["lnst"], W=["lnst"], out=lnst[:, 3:4], in0=lnst[:, 2:3], in1=lnst[:, 2:3],
                      op=ALU.mult)
                P.dve("scalar_tensor_tensor", R=["lnst"], W=["lnst"], out=lnst[:, 4:5], in0=lnst[:, 1:2],
                      scalar=1.0 / DB, in1=lnst[:, 3:4], op0=ALU.mult, op1=ALU.subtract)
                P.act("activation", R=["lnst"], W=["lnst"], out=lnst[:, 5:6], in_=lnst[:, 4:5], func=AF.Ln, bias=EPS)
                P.act("activation", R=["lnst"], W=["lnst"], out=lnst[:, 5:6], in_=lnst[:, 5:6], func=AF.Exp, scale=-0.5)
                P.dve("tensor_scalar", R=[cfk, "lnst"], W=[cfk], out=cf, in0=cf, scalar1=lnst[:, 2:3],
                      scalar2=lnst[:, 5:6], op0=ALU.subtract, op1=ALU.mult)
                P.dve("tensor_tensor", R=[cfk, "g_bc"], W=[cfk], out=cf, in0=cf, in1=g_bc, op=ALU.mult)
                if isp:
                    P.dve("tensor_tensor", R=[cfk, "b_bc"], W=["cvn"], out=cvn[:, s, :], in0=cf, in1=b_bc, op=ALU.add)
                else:
                    P.dve("tensor_tensor", R=[cfk, "b_bc"], W=[cfk], out=cf, in0=cf, in1=b_bc, op=ALU.add)
                    P.act("activation", R=[cfk], W=["cvn"], out=cvn[:, s, :], in_=cf, func=AF.Copy)
                    P.dma(R=[cfk], out=ncv_s, in_=cf, is_output=True)
                if cfg.stop == 22:
                    continue
                b = proj_tm(4 * DB if cfg.stop != 25 else 5 * DB)
                st_, stk = stage()
                P.act("activation", R=[PS(b)], W=[stk], out=st_, in_=psf[b], func=AF.Copy)
                P.dma(R=[stk], out=((nk_p if cfg.stop != 25 else nv_p)[rows, :] if isp else (nk_s if cfg.stop != 25 else nv_s)), in_=st_, is_output=True)
                kb_ = kbf[s % 2]
                P.dve("tensor_copy", R=[PS(b)], W=[("kbf", s % 2)], out=kb_, in_=psf[b])
                if isp:
                    transpose_into(kb_, ("kbf", s % 2), 4, kT, "kT", kb0 + s)
                else:
                    transpose_into(kb_, ("kbf", s % 2), 4, kT_new, "kT_new", 0)
                if cfg.stop in (23, 25):
                    continue
                b = proj_tm(5 * DB)
                st_, stk = stage()
                P.act("activation", R=[PS(b)], W=[stk], out=st_, in_=psf[b], func=AF.Copy)
                P.dma(R=[stk], out=(nv_p[rows, :] if isp else nv_s), in_=st_, is_output=True)
                if cfg.stop == 26 or (cfg.stop == 27 and isp) or (cfg.stop == 28 and not isp):
                    continue
                if isp:
                    P.act("activation", R=[PS(b)], W=["Vc"], out=Vc[:, kb0 + s, :], in_=psf[b], func=AF.Copy)
                else:
                    P.act("activation", R=[PS(b)], W=[("kbf", 1)], out=kbf[1], in_=psf[b], func=AF.Copy)
                    for q in range(NS if cfg.stop not in (24, 27, 28) else 0):
                        P.dma(R=[("kbf", 1)], W=[("Vn", q)], out=Vn[q][0:DS, :], in_=kbf[1][q * DS:(q + 1) * DS, :])

            if cfg.stop in (2, 21, 22, 23, 24, 25, 26, 27, 28):
                continue
            Wm, Wk = (WT, "WT") if isp else (WTs, "WTs")
            cbr, cbk = (cb_row, "cb_row") if isp else (cb_row_s, "cb_row_s")
            for hh in range(4):
                bm = bank()
                for s in range(NT):
                    sl = slice(s * 128, (s + 1) * 128)
                    P.pe("matmul", R=["cvn", Wk], W=[PS(bm)], out=psf[bm][:, sl], lhsT=cvn[:, s, hh * 128:(hh + 1) * 128],
                         rhs=Wm[:, hh, :], start=True, stop=False)
                    P.pe("matmul", R=["ones_row", cbk], W=[PS(bm)], out=psf[bm][:, sl], lhsT=ones_row[0:1, :],
                         rhs=cbr[0:1, hh, :], start=False, stop=True)
                bu, bzc = bank(), bank()
                proj_fm(E, bu, E.w_big, 0 + hh, T)
                proj_fm(E, bzc, E.w_big, 8 + hh, T)
                sg, sgk = sigmoid_from(E, psf[bzc][:, 0:T], PS(bzc), T)
                P.dve("tensor_tensor", R=[PS(bzc), sgk], W=[sgk], out=sg[:, 0:T], in0=psf[bzc][:, 0:T], in1=sg[:, 0:T],
                      op=ALU.mult)
                P.dve("tensor_tensor", R=[PS(bm), sgk], W=[sgk], out=sg[:, 0:T], in0=psf[bm][:, 0:T], in1=sg[:, 0:T],
                      op=ALU.mult)
                P.dve("tensor_tensor", R=[PS(bu), sgk], W=["catT"], out=E.catT[:, hh, 0:T], in0=psf[bu][:, 0:T],
                      in1=sg[:, 0:T], op=ALU.mult)

            if cfg.stop == 3:
                continue
            for hp in range(4):
                bq = bank()
                proj_fm(E, bq, E.w_big, 12 + hp, T)
                P.act("activation", R=[PS(bq)], W=["qT"], out=qT[:, hp, 0:T], in_=psf[bq][:, 0:T], func=AF.Copy,
                      scale=0.125)

            def silu_dz(hp, dst, dkey):
                bd = bank()
                proj_fm(E, bd, E.w_big, 24 + hp, T)
                sg, sgk = sigmoid_from(E, psf[bd][:, 0:T], PS(bd), T)
                P.dve("tensor_tensor", R=[PS(bd), sgk], W=[dkey], out=dst, in0=psf[bd][:, 0:T], in1=sg[:, 0:T],
                      op=ALU.mult)

            def finalize(hp, qa, qb, m1, m1k):
                for h in range(2):
                    r0 = 64 * h
                    P.dve("tensor_tensor", R=[PS(6 + h), m1k], W=["catT"], out=E.catT[r0:r0 + 64, 4 + hp, qa:qb],
                          in0=psf[6 + h][r0:r0 + 64, qa:qb], in1=m1[r0:r0 + 64, qa:qb], op=ALU.mult)

            if cfg.stop == 4 or (cfg.stop == 5 and not isp):
                continue
            if isp:
                for hp in range(4):
                    silu_dz(hp, E.catT[:, 4 + hp, 0:T], "catT")
                rmsnorm_to_T(E, tiles[ti + 1], hb, hkey)
                if ti + 2 < len(tiles):
                    p2_load(ti + 2)
                attention_prompt(T, kb0, NT)
            else:
                for hp in range(4):
                    silu_dz(hp, dzs[:, hp, :], "dzs")
                npast = PL // 128
                for q in range(NS):
                    for kb in range(npast):
                        st_, stk = stage()
                        P.dma(W=[stk], out=st_, in_=ck[q, kb * 128:(kb + 1) * 128, :])
                        P.act("activation", R=[stk], W=[("kbf", kb % 2)], out=kbf[kb % 2], in_=st_, func=AF.Copy)
                        transpose_into(kbf[kb % 2], ("kbf", kb % 2), 4, kT, "kT", kb)
                        st_, stk = stage()
                        P.dma(W=[stk], out=st_, in_=cv[q, kb * 128:(kb + 1) * 128, :])
                        P.dve("tensor_copy", R=[stk], W=["Vc"], out=Vc[:, kb, :], in_=st_)
                    qa, qb = q * DS, (q + 1) * DS
                    for hp in range(4):
                        blocks = [(kT_new[:, hp, qa:qb], Vn[q][0:DS, hp * 128:(hp + 1) * 128], DS, qa, True)]
                        for kb in range(npast - 1, -1, -1):
                            blocks.append((kT[:, hp, kb * 128:(kb + 1) * 128], Vc[:, kb, hp * 128:(hp + 1) * 128], 128,
                                           qa, False))
                        attention(hp, qa, qb, blocks)
                        finalize(hp, qa, qb, dzs[:, hp, :], "dzs")
            P.dma(R=["catT"], W=[("catD", ti)], out=catD[:, t.row0:t.row0 + T].rearrange("(c p) t -> p c t", p=128),
                  in_=E.catT[:, :, 0:T], is_output=cfg.debug)
        P.barrier()
        AR.off = base_off

    if 3 in cfg.passes:
        E = common_bufs(n_h=2, n_tmp=6)
        E.catT2 = [E.catT, AR.alloc([128, 8, 512], BF16)]
        E.pbuf = [AR.alloc([128, 4, PLE]) for _ in range(2)]
        E.pT = AR.alloc([128, 2, 512], BF16)
        E.xnp = [AR.alloc([128, PLE], BF16) for _ in range(2)]
        E.w_out = AR.alloc([128, 8, D], BF16)
        E.w_gate = AR.alloc([128, 8, D], BF16)
        E.w_proj = AR.alloc([128, 2, D], BF16)
        fg_bc = AR.alloc([128, D])
        print("P3 SBUF bytes/partition:", AR.off * 4)
        load_weight(E, E.w_out, "w_out", cd_w_out, 8, D)
        load_weight(E, E.w_gate, "w_gate", ple_gate[1], 8, D)
        load_weight(E, E.w_proj, "w_proj", ple_proj[1], 2, D)
        P.dma(W=["fg_bc"], out=fg_bc, in_=final_g.partition_broadcast(128))

        def p3_load(ti):
            t = tiles[ti]
            par = ti % 2
            P.dma(W=[(("hbuf", par), s_) for s_ in range(4)], R=[("hA", ti)], out=E.hbuf[par][:, 0:t.NT, :],
                  in_=hA[t.row0:t.row0 + t.T, :].rearrange("(n p) d -> p n d", p=128))
            P.dma(W=[("pbuf", par)], out=E.pbuf[par][:, 0:t.NT, :],
                  in_=p_rows(t, 1).rearrange("(n p) d -> p n d", p=128))
            P.dma(W=[(("catT", par), s_) for s_ in range(4)], R=[("catD", ti)], out=E.catT2[par][:, :, 0:t.T],
                  in_=catD[:, t.row0:t.row0 + t.T].rearrange("(c p) t -> p c t", p=128))

        p3_load(0)
        for ti, t in enumerate(tiles):
            T, NT = t.T, t.NT
            par = ti % 2
            hb, hkey = E.hbuf[par], ("hbuf", par)
            pb, pkey = E.pbuf[par], ("pbuf", par)
            if ti + 1 < len(tiles):
                p3_load(ti + 1)
            E.catT = E.catT2[par]
            tail(E, t, hb, hkey, pb, pkey, catkey=("catT", par))
            P.pool("memset", W=["ss"], ap=E.ss, constant=0.0)
            for s in range(NT):
                P.act("activation", R=[(hkey, s)], W=[("xnb", s % 2), "ss"], out=E.xnb[s % 2], in_=hb[:, s, :],
                      func=AF.Square, accum_out=E.ss[:, s:s + 1])
            P.act("activation", R=["ss"], W=["rstd"], out=E.rstd[:, 0:NT], in_=E.ss[:, 0:NT], func=AF.Ln,
                  scale=1.0 / D, bias=EPS)
            P.act("activation", R=["rstd"], W=["rstd"], out=E.rstd[:, 0:NT], in_=E.rstd[:, 0:NT], func=AF.Exp,
                  scale=-0.5)
            for s in range(NT):
                P.dve("scalar_tensor_tensor", R=[(hkey, s), "rstd", "fg_bc"], W=[(hkey, s)], out=hb[:, s, :], in0=hb[:, s, :],
                      scalar=E.rstd[:, s:s + 1], in1=fg_bc, op0=ALU.mult, op1=ALU.mult)
            dst = y_p[t.row0:t.row0 + T, :] if t.kind == "p" else y_s
            P.dma(R=[(hkey, s_) for s_ in range(4)], out=dst.rearrange("(n p) d -> p n d", p=128), in_=hb[:, 0:NT, :],
                  is_output=True)
        P.barrier()
        AR.off = base_off

    P.emit()
    global LAST_PROG
    LAST_PROG = P
    return nc


LAST_PROG = None


def shard_inputs(inp, cfg, n_cores):
    NP, S, NS, DS, PL = cfg.NP, cfg.S, cfg.NS, cfg.DS, cfg.PL
    f = lambda a: np.ascontiguousarray(np.asarray(a, dtype=np.float32))
    maps = []
    for c in range(n_cores):
        ps = slice(c * NP, (c + 1) * NP)
        ss = slice(c * NS, (c + 1) * NS)
        m = {
            "x_prompt": f(inp["x_prompt"][ps]).reshape(NP * S, D),
            "x_sample": f(inp["x_sample"][ss]).reshape(NS * DS, D),
            "state_a_conv": f(inp["state_a_conv"][0, ss]),
            "state_b_conv": f(inp["state_b_conv"][0, ss]),
            "cache_d_k": f(inp["cache_d_k"][0, ss]).reshape(NS, PL, DB),
            "cache_d_v": f(inp["cache_d_v"][0, ss]).reshape(NS, PL, DB),
            "p_prompt": f(inp["p_prompt"][:, ps]).reshape(2, NP * S, PLE),
            "p_sample": f(inp["p_sample"][:, ss]).reshape(2, NS * DS, PLE),
            "norm_g": f(inp["norm_g"]),
            "ple_gate": f(inp["ple_gate"]),
            "ple_proj": f(inp["ple_proj"]),
            "ab_w_in": f(inp["ab_w_in"][0]),
            "a_conv_w": f(inp["a_conv_w"][0]),
            "b_conv_w": f(inp["b_conv_w"][0]),
            "b_ln_g": f(inp["b_ln_g"][0]),
            "b_ln_b": f(inp["b_ln_b"][0]),
            "ab_w_out": f(inp["ab_w_out"][0]),
            "cd_w_in": f(inp["cd_w_in"][0]),
            "c_ln_g": f(inp["c_ln_g"][0]),
            "c_ln_b": f(inp["c_ln_b"][0]),
            "c_ws": f(inp["c_ws"][0]),
            "c_b": f(inp["c_b"][0]),
            "cd_w_out": f(inp["cd_w_out"][0]),
            "final_g": f(inp["final_g"]),
        }
        maps.append(m)
    return maps


def kernel(**inputs):
    n_cores = 8
    cfg = Cfg(NP=2, S=4096, NS=4, DS=32, PL=2048)
    nc = build_program(cfg)
    in_maps = shard_inputs(inputs, cfg, n_cores)
    res = run_bass_kernel_spmd(nc, in_maps, core_ids=list(range(n_cores)))
    r = res.results
    NP, S, NS, DS = cfg.NP, cfg.S, cfg.NS, cfg.DS
    cat = lambda name, shp: np.concatenate([np.asarray(r[c][name], dtype=np.float32).reshape(shp) for c in range(n_cores)], axis=0)
    y_prompt = cat("y_prompt", (NP, S, D))
    y_sample = cat("y_sample", (NS, DS, D))
    na_p = cat("new_a_prompt", (NP, HA, DB))[None]
    na_s = cat("new_a_sample", (NS, HA, DB))[None]
    nb_p = cat("new_b_prompt", (NP, HB, DB))[None]
    nb_s = cat("new_b_sample", (NS, HB, DB))[None]
    ncv = cat("new_cv_sample", (NS, DS, DB))[None]
    nk_p = cat("new_k_prompt", (NP, S, 8, 64))[None]
    nv_p = cat("new_v_prompt", (NP, S, 8, 64))[None]
    nk_s = cat("new_k_sample", (NS, DS, 8, 64))[None]
    nv_s = cat("new_v_sample", (NS, DS, 8, 64))[None]
    return (y_prompt, y_sample, na_p, na_s, nb_p, nb_s, ncv, nk_p, nv_p, nk_s, nv_s)
```

```python
import contextlib
import numpy as np
import concourse.bass as bass
import concourse.mybir as mybir
from concourse.bass_utils import run_bass_kernel_spmd

F32 = mybir.dt.float32
BF16 = mybir.dt.bfloat16
AF = mybir.ActivationFunctionType
ALU = mybir.AluOpType

ENGINES = ["pe", "act", "dve", "pool", "sp"]
D = 1024
DB = 512
PLE = 256
EPS = 1e-6
HB = 30
HA = 2


class Prog:
    EPOCH = 12000

    def __init__(self, nc, n_dma_slots=48):
        self.nc = nc
        self.ops = {e: [] for e in ENGINES}
        self.cnt = {e: 0 for e in ENGINES}
        self.last_w = {}
        self.readers = {}
        self.seen = {e: {} for e in ENGINES}
        self.n_dma_slots = n_dma_slots
        self.dma_next = 0
        self.dma_val = [0] * n_dma_slots
        self.out_tokens = []
        self.bank_i = 0

    def _need(self, eng, tok, waits):
        if tok is None:
            return
        if tok[0] == "E":
            _, e2, idx = tok
            if e2 == eng and eng == "pe":
                return
            k = ("E", e2)
        else:
            _, slot, idx = tok
            k = ("D", slot)
        if self.seen[eng].get(k, -1) >= idx:
            return
        waits[k] = max(waits.get(k, -1), idx)

    def op(self, eng, name, R=(), W=(), dma=False, is_output=False, **kw):
        ps_r = [k for k in R if isinstance(k, tuple) and k and k[0] == "ps" and k not in W]
        if ps_r:
            R = [k for k in R if k not in ps_r]
            W = list(W) + ps_r
        waits = {}
        for k in R:
            self._need(eng, self.last_w.get(k), waits)
        for k in W:
            self._need(eng, self.last_w.get(k), waits)
            for t in self.readers.get(k, ()):
                self._need(eng, t, waits)
        if dma:
            slot = self.dma_next
            self.dma_next = (self.dma_next + 1) % self.n_dma_slots
            prev = self.dma_val[slot]
            if prev > 0:
                self._need(eng, ("D", slot, prev), waits)
            self.dma_val[slot] = prev + 16
            tok = ("D", slot, prev + 16)
        else:
            idx = self.cnt[eng]
            self.cnt[eng] += 1
            tok = ("E", eng, idx)
        for k, v in waits.items():
            self.seen[eng][k] = v
        self.ops[eng].append((name, kw, waits, tok))
        for k in W:
            self.last_w[k] = tok
            self.readers[k] = []
        for k in R:
            if k in W:
                continue
            self.readers.setdefault(k, []).append(tok)
        if is_output:
            self.out_tokens.append(tok)
        return tok

    def pe(self, name, **kw):
        return self.op("pe", name, **kw)

    def act(self, name, **kw):
        return self.op("act", name, **kw)

    def dve(self, name, **kw):
        return self.op("dve", name, **kw)

    def pool(self, name, **kw):
        return self.op("pool", name, **kw)

    def dma(self, **kw):
        return self.op("sp", "dma_start", dma=True, **kw)

    def barrier(self):
        for eng in ENGINES:
            waits = {}
            for e2 in ENGINES:
                if e2 != eng and self.cnt[e2] > 0:
                    self._need(eng, ("E", e2, self.cnt[e2] - 1), waits)
            for slot in range(self.n_dma_slots):
                if self.dma_val[slot] > 0:
                    self._need(eng, ("D", slot, self.dma_val[slot]), waits)
            for k, v in waits.items():
                self.seen[eng][k] = v
            self.ops[eng].append((None, None, waits, None))

    def emit(self):
        nc = self.nc
        with contextlib.ExitStack() as st:
            esem = {}
            for e in ENGINES:
                n_ep = (self.cnt[e] + self.EPOCH - 1) // self.EPOCH
                esem[e] = [st.enter_context(nc.semaphore(f"s_{e}{i}")) for i in range(max(n_ep, 1))]
            dsem = [st.enter_context(nc.semaphore(f"s_d{i}")) for i in range(self.n_dma_slots)]
            block = st.enter_context(nc.Block())

            def do_wait(h, k, v):
                if k[0] == "E":
                    h.wait_ge(esem[k[1]][v // self.EPOCH], v % self.EPOCH + 1)
                else:
                    h.wait_ge(dsem[k[1]], v)

            def run(ename):
                def body(h):
                    for name, kw, waits, tok in self.ops[ename]:
                        ws = list(waits.items())
                        if name is None:
                            for k, v in ws:
                                do_wait(h, k, v)
                            continue
                        for k, v in ws[1:]:
                            do_wait(h, k, v)
                        ins = getattr(h, name)(**kw)
                        if ws:
                            k, v = ws[0]
                            if k[0] == "E":
                                ins._wait_ge(esem[k[1]][v // self.EPOCH], v % self.EPOCH + 1)
                            else:
                                ins._wait_ge(dsem[k[1]], v)
                        if tok[0] == "E":
                            ins.then_inc(esem[tok[1]][tok[2] // self.EPOCH], 1)
                        else:
                            ins.then_inc(dsem[tok[1]], 16)
                    if ename == "sp":
                        for tok in self.out_tokens:
                            do_wait(h, ("D", tok[1]), tok[2])
                return body

            block.tensor(run("pe"))
            block.scalar(run("act"))
            block.vector(run("dve"))
            block.gpsimd(run("pool"))
            block.sync(run("sp"))


class Arena:
    def __init__(self, nc, nbytes):
        self.t = nc.alloc_sbuf_tensor("arena", [128, nbytes // 4], F32)
        self.off = 0
        self.cap = nbytes // 4

    def alloc(self, shape, dt=F32):
        n = int(np.prod(shape[1:]))
        nw = (n if dt == F32 else (n + 1) // 2)
        nw = (nw + 7) // 8 * 8
        assert self.off + nw <= self.cap, f"SBUF arena overflow: need {(self.off + nw) * 4} B"
        ap = self.t[0:shape[0], self.off:self.off + nw]
        self.off += nw
        if dt != F32:
            ap = ap.bitcast(dt)
        ap = ap[:, 0:n]
        if len(shape) == 3:
            ap = ap.rearrange("p (a b) -> p a b", a=shape[1])
        elif len(shape) == 4:
            ap = ap.rearrange("p (a b c) -> p a b c", a=shape[1], b=shape[2])
        return ap


class Cfg:
    def __init__(self, NP=2, S=4096, NS=4, DS=32, PL=2048, passes=(1, 2, 3), debug=False, stop=0):
        self.NP, self.S, self.NS, self.DS, self.PL = NP, S, NS, DS, PL
        self.passes = passes
        self.debug = debug
        self.stop = stop
        assert NS * DS == 128 and S % 512 == 0 and PL % 128 == 0
        self.NTOK = NP * S + 128


class Tile:
    def __init__(self, kind, seq, t0, T, nseg, L, row0, last):
        self.kind, self.seq, self.t0, self.T, self.nseg, self.L = kind, seq, t0, T, nseg, L
        self.row0 = row0
        self.last = last
        self.NT = T // 128


def make_tiles(cfg):
    tiles = []
    for q in range(cfg.NP):
        n = cfg.S // 512
        for i in range(n):
            tiles.append(Tile("p", q, i * 512, 512, 1, 512, q * cfg.S + i * 512, i == n - 1))
    tiles.append(Tile("s", 0, 0, 128, cfg.NS, cfg.DS, cfg.NP * cfg.S, True))
    return tiles


def build_program(cfg):
    nc = bass.Bass("TRN2", target_bir_lowering=False)
    NP, S, NS, DS, PL = cfg.NP, cfg.S, cfg.NS, cfg.DS, cfg.PL
    NTOK = cfg.NTOK

    def din(name, shape):
        return nc.dram_tensor(name, list(shape), F32, kind="ExternalInput").ap()

    def dout(name, shape):
        return nc.dram_tensor(name, list(shape), F32, kind="ExternalOutput").ap()

    x_p = din("x_prompt", [NP * S, D])
    x_s = din("x_sample", [128, D])
    st_a = din("state_a_conv", [NS, HA, DB])
    st_b = din("state_b_conv", [NS, HB, DB])
    ck = din("cache_d_k", [NS, PL, DB])
    cv = din("cache_d_v", [NS, PL, DB])
    p_p = din("p_prompt", [2, NP * S, PLE])
    p_s = din("p_sample", [2, 128, PLE])
    norm_g = din("norm_g", [2, D])
    ple_gate = din("ple_gate", [2, D, D])
    ple_proj = din("ple_proj", [2, PLE, D])
    ab_w_in = din("ab_w_in", [D, 7 * DB])
    a_conv_w = din("a_conv_w", [3, DB])
    b_conv_w = din("b_conv_w", [31, DB])
    b_ln_g = din("b_ln_g", [DB])
    b_ln_b = din("b_ln_b", [DB])
    ab_w_out = din("ab_w_out", [D, D])
    cd_w_in = din("cd_w_in", [D, 7 * DB])
    c_ln_g = din("c_ln_g", [DB])
    c_ln_b = din("c_ln_b", [DB])
    c_ws = din("c_ws", [4, 128, 128])
    c_b = din("c_b", [4, 128])
    cd_w_out = din("cd_w_out", [D, D])
    final_g = din("final_g", [D])

    y_p = dout("y_prompt", [NP * S, D])
    y_s = dout("y_sample", [128, D])
    na_p = dout("new_a_prompt", [NP, HA, DB])
    na_s = dout("new_a_sample", [NS, HA, DB])
    nb_p = dout("new_b_prompt", [NP, HB, DB])
    nb_s = dout("new_b_sample", [NS, HB, DB])
    ncv_s = dout("new_cv_sample", [128, DB])
    nk_p = dout("new_k_prompt", [NP * S, DB])
    nv_p = dout("new_v_prompt", [NP * S, DB])
    nk_s = dout("new_k_sample", [128, DB])
    nv_s = dout("new_v_sample", [128, DB])

    kind_scr = "ExternalOutput" if cfg.debug else "Internal"
    hA = nc.dram_tensor("hA", [NTOK, D], F32, kind=kind_scr).ap()
    catD = nc.dram_tensor("catD", [D, NTOK], BF16, kind=kind_scr).ap()

    P = Prog(nc)
    tiles = make_tiles(cfg)
    AR = Arena(nc, 207 * 1024)

    psf = [nc.alloc_psum_tensor(f"ps{i}", [128, 512], F32)[:] for i in range(8)]
    psb = [p.bitcast(BF16) for p in psf]

    def bank():
        b = P.bank_i
        P.bank_i = (P.bank_i + 1) % 6
        return b

    def PS(b):
        return ("ps", b)

    identf = AR.alloc([128, 128])
    identb = AR.alloc([128, 128], BF16)
    onesf = AR.alloc([128, 128])
    ngcol = AR.alloc([128, 2, 8])
    P.pool("memset", W=["identf"], ap=identf, constant=0.0)
    P.pool("affine_select", R=["identf"], W=["identf"], out=identf, in_=identf, pattern=[[-1, 128]],
           compare_op=ALU.not_equal, fill=1.0, base=0, channel_multiplier=1)
    P.dve("tensor_copy", R=["identf"], W=["identb"], out=identb, in_=identf)
    P.pool("memset", W=["onesf"], ap=onesf, constant=1.0)
    import os as _os
    for _i in range(int(_os.environ.get('KDUMMY', '0'))):
        P.dve("tensor_copy", R=["identf"], W=["identb"], out=identb, in_=identf)
    for i in range(2):
        P.dma(W=["ngcol"], out=ngcol[:, i, :], in_=norm_g[i].rearrange("(kc p) -> p kc", p=128),
              allow_slow_non_contiguous=True)
    base_off = AR.off

    def x_rows(t):
        return x_p[t.row0:t.row0 + t.T, :] if t.kind == "p" else x_s

    def p_rows(t, layer):
        return p_p[layer, t.row0:t.row0 + t.T, :] if t.kind == "p" else p_s[layer]

    class Env:
        pass

    def common_bufs(n_h=2, n_tmp=6):
        E = Env()
        E.hbuf = [AR.alloc([128, 4, D]) for _ in range(n_h)]
        E.xnb = [AR.alloc([128, D], BF16) for _ in range(2)]
        E.actT = AR.alloc([128, 8, 512], BF16)
        E.catT = AR.alloc([128, 8, 512], BF16)
        E.ss = AR.alloc([128, 4])
        E.rstd = AR.alloc([128, 4])
        E.ftmp = [AR.alloc([128, 512]) for _ in range(n_tmp)]
        E.ftmp_i = 0
        E.cast_i = 0
        return E

    def tmp(E):
        i = E.ftmp_i
        E.ftmp_i = (i + 1) % len(E.ftmp)
        return E.ftmp[i], ("ftmp", i)

    def load_weight(E, dst, dkey, src, nk, ncols, gain_i=None):
        for kc in range(nk):
            for c0 in range(0, ncols, 512):
                st_, sk = tmp(E)
                P.dma(W=[sk], out=st_, in_=src[kc * 128:(kc + 1) * 128, c0:c0 + 512])
                if gain_i is not None:
                    P.act("activation", R=[sk, "ngcol"], W=[dkey], out=dst[:, kc, c0:c0 + 512], in_=st_, func=AF.Copy,
                          scale=ngcol[:, gain_i, kc:kc + 1])
                elif E.cast_i % 2 == 0:
                    P.act("activation", R=[sk], W=[dkey], out=dst[:, kc, c0:c0 + 512], in_=st_, func=AF.Copy)
                else:
                    P.dve("tensor_copy", R=[sk], W=[dkey], out=dst[:, kc, c0:c0 + 512], in_=st_)
                E.cast_i += 1

    def transpose_into(src, skey, nchunk, dstT, dkey, s):
        b = bank()
        for c in range(nchunk):
            P.pe("transpose", R=[skey, "identb"], W=[PS(b)], out=psb[b][:, c * 128:(c + 1) * 128],
                 in_=src[:, c * 128:(c + 1) * 128], identity=identb)
        P.dve("tensor_copy", R=[PS(b)], W=[dkey], out=dstT[:, 0:nchunk, s * 128:(s + 1) * 128],
              in_=psb[b][:, 0:nchunk * 128].rearrange("p (c t) -> p c t", c=nchunk))

    def rmsnorm_to_T(E, t, hb, hkey):
        NT = t.NT
        P.pool("memset", W=["ss"], ap=E.ss, constant=0.0)
        for s in range(NT):
            P.act("activation", R=[(hkey, s)], W=[("xnb", s % 2), "ss"], out=E.xnb[s % 2], in_=hb[:, s, :],
                  func=AF.Square, accum_out=E.ss[:, s:s + 1])
        P.act("activation", R=["ss"], W=["rstd"], out=E.rstd[:, 0:NT], in_=E.ss[:, 0:NT], func=AF.Ln, scale=1.0 / D,
              bias=EPS)
        P.act("activation", R=["rstd"], W=["rstd"], out=E.rstd[:, 0:NT], in_=E.rstd[:, 0:NT], func=AF.Exp, scale=-0.5)
        for s in range(NT):
            P.act("activation", R=[(hkey, s), "rstd"], W=[("xnb", s % 2)], out=E.xnb[s % 2], in_=hb[:, s, :],
                  func=AF.Copy, scale=E.rstd[:, s:s + 1])
            transpose_into(E.xnb[s % 2], ("xnb", s % 2), 8, E.actT, ("actT", s), s)

    def proj_fm(E, b, w, oc, T):
        for kc in range(8):
            P.pe("matmul", R=[("actT", s_) for s_ in range(T // 128)] + ["w_big"], W=[PS(b)], out=psf[b][:, 0:T],
                 lhsT=w[:, kc, oc * 128:(oc + 1) * 128], rhs=E.actT[:, kc, 0:T], start=(kc == 0), stop=(kc == 7))

    def sigmoid_from(E, src_ap, skey, T, scale=-1.0, bias=None, extra=()):
        tt, tk = tmp(E)
        kw = {} if bias is None else {"bias": bias}
        P.act("activation", R=[skey] + list(extra), W=[tk], out=tt[:, 0:T], in_=src_ap, func=AF.Exp, scale=scale, **kw)
        P.act("activation", R=[tk], W=[tk], out=tt[:, 0:T], in_=tt[:, 0:T], func=AF.Ln, bias=1.0)
        P.act("activation", R=[tk], W=[tk], out=tt[:, 0:T], in_=tt[:, 0:T], func=AF.Exp, scale=-1.0)
        return tt, tk

    def sigmoid_multi(E, srcs, T):
        outs = []
        for (ap, key, scale, bias, extra) in srcs:
            tt, tk = tmp(E)
            kw = {} if bias is None else {"bias": bias}
            P.act("activation", R=[key] + list(extra), W=[tk], out=tt[:, 0:T], in_=ap, func=AF.Exp, scale=scale, **kw)
            outs.append((tt, tk))
        for (tt, tk) in outs:
            P.act("activation", R=[tk], W=[tk], out=tt[:, 0:T], in_=tt[:, 0:T], func=AF.Ln, bias=1.0)
        for (tt, tk) in outs:
            P.act("activation", R=[tk], W=[tk], out=tt[:, 0:T], in_=tt[:, 0:T], func=AF.Exp, scale=-1.0)
        return outs

    def tail(E, t, hb, hkey, pb, pkey, catkey="catT"):
        for _ in tail_gen(E, t, hb, hkey, pb, pkey, catkey):
            pass

    def tail_gen(E, t, hb, hkey, pb, pkey, catkey="catT"):
        NT = t.NT
        halves = [slice(0, 512), slice(512, 1024)]
        for s in range(NT):
            sl = slice(s * 128, (s + 1) * 128)
            for hs in halves:
                b = bank()
                for kc in range(8):
                    P.pe("matmul", R=[(catkey, s), "w_out"], W=[PS(b)], out=psf[b], lhsT=E.catT[:, kc, sl],
                         rhs=E.w_out[:, kc, hs], start=(kc == 0), stop=(kc == 7))
                P.dve("tensor_tensor", R=[PS(b), (hkey, s)], W=[(hkey, s)], out=hb[:, s, hs], in0=hb[:, s, hs],
                      in1=psf[b], op=ALU.add)
                yield
        for s in range(NT):
            xi = s % 2
            P.act("activation", R=[(hkey, s)], W=[("xnb", xi)], out=E.xnb[xi], in_=hb[:, s, :], func=AF.Copy)
            transpose_into(E.xnb[xi], ("xnb", xi), 8, E.catT, (catkey, s), s)
            P.act("activation", R=[pkey], W=[("xnp", xi)], out=E.xnp[xi], in_=pb[:, s, :], func=AF.Copy)
            transpose_into(E.xnp[xi], ("xnp", xi), 2, E.pT, ("pT", s), s)
            yield
        for s in range(NT):
            sl = slice(s * 128, (s + 1) * 128)
            bgs, bps = [], []
            for hs in halves:
                bg = bank()
                for kc in range(8):
                    P.pe("matmul", R=[(catkey, s), "w_gate"], W=[PS(bg)], out=psf[bg], lhsT=E.catT[:, kc, sl],
                         rhs=E.w_gate[:, kc, hs], start=(kc == 0), stop=(kc == 7))
                bp = bank()
                for kc in range(2):
                    P.pe("matmul", R=[("pT", s), "w_proj"], W=[PS(bp)], out=psf[bp], lhsT=E.pT[:, kc, sl],
                         rhs=E.w_proj[:, kc, hs], start=(kc == 0), stop=(kc == 1))
                bgs.append(bg)
                bps.append(bp)
            yield
            sgs = sigmoid_multi(E, [(psf[bg], PS(bg), -1.0, None, ()) for bg in bgs], 512)
            yield
            for (sg, sgk), bp in zip(sgs, bps):
                P.dve("tensor_tensor", R=[PS(bp), sgk], W=[sgk], out=sg, in0=psf[bp], in1=sg, op=ALU.mult)
            for (sg, sgk), hs in zip(sgs, halves):
                P.dve("tensor_tensor", R=[sgk, (hkey, s)], W=[(hkey, s)], out=hb[:, s, hs], in0=hb[:, s, hs], in1=sg,
                      op=ALU.add)
            yield

    def state_out(E, t, bufs, key, H, L, dst):
        for q in range(t.nseg):
            b = bank()
            for j in range(4):
                P.pe("transpose", R=[(key, j), "identf"], W=[PS(b)], out=psf[b][0:H, j * 128:(j + 1) * 128],
                     in_=bufs[j][:, q, L:L + H], identity=identf)
            P.act("activation", R=[PS(b)], W=["hst"], out=E.hst[0:H, :], in_=psf[b][0:H, :], func=AF.Copy)
            P.dma(R=["hst"], out=dst[t.seq if t.kind == "p" else q], in_=E.hst[0:H, :], is_output=True)

    if 1 in cfg.passes:
        E = common_bufs()
        E.pbuf = [AR.alloc([128, 4, PLE]) for _ in range(2)]
        E.pT = AR.alloc([128, 2, 512], BF16)
        E.xnp = [AR.alloc([128, PLE], BF16) for _ in range(2)]
        E.w_big = AR.alloc([128, 8, 7 * DB], BF16)
        E.w_out = AR.alloc([128, 8, D], BF16)
        E.w_gate = AR.alloc([128, 8, D], BF16)
        E.w_proj = AR.alloc([128, 2, D], BF16)
        awc = AR.alloc([128, 4, 3])
        bwc = AR.alloc([128, 4, 31])
        lncol = AR.alloc([128, 4, 4])
        ubuf_p = [AR.alloc([128, 1, HA + 512]) for j in range(4)]
        gbuf_p = [AR.alloc([128, 1, HB + 512]) for j in range(4)]
        ubuf_s = [AR.alloc([128, NS, HA + DS]) for j in range(4)]
        gbuf_s = [AR.alloc([128, NS, HB + DS]) for j in range(4)]
        bconv = [AR.alloc([128, 512]) for j in range(4)]
        negmean = AR.alloc([128, 512])
        rstdB = AR.alloc([128, 512])
        E.hst = AR.alloc([32, 512])
        print("P1 SBUF bytes/partition:", AR.off * 4)

        load_weight(E, E.w_big, "w_big", ab_w_in, 8, 7 * DB, gain_i=0)
        load_weight(E, E.w_out, "w_out", ab_w_out, 8, D)
        load_weight(E, E.w_gate, "w_gate", ple_gate[0], 8, D)
        load_weight(E, E.w_proj, "w_proj", ple_proj[0], 2, D)
        for j in range(4):
            P.dma(W=["awc"], out=awc[:, j, :], in_=a_conv_w[:, j * 128:(j + 1) * 128].rearrange("w p -> p w"),
                  allow_slow_non_contiguous=True)
            P.dma(W=["bwc"], out=bwc[:, j, :], in_=b_conv_w[:, j * 128:(j + 1) * 128].rearrange("w p -> p w"),
                  allow_slow_non_contiguous=True)
        P.dma(W=["lncol"], out=lncol[:, :, 0], in_=b_ln_g.rearrange("(j p) -> p j", p=128),
              allow_slow_non_contiguous=True)
        P.dma(W=["lncol"], out=lncol[:, :, 1], in_=b_ln_b.rearrange("(j p) -> p j", p=128),
              allow_slow_non_contiguous=True)
        P.pool("tensor_scalar", R=["lncol"], W=["lncol"], out=lncol[:, :, 2:4], in0=lncol[:, :, 0:2], scalar1=-1.0,
               scalar2=None, op0=ALU.mult)

        def p1_load(ti):
            t = tiles[ti]
            par = ti % 2
            P.dma(W=[(("hbuf", par), s_) for s_ in range(4)], out=E.hbuf[par][:, 0:t.NT, :],
                  in_=x_rows(t).rearrange("(n p) d -> p n d", p=128))
            P.dma(W=[("pbuf", par)], out=E.pbuf[par][:, 0:t.NT, :],
                  in_=p_rows(t, 0).rearrange("(n p) d -> p n d", p=128))

        CATK = [("catT", s_) for s_ in range(4)]
        for ti, t in enumerate(tiles):
            T, NT, nseg, L = t.T, t.NT, t.nseg, t.L
            par = ti % 2
            hb, hkey = E.hbuf[par], ("hbuf", par)
            pb, pkey = E.pbuf[par], ("pbuf", par)
            isp = t.kind == "p"
            ub, gb = (ubuf_p, gbuf_p) if isp else (ubuf_s, gbuf_s)
            ukey, gkey = ("ubuf_p", "gbuf_p") if isp else ("ubuf_s", "gbuf_s")

            def v3(ap):
                return ap.rearrange("p (n l) -> p n l", n=nseg)

            if ti == 0:
                p1_load(0)
            if isp and t.t0 == 0:
                for j in range(4):
                    P.pool("memset", W=[(ukey, j)], ap=ub[j][:, :, 0:HA], constant=0.0)
                    P.pool("memset", W=[(gkey, j)], ap=gb[j][:, :, 0:HB], constant=0.0)
            if not isp:
                for q in range(NS):
                    for (stt, H, bufs, key) in ((st_a, HA, ub, ukey), (st_b, HB, gb, gkey)):
                        P.dma(W=["hst"], out=E.hst[0:H, :], in_=stt[q])
                        b = bank()
                        for j in range(4):
                            P.pe("transpose", R=["hst", "identf"], W=[PS(b)], out=psf[b][:, j * 32:j * 32 + H],
                                 in_=E.hst[0:H, j * 128:(j + 1) * 128], identity=identf[0:H, 0:H])
                        for j in range(4):
                            P.act("activation", R=[PS(b)], W=[(key, j)], out=bufs[j][:, q, 0:H],
                                  in_=psf[b][:, j * 32:j * 32 + H], func=AF.Copy)

            if ti == 0:
                rmsnorm_to_T(E, t, hb, hkey)

            for pair in ((0, 1), (2, 3)):
                bvs, bgs = {}, {}
                for j in pair:
                    bvs[j], bgs[j] = bank(), bank()
                    proj_fm(E, bvs[j], E.w_big, 16 + j, T)
                    proj_fm(E, bgs[j], E.w_big, 20 + j, T)
                sgs = sigmoid_multi(E, [(psf[bgs[j]][:, 0:T], PS(bgs[j]), -1.0, None, ()) for j in pair], T)
                for (sg, sgk), j in zip(sgs, pair):
                    P.dve("tensor_tensor", R=[PS(bvs[j]), sgk], W=[(gkey, j)], out=gb[j][:, :, HB:HB + L],
                          in0=v3(psf[bvs[j]][:, 0:T]), in1=v3(sg[:, 0:T]), op=ALU.mult)
            def gen_A():
                for pair in ((0, 1), (2, 3)):
                    bxs, bcs = {}, {}
                    for j in pair:
                        bxs[j], bcs[j] = bank(), bank()
                        proj_fm(E, bxs[j], E.w_big, 0 + j, T)
                        proj_fm(E, bcs[j], E.w_big, 4 + j, T)
                    yield
                    tcs = {}
                    for j in pair:
                        tcs[j] = tmp(E)
                        P.act("activation", R=[PS(bcs[j])], W=[tcs[j][1]], out=tcs[j][0][:, 0:T], in_=psf[bcs[j]][:, 0:T],
                              func=AF.Copy)
                    for j in pair:
                        P.dve("tensor_tensor", R=[PS(bxs[j]), tcs[j][1]], W=[(ukey, j)], out=ub[j][:, :, HA:HA + L],
                              in0=v3(psf[bxs[j]][:, 0:T]), in1=v3(tcs[j][0][:, 0:T]), op=ALU.mult)
                    yield
                    bzs = {}
                    for j in pair:
                        bzs[j] = bank()
                        proj_fm(E, bzs[j], E.w_big, 12 + j, T)
                    yield
                    sgs = sigmoid_multi(E, [(psf[bzs[j]][:, 0:T], PS(bzs[j]), -1.0, None, ()) for j in pair], T)
                    yield
                    cvs = {}
                    for j in pair:
                        cvs[j] = tmp(E)
                        P.dve("tensor_scalar", R=[(ukey, j), "awc"], W=[cvs[j][1]], out=v3(cvs[j][0][:, 0:T]),
                              in0=ub[j][:, :, 2:2 + L], scalar1=awc[:, j, 2:3], scalar2=None, op0=ALU.mult)
                    yield
                    for w in (1, 0):
                        for j in pair:
                            P.dve("scalar_tensor_tensor", R=[(ukey, j), "awc", cvs[j][1]], W=[cvs[j][1]],
                                  out=v3(cvs[j][0][:, 0:T]), in0=ub[j][:, :, w:w + L], scalar=awc[:, j, w:w + 1],
                                  in1=v3(cvs[j][0][:, 0:T]), op0=ALU.mult, op1=ALU.add)
                    yield
                    bbs = {}
                    for j in pair:
                        bbs[j] = bank()
                        proj_fm(E, bbs[j], E.w_big, 8 + j, T)
                    yield
                    for (sg, sgk), j in zip(sgs, pair):
                        P.dve("tensor_tensor", R=[PS(bzs[j]), sgk], W=[sgk], out=sg[:, 0:T], in0=psf[bzs[j]][:, 0:T],
                              in1=sg[:, 0:T], op=ALU.mult)
                    for j in pair:
                        P.dve("tensor_tensor", R=[PS(bbs[j]), cvs[j][1]], W=[cvs[j][1]], out=cvs[j][0][:, 0:T],
                              in0=psf[bbs[j]][:, 0:T], in1=cvs[j][0][:, 0:T], op=ALU.mult)
                    yield
                    for (sg, sgk), j in zip(sgs, pair):
                        P.dve("tensor_tensor", R=[sgk, cvs[j][1]], W=CATK, out=E.catT[:, j, 0:T], in0=sg[:, 0:T],
                              in1=cvs[j][0][:, 0:T], op=ALU.mult)
                    yield
            def gen_bz():
                for pair in ((0, 1), (2, 3)):
                    bzs = {}
                    for j in pair:
                        bzs[j] = bank()
                        proj_fm(E, bzs[j], E.w_big, 24 + j, T)
                    yield
                    sgs = sigmoid_multi(E, [(psf[bzs[j]][:, 0:T], PS(bzs[j]), -1.0, None, ()) for j in pair], T)
                    yield
                    for (sg, sgk), j in zip(sgs, pair):
                        P.dve("tensor_tensor", R=[PS(bzs[j]), sgk], W=CATK, out=E.catT[:, 4 + j, 0:T],
                              in0=psf[bzs[j]][:, 0:T], in1=sg[:, 0:T], op=ALU.mult)
                    yield
            def gen_conv():
                for w in range(31):
                    for j in range(4):
                        if w == 0:
                            P.dve("tensor_scalar", R=[(gkey, j), "bwc"], W=[("bconv", j)], out=v3(bconv[j][:, 0:T]),
                                  in0=gb[j][:, :, 0:L], scalar1=bwc[:, j, 0:1], scalar2=None, op0=ALU.mult)
                        else:
                            P.dve("scalar_tensor_tensor", R=[(gkey, j), "bwc", ("bconv", j)], W=[("bconv", j)],
                                  out=v3(bconv[j][:, 0:T]), in0=gb[j][:, :, w:w + L], scalar=bwc[:, j, w:w + 1],
                                  in1=v3(bconv[j][:, 0:T]), op0=ALU.mult, op1=ALU.add)
                    yield
            def chain(*gs):
                for g in gs:
                    yield from g

            def gen_prev_tail():
                if ti == 0:
                    return
                tp, pp = tiles[ti - 1], (ti - 1) % 2
                yield from tail_gen(E, tp, E.hbuf[pp], ("hbuf", pp), E.pbuf[pp], ("pbuf", pp))
                P.dma(R=[(("hbuf", pp), s_) for s_ in range(4)], W=[("hA", ti - 1)],
                      out=hA[tp.row0:tp.row0 + tp.T, :].rearrange("(n p) d -> p n d", p=128),
                      in_=E.hbuf[pp][:, 0:tp.NT, :], is_output=cfg.debug)
                if ti + 1 < len(tiles):
                    p1_load(ti + 1)
                yield

            g1, g2 = gen_conv(), chain(gen_prev_tail(), gen_A(), gen_bz())
            alive = [g1, g2]
            while alive:
                for g in list(alive):
                    try:
                        next(g)
                    except StopIteration:
                        alive.remove(g)
            if t.last:
                state_out(E, t, ub, ukey, HA, L, na_p if isp else na_s)
            else:
                for j in range(4):
                    P.pool("tensor_copy", R=[(ukey, j)], W=[(ukey, j)], out=ub[j][:, :, 0:HA], in_=ub[j][:, :, L:L + HA])

            if t.last:
                state_out(E, t, gb, gkey, HB, L, nb_p if isp else nb_s)
            else:
                for j in range(4):
                    P.pool("tensor_copy", R=[(gkey, j)], W=[(gkey, j)], out=gb[j][:, :, 0:HB], in_=gb[j][:, :, L:L + HB])
            sqs = []
            for j in range(4):
                sq, sqk = tmp(E)
                P.act("activation", R=[("bconv", j)], W=[sqk], out=sq[:, 0:T], in_=bconv[j][:, 0:T], func=AF.Square)
                sqs.append((sq, sqk))
            for j in range(4):
                P.pe("matmul", R=[("bconv", j), "onesf"], W=[PS(6)], out=psf[6][:, 0:T], lhsT=onesf,
                     rhs=bconv[j][:, 0:T], start=(j == 0), stop=(j == 3))
            for j in range(4):
                sq, sqk = sqs[j]
                P.pe("matmul", R=[sqk, "onesf"], W=[PS(7)], out=psf[7][:, 0:T], lhsT=onesf, rhs=sq[:, 0:T],
                     start=(j == 0), stop=(j == 3))
            P.act("activation", R=[PS(6)], W=["negmean"], out=negmean[:, 0:T], in_=psf[6][:, 0:T], func=AF.Copy,
                  scale=-1.0 / DB)
            P.dve("tensor_tensor", R=["negmean"], W=["rstdB"], out=rstdB[:, 0:T], in0=negmean[:, 0:T],
                  in1=negmean[:, 0:T], op=ALU.mult)
            P.dve("scalar_tensor_tensor", R=[PS(7), "rstdB"], W=["rstdB"], out=rstdB[:, 0:T], in0=psf[7][:, 0:T],
                  scalar=1.0 / DB, in1=rstdB[:, 0:T], op0=ALU.mult, op1=ALU.subtract)
            P.act("activation", R=["rstdB"], W=["rstdB"], out=rstdB[:, 0:T], in_=rstdB[:, 0:T], func=AF.Ln, bias=EPS)
            P.act("activation", R=["rstdB"], W=["rstdB"], out=rstdB[:, 0:T], in_=rstdB[:, 0:T], func=AF.Exp, scale=-0.5)
            for pair in ((0, 1), (2, 3)):
                yvs = {}
                for j in pair:
                    yvs[j] = tmp(E)
                    P.dve("tensor_tensor", R=[("bconv", j), "negmean"], W=[yvs[j][1]], out=yvs[j][0][:, 0:T],
                          in0=bconv[j][:, 0:T], in1=negmean[:, 0:T], op=ALU.add)
                for j in pair:
                    P.dve("tensor_tensor", R=[yvs[j][1], "rstdB"], W=[yvs[j][1]], out=yvs[j][0][:, 0:T],
                          in0=yvs[j][0][:, 0:T], in1=rstdB[:, 0:T], op=ALU.mult)
                sgs = sigmoid_multi(E, [(yvs[j][0][:, 0:T], yvs[j][1], lncol[:, j, 2:3], lncol[:, j, 3:4], ["lncol"])
                                        for j in pair], T)
                for k_, j in enumerate(pair):
                    sgy, sgyk = sgs[k_]
                    P.dve("tensor_scalar", R=[yvs[j][1], "lncol", sgyk], W=[yvs[j][1]], out=yvs[j][0][:, 0:T],
                          in0=yvs[j][0][:, 0:T], scalar1=lncol[:, j, 0:1], scalar2=lncol[:, j, 1:2], op0=ALU.mult,
                          op1=ALU.add)
                for k_, j in enumerate(pair):
                    sgy, sgyk = sgs[k_]
                    P.dve("tensor_tensor", R=[yvs[j][1], sgyk], W=[yvs[j][1]], out=yvs[j][0][:, 0:T],
                          in0=yvs[j][0][:, 0:T], in1=sgy[:, 0:T], op=ALU.mult)
                for k_, j in enumerate(pair):
                    P.dve("tensor_tensor", R=[yvs[j][1]] + CATK, W=CATK, out=E.catT[:, 4 + j, 0:T],
                          in0=yvs[j][0][:, 0:T], in1=E.catT[:, 4 + j, 0:T], op=ALU.mult)

            if ti == 0 and len(tiles) > 1:
                p1_load(1)
            if ti + 1 < len(tiles):
                rmsnorm_to_T(E, tiles[ti + 1], E.hbuf[(ti + 1) % 2], ("hbuf", (ti + 1) % 2))
            else:
                tail(E, t, hb, hkey, pb, pkey)
                P.dma(R=[(hkey, s_) for s_ in range(4)], W=[("hA", ti)],
                      out=hA[t.row0:t.row0 + T, :].rearrange("(n p) d -> p n d", p=128),
                      in_=hb[:, 0:NT, :], is_output=cfg.debug)
        P.barrier()
        AR.off = base_off


    if 2 in cfg.passes:
        CAP = max(S, PL)
        E = common_bufs(n_h=1, n_tmp=4)
        E.w_big = AR.alloc([128, 8, 7 * DB], BF16)
        kT = AR.alloc([128, 4, CAP], BF16)
        Vc = AR.alloc([128, CAP // 128, DB], BF16)
        qT = AR.alloc([128, 4, 512], BF16)
        cvn = AR.alloc([128, 4, DB], BF16)
        kst = [AR.alloc([128, DB]) for _ in range(3)]
        kst_i = [0]
        kbf = [AR.alloc([128, DB], BF16) for _ in range(2)]
        spb = [AR.alloc([128, 512], BF16) for _ in range(4)]
        abf = [AR.alloc([128, 512], BF16) for _ in range(4)]
        rr = [0, 0, 0]
        sacc = [AR.alloc([128, 512], BF16) for _ in range(2)]
        g_bc = AR.alloc([128, DB])
        b_bc = AR.alloc([128, DB])
        WT = AR.alloc([128, 4, 128], BF16)
        WTs = AR.alloc([128, 4, 128], BF16)
        negU = AR.alloc([128, 128], BF16)
        negOnes = AR.alloc([128, 128], BF16)
        ones_row = AR.alloc([1, 128])
        cb_row = AR.alloc([1, 4, 128])
        cb_row_s = AR.alloc([1, 4, 128])
        lnst = AR.alloc([128, 4, 8])
        kT_new = AR.alloc([128, 4, 128], BF16)
        if CAP - PL >= 2048:
            Vn = [kT[0:32, q, PL:PL + DB] for q in range(4)]
            dzs = kT[:, 0, PL + DB:PL + DB + 1024].bitcast(F32).rearrange("p (a b) -> p a b", a=4)
        else:
            Vn = [AR.alloc([32, DB], BF16) for _ in range(4)]
            dzs = AR.alloc([128, 4, 128])
        print("P2 SBUF bytes/partition:", AR.off * 4)

        def stage():
            i = kst_i[0]
            kst_i[0] = (i + 1) % len(kst)
            return kst[i], ("kst", i)

        load_weight(E, E.w_big, "w_big", cd_w_in, 8, 7 * DB, gain_i=1)
        P.dma(W=["g_bc"], out=g_bc, in_=c_ln_g.partition_broadcast(128))
        P.dma(W=["b_bc"], out=b_bc, in_=c_ln_b.partition_broadcast(128))
        P.dma(W=["cb_row"], out=cb_row, in_=c_b.rearrange("(o h) t -> o h t", o=1))
        for q in range(NS):
            P.dma(W=["cb_row_s"], out=cb_row_s[:, :, q * DS:(q + 1) * DS],
                  in_=c_b[:, 0:DS].rearrange("(o h) t -> o h t", o=1))
        P.pool("memset", W=["ones_row"], ap=ones_row, constant=1.0)
        P.pool("memset", W=["negOnes"], ap=negOnes, constant=-1.0)
        tU, tUk = tmp(E)
        P.pool("memset", W=[tUk], ap=tU[:, 0:128], constant=-1.0)
        P.pool("affine_select", R=[tUk], W=[tUk], out=tU[:, 0:128], in_=tU[:, 0:128], pattern=[[-1, 128]],
               compare_op=ALU.is_ge, fill=0.0, base=0, channel_multiplier=1)
        P.dve("tensor_copy", R=[tUk], W=["negU"], out=negU, in_=tU[:, 0:128])
        for variant, dstW, dk in ((0, WT, "WT"), (1, WTs, "WTs")):
            for hh in range(4):
                wt_, wk = tmp(E)
                if variant == 0:
                    P.dma(W=[wk], out=wt_[:, 0:128], in_=c_ws[hh])
                else:
                    P.pool("memset", W=[wk], ap=wt_[:, 0:128], constant=0.0)
                    for q in range(NS):
                        P.dma(W=[wk], out=wt_[q * DS:(q + 1) * DS, q * DS:(q + 1) * DS], in_=c_ws[hh, 0:DS, 0:DS])
                P.pool("affine_select", R=[wk], W=[wk], out=wt_[:, 0:128], in_=wt_[:, 0:128], pattern=[[-1, 128]],
                       compare_op=ALU.is_ge, fill=0.0, base=0, channel_multiplier=1)
                P.act("activation", R=[wk], W=[("xnb", 0)], out=E.xnb[0][:, 0:128], in_=wt_[:, 0:128], func=AF.Copy)
                b = bank()
                P.pe("transpose", R=[("xnb", 0), "identb"], W=[PS(b)], out=psb[b][:, 0:128], in_=E.xnb[0][:, 0:128],
                     identity=identb)
                P.dve("tensor_copy", R=[PS(b)], W=[dk], out=dstW[:, hh, :], in_=psb[b][:, 0:128])

        def tri_mask(buf, key, nk, c0):
            P.pool("affine_select", R=[key], W=[key], out=buf[0:nk, c0:c0 + nk], in_=buf[0:nk, c0:c0 + nk],
                   pattern=[[1, nk]], compare_op=ALU.is_gt, fill=0.0, base=0, channel_multiplier=-1)

        def attention(hp, qa, qb, blocks):
            for h in range(2):
                P.pool("memset", W=[("sacc", h)], ap=sacc[h][:, qa:qb], constant=0.0)
            nb = len(blocks)
            items = [(bi, h) for bi in range(nb) for h in range(2)]
            n = len(items)
            st = {}

            def Sz(i):
                bi, h = items[i]
                kTa, Va, nk, c0, tri = blocks[bi]
                r0 = 64 * h
                bz = rr[2] % 6
                rr[2] += 1
                P.pe("matmul", R=["kT", "qT"], W=[PS(bz)], out=psf[bz][0:nk, c0:qb], lhsT=kTa[r0:r0 + 64, 0:nk],
                     rhs=qT[r0:r0 + 64, hp, c0:qb], start=True, stop=True)
                st[i] = {"bz": bz}

            def Se(i):
                bi, h = items[i]
                kTa, Va, nk, c0, tri = blocks[bi]
                bz = st[i]["bz"]
                e_, ek = tmp(E)
                P.act("activation", R=[PS(bz)], W=[ek], out=e_[0:nk, c0:qb], in_=psf[bz][0:nk, c0:qb], func=AF.Exp)
                st[i]["e"] = (e_, ek)

            def Sl(i):
                bi, h = items[i]
                kTa, Va, nk, c0, tri = blocks[bi]
                e_, ek = st[i]["e"]
                si = rr[0] % 4
                rr[0] += 1
                sp_, spk = spb[si], ("spb", si)
                P.act("activation", R=[ek], W=[spk], out=sp_[0:nk, c0:qb], in_=e_[0:nk, c0:qb], func=AF.Ln, bias=1.0)
                if tri:
                    tri_mask(sp_, spk, nk, c0)
                st[i]["sp"] = (sp_, spk)

            def Sw(i):
                bi, h = items[i]
                kTa, Va, nk, c0, tri = blocks[bi]
                bz = st[i]["bz"]
                sp_, spk = st[i]["sp"]
                P.pe("matmul", R=[spk, "negU"], W=[PS(bz)], out=psf[bz][0:nk, c0:qb], lhsT=negU[0:nk, 0:nk],
                     rhs=sp_[0:nk, c0:qb], start=False, stop=(bi == 0), skip_group_check=True)
                if bi > 0:
                    P.pe("matmul", R=[("sacc", h), "negOnes"], W=[PS(bz)], out=psf[bz][0:nk, c0:qb],
                         lhsT=negOnes[:, 0:nk], rhs=sacc[h][:, c0:qb], start=False, stop=True, skip_group_check=True)
                if bi < nb - 1:
                    P.dve("tensor_tensor", R=[spk, ("sacc", h)], W=[("sacc", h)], out=sacc[h][0:nk, c0:qb],
                          in0=sacc[h][0:nk, c0:qb], in1=sp_[0:nk, c0:qb], op=ALU.add)

            def Sa(i):
                bi, h = items[i]
                kTa, Va, nk, c0, tri = blocks[bi]
                bz = st[i]["bz"]
                ai = rr[1] % 4
                rr[1] += 1
                a_, ak = abf[ai], ("abf", ai)
                P.act("activation", R=[PS(bz)], W=[ak], out=a_[0:nk, c0:qb], in_=psf[bz][0:nk, c0:qb], func=AF.Exp)
                if tri:
                    tri_mask(a_, ak, nk, c0)
                if c0 > qa:
                    P.pool("memset", W=[ak], ap=a_[0:nk, qa:c0], constant=0.0)
                st[i]["a"] = (a_, ak)

            def Sv(i):
                bi, h = items[i]
                kTa, Va, nk, c0, tri = blocks[bi]
                a_, ak = st.pop(i)["a"]
                P.pe("matmul", R=[ak, "Vc"], W=[PS(6 + h)], out=psf[6 + h][:, qa:qb], lhsT=Va[0:nk, :],
                     rhs=a_[0:nk, qa:qb], start=(bi == 0), stop=(bi == nb - 1))

            for k in range(-3, n):
                if 0 <= k + 3 < n:
                    Sz(k + 3)
                if 0 <= k + 2 < n:
                    Se(k + 2)
                if 0 <= k + 1 < n:
                    Sl(k + 1)
                    Sw(k + 1)
                if 0 <= k < n:
                    Sa(k)
                    Sv(k)

        def attention_prompt(T, kb0, NT):
            nkb = kb0 + NT
            blk = []
            for kb in range(nkb - 1, -1, -1):
                i = kb - kb0
                blk.append((kb, 128, max(i, 0) * 128, i >= 0))
            nb = len(blk)
            items = [(hp, bi, h) for hp in range(4) for bi in range(nb) for h in range(2)]
            n = len(items)
            st = {}
            qa, qb = 0, T

            def obank(hp, h):
                return 4 + 2 * (hp % 2) + h

            def Sz(i):
                hp, bi, h = items[i]
                kb, nk, c0, tri = blk[bi]
                r0 = 64 * h
                bz = rr[2] % 4
                rr[2] += 1
                P.pe("matmul", R=["kT", "qT"], W=[PS(bz)], out=psf[bz][0:nk, c0:qb],
                     lhsT=kT[r0:r0 + 64, hp, kb * 128:kb * 128 + nk], rhs=qT[r0:r0 + 64, hp, c0:qb], start=True, stop=True)
                st[i] = {"bz": bz}

            def Se(i):
                hp, bi, h = items[i]
                kb, nk, c0, tri = blk[bi]
                bz = st[i]["bz"]
                e_, ek = tmp(E)
                P.act("activation", R=[PS(bz)], W=[ek], out=e_[0:nk, c0:qb], in_=psf[bz][0:nk, c0:qb], func=AF.Exp)
                st[i]["e"] = (e_, ek)

            def Sl(i):
                hp, bi, h = items[i]
                kb, nk, c0, tri = blk[bi]
                e_, ek = st[i]["e"]
                si = rr[0] % 4
                rr[0] += 1
                sp_, spk = spb[si], ("spb", si)
                P.act("activation", R=[ek], W=[spk], out=sp_[0:nk, c0:qb], in_=e_[0:nk, c0:qb], func=AF.Ln, bias=1.0)
                if tri:
                    tri_mask(sp_, spk, nk, c0)
                st[i]["sp"] = (sp_, spk)

            def Sw(i):
                hp, bi, h = items[i]
                kb, nk, c0, tri = blk[bi]
                bz = st[i]["bz"]
                sp_, spk = st[i]["sp"]
                if bi == 0 and h == 0:
                    for h2 in range(2):
                        P.pool("memset", W=[("sacc", h2)], ap=sacc[h2][:, qa:qb], constant=0.0)
                P.pe("matmul", R=[spk, "negU"], W=[PS(bz)], out=psf[bz][0:nk, c0:qb], lhsT=negU[0:nk, 0:nk],
                     rhs=sp_[0:nk, c0:qb], start=False, stop=(bi == 0), skip_group_check=True)
                if bi > 0:
                    P.pe("matmul", R=[("sacc", h), "negOnes"], W=[PS(bz)], out=psf[bz][0:nk, c0:qb],
                         lhsT=negOnes[:, 0:nk], rhs=sacc[h][:, c0:qb], start=False, stop=True, skip_group_check=True)
                if bi < nb - 1:
                    P.dve("tensor_tensor", R=[spk, ("sacc", h)], W=[("sacc", h)], out=sacc[h][0:nk, c0:qb],
                          in0=sacc[h][0:nk, c0:qb], in1=sp_[0:nk, c0:qb], op=ALU.add)

            def Sa(i):
                hp, bi, h = items[i]
                kb, nk, c0, tri = blk[bi]
                bz = st[i]["bz"]
                ai = rr[1] % 4
                rr[1] += 1
                a_, ak = abf[ai], ("abf", ai)
                P.act("activation", R=[PS(bz)], W=[ak], out=a_[0:nk, c0:qb], in_=psf[bz][0:nk, c0:qb], func=AF.Exp)
                if tri:
                    tri_mask(a_, ak, nk, c0)
                if c0 > qa:
                    P.pool("memset", W=[ak], ap=a_[0:nk, qa:c0], constant=0.0)
                st[i]["a"] = (a_, ak)

            def Sv(i):
                hp, bi, h = items[i]
                kb, nk, c0, tri = blk[bi]
                a_, ak = st.pop(i)["a"]
                ob = obank(hp, h)
                P.pe("matmul", R=[ak, "Vc"], W=[PS(ob)], out=psf[ob][:, qa:qb], lhsT=Vc[0:nk, kb, hp * 128:(hp + 1) * 128],
                     rhs=a_[0:nk, qa:qb], start=(bi == 0), stop=(bi == nb - 1))
                if bi == nb - 1:
                    r0 = 64 * h
                    P.dve("tensor_tensor", R=[PS(ob), "catT"], W=["catT"], out=E.catT[r0:r0 + 64, 4 + hp, qa:qb],
                          in0=psf[ob][r0:r0 + 64, qa:qb], in1=E.catT[r0:r0 + 64, 4 + hp, qa:qb], op=ALU.mult)

            for k in range(-3, n):
                if 0 <= k + 3 < n:
                    Sz(k + 3)
                if 0 <= k + 2 < n:
                    Se(k + 2)
                if 0 <= k + 1 < n:
                    Sl(k + 1)
                    Sw(k + 1)
                if 0 <= k < n:
                    Sa(k)
                    Sv(k)

        def p2_load(ti):
            t = tiles[ti]
            P.dma(W=[(("hbuf", 0), s_) for s_ in range(4)], out=E.hbuf[0][:, 0:t.NT, :],
                  in_=hA[t.row0:t.row0 + t.T, :].rearrange("(n p) d -> p n d", p=128), R=[("hA", ti)])

        p2_load(0)
        for ti, t in enumerate(tiles):
            if cfg.stop == 1:
                break
            T, NT, nseg, L = t.T, t.NT, t.nseg, t.L
            isp = t.kind == "p"
            hb, hkey = E.hbuf[0], ("hbuf", 0)
            if ti == 0:
                rmsnorm_to_T(E, t, hb, hkey)
                p2_load(1)
            kb0 = t.t0 // 128

            for s in range(NT):
                sl = slice(s * 128, (s + 1) * 128)
                rows = slice(t.row0 + s * 128, t.row0 + (s + 1) * 128)

                def proj_tm(c0):
                    b = bank()
                    for kc in range(8):
                        P.pe("matmul", R=[("actT", s), "w_big"], W=[PS(b)], out=psf[b], lhsT=E.actT[:, kc, sl],
                             rhs=E.w_big[:, kc, c0:c0 + DB], start=(kc == 0), stop=(kc == 7))
                    return b

                b_cv, b_k, b_v = proj_tm(DB), proj_tm(4 * DB), proj_tm(5 * DB)
                ls, lsk = lnst[:, s, :], ("lnst", s)
                cf, cfk = tmp(E)
                P.pool("memset", W=[lsk], ap=ls, constant=0.0)
                P.act("activation", R=[PS(b_cv)], W=[cfk, lsk], out=cf, in_=psf[b_cv], func=AF.Copy, accum_out=ls[:, 0:1])
                jk, jkk = tmp(E)
                P.act("activation", R=[PS(b_cv)], W=[jkk, lsk], out=jk, in_=psf[b_cv], func=AF.Square,
                      accum_out=ls[:, 1:2])
                st_, stk = stage()
                P.act("activation", R=[PS(b_k)], W=[stk], out=st_, in_=psf[b_k], func=AF.Copy)
                P.dma(R=[stk], out=(nk_p[rows, :] if isp else nk_s), in_=st_, is_output=True)
                kb_ = kbf[s % 2]
                P.dve("tensor_copy", R=[PS(b_k)], W=[("kbf", s % 2)], out=kb_, in_=psf[b_k])
                if isp:
                    transpose_into(kb_, ("kbf", s % 2), 4, kT, "kT", kb0 + s)
                else:
                    transpose_into(kb_, ("kbf", s % 2), 4, kT_new, "kT_new", 0)
                st_, stk = stage()
                P.act("activation", R=[PS(b_v)], W=[stk], out=st_, in_=psf[b_v], func=AF.Copy)
                P.dma(R=[stk], out=(nv_p[rows, :] if isp else nv_s), in_=st_, is_output=True)
                if isp:
                    P.act("activation", R=[PS(b_v)], W=["Vc"], out=Vc[:, kb0 + s, :], in_=psf[b_v], func=AF.Copy)
                else:
                    P.act("activation", R=[PS(b_v)], W=[("kbf", 1)], out=kbf[1], in_=psf[b_v], func=AF.Copy)
                    for q in range(NS):
                        P.dma(R=[("kbf", 1)], W=[("Vn", q)], out=Vn[q][0:DS, :], in_=kbf[1][q * DS:(q + 1) * DS, :])
                P.dve("tensor_scalar", R=[lsk], W=[lsk], out=ls[:, 2:3], in0=ls[:, 0:1], scalar1=1.0 / DB,
                      scalar2=None, op0=ALU.mult)
                P.dve("tensor_tensor", R=[lsk], W=[lsk], out=ls[:, 3:4], in0=ls[:, 2:3], in1=ls[:, 2:3],
                      op=ALU.mult)
                P.dve("scalar_tensor_tensor", R=[lsk], W=[lsk], out=ls[:, 4:5], in0=ls[:, 1:2],
                      scalar=1.0 / DB, in1=ls[:, 3:4], op0=ALU.mult, op1=ALU.subtract)
                P.act("activation", R=[lsk], W=[lsk], out=ls[:, 5:6], in_=ls[:, 4:5], func=AF.Ln, bias=EPS)
                P.act("activation", R=[lsk], W=[lsk], out=ls[:, 5:6], in_=ls[:, 5:6], func=AF.Exp, scale=-0.5)
                P.dve("tensor_scalar", R=[cfk, lsk], W=[cfk], out=cf, in0=cf, scalar1=ls[:, 2:3],
                      scalar2=ls[:, 5:6], op0=ALU.subtract, op1=ALU.mult)
                P.dve("tensor_tensor", R=[cfk, "g_bc"], W=[cfk], out=cf, in0=cf, in1=g_bc, op=ALU.mult)
                if isp:
                    P.dve("tensor_tensor", R=[cfk, "b_bc"], W=["cvn"], out=cvn[:, s, :], in0=cf, in1=b_bc, op=ALU.add)
                else:
                    P.dve("tensor_tensor", R=[cfk, "b_bc"], W=[cfk], out=cf, in0=cf, in1=b_bc, op=ALU.add)
                    P.act("activation", R=[cfk], W=["cvn"], out=cvn[:, s, :], in_=cf, func=AF.Copy)
                    P.dma(R=[cfk], out=ncv_s, in_=cf, is_output=True)

            if cfg.stop in (2, 21, 22, 23, 24, 25, 26, 27, 28):
                continue
            Wm, Wk = (WT, "WT") if isp else (WTs, "WTs")
            cbr, cbk = (cb_row, "cb_row") if isp else (cb_row_s, "cb_row_s")
            for hh in range(4):
                bm = bank()
                for s in range(NT):
                    sl = slice(s * 128, (s + 1) * 128)
                    P.pe("matmul", R=["cvn", Wk], W=[PS(bm)], out=psf[bm][:, sl], lhsT=cvn[:, s, hh * 128:(hh + 1) * 128],
                         rhs=Wm[:, hh, :], start=True, stop=False)
                    P.pe("matmul", R=["ones_row", cbk], W=[PS(bm)], out=psf[bm][:, sl], lhsT=ones_row[0:1, :],
                         rhs=cbr[0:1, hh, :], start=False, stop=True)
                bu, bzc = bank(), bank()
                proj_fm(E, bu, E.w_big, 0 + hh, T)
                proj_fm(E, bzc, E.w_big, 8 + hh, T)
                sg, sgk = sigmoid_from(E, psf[bzc][:, 0:T], PS(bzc), T)
                P.dve("tensor_tensor", R=[PS(bzc), sgk], W=[sgk], out=sg[:, 0:T], in0=psf[bzc][:, 0:T], in1=sg[:, 0:T],
                      op=ALU.mult)
                P.dve("tensor_tensor", R=[PS(bm), sgk], W=[sgk], out=sg[:, 0:T], in0=psf[bm][:, 0:T], in1=sg[:, 0:T],
                      op=ALU.mult)
                P.dve("tensor_tensor", R=[PS(bu), sgk], W=["catT"], out=E.catT[:, hh, 0:T], in0=psf[bu][:, 0:T],
                      in1=sg[:, 0:T], op=ALU.mult)

            if cfg.stop == 3:
                continue
            for hp in range(4):
                bq = bank()
                proj_fm(E, bq, E.w_big, 12 + hp, T)
                P.act("activation", R=[PS(bq)], W=["qT"], out=qT[:, hp, 0:T], in_=psf[bq][:, 0:T], func=AF.Copy,
                      scale=0.125)

            def silu_dz(hp, dst, dkey):
                bd = bank()
                proj_fm(E, bd, E.w_big, 24 + hp, T)
                sg, sgk = sigmoid_from(E, psf[bd][:, 0:T], PS(bd), T)
                P.dve("tensor_tensor", R=[PS(bd), sgk], W=[dkey], out=dst, in0=psf[bd][:, 0:T], in1=sg[:, 0:T],
                      op=ALU.mult)

            def finalize(hp, qa, qb, m1, m1k):
                for h in range(2):
                    r0 = 64 * h
                    P.dve("tensor_tensor", R=[PS(6 + h), m1k], W=["catT"], out=E.catT[r0:r0 + 64, 4 + hp, qa:qb],
                          in0=psf[6 + h][r0:r0 + 64, qa:qb], in1=m1[r0:r0 + 64, qa:qb], op=ALU.mult)

            if cfg.stop == 4 or (cfg.stop == 5 and not isp):
                continue
            if isp:
                for hp in range(4):
                    silu_dz(hp, E.catT[:, 4 + hp, 0:T], "catT")
                rmsnorm_to_T(E, tiles[ti + 1], hb, hkey)
                if ti + 2 < len(tiles):
                    p2_load(ti + 2)
                attention_prompt(T, kb0, NT)
            else:
                for hp in range(4):
                    silu_dz(hp, dzs[:, hp, :], "dzs")
                npast = PL // 128
                for q in range(NS):
                    for kb in range(npast):
                        st_, stk = stage()
                        P.dma(W=[stk], out=st_, in_=ck[q, kb * 128:(kb + 1) * 128, :])
                        P.act("activation", R=[stk], W=[("kbf", kb % 2)], out=kbf[kb % 2], in_=st_, func=AF.Copy)
                        transpose_into(kbf[kb % 2], ("kbf", kb % 2), 4, kT, "kT", kb)
                        st_, stk = stage()
                        P.dma(W=[stk], out=st_, in_=cv[q, kb * 128:(kb + 1) * 128, :])
                        P.dve("tensor_copy", R=[stk], W=["Vc"], out=Vc[:, kb, :], in_=st_)
                    qa, qb = q * DS, (q + 1) * DS
                    for hp in range(4):
                        blocks = [(kT_new[:, hp, qa:qb], Vn[q][0:DS, hp * 128:(hp + 1) * 128], DS, qa, True)]
                        for kb in range(npast - 1, -1, -1):
                            blocks.append((kT[:, hp, kb * 128:(kb + 1) * 128], Vc[:, kb, hp * 128:(hp + 1) * 128], 128,
                                           qa, False))
                        attention(hp, qa, qb, blocks)
                        finalize(hp, qa, qb, dzs[:, hp, :], "dzs")
            P.dma(R=["catT"], W=[("catD", ti)], out=catD[:, t.row0:t.row0 + T].rearrange("(c p) t -> p c t", p=128),
                  in_=E.catT[:, :, 0:T], is_output=cfg.debug)
        P.barrier()
        AR.off = base_off

    if 3 in cfg.passes:
        E = common_bufs(n_h=2, n_tmp=6)
        E.catT2 = [E.catT, AR.alloc([128, 8, 512], BF16)]
        E.pbuf = [AR.alloc([128, 4, PLE]) for _ in range(2)]
        E.pT = AR.alloc([128, 2, 512], BF16)
        E.xnp = [AR.alloc([128, PLE], BF16) for _ in range(2)]
        E.w_out = AR.alloc([128, 8, D], BF16)
        E.w_gate = AR.alloc([128, 8, D], BF16)
        E.w_proj = AR.alloc([128, 2, D], BF16)
        fg_bc = AR.alloc([128, D])
        print("P3 SBUF bytes/partition:", AR.off * 4)
        load_weight(E, E.w_out, "w_out", cd_w_out, 8, D)
        load_weight(E, E.w_gate, "w_gate", ple_gate[1], 8, D)
        load_weight(E, E.w_proj, "w_proj", ple_proj[1], 2, D)
        P.dma(W=["fg_bc"], out=fg_bc, in_=final_g.partition_broadcast(128))

        def p3_load(ti):
            t = tiles[ti]
            par = ti % 2
            P.dma(W=[(("hbuf", par), s_) for s_ in range(4)], R=[("hA", ti)], out=E.hbuf[par][:, 0:t.NT, :],
                  in_=hA[t.row0:t.row0 + t.T, :].rearrange("(n p) d -> p n d", p=128))
            P.dma(W=[("pbuf", par)], out=E.pbuf[par][:, 0:t.NT, :],
                  in_=p_rows(t, 1).rearrange("(n p) d -> p n d", p=128))
            P.dma(W=[(("catT", par), s_) for s_ in range(4)], R=[("catD", ti)], out=E.catT2[par][:, :, 0:t.T],
                  in_=catD[:, t.row0:t.row0 + t.T].rearrange("(c p) t -> p c t", p=128))

        p3_load(0)
        for ti, t in enumerate(tiles):
            T, NT = t.T, t.NT
            par = ti % 2
            hb, hkey = E.hbuf[par], ("hbuf", par)
            pb, pkey = E.pbuf[par], ("pbuf", par)
            if ti + 1 < len(tiles):
                p3_load(ti + 1)
            E.catT = E.catT2[par]
            tail(E, t, hb, hkey, pb, pkey, catkey=("catT", par))
            P.pool("memset", W=["ss"], ap=E.ss, constant=0.0)
            for s in range(NT):
                P.act("activation", R=[(hkey, s)], W=[("xnb", s % 2), "ss"], out=E.xnb[s % 2], in_=hb[:, s, :],
                      func=AF.Square, accum_out=E.ss[:, s:s + 1])
            P.act("activation", R=["ss"], W=["rstd"], out=E.rstd[:, 0:NT], in_=E.ss[:, 0:NT], func=AF.Ln,
                  scale=1.0 / D, bias=EPS)
            P.act("activation", R=["rstd"], W=["rstd"], out=E.rstd[:, 0:NT], in_=E.rstd[:, 0:NT], func=AF.Exp,
                  scale=-0.5)
            for s in range(NT):
                P.dve("scalar_tensor_tensor", R=[(hkey, s), "rstd", "fg_bc"], W=[(hkey, s)], out=hb[:, s, :], in0=hb[:, s, :],
                      scalar=E.rstd[:, s:s + 1], in1=fg_bc, op0=ALU.mult, op1=ALU.mult)
            dst = y_p[t.row0:t.row0 + T, :] if t.kind == "p" else y_s
            P.dma(R=[(hkey, s_) for s_ in range(4)], out=dst.rearrange("(n p) d -> p n d", p=128), in_=hb[:, 0:NT, :],
                  is_output=True)
        P.barrier()
        AR.off = base_off

    P.emit()
    global LAST_PROG
    LAST_PROG = P
    return nc


LAST_PROG = None


def shard_inputs(inp, cfg, n_cores):
    NP, S, NS, DS, PL = cfg.NP, cfg.S, cfg.NS, cfg.DS, cfg.PL
    f = lambda a: np.ascontiguousarray(np.asarray(a, dtype=np.float32))
    maps = []
    for c in range(n_cores):
        ps = slice(c * NP, (c + 1) * NP)
        ss = slice(c * NS, (c + 1) * NS)
        m = {
            "x_prompt": f(inp["x_prompt"][ps]).reshape(NP * S, D),
            "x_sample": f(inp["x_sample"][ss]).reshape(NS * DS, D),
            "state_a_conv": f(inp["state_a_conv"][0, ss]),
            "state_b_conv": f(inp["state_b_conv"][0, ss]),
            "cache_d_k": f(inp["cache_d_k"][0, ss]).reshape(NS, PL, DB),
            "cache_d_v": f(inp["cache_d_v"][0, ss]).reshape(NS, PL, DB),
            "p_prompt": f(inp["p_prompt"][:, ps]).reshape(2, NP * S, PLE),
            "p_sample": f(inp["p_sample"][:, ss]).reshape(2, NS * DS, PLE),
            "norm_g": f(inp["norm_g"]),
            "ple_gate": f(inp["ple_gate"]),
            "ple_proj": f(inp["ple_proj"]),
            "ab_w_in": f(inp["ab_w_in"][0]),
            "a_conv_w": f(inp["a_conv_w"][0]),
            "b_conv_w": f(inp["b_conv_w"][0]),
            "b_ln_g": f(inp["b_ln_g"][0]),
            "b_ln_b": f(inp["b_ln_b"][0]),
            "ab_w_out": f(inp["ab_w_out"][0]),
            "cd_w_in": f(inp["cd_w_in"][0]),
            "c_ln_g": f(inp["c_ln_g"][0]),
            "c_ln_b": f(inp["c_ln_b"][0]),
            "c_ws": f(inp["c_ws"][0]),
            "c_b": f(inp["c_b"][0]),
            "cd_w_out": f(inp["cd_w_out"][0]),
            "final_g": f(inp["final_g"]),
        }
        maps.append(m)
    return maps


def kernel(**inputs):
    n_cores = 8
    cfg = Cfg(NP=2, S=4096, NS=4, DS=32, PL=2048)
    nc = build_program(cfg)
    in_maps = shard_inputs(inputs, cfg, n_cores)
    res = run_bass_kernel_spmd(nc, in_maps, core_ids=list(range(n_cores)))
    r = res.results
    NP, S, NS, DS = cfg.NP, cfg.S, cfg.NS, cfg.DS
    cat = lambda name, shp: np.concatenate([np.asarray(r[c][name], dtype=np.float32).reshape(shp) for c in range(n_cores)], axis=0)
    y_prompt = cat("y_prompt", (NP, S, D))
    y_sample = cat("y_sample", (NS, DS, D))
    na_p = cat("new_a_prompt", (NP, HA, DB))[None]
    na_s = cat("new_a_sample", (NS, HA, DB))[None]
    nb_p = cat("new_b_prompt", (NP, HB, DB))[None]
    nb_s = cat("new_b_sample", (NS, HB, DB))[None]
    ncv = cat("new_cv_sample", (NS, DS, DB))[None]
    nk_p = cat("new_k_prompt", (NP, S, 8, 64))[None]
    nv_p = cat("new_v_prompt", (NP, S, 8, 64))[None]
    nk_s = cat("new_k_sample", (NS, DS, 8, 64))[None]
    nv_s = cat("new_v_sample", (NS, DS, 8, 64))[None]
    return (y_prompt, y_sample, na_p, na_s, nb_p, nb_s, ncv, nk_p, nv_p, nk_s, nv_s)
```

```python
import contextlib
import numpy as np
import concourse.bass as bass
import concourse.mybir as mybir
from concourse.bass_utils import run_bass_kernel_spmd

F32 = mybir.dt.float32
BF16 = mybir.dt.bfloat16
AF = mybir.ActivationFunctionType
ALU = mybir.AluOpType

ENGINES = ["pe", "act", "dve", "pool", "sp"]
D = 1024
DB = 512
PLE = 256
EPS = 1e-6
HB = 30
HA = 2


class Prog:
    EPOCH = 12000

    def __init__(self, nc, n_dma_slots=48):
        self.nc = nc
        self.ops = {e: [] for e in ENGINES}
        self.cnt = {e: 0 for e in ENGINES}
        self.last_w = {}
        self.readers = {}
        self.seen = {e: {} for e in ENGINES}
        self.n_dma_slots = n_dma_slots
        self.dma_next = 0
        self.dma_val = [0] * n_dma_slots
        self.out_tokens = []
        self.bank_i = 0

    def _need(self, eng, tok, waits):
        if tok is None:
            return
        if tok[0] == "E":
            _, e2, idx = tok
            if e2 == eng and eng == "pe":
                return
            k = ("E", e2)
        else:
            _, slot, idx = tok
            k = ("D", slot)
        if self.seen[eng].get(k, -1) >= idx:
            return
        waits[k] = max(waits.get(k, -1), idx)

    def op(self, eng, name, R=(), W=(), dma=False, is_output=False, **kw):
        ps_r = [k for k in R if isinstance(k, tuple) and k and k[0] == "ps" and k not in W]
        if ps_r:
            R = [k for k in R if k not in ps_r]
            W = list(W) + ps_r
        waits = {}
        for k in R:
            self._need(eng, self.last_w.get(k), waits)
        for k in W:
            self._need(eng, self.last_w.get(k), waits)
            for t in self.readers.get(k, ()):
                self._need(eng, t, waits)
        if dma:
            slot = self.dma_next
            self.dma_next = (self.dma_next + 1) % self.n_dma_slots
            prev = self.dma_val[slot]
            if prev > 0:
                self._need(eng, ("D", slot, prev), waits)
            self.dma_val[slot] = prev + 16
            tok = ("D", slot, prev + 16)
        else:
            idx = self.cnt[eng]
            self.cnt[eng] += 1
            tok = ("E", eng, idx)
        for k, v in waits.items():
            self.seen[eng][k] = v
        self.ops[eng].append((name, kw, waits, tok))
        for k in W:
            self.last_w[k] = tok
            self.readers[k] = []
        for k in R:
            if k in W:
                continue
            self.readers.setdefault(k, []).append(tok)
        if is_output:
            self.out_tokens.append(tok)
        return tok

    def pe(self, name, **kw):
        return self.op("pe", name, **kw)

    def act(self, name, **kw):
        return self.op("act", name, **kw)

    def dve(self, name, **kw):
        return self.op("dve", name, **kw)

    def pool(self, name, **kw):
        return self.op("pool", name, **kw)

    def dma(self, **kw):
        return self.op("sp", "dma_start", dma=True, **kw)

    def barrier(self):
        for eng in ENGINES:
            waits = {}
            for e2 in ENGINES:
                if e2 != eng and self.cnt[e2] > 0:
                    self._need(eng, ("E", e2, self.cnt[e2] - 1), waits)
            for slot in range(self.n_dma_slots):
                if self.dma_val[slot] > 0:
                    self._need(eng, ("D", slot, self.dma_val[slot]), waits)
            for k, v in waits.items():
                self.seen[eng][k] = v
            self.ops[eng].append((None, None, waits, None))

    def emit(self):
        nc = self.nc
        with contextlib.ExitStack() as st:
            esem = {}
            for e in ENGINES:
                n_ep = (self.cnt[e] + self.EPOCH - 1) // self.EPOCH
                esem[e] = [st.enter_context(nc.semaphore(f"s_{e}{i}")) for i in range(max(n_ep, 1))]
            dsem = [st.enter_context(nc.semaphore(f"s_d{i}")) for i in range(self.n_dma_slots)]
            block = st.enter_context(nc.Block())

            def do_wait(h, k, v):
                if k[0] == "E":
                    h.wait_ge(esem[k[1]][v // self.EPOCH], v % self.EPOCH + 1)
                else:
                    h.wait_ge(dsem[k[1]], v)

            def run(ename):
                def body(h):
                    for name, kw, waits, tok in self.ops[ename]:
                        ws = list(waits.items())
                        if name is None:
                            for k, v in ws:
                                do_wait(h, k, v)
                            continue
                        for k, v in ws[1:]:
                            do_wait(h, k, v)
                        ins = getattr(h, name)(**kw)
                        if ws:
                            k, v = ws[0]
                            if k[0] == "E":
                                ins._wait_ge(esem[k[1]][v // self.EPOCH], v % self.EPOCH + 1)
                            else:
                                ins._wait_ge(dsem[k[1]], v)
                        if tok[0] == "E":
                            ins.then_inc(esem[tok[1]][tok[2] // self.EPOCH], 1)
                        else:
                            ins.then_inc(dsem[tok[1]], 16)
                    if ename == "sp":
                        for tok in self.out_tokens:
                            do_wait(h, ("D", tok[1]), tok[2])
                return body

            block.tensor(run("pe"))
            block.scalar(run("act"))
            block.vector(run("dve"))
            block.gpsimd(run("pool"))
            block.sync(run("sp"))


class Arena:
    def __init__(self, nc, nbytes):
        self.t = nc.alloc_sbuf_tensor("arena", [128, nbytes // 4], F32)
        self.off = 0
        self.cap = nbytes // 4

    def alloc(self, shape, dt=F32):
        n = int(np.prod(shape[1:]))
        nw = (n if dt == F32 else (n + 1) // 2)
        nw = (nw + 7) // 8 * 8
        assert self.off + nw <= self.cap, f"SBUF arena overflow: need {(self.off + nw) * 4} B"
        ap = self.t[0:shape[0], self.off:self.off + nw]
        self.off += nw
        if dt != F32:
            ap = ap.bitcast(dt)
        ap = ap[:, 0:n]
        if len(shape) == 3:
            ap = ap.rearrange("p (a b) -> p a b", a=shape[1])
        elif len(shape) == 4:
            ap = ap.rearrange("p (a b c) -> p a b c", a=shape[1], b=shape[2])
        return ap


class Cfg:
    def __init__(self, NP=2, S=4096, NS=4, DS=32, PL=2048, passes=(1, 2, 3), debug=False, stop=0):
        self.NP, self.S, self.NS, self.DS, self.PL = NP, S, NS, DS, PL
        self.passes = passes
        self.debug = debug
        self.stop = stop
        assert NS * DS == 128 and S % 512 == 0 and PL % 128 == 0
        self.NTOK = NP * S + 128


class Tile:
    def __init__(self, kind, seq, t0, T, nseg, L, row0, last):
        self.kind, self.seq, self.t0, self.T, self.nseg, self.L = kind, seq, t0, T, nseg, L
        self.row0 = row0
        self.last = last
        self.NT = T // 128


def make_tiles(cfg):
    tiles = []
    for q in range(cfg.NP):
        n = cfg.S // 512
        for i in range(n):
            tiles.append(Tile("p", q, i * 512, 512, 1, 512, q * cfg.S + i * 512, i == n - 1))
    tiles.append(Tile("s", 0, 0, 128, cfg.NS, cfg.DS, cfg.NP * cfg.S, True))
    return tiles


def build_program(cfg):
    nc = bass.Bass("TRN2", target_bir_lowering=False)
    NP, S, NS, DS, PL = cfg.NP, cfg.S, cfg.NS, cfg.DS, cfg.PL
    NTOK = cfg.NTOK

    def din(name, shape):
        return nc.dram_tensor(name, list(shape), F32, kind="ExternalInput").ap()

    def dout(name, shape):
        return nc.dram_tensor(name, list(shape), F32, kind="ExternalOutput").ap()

    x_p = din("x_prompt", [NP * S, D])
    x_s = din("x_sample", [128, D])
    st_a = din("state_a_conv", [NS, HA, DB])
    st_b = din("state_b_conv", [NS, HB, DB])
    ck = din("cache_d_k", [NS, PL, DB])
    cv = din("cache_d_v", [NS, PL, DB])
    p_p = din("p_prompt", [2, NP * S, PLE])
    p_s = din("p_sample", [2, 128, PLE])
    norm_g = din("norm_g", [2, D])
    ple_gate = din("ple_gate", [2, D, D])
    ple_proj = din("ple_proj", [2, PLE, D])
    ab_w_in = din("ab_w_in", [D, 7 * DB])
    a_conv_w = din("a_conv_w", [3, DB])
    b_conv_w = din("b_conv_w", [31, DB])
    b_ln_g = din("b_ln_g", [DB])
    b_ln_b = din("b_ln_b", [DB])
    ab_w_out = din("ab_w_out", [D, D])
    cd_w_in = din("cd_w_in", [D, 7 * DB])
    c_ln_g = din("c_ln_g", [DB])
    c_ln_b = din("c_ln_b", [DB])
    c_ws = din("c_ws", [4, 128, 128])
    c_b = din("c_b", [4, 128])
    cd_w_out = din("cd_w_out", [D, D])
    final_g = din("final_g", [D])

    y_p = dout("y_prompt", [NP * S, D])
    y_s = dout("y_sample", [128, D])
    na_p = dout("new_a_prompt", [NP, HA, DB])
    na_s = dout("new_a_sample", [NS, HA, DB])
    nb_p = dout("new_b_prompt", [NP, HB, DB])
    nb_s = dout("new_b_sample", [NS, HB, DB])
    ncv_s = dout("new_cv_sample", [128, DB])
    nk_p = dout("new_k_prompt", [NP * S, DB])
    nv_p = dout("new_v_prompt", [NP * S, DB])
    nk_s = dout("new_k_sample", [128, DB])
    nv_s = dout("new_v_sample", [128, DB])

    kind_scr = "ExternalOutput" if cfg.debug else "Internal"
    hA = nc.dram_tensor("hA", [NTOK, D], F32, kind=kind_scr).ap()
    catD = nc.dram_tensor("catD", [D, NTOK], BF16, kind=kind_scr).ap()

    P = Prog(nc)
    tiles = make_tiles(cfg)
    AR = Arena(nc, 207 * 1024)

    psf = [nc.alloc_psum_tensor(f"ps{i}", [128, 512], F32)[:] for i in range(8)]
    psb = [p.bitcast(BF16) for p in psf]

    def bank():
        b = P.bank_i
        P.bank_i = (P.bank_i + 1) % 6
        return b

    def PS(b):
        return ("ps", b)

    identf = AR.alloc([128, 128])
    identb = AR.alloc([128, 128], BF16)
    onesf = AR.alloc([128, 128])
    ngcol = AR.alloc([128, 2, 8])
    P.pool("memset", W=["identf"], ap=identf, constant=0.0)
    P.pool("affine_select", R=["identf"], W=["identf"], out=identf, in_=identf, pattern=[[-1, 128]],
           compare_op=ALU.not_equal, fill=1.0, base=0, channel_multiplier=1)
    P.dve("tensor_copy", R=["identf"], W=["identb"], out=identb, in_=identf)
    P.pool("memset", W=["onesf"], ap=onesf, constant=1.0)
    import os as _os
    for _i in range(int(_os.environ.get('KDUMMY', '0'))):
        P.dve("tensor_copy", R=["identf"], W=["identb"], out=identb, in_=identf)
    for i in range(2):
        P.dma(W=["ngcol"], out=ngcol[:, i, :], in_=norm_g[i].rearrange("(kc p) -> p kc", p=128),
              allow_slow_non_contiguous=True)
    base_off = AR.off

    def x_rows(t):
        return x_p[t.row0:t.row0 + t.T, :] if t.kind == "p" else x_s

    def p_rows(t, layer):
        return p_p[layer, t.row0:t.row0 + t.T, :] if t.kind == "p" else p_s[layer]

    class Env:
        pass

    def common_bufs(n_h=2, n_tmp=6):
        E = Env()
        E.hbuf = [AR.alloc([128, 4, D]) for _ in range(n_h)]
        E.xnb = [AR.alloc([128, D], BF16) for _ in range(2)]
        E.actT = AR.alloc([128, 8, 512], BF16)
        E.catT = AR.alloc([128, 8, 512], BF16)
        E.ss = AR.alloc([128, 4])
        E.rstd = AR.alloc([128, 4])
        E.ftmp = [AR.alloc([128, 512]) for _ in range(n_tmp)]
        E.ftmp_i = 0
        E.cast_i = 0
        return E

    def tmp(E):
        i = E.ftmp_i
        E.ftmp_i = (i + 1) % len(E.ftmp)
        return E.ftmp[i], ("ftmp", i)

    def load_weight(E, dst, dkey, src, nk, ncols, gain_i=None):
        for kc in range(nk):
            for c0 in range(0, ncols, 512):
                st_, sk = tmp(E)
                P.dma(W=[sk], out=st_, in_=src[kc * 128:(kc + 1) * 128, c0:c0 + 512])
                if gain_i is not None:
                    P.act("activation", R=[sk, "ngcol"], W=[dkey], out=dst[:, kc, c0:c0 + 512], in_=st_, func=AF.Copy,
                          scale=ngcol[:, gain_i, kc:kc + 1])
                elif E.cast_i % 2 == 0:
                    P.act("activation", R=[sk], W=[dkey], out=dst[:, kc, c0:c0 + 512], in_=st_, func=AF.Copy)
                else:
                    P.dve("tensor_copy", R=[sk], W=[dkey], out=dst[:, kc, c0:c0 + 512], in_=st_)
                E.cast_i += 1

    def transpose_into(src, skey, nchunk, dstT, dkey, s):
        b = bank()
        for c in range(nchunk):
            P.pe("transpose", R=[skey, "identb"], W=[PS(b)], out=psb[b][:, c * 128:(c + 1) * 128],
                 in_=src[:, c * 128:(c + 1) * 128], identity=identb)
        P.dve("tensor_copy", R=[PS(b)], W=[dkey], out=dstT[:, 0:nchunk, s * 128:(s + 1) * 128],
              in_=psb[b][:, 0:nchunk * 128].rearrange("p (c t) -> p c t", c=nchunk))

    def rmsnorm_to_T(E, t, hb, hkey):
        NT = t.NT
        P.pool("memset", W=["ss"], ap=E.ss, constant=0.0)
        for s in range(NT):
            P.act("activation", R=[(hkey, s)], W=[("xnb", s % 2), "ss"], out=E.xnb[s % 2], in_=hb[:, s, :],
                  func=AF.Square, accum_out=E.ss[:, s:s + 1])
        P.act("activation", R=["ss"], W=["rstd"], out=E.rstd[:, 0:NT], in_=E.ss[:, 0:NT], func=AF.Ln, scale=1.0 / D,
              bias=EPS)
        P.act("activation", R=["rstd"], W=["rstd"], out=E.rstd[:, 0:NT], in_=E.rstd[:, 0:NT], func=AF.Exp, scale=-0.5)
        for s in range(NT):
            P.act("activation", R=[(hkey, s), "rstd"], W=[("xnb", s % 2)], out=E.xnb[s % 2], in_=hb[:, s, :],
                  func=AF.Copy, scale=E.rstd[:, s:s + 1])
            transpose_into(E.xnb[s % 2], ("xnb", s % 2), 8, E.actT, ("actT", s), s)

    def proj_fm(E, b, w, oc, T):
        for kc in range(8):
            P.pe("matmul", R=[("actT", s_) for s_ in range(T // 128)] + ["w_big"], W=[PS(b)], out=psf[b][:, 0:T],
                 lhsT=w[:, kc, oc * 128:(oc + 1) * 128], rhs=E.actT[:, kc, 0:T], start=(kc == 0), stop=(kc == 7))

    def sigmoid_from(E, src_ap, skey, T, scale=-1.0, bias=None, extra=()):
        tt, tk = tmp(E)
        kw = {} if bias is None else {"bias": bias}
        P.act("activation", R=[skey] + list(extra), W=[tk], out=tt[:, 0:T], in_=src_ap, func=AF.Exp, scale=scale, **kw)
        P.act("activation", R=[tk], W=[tk], out=tt[:, 0:T], in_=tt[:, 0:T], func=AF.Ln, bias=1.0)
        P.act("activation", R=[tk], W=[tk], out=tt[:, 0:T], in_=tt[:, 0:T], func=AF.Exp, scale=-1.0)
        return tt, tk

    def sigmoid_multi(E, srcs, T):
        outs = []
        for (ap, key, scale, bias, extra) in srcs:
            tt, tk = tmp(E)
            kw = {} if bias is None else {"bias": bias}
            P.act("activation", R=[key] + list(extra), W=[tk], out=tt[:, 0:T], in_=ap, func=AF.Exp, scale=scale, **kw)
            outs.append((tt, tk))
        for (tt, tk) in outs:
            P.act("activation", R=[tk], W=[tk], out=tt[:, 0:T], in_=tt[:, 0:T], func=AF.Ln, bias=1.0)
        for (tt, tk) in outs:
            P.act("activation", R=[tk], W=[tk], out=tt[:, 0:T], in_=tt[:, 0:T], func=AF.Exp, scale=-1.0)
        return outs

    def tail(E, t, hb, hkey, pb, pkey, catkey="catT"):
        for _ in tail_gen(E, t, hb, hkey, pb, pkey, catkey):
            pass

    def tail_gen(E, t, hb, hkey, pb, pkey, catkey="catT"):
        NT = t.NT
        halves = [slice(0, 512), slice(512, 1024)]
        for s in range(NT):
            sl = slice(s * 128, (s + 1) * 128)
            for hs in halves:
                b = bank()
                for kc in range(8):
                    P.pe("matmul", R=[(catkey, s), "w_out"], W=[PS(b)], out=psf[b], lhsT=E.catT[:, kc, sl],
                         rhs=E.w_out[:, kc, hs], start=(kc == 0), stop=(kc == 7))
                P.dve("tensor_tensor", R=[PS(b), (hkey, s)], W=[(hkey, s)], out=hb[:, s, hs], in0=hb[:, s, hs],
                      in1=psf[b], op=ALU.add)
                yield
        for s in range(NT):
            xi = s % 2
            P.act("activation", R=[(hkey, s)], W=[("xnb", xi)], out=E.xnb[xi], in_=hb[:, s, :], func=AF.Copy)
            transpose_into(E.xnb[xi], ("xnb", xi), 8, E.catT, (catkey, s), s)
            P.act("activation", R=[pkey], W=[("xnp", xi)], out=E.xnp[xi], in_=pb[:, s, :], func=AF.Copy)
            transpose_into(E.xnp[xi], ("xnp", xi), 2, E.pT, ("pT", s), s)
            yield
        for s in range(NT):
            sl = slice(s * 128, (s + 1) * 128)
            bgs, bps = [], []
            for hs in halves:
                bg = bank()
                for kc in range(8):
                    P.pe("matmul", R=[(catkey, s), "w_gate"], W=[PS(bg)], out=psf[bg], lhsT=E.catT[:, kc, sl],
                         rhs=E.w_gate[:, kc, hs], start=(kc == 0), stop=(kc == 7))
                bp = bank()
                for kc in range(2):
                    P.pe("matmul", R=[("pT", s), "w_proj"], W=[PS(bp)], out=psf[bp], lhsT=E.pT[:, kc, sl],
                         rhs=E.w_proj[:, kc, hs], start=(kc == 0), stop=(kc == 1))
                bgs.append(bg)
                bps.append(bp)
            yield
            sgs = sigmoid_multi(E, [(psf[bg], PS(bg), -1.0, None, ()) for bg in bgs], 512)
            yield
            for (sg, sgk), bp in zip(sgs, bps):
                P.dve("tensor_tensor", R=[PS(bp), sgk], W=[sgk], out=sg, in0=psf[bp], in1=sg, op=ALU.mult)
            for (sg, sgk), hs in zip(sgs, halves):
                P.dve("tensor_tensor", R=[sgk, (hkey, s)], W=[(hkey, s)], out=hb[:, s, hs], in0=hb[:, s, hs], in1=sg,
                      op=ALU.add)
            yield

    def state_out(E, t, bufs, key, H, L, dst):
        for q in range(t.nseg):
            b = bank()
            for j in range(4):
                P.pe("transpose", R=[(key, j), "identf"], W=[PS(b)], out=psf[b][0:H, j * 128:(j + 1) * 128],
                     in_=bufs[j][:, q, L:L + H], identity=identf)
            P.act("activation", R=[PS(b)], W=["hst"], out=E.hst[0:H, :], in_=psf[b][0:H, :], func=AF.Copy)
            P.dma(R=["hst"], out=dst[t.seq if t.kind == "p" else q], in_=E.hst[0:H, :], is_output=True)

    if 1 in cfg.passes:
        E = common_bufs()
        E.pbuf = [AR.alloc([128, 4, PLE]) for _ in range(2)]
        E.pT = AR.alloc([128, 2, 512], BF16)
        E.xnp = [AR.alloc([128, PLE], BF16) for _ in range(2)]
        E.w_big = AR.alloc([128, 8, 7 * DB], BF16)
        E.w_out = AR.alloc([128, 8, D], BF16)
        E.w_gate = AR.alloc([128, 8, D], BF16)
        E.w_proj = AR.alloc([128, 2, D], BF16)
        awc = AR.alloc([128, 4, 3])
        bwc = AR.alloc([128, 4, 31])
        lncol = AR.alloc([128, 4, 4])
        ubuf_p = [AR.alloc([128, 1, HA + 512]) for j in range(4)]
        gbuf_p = [AR.alloc([128, 1, HB + 512]) for j in range(4)]
        ubuf_s = [AR.alloc([128, NS, HA + DS]) for j in range(4)]
        gbuf_s = [AR.alloc([128, NS, HB + DS]) for j in range(4)]
        bconv = [AR.alloc([128, 512]) for j in range(4)]
        negmean = AR.alloc([128, 512])
        rstdB = AR.alloc([128, 512])
        E.hst = AR.alloc([32, 512])
        print("P1 SBUF bytes/partition:", AR.off * 4)

        load_weight(E, E.w_big, "w_big", ab_w_in, 8, 7 * DB, gain_i=0)
        load_weight(E, E.w_out, "w_out", ab_w_out, 8, D)
        load_weight(E, E.w_gate, "w_gate", ple_gate[0], 8, D)
        load_weight(E, E.w_proj, "w_proj", ple_proj[0], 2, D)
        for j in range(4):
            P.dma(W=["awc"], out=awc[:, j, :], in_=a_conv_w[:, j * 128:(j + 1) * 128].rearrange("w p -> p w"),
                  allow_slow_non_contiguous=True)
            P.dma(W=["bwc"], out=bwc[:, j, :], in_=b_conv_w[:, j * 128:(j + 1) * 128].rearrange("w p -> p w"),
                  allow_slow_non_contiguous=True)
        P.dma(W=["lncol"], out=lncol[:, :, 0], in_=b_ln_g.rearrange("(j p) -> p j", p=128),
              allow_slow_non_contiguous=True)
        P.dma(W=["lncol"], out=lncol[:, :, 1], in_=b_ln_b.rearrange("(j p) -> p j", p=128),
              allow_slow_non_contiguous=True)
        P.pool("tensor_scalar", R=["lncol"], W=["lncol"], out=lncol[:, :, 2:4], in0=lncol[:, :, 0:2], scalar1=-1.0,
               scalar2=None, op0=ALU.mult)

        def p1_load(ti):
            t = tiles[ti]
            par = ti % 2
            P.dma(W=[(("hbuf", par), s_) for s_ in range(4)], out=E.hbuf[par][:, 0:t.NT, :],
                  in_=x_rows(t).rearrange("(n p) d -> p n d", p=128))
            P.dma(W=[("pbuf", par)], out=E.pbuf[par][:, 0:t.NT, :],
                  in_=p_rows(t, 0).rearrange("(n p) d -> p n d", p=128))

        CATK = [("catT", s_) for s_ in range(4)]
        for ti, t in enumerate(tiles):
            T, NT, nseg, L = t.T, t.NT, t.nseg, t.L
            par = ti % 2
            hb, hkey = E.hbuf[par], ("hbuf", par)
            pb, pkey = E.pbuf[par], ("pbuf", par)
            isp = t.kind == "p"
            ub, gb = (ubuf_p, gbuf_p) if isp else (ubuf_s, gbuf_s)
            ukey, gkey = ("ubuf_p", "gbuf_p") if isp else ("ubuf_s", "gbuf_s")

            def v3(ap):
                return ap.rearrange("p (n l) -> p n l", n=nseg)

            if ti == 0:
                p1_load(0)
            if isp and t.t0 == 0:
                for j in range(4):
                    P.pool("memset", W=[(ukey, j)], ap=ub[j][:, :, 0:HA], constant=0.0)
                    P.pool("memset", W=[(gkey, j)], ap=gb[j][:, :, 0:HB], constant=0.0)
            if not isp:
                for q in range(NS):
                    for (stt, H, bufs, key) in ((st_a, HA, ub, ukey), (st_b, HB, gb, gkey)):
                        P.dma(W=["hst"], out=E.hst[0:H, :], in_=stt[q])
                        b = bank()
                        for j in range(4):
                            P.pe("transpose", R=["hst", "identf"], W=[PS(b)], out=psf[b][:, j * 32:j * 32 + H],
                                 in_=E.hst[0:H, j * 128:(j + 1) * 128], identity=identf[0:H, 0:H])
                        for j in range(4):
                            P.act("activation", R=[PS(b)], W=[(key, j)], out=bufs[j][:, q, 0:H],
                                  in_=psf[b][:, j * 32:j * 32 + H], func=AF.Copy)

            if ti == 0:
                rmsnorm_to_T(E, t, hb, hkey)

            for pair in ((0, 1), (2, 3)):
                bvs, bgs = {}, {}
                for j in pair:
                    bvs[j], bgs[j] = bank(), bank()
                    proj_fm(E, bvs[j], E.w_big, 16 + j, T)
                    proj_fm(E, bgs[j], E.w_big, 20 + j, T)
                sgs = sigmoid_multi(E, [(psf[bgs[j]][:, 0:T], PS(bgs[j]), -1.0, None, ()) for j in pair], T)
                for (sg, sgk), j in zip(sgs, pair):
                    P.dve("tensor_tensor", R=[PS(bvs[j]), sgk], W=[(gkey, j)], out=gb[j][:, :, HB:HB + L],
                          in0=v3(psf[bvs[j]][:, 0:T]), in1=v3(sg[:, 0:T]), op=ALU.mult)
            def gen_A():
                for pair in ((0, 1), (2, 3)):
                    bxs, bcs = {}, {}
                    for j in pair:
                        bxs[j], bcs[j] = bank(), bank()
                        proj_fm(E, bxs[j], E.w_big, 0 + j, T)
                        proj_fm(E, bcs[j], E.w_big, 4 + j, T)
                    yield
                    tcs = {}
                    for j in pair:
                        tcs[j] = tmp(E)
                        P.act("activation", R=[PS(bcs[j])], W=[tcs[j][1]], out=tcs[j][0][:, 0:T], in_=psf[bcs[j]][:, 0:T],
                              func=AF.Copy)
                    for j in pair:
                        P.dve("tensor_tensor", R=[PS(bxs[j]), tcs[j][1]], W=[(ukey, j)], out=ub[j][:, :, HA:HA + L],
                              in0=v3(psf[bxs[j]][:, 0:T]), in1=v3(tcs[j][0][:, 0:T]), op=ALU.mult)
                    yield
                    bzs = {}
                    for j in pair:
                        bzs[j] = bank()
                        proj_fm(E, bzs[j], E.w_big, 12 + j, T)
                    yield
                    sgs = sigmoid_multi(E, [(psf[bzs[j]][:, 0:T], PS(bzs[j]), -1.0, None, ()) for j in pair], T)
                    yield
                    cvs = {}
                    for j in pair:
                        cvs[j] = tmp(E)
                        P.dve("tensor_scalar", R=[(ukey, j), "awc"], W=[cvs[j][1]], out=v3(cvs[j][0][:, 0:T]),
                              in0=ub[j][:, :, 2:2 + L], scalar1=awc[:, j, 2:3], scalar2=None, op0=ALU.mult)
                    yield
                    for w in (1, 0):
                        for j in pair:
                            P.dve("scalar_tensor_tensor", R=[(ukey, j), "awc", cvs[j][1]], W=[cvs[j][1]],
                                  out=v3(cvs[j][0][:, 0:T]), in0=ub[j][:, :, w:w + L], scalar=awc[:, j, w:w + 1],
                                  in1=v3(cvs[j][0][:, 0:T]), op0=ALU.mult, op1=ALU.add)
                    yield
                    bbs = {}
                    for j in pair:
                        bbs[j] = bank()
                        proj_fm(E, bbs[j], E.w_big, 8 + j, T)
                    yield
                    for (sg, sgk), j in zip(sgs, pair):
                        P.dve("tensor_tensor", R=[PS(bzs[j]), sgk], W=[sgk], out=sg[:, 0:T], in0=psf[bzs[j]][:, 0:T],
                              in1=sg[:, 0:T], op=ALU.mult)
                    for j in pair:
                        P.dve("tensor_tensor", R=[PS(bbs[j]), cvs[j][1]], W=[cvs[j][1]], out=cvs[j][0][:, 0:T],
                              in0=psf[bbs[j]][:, 0:T], in1=cvs[j][0][:, 0:T], op=ALU.mult)
                    yield
                    for (sg, sgk), j in zip(sgs, pair):
                        P.dve("tensor_tensor", R=[sgk, cvs[j][1]], W=CATK, out=E.catT[:, j, 0:T], in0=sg[:, 0:T],
                              in1=cvs[j][0][:, 0:T], op=ALU.mult)
                    yield
            def gen_bz():
                for pair in ((0, 1), (2, 3)):
                    bzs = {}
                    for j in pair:
                        bzs[j] = bank()
                        proj_fm(E, bzs[j], E.w_big, 24 + j, T)
                    yield
                    sgs = sigmoid_multi(E, [(psf[bzs[j]][:, 0:T], PS(bzs[j]), -1.0, None, ()) for j in pair], T)
                    yield
                    for (sg, sgk), j in zip(sgs, pair):
                        P.dve("tensor_tensor", R=[PS(bzs[j]), sgk], W=CATK, out=E.catT[:, 4 + j, 0:T],
                              in0=psf[bzs[j]][:, 0:T], in1=sg[:, 0:T], op=ALU.mult)
                    yield
            def gen_conv():
                for w in range(31):
                    for j in range(4):
                        if w == 0:
                            P.dve("tensor_scalar", R=[(gkey, j), "bwc"], W=[("bconv", j)], out=v3(bconv[j][:, 0:T]),
                                  in0=gb[j][:, :, 0:L], scalar1=bwc[:, j, 0:1], scalar2=None, op0=ALU.mult)
                        else:
                            P.dve("scalar_tensor_tensor", R=[(gkey, j), "bwc", ("bconv", j)], W=[("bconv", j)],
                                  out=v3(bconv[j][:, 0:T]), in0=gb[j][:, :, w:w + L], scalar=bwc[:, j, w:w + 1],
                                  in1=v3(bconv[j][:, 0:T]), op0=ALU.mult, op1=ALU.add)
                    yield
            def chain(*gs):
                for g in gs:
                    yield from g

            def gen_prev_tail():
                if ti == 0:
                    return
                tp, pp = tiles[ti - 1], (ti - 1) % 2
                yield from tail_gen(E, tp, E.hbuf[pp], ("hbuf", pp), E.pbuf[pp], ("pbuf", pp))
                P.dma(R=[(("hbuf", pp), s_) for s_ in range(4)], W=[("hA", ti - 1)],
                      out=hA[tp.row0:tp.row0 + tp.T, :].rearrange("(n p) d -> p n d", p=128),
                      in_=E.hbuf[pp][:, 0:tp.NT, :], is_output=cfg.debug)
                if ti + 1 < len(tiles):
                    p1_load(ti + 1)
                yield

            g1, g2 = gen_conv(), chain(gen_prev_tail(), gen_A(), gen_bz())
            alive = [g1, g2]
            while alive:
                for g in list(alive):
                    try:
                        next(g)
                    except StopIteration:
                        alive.remove(g)
            if t.last:
                state_out(E, t, ub, ukey, HA, L, na_p if isp else na_s)
            else:
                for j in range(4):
                    P.pool("tensor_copy", R=[(ukey, j)], W=[(ukey, j)], out=ub[j][:, :, 0:HA], in_=ub[j][:, :, L:L + HA])

            if t.last:
                state_out(E, t, gb, gkey, HB, L, nb_p if isp else nb_s)
            else:
                for j in range(4):
                    P.pool("tensor_copy", R=[(gkey, j)], W=[(gkey, j)], out=gb[j][:, :, 0:HB], in_=gb[j][:, :, L:L + HB])
            sqs = []
            for j in range(4):
                sq, sqk = tmp(E)
                P.act("activation", R=[("bconv", j)], W=[sqk], out=sq[:, 0:T], in_=bconv[j][:, 0:T], func=AF.Square)
                sqs.append((sq, sqk))
            for j in range(4):
                P.pe("matmul", R=[("bconv", j), "onesf"], W=[PS(6)], out=psf[6][:, 0:T], lhsT=onesf,
                     rhs=bconv[j][:, 0:T], start=(j == 0), stop=(j == 3))
            for j in range(4):
                sq, sqk = sqs[j]
                P.pe("matmul", R=[sqk, "onesf"], W=[PS(7)], out=psf[7][:, 0:T], lhsT=onesf, rhs=sq[:, 0:T],
                     start=(j == 0), stop=(j == 3))
            P.act("activation", R=[PS(6)], W=["negmean"], out=negmean[:, 0:T], in_=psf[6][:, 0:T], func=AF.Copy,
                  scale=-1.0 / DB)
            P.dve("tensor_tensor", R=["negmean"], W=["rstdB"], out=rstdB[:, 0:T], in0=negmean[:, 0:T],
                  in1=negmean[:, 0:T], op=ALU.mult)
            P.dve("scalar_tensor_tensor", R=[PS(7), "rstdB"], W=["rstdB"], out=rstdB[:, 0:T], in0=psf[7][:, 0:T],
                  scalar=1.0 / DB, in1=rstdB[:, 0:T], op0=ALU.mult, op1=ALU.subtract)
            P.act("activation", R=["rstdB"], W=["rstdB"], out=rstdB[:, 0:T], in_=rstdB[:, 0:T], func=AF.Ln, bias=EPS)
            P.act("activation", R=["rstdB"], W=["rstdB"], out=rstdB[:, 0:T], in_=rstdB[:, 0:T], func=AF.Exp, scale=-0.5)
            for pair in ((0, 1), (2, 3)):
                yvs = {}
                for j in pair:
                    yvs[j] = tmp(E)
                    P.dve("tensor_tensor", R=[("bconv", j), "negmean"], W=[yvs[j][1]], out=yvs[j][0][:, 0:T],
                          in0=bconv[j][:, 0:T], in1=negmean[:, 0:T], op=ALU.add)
                for j in pair:
                    P.dve("tensor_tensor", R=[yvs[j][1], "rstdB"], W=[yvs[j][1]], out=yvs[j][0][:, 0:T],
                          in0=yvs[j][0][:, 0:T], in1=rstdB[:, 0:T], op=ALU.mult)
                sgs = sigmoid_multi(E, [(yvs[j][0][:, 0:T], yvs[j][1], lncol[:, j, 2:3], lncol[:, j, 3:4], ["lncol"])
                                        for j in pair], T)
                for k_, j in enumerate(pair):
                    sgy, sgyk = sgs[k_]
                    P.dve("tensor_scalar", R=[yvs[j][1], "lncol", sgyk], W=[yvs[j][1]], out=yvs[j][0][:, 0:T],
                          in0=yvs[j][0][:, 0:T], scalar1=lncol[:, j, 0:1], scalar2=lncol[:, j, 1:2], op0=ALU.mult,
                          op1=ALU.add)
                for k_, j in enumerate(pair):
                    sgy, sgyk = sgs[k_]
                    P.dve("tensor_tensor", R=[yvs[j][1], sgyk], W=[yvs[j][1]], out=yvs[j][0][:, 0:T],
                          in0=yvs[j][0][:, 0:T], in1=sgy[:, 0:T], op=ALU.mult)
                for k_, j in enumerate(pair):
                    P.dve("tensor_tensor", R=[yvs[j][1]] + CATK, W=CATK, out=E.catT[:, 4 + j, 0:T],
                          in0=yvs[j][0][:, 0:T], in1=E.catT[:, 4 + j, 0:T], op=ALU.mult)

            if ti == 0 and len(tiles) > 1:
                p1_load(1)
            if ti + 1 < len(tiles):
                rmsnorm_to_T(E, tiles[ti + 1], E.hbuf[(ti + 1) % 2], ("hbuf", (ti + 1) % 2))
            else:
                tail(E, t, hb, hkey, pb, pkey)
                P.dma(R=[(hkey, s_) for s_ in range(4)], W=[("hA", ti)],
                      out=hA[t.row0:t.row0 + T, :].rearrange("(n p) d -> p n d", p=128),
                      in_=hb[:, 0:NT, :], is_output=cfg.debug)
        P.barrier()
        AR.off = base_off


    if 2 in cfg.passes:
        CAP = max(S, PL)
        E = common_bufs(n_h=1, n_tmp=4)
        E.w_big = AR.alloc([128, 8, 7 * DB], BF16)
        kT = AR.alloc([128, 4, CAP], BF16)
        Vc = AR.alloc([128, CAP // 128, DB], BF16)
        qT = AR.alloc([128, 4, 512], BF16)
        cvn = AR.alloc([128, 4, DB], BF16)
        kst = [AR.alloc([128, DB]) for _ in range(3)]
        kst_i = [0]
        kbf = [AR.alloc([128, DB], BF16) for _ in range(2)]
        spb = [AR.alloc([128, 512], BF16) for _ in range(4)]
        abf = [AR.alloc([128, 512], BF16) for _ in range(4)]
        rr = [0, 0, 0]
        sacc = [AR.alloc([128, 512], BF16) for _ in range(2)]
        g_bc = AR.alloc([128, DB])
        b_bc = AR.alloc([128, DB])
        WT = AR.alloc([128, 4, 128], BF16)
        WTs = AR.alloc([128, 4, 128], BF16)
        negU = AR.alloc([128, 128], BF16)
        negOnes = AR.alloc([128, 128], BF16)
        ones_row = AR.alloc([1, 128])
        cb_row = AR.alloc([1, 4, 128])
        cb_row_s = AR.alloc([1, 4, 128])
        lnst = AR.alloc([128, 4, 8])
        kT_new = AR.alloc([128, 4, 128], BF16)
        if CAP - PL >= 2048:
            Vn = [kT[0:32, q, PL:PL + DB] for q in range(4)]
            dzs = kT[:, 0, PL + DB:PL + DB + 1024].bitcast(F32).rearrange("p (a b) -> p a b", a=4)
        else:
            Vn = [AR.alloc([32, DB], BF16) for _ in range(4)]
            dzs = AR.alloc([128, 4, 128])
        print("P2 SBUF bytes/partition:", AR.off * 4)

        def stage():
            i = kst_i[0]
            kst_i[0] = (i + 1) % len(kst)
            return kst[i], ("kst", i)

        load_weight(E, E.w_big, "w_big", cd_w_in, 8, 7 * DB, gain_i=1)
        P.dma(W=["g_bc"], out=g_bc, in_=c_ln_g.partition_broadcast(128))
        P.dma(W=["b_bc"], out=b_bc, in_=c_ln_b.partition_broadcast(128))
        P.dma(W=["cb_row"], out=cb_row, in_=c_b.rearrange("(o h) t -> o h t", o=1))
        for q in range(NS):
            P.dma(W=["cb_row_s"], out=cb_row_s[:, :, q * DS:(q + 1) * DS],
                  in_=c_b[:, 0:DS].rearrange("(o h) t -> o h t", o=1))
        P.pool("memset", W=["ones_row"], ap=ones_row, constant=1.0)
        P.pool("memset", W=["negOnes"], ap=negOnes, constant=-1.0)
        tU, tUk = tmp(E)
        P.pool("memset", W=[tUk], ap=tU[:, 0:128], constant=-1.0)
        P.pool("affine_select", R=[tUk], W=[tUk], out=tU[:, 0:128], in_=tU[:, 0:128], pattern=[[-1, 128]],
               compare_op=ALU.is_ge, fill=0.0, base=0, channel_multiplier=1)
        P.dve("tensor_copy", R=[tUk], W=["negU"], out=negU, in_=tU[:, 0:128])
        for variant, dstW, dk in ((0, WT, "WT"), (1, WTs, "WTs")):
            for hh in range(4):
                wt_, wk = tmp(E)
                if variant == 0:
                    P.dma(W=[wk], out=wt_[:, 0:128], in_=c_ws[hh])
                else:
                    P.pool("memset", W=[wk], ap=wt_[:, 0:128], constant=0.0)
                    for q in range(NS):
                        P.dma(W=[wk], out=wt_[q * DS:(q + 1) * DS, q * DS:(q + 1) * DS], in_=c_ws[hh, 0:DS, 0:DS])
                P.pool("affine_select", R=[wk], W=[wk], out=wt_[:, 0:128], in_=wt_[:, 0:128], pattern=[[-1, 128]],
                       compare_op=ALU.is_ge, fill=0.0, base=0, channel_multiplier=1)
                P.act("activation", R=[wk], W=[("xnb", 0)], out=E.xnb[0][:, 0:128], in_=wt_[:, 0:128], func=AF.Copy)
                b = bank()
                P.pe("transpose", R=[("xnb", 0), "identb"], W=[PS(b)], out=psb[b][:, 0:128], in_=E.xnb[0][:, 0:128],
                     identity=identb)
                P.dve("tensor_copy", R=[PS(b)], W=[dk], out=dstW[:, hh, :], in_=psb[b][:, 0:128])

        def tri_mask(buf, key, nk, c0):
            P.pool("affine_select", R=[key], W=[key], out=buf[0:nk, c0:c0 + nk], in_=buf[0:nk, c0:c0 + nk],
                   pattern=[[1, nk]], compare_op=ALU.is_gt, fill=0.0, base=0, channel_multiplier=-1)

        def attention(hp, qa, qb, blocks):
            for h in range(2):
                P.pool("memset", W=[("sacc", h)], ap=sacc[h][:, qa:qb], constant=0.0)
            nb = len(blocks)
            items = [(bi, h) for bi in range(nb) for h in range(2)]
            n = len(items)
            st = {}

            def Sz(i):
                bi, h = items[i]
                kTa, Va, nk, c0, tri = blocks[bi]
                r0 = 64 * h
                bz = rr[2] % 6
                rr[2] += 1
                P.pe("matmul", R=["kT", "qT"], W=[PS(bz)], out=psf[bz][0:nk, c0:qb], lhsT=kTa[r0:r0 + 64, 0:nk],
                     rhs=qT[r0:r0 + 64, hp, c0:qb], start=True, stop=True)
                st[i] = {"bz": bz}

            def Se(i):
                bi, h = items[i]
                kTa, Va, nk, c0, tri = blocks[bi]
                bz = st[i]["bz"]
                e_, ek = tmp(E)
                P.act("activation", R=[PS(bz)], W=[ek], out=e_[0:nk, c0:qb], in_=psf[bz][0:nk, c0:qb], func=AF.Exp)
                st[i]["e"] = (e_, ek)

            def Sl(i):
                bi, h = items[i]
                kTa, Va, nk, c0, tri = blocks[bi]
                e_, ek = st[i]["e"]
                si = rr[0] % 4
                rr[0] += 1
                sp_, spk = spb[si], ("spb", si)
                P.act("activation", R=[ek], W=[spk], out=sp_[0:nk, c0:qb], in_=e_[0:nk, c0:qb], func=AF.Ln, bias=1.0)
                if tri:
                    tri_mask(sp_, spk, nk, c0)
                st[i]["sp"] = (sp_, spk)

            def Sw(i):
                bi, h = items[i]
                kTa, Va, nk, c0, tri = blocks[bi]
                bz = st[i]["bz"]
                sp_, spk = st[i]["sp"]
                P.pe("matmul", R=[spk, "negU"], W=[PS(bz)], out=psf[bz][0:nk, c0:qb], lhsT=negU[0:nk, 0:nk],
                     rhs=sp_[0:nk, c0:qb], start=False, stop=(bi == 0), skip_group_check=True)
                if bi > 0:
                    P.pe("matmul", R=[("sacc", h), "negOnes"], W=[PS(bz)], out=psf[bz][0:nk, c0:qb],
                         lhsT=negOnes[:, 0:nk], rhs=sacc[h][:, c0:qb], start=False, stop=True, skip_group_check=True)
                if bi < nb - 1:
                    P.dve("tensor_tensor", R=[spk, ("sacc", h)], W=[("sacc", h)], out=sacc[h][0:nk, c0:qb],
                          in0=sacc[h][0:nk, c0:qb], in1=sp_[0:nk, c0:qb], op=ALU.add)

            def Sa(i):
                bi, h = items[i]
                kTa, Va, nk, c0, tri = blocks[bi]
                bz = st[i]["bz"]
                ai = rr[1] % 4
                rr[1] += 1
                a_, ak = abf[ai], ("abf", ai)
                P.act("activation", R=[PS(bz)], W=[ak], out=a_[0:nk, c0:qb], in_=psf[bz][0:nk, c0:qb], func=AF.Exp)
                if tri:
                    tri_mask(a_, ak, nk, c0)
                if c0 > qa:
                    P.pool("memset", W=[ak], ap=a_[0:nk, qa:c0], constant=0.0)
                st[i]["a"] = (a_, ak)

            def Sv(i):
                bi, h = items[i]
                kTa, Va, nk, c0, tri = blocks[bi]
                a_, ak = st.pop(i)["a"]
                P.pe("matmul", R=[ak, "Vc"], W=[PS(6 + h)], out=psf[6 + h][:, qa:qb], lhsT=Va[0:nk, :],
                     rhs=a_[0:nk, qa:qb], start=(bi == 0), stop=(bi == nb - 1))

            for k in range(-3, n):
                if 0 <= k + 3 < n:
                    Sz(k + 3)
                if 0 <= k + 2 < n:
                    Se(k + 2)
                if 0 <= k + 1 < n:
                    Sl(k + 1)
                    Sw(k + 1)
                if 0 <= k < n:
                    Sa(k)
                    Sv(k)

        def attention_prompt(T, kb0, NT):
            nkb = kb0 + NT
            blk = []
            for kb in range(nkb - 1, -1, -1):
                i = kb - kb0
                blk.append((kb, 128, max(i, 0) * 128, i >= 0))
            nb = len(blk)
            items = [(hp, bi, h) for hp in range(4) for bi in range(nb) for h in range(2)]
            n = len(items)
            st = {}
            qa, qb = 0, T

            def obank(hp, h):
                return 4 + 2 * (hp % 2) + h

            def Sz(i):
                hp, bi, h = items[i]
                kb, nk, c0, tri = blk[bi]
                r0 = 64 * h
                bz = rr[2] % 4
                rr[2] += 1
                P.pe("matmul", R=["kT", "qT"], W=[PS(bz)], out=psf[bz][0:nk, c0:qb],
                     lhsT=kT[r0:r0 + 64, hp, kb * 128:kb * 128 + nk], rhs=qT[r0:r0 + 64, hp, c0:qb], start=True, stop=True)
                st[i] = {"bz": bz}

            def Se(i):
                hp, bi, h = items[i]
                kb, nk, c0, tri = blk[bi]
                bz = st[i]["bz"]
                e_, ek = tmp(E)
                P.act("activation", R=[PS(bz)], W=[ek], out=e_[0:nk, c0:qb], in_=psf[bz][0:nk, c0:qb], func=AF.Exp)
                st[i]["e"] = (e_, ek)

            def Sl(i):
                hp, bi, h = items[i]
                kb, nk, c0, tri = blk[bi]
                e_, ek = st[i]["e"]
                si = rr[0] % 4
                rr[0] += 1
                sp_, spk = spb[si], ("spb", si)
                P.act("activation", R=[ek], W=[spk], out=sp_[0:nk, c0:qb], in_=e_[0:nk, c0:qb], func=AF.Ln, bias=1.0)
                if tri:
                    tri_mask(sp_, spk, nk, c0)
                st[i]["sp"] = (sp_, spk)

            def Sw(i):
                hp, bi, h = items[i]
                kb, nk, c0, tri = blk[bi]
                bz = st[i]["bz"]
                sp_, spk = st[i]["sp"]
                if bi == 0 and h == 0:
                    for h2 in range(2):
                        P.pool("memset", W=[("sacc", h2)], ap=sacc[h2][:, qa:qb], constant=0.0)
                P.pe("matmul", R=[spk, "negU"], W=[PS(bz)], out=psf[bz][0:nk, c0:qb], lhsT=negU[0:nk, 0:nk],
                     rhs=sp_[0:nk, c0:qb], start=False, stop=(bi == 0), skip_group_check=True)
                if bi > 0:
                    P.pe("matmul", R=[("sacc", h), "negOnes"], W=[PS(bz)], out=psf[bz][0:nk, c0:qb],
                         lhsT=negOnes[:, 0:nk], rhs=sacc[h][:, c0:qb], start=False, stop=True, skip_group_check=True)
                if bi < nb - 1:
                    P.dve("tensor_tensor", R=[spk, ("sacc", h)], W=[("sacc", h)], out=sacc[h][0:nk, c0:qb],
                          in0=sacc[h][0:nk, c0:qb], in1=sp_[0:nk, c0:qb], op=ALU.add)

            def Sa(i):
                hp, bi, h = items[i]
                kb, nk, c0, tri = blk[bi]
                bz = st[i]["bz"]
                ai = rr[1] % 4
                rr[1] += 1
                a_, ak = abf[ai], ("abf", ai)
                P.act("activation", R=[PS(bz)], W=[ak], out=a_[0:nk, c0:qb], in_=psf[bz][0:nk, c0:qb], func=AF.Exp)
                if tri:
                    tri_mask(a_, ak, nk, c0)
                if c0 > qa:
                    P.pool("memset", W=[ak], ap=a_[0:nk, qa:c0], constant=0.0)
                st[i]["a"] = (a_, ak)

            def Sv(i):
                hp, bi, h = items[i]
                kb, nk, c0, tri = blk[bi]
                a_, ak = st.pop(i)["a"]
                ob = obank(hp, h)
                P.pe("matmul", R=[ak, "Vc"], W=[PS(ob)], out=psf[ob][:, qa:qb], lhsT=Vc[0:nk, kb, hp * 128:(hp + 1) * 128],
                     rhs=a_[0:nk, qa:qb], start=(bi == 0), stop=(bi == nb - 1))
                if bi == nb - 1:
                    r0 = 64 * h
                    P.dve("tensor_tensor", R=[PS(ob), "catT"], W=["catT"], out=E.catT[r0:r0 + 64, 4 + hp, qa:qb],
                          in0=psf[ob][r0:r0 + 64, qa:qb], in1=E.catT[r0:r0 + 64, 4 + hp, qa:qb], op=ALU.mult)

            for k in range(-3, n):
                if 0 <= k + 3 < n:
                    Sz(k + 3)
                if 0 <= k + 2 < n:
                    Se(k + 2)
                if 0 <= k + 1 < n:
                    Sl(k + 1)
                    Sw(k + 1)
                if 0 <= k < n:
                    Sa(k)
                    Sv(k)

        def p2_load(ti):
            t = tiles[ti]
            P.dma(W=[(("hbuf", 0), s_) for s_ in range(4)], out=E.hbuf[0][:, 0:t.NT, :],
                  in_=hA[t.row0:t.row0 + t.T, :].rearrange("(n p) d -> p n d", p=128), R=[("hA", ti)])

        p2_load(0)
        for ti, t in enumerate(tiles):
            if cfg.stop == 1:
                break
            T, NT, nseg, L = t.T, t.NT, t.nseg, t.L
            isp = t.kind == "p"
            hb, hkey = E.hbuf[0], ("hbuf", 0)
            if ti == 0:
                rmsnorm_to_T(E, t, hb, hkey)
                p2_load(1)
            kb0 = t.t0 // 128

            for s in range(NT):
                sl = slice(s * 128, (s + 1) * 128)
                rows = slice(t.row0 + s * 128, t.row0 + (s + 1) * 128)

                def proj_tm(c0):
                    b = bank()
                    for kc in range(8):
                        P.pe("matmul", R=[("actT", s), "w_big"], W=[PS(b)], out=psf[b], lhsT=E.actT[:, kc, sl],
                             rhs=E.w_big[:, kc, c0:c0 + DB], start=(kc == 0), stop=(kc == 7))
                    return b

                b_cv, b_k, b_v = proj_tm(DB), proj_tm(4 * DB), proj_tm(5 * DB)
                ls, lsk = lnst[:, s, :], ("lnst", s)
                cf, cfk = tmp(E)
                P.pool("memset", W=[lsk], ap=ls, constant=0.0)
                P.act("activation", R=[PS(b_cv)], W=[cfk, lsk], out=cf, in_=psf[b_cv], func=AF.Copy, accum_out=ls[:, 0:1])
                jk, jkk = tmp(E)
                P.act("activation", R=[PS(b_cv)], W=[jkk, lsk], out=jk, in_=psf[b_cv], func=AF.Square,
                      accum_out=ls[:, 1:2])
                st_, stk = stage()
                P.dve("tensor_copy", R=[PS(b_k)], W=[stk], out=st_, in_=psf[b_k])
                P.dma(R=[stk], out=(nk_p[rows, :] if isp else nk_s), in_=st_, is_output=True)
                kb_ = kbf[s % 2]
                P.dve("tensor_copy", R=[PS(b_k)], W=[("kbf", s % 2)], out=kb_, in_=psf[b_k])
                if isp:
                    transpose_into(kb_, ("kbf", s % 2), 4, kT, "kT", kb0 + s)
                else:
                    transpose_into(kb_, ("kbf", s % 2), 4, kT_new, "kT_new", 0)
                st_, stk = stage()
                P.dve("tensor_copy", R=[PS(b_v)], W=[stk], out=st_, in_=psf[b_v])
                P.dma(R=[stk], out=(nv_p[rows, :] if isp else nv_s), in_=st_, is_output=True)
                if isp:
                    P.dve("tensor_copy", R=[PS(b_v)], W=["Vc"], out=Vc[:, kb0 + s, :], in_=psf[b_v])
                else:
                    P.act("activation", R=[PS(b_v)], W=[("kbf", 1)], out=kbf[1], in_=psf[b_v], func=AF.Copy)
                    for q in range(NS):
                        P.dma(R=[("kbf", 1)], W=[("Vn", q)], out=Vn[q][0:DS, :], in_=kbf[1][q * DS:(q + 1) * DS, :])
                P.dve("tensor_scalar", R=[lsk], W=[lsk], out=ls[:, 2:3], in0=ls[:, 0:1], scalar1=1.0 / DB,
                      scalar2=None, op0=ALU.mult)
                P.dve("tensor_tensor", R=[lsk], W=[lsk], out=ls[:, 3:4], in0=ls[:, 2:3], in1=ls[:, 2:3],
                      op=ALU.mult)
                P.dve("scalar_tensor_tensor", R=[lsk], W=[lsk], out=ls[:, 4:5], in0=ls[:, 1:2],
                      scalar=1.0 / DB, in1=ls[:, 3:4], op0=ALU.mult, op1=ALU.subtract)
                P.act("activation", R=[lsk], W=[lsk], out=ls[:, 5:6], in_=ls[:, 4:5], func=AF.Ln, bias=EPS)
                P.act("activation", R=[lsk], W=[lsk], out=ls[:, 5:6], in_=ls[:, 5:6], func=AF.Exp, scale=-0.5)
                P.dve("tensor_scalar", R=[cfk, lsk], W=[cfk], out=cf, in0=cf, scalar1=ls[:, 2:3],
                      scalar2=ls[:, 5:6], op0=ALU.subtract, op1=ALU.mult)
                P.dve("tensor_tensor", R=[cfk, "g_bc"], W=[cfk], out=cf, in0=cf, in1=g_bc, op=ALU.mult)
                if isp:
                    P.dve("tensor_tensor", R=[cfk, "b_bc"], W=["cvn"], out=cvn[:, s, :], in0=cf, in1=b_bc, op=ALU.add)
                else:
                    P.dve("tensor_tensor", R=[cfk, "b_bc"], W=[cfk], out=cf, in0=cf, in1=b_bc, op=ALU.add)
                    P.act("activation", R=[cfk], W=["cvn"], out=cvn[:, s, :], in_=cf, func=AF.Copy)
                    P.dma(R=[cfk], out=ncv_s, in_=cf, is_output=True)

            if cfg.stop in (2, 21, 22, 23, 24, 25, 26, 27, 28):
                continue
            Wm, Wk = (WT, "WT") if isp else (WTs, "WTs")
            cbr, cbk = (cb_row, "cb_row") if isp else (cb_row_s, "cb_row_s")
            for hh in range(4):
                bm = bank()
                for s in range(NT):
                    sl = slice(s * 128, (s + 1) * 128)
                    P.pe("matmul", R=["cvn", Wk], W=[PS(bm)], out=psf[bm][:, sl], lhsT=cvn[:, s, hh * 128:(hh + 1) * 128],
                         rhs=Wm[:, hh, :], start=True, stop=False)
                    P.pe("matmul", R=["ones_row", cbk], W=[PS(bm)], out=psf[bm][:, sl], lhsT=ones_row[0:1, :],
                         rhs=cbr[0:1, hh, :], start=False, stop=True)
                bu, bzc = bank(), bank()
                proj_fm(E, bu, E.w_big, 0 + hh, T)
                proj_fm(E, bzc, E.w_big, 8 + hh, T)
                sg, sgk = sigmoid_from(E, psf[bzc][:, 0:T], PS(bzc), T)
                P.dve("tensor_tensor", R=[PS(bzc), sgk], W=[sgk], out=sg[:, 0:T], in0=psf[bzc][:, 0:T], in1=sg[:, 0:T],
                      op=ALU.mult)
                P.dve("tensor_tensor", R=[PS(bm), sgk], W=[sgk], out=sg[:, 0:T], in0=psf[bm][:, 0:T], in1=sg[:, 0:T],
                      op=ALU.mult)
                P.dve("tensor_tensor", R=[PS(bu), sgk], W=["catT"], out=E.catT[:, hh, 0:T], in0=psf[bu][:, 0:T],
                      in1=sg[:, 0:T], op=ALU.mult)

            if cfg.stop == 3:
                continue
            for hp in range(4):
                bq = bank()
                proj_fm(E, bq, E.w_big, 12 + hp, T)
                P.dve("tensor_scalar", R=[PS(bq)], W=["qT"], out=qT[:, hp, 0:T], in0=psf[bq][:, 0:T], scalar1=0.125,
                      scalar2=None, op0=ALU.mult)

            def silu_dz(hp, dst, dkey):
                bd = bank()
                proj_fm(E, bd, E.w_big, 24 + hp, T)
                sg, sgk = sigmoid_from(E, psf[bd][:, 0:T], PS(bd), T)
                P.dve("tensor_tensor", R=[PS(bd), sgk], W=[dkey], out=dst, in0=psf[bd][:, 0:T], in1=sg[:, 0:T],
                      op=ALU.mult)

            def finalize(hp, qa, qb, m1, m1k):
                for h in range(2):
                    r0 = 64 * h
                    P.dve("tensor_tensor", R=[PS(6 + h), m1k], W=["catT"], out=E.catT[r0:r0 + 64, 4 + hp, qa:qb],
                          in0=psf[6 + h][r0:r0 + 64, qa:qb], in1=m1[r0:r0 + 64, qa:qb], op=ALU.mult)

            if cfg.stop == 4 or (cfg.stop == 5 and not isp):
                continue
            if isp:
                for hp in range(4):
                    silu_dz(hp, E.catT[:, 4 + hp, 0:T], "catT")
                rmsnorm_to_T(E, tiles[ti + 1], hb, hkey)
                if ti + 2 < len(tiles):
                    p2_load(ti + 2)
                attention_prompt(T, kb0, NT)
            else:
                for hp in range(4):
                    silu_dz(hp, dzs[:, hp, :], "dzs")
                npast = PL // 128
                for q in range(NS):
                    for kb in range(npast):
                        st_, stk = stage()
                        P.dma(W=[stk], out=st_, in_=ck[q, kb * 128:(kb + 1) * 128, :])
                        P.act("activation", R=[stk], W=[("kbf", kb % 2)], out=kbf[kb % 2], in_=st_, func=AF.Copy)
                        transpose_into(kbf[kb % 2], ("kbf", kb % 2), 4, kT, "kT", kb)
                        st_, stk = stage()
                        P.dma(W=[stk], out=st_, in_=cv[q, kb * 128:(kb + 1) * 128, :])
                        P.dve("tensor_copy", R=[stk], W=["Vc"], out=Vc[:, kb, :], in_=st_)
                    qa, qb = q * DS, (q + 1) * DS
                    for hp in range(4):
                        blocks = [(kT_new[:, hp, qa:qb], Vn[q][0:DS, hp * 128:(hp + 1) * 128], DS, qa, True)]
                        for kb in range(npast - 1, -1, -1):
                            blocks.append((kT[:, hp, kb * 128:(kb + 1) * 128], Vc[:, kb, hp * 128:(hp + 1) * 128], 128,
                                           qa, False))
                        attention(hp, qa, qb, blocks)
                        finalize(hp, qa, qb, dzs[:, hp, :], "dzs")
            P.dma(R=["catT"], W=[("catD", ti)], out=catD[:, t.row0:t.row0 + T].rearrange("(c p) t -> p c t", p=128),
                  in_=E.catT[:, :, 0:T], is_output=cfg.debug)
        P.barrier()
        AR.off = base_off

    if 3 in cfg.passes:
        E = common_bufs(n_h=2, n_tmp=6)
        E.catT2 = [E.catT, AR.alloc([128, 8, 512], BF16)]
        E.pbuf = [AR.alloc([128, 4, PLE]) for _ in range(2)]
        E.pT = AR.alloc([128, 2, 512], BF16)
        E.xnp = [AR.alloc([128, PLE], BF16) for _ in range(2)]
        E.w_out = AR.alloc([128, 8, D], BF16)
        E.w_gate = AR.alloc([128, 8, D], BF16)
        E.w_proj = AR.alloc([128, 2, D], BF16)
        fg_bc = AR.alloc([128, D])
        print("P3 SBUF bytes/partition:", AR.off * 4)
        load_weight(E, E.w_out, "w_out", cd_w_out, 8, D)
        load_weight(E, E.w_gate, "w_gate", ple_gate[1], 8, D)
        load_weight(E, E.w_proj, "w_proj", ple_proj[1], 2, D)
        P.dma(W=["fg_bc"], out=fg_bc, in_=final_g.partition_broadcast(128))

        def p3_load(ti):
            t = tiles[ti]
            par = ti % 2
            P.dma(W=[(("hbuf", par), s_) for s_ in range(4)], R=[("hA", ti)], out=E.hbuf[par][:, 0:t.NT, :],
                  in_=hA[t.row0:t.row0 + t.T, :].rearrange("(n p) d -> p n d", p=128))
            P.dma(W=[("pbuf", par)], out=E.pbuf[par][:, 0:t.NT, :],
                  in_=p_rows(t, 1).rearrange("(n p) d -> p n d", p=128))
            P.dma(W=[(("catT", par), s_) for s_ in range(4)], R=[("catD", ti)], out=E.catT2[par][:, :, 0:t.T],
                  in_=catD[:, t.row0:t.row0 + t.T].rearrange("(c p) t -> p c t", p=128))

        p3_load(0)
        for ti, t in enumerate(tiles):
            T, NT = t.T, t.NT
            par = ti % 2
            hb, hkey = E.hbuf[par], ("hbuf", par)
            pb, pkey = E.pbuf[par], ("pbuf", par)
            if ti + 1 < len(tiles):
                p3_load(ti + 1)
            E.catT = E.catT2[par]
            tail(E, t, hb, hkey, pb, pkey, catkey=("catT", par))
            P.pool("memset", W=["ss"], ap=E.ss, constant=0.0)
            for s in range(NT):
                P.act("activation", R=[(hkey, s)], W=[("xnb", s % 2), "ss"], out=E.xnb[s % 2], in_=hb[:, s, :],
                      func=AF.Square, accum_out=E.ss[:, s:s + 1])
            P.act("activation", R=["ss"], W=["rstd"], out=E.rstd[:, 0:NT], in_=E.ss[:, 0:NT], func=AF.Ln,
                  scale=1.0 / D, bias=EPS)
            P.act("activation", R=["rstd"], W=["rstd"], out=E.rstd[:, 0:NT], in_=E.rstd[:, 0:NT], func=AF.Exp,
                  scale=-0.5)
            for s in range(NT):
                P.dve("scalar_tensor_tensor", R=[(hkey, s), "rstd", "fg_bc"], W=[(hkey, s)], out=hb[:, s, :], in0=hb[:, s, :],
                      scalar=E.rstd[:, s:s + 1], in1=fg_bc, op0=ALU.mult, op1=ALU.mult)
            dst = y_p[t.row0:t.row0 + T, :] if t.kind == "p" else y_s
            P.dma(R=[(hkey, s_) for s_ in range(4)], out=dst.rearrange("(n p) d -> p n d", p=128), in_=hb[:, 0:NT, :],
                  is_output=True)
        P.barrier()
        AR.off = base_off

    P.emit()
    global LAST_PROG
    LAST_PROG = P
    return nc


LAST_PROG = None


def shard_inputs(inp, cfg, n_cores):
    NP, S, NS, DS, PL = cfg.NP, cfg.S, cfg.NS, cfg.DS, cfg.PL
    f = lambda a: np.ascontiguousarray(np.asarray(a, dtype=np.float32))
    maps = []
    for c in range(n_cores):
        ps = slice(c * NP, (c + 1) * NP)
        ss = slice(c * NS, (c + 1) * NS)
        m = {
            "x_prompt": f(inp["x_prompt"][ps]).reshape(NP * S, D),
            "x_sample": f(inp["x_sample"][ss]).reshape(NS * DS, D),
            "state_a_conv": f(inp["state_a_conv"][0, ss]),
            "state_b_conv": f(inp["state_b_conv"][0, ss]),
            "cache_d_k": f(inp["cache_d_k"][0, ss]).reshape(NS, PL, DB),
            "cache_d_v": f(inp["cache_d_v"][0, ss]).reshape(NS, PL, DB),
            "p_prompt": f(inp["p_prompt"][:, ps]).reshape(2, NP * S, PLE),
            "p_sample": f(inp["p_sample"][:, ss]).reshape(2, NS * DS, PLE),
            "norm_g": f(inp["norm_g"]),
            "ple_gate": f(inp["ple_gate"]),
            "ple_proj": f(inp["ple_proj"]),
            "ab_w_in": f(inp["ab_w_in"][0]),
            "a_conv_w": f(inp["a_conv_w"][0]),
            "b_conv_w": f(inp["b_conv_w"][0]),
            "b_ln_g": f(inp["b_ln_g"][0]),
            "b_ln_b": f(inp["b_ln_b"][0]),
            "ab_w_out": f(inp["ab_w_out"][0]),
            "cd_w_in": f(inp["cd_w_in"][0]),
            "c_ln_g": f(inp["c_ln_g"][0]),
            "c_ln_b": f(inp["c_ln_b"][0]),
            "c_ws": f(inp["c_ws"][0]),
            "c_b": f(inp["c_b"][0]),
            "cd_w_out": f(inp["cd_w_out"][0]),
            "final_g": f(inp["final_g"]),
        }
        maps.append(m)
    return maps


def kernel(**inputs):
    n_cores = 8
    cfg = Cfg(NP=2, S=4096, NS=4, DS=32, PL=2048)
    nc = build_program(cfg)
    in_maps = shard_inputs(inputs, cfg, n_cores)
    res = run_bass_kernel_spmd(nc, in_maps, core_ids=list(range(n_cores)))
    r = res.results
    NP, S, NS, DS = cfg.NP, cfg.S, cfg.NS, cfg.DS
    cat = lambda name, shp: np.concatenate([np.asarray(r[c][name], dtype=np.float32).reshape(shp) for c in range(n_cores)], axis=0)
    y_prompt = cat("y_prompt", (NP, S, D))
    y_sample = cat("y_sample", (NS, DS, D))
    na_p = cat("new_a_prompt", (NP, HA, DB))[None]
    na_s = cat("new_a_sample", (NS, HA, DB))[None]
    nb_p = cat("new_b_prompt", (NP, HB, DB))[None]
    nb_s = cat("new_b_sample", (NS, HB, DB))[None]
    ncv = cat("new_cv_sample", (NS, DS, DB))[None]
    nk_p = cat("new_k_prompt", (NP, S, 8, 64))[None]
    nv_p = cat("new_v_prompt", (NP, S, 8, 64))[None]
    nk_s = cat("new_k_sample", (NS, DS, 8, 64))[None]
    nv_s = cat("new_v_sample", (NS, DS, 8, 64))[None]
    return (y_prompt, y_sample, na_p, na_s, nb_p, nb_s, ncv, nk_p, nv_p, nk_s, nv_s)
```
